# Optimizing a Trainium2 kernel written in Bass

```python
import math
import jax, jax.numpy as jnp
from jax import lax
import numpy as np

D_MODEL = 1024
BATCH = 2
SEQ = 8192
DEPTH = 2

GRID_W = 64
CTX_LEN = 256
N_EVEN = (DEPTH + 1) // 2
N_ODD = DEPTH // 2
EPS = 1e-6
ROPE_BASE = 10000.0
BLOCK = 128

F_GROUPS = 4
F_GROUP_DIM = 128
F_WIDTH = F_GROUPS * F_GROUP_DIM

MLA_HEADS = 8
MLA_NOPE = 64
MLA_ROPE = 32
MLA_V = 64
MLA_Q_RANK = 384
MLA_KV_RANK = 256
MLA_WIDTH = MLA_HEADS * MLA_V
MLA_SCALE = 1.0 / math.sqrt(MLA_NOPE + MLA_ROPE)
EVEN_SPLITS = (F_WIDTH, F_WIDTH, MLA_Q_RANK, MLA_KV_RANK, MLA_ROPE, MLA_WIDTH)
EVEN_IN = sum(EVEN_SPLITS)
KV_COL0 = 2 * F_WIDTH + MLA_Q_RANK
KV_COL1 = KV_COL0 + MLA_KV_RANK + MLA_ROPE
EVEN_OUT = F_WIDTH + MLA_WIDTH

GQA_HEADS = 16
GQA_KV_HEADS = 4
GQA_GROUP = GQA_HEADS // GQA_KV_HEADS
GQA_HEAD_DIM = 64
WINDOW = 128
GQA_Q = GQA_HEADS * GQA_HEAD_DIM
GQA_KV = GQA_KV_HEADS * GQA_HEAD_DIM
GQA_WIDTH = GQA_Q
GQA_SCALE = 1.0 / math.sqrt(GQA_HEAD_DIM)
ODD_SPLITS = (GQA_Q, GQA_KV, GQA_KV, GQA_WIDTH)
ODD_IN = sum(ODD_SPLITS)

kernel_name = "hybrid_fnet_mla_swa_dit_block"


def split_cols(p, sizes):
    out = []
    start = 0
    for s in sizes:
        out.append(p[..., start:start + s])
        start += s
    return out


def rmsnorm(x, g):
    x32 = x.astype(jnp.float32)
    y = x32 * lax.rsqrt(jnp.mean(x32 * x32, axis=-1, keepdims=True) + EPS)
    return (y * g.astype(jnp.float32)).astype(x.dtype)


def axial_rope_tables(n, rot_dim):
    rows = n // GRID_W
    row = jnp.broadcast_to(jnp.arange(rows)[:, None], (rows, GRID_W)).reshape(-1).astype(jnp.float32)
    col = jnp.broadcast_to(jnp.arange(GRID_W)[None, :], (rows, GRID_W)).reshape(-1).astype(jnp.float32)
    nf = rot_dim // 4
    inv = ROPE_BASE ** (-jnp.arange(nf, dtype=jnp.float32) / nf)
    ang = jnp.concatenate([row[:, None] * inv, col[:, None] * inv], axis=-1)
    return jnp.cos(ang), jnp.sin(ang)


def apply_rope(x, cos, sin):
    extra = x.ndim - 3
    shp = (cos.shape[0],) + (1,) * extra + (cos.shape[1],)
    cs, sn = cos.reshape(shp), sin.reshape(shp)
    x32 = x.astype(jnp.float32)
    half = x.shape[-1] // 2
    x1, x2 = x32[..., :half], x32[..., half:]
    return jnp.concatenate([x1 * cs - x2 * sn, x2 * cs + x1 * sn], axis=-1).astype(x.dtype)


def dense_attention(q, k, v, scale):
    s = jnp.einsum('bqhd,bkhd->bhqk', q, k).astype(jnp.float32) * scale
    p = jax.nn.softmax(s, axis=-1).astype(v.dtype)
    return jnp.einsum('bhqk,bkhd->bqhd', p, v)


def blocked_dense_attention(q, k, v, scale):
    b, n, h, d = q.shape
    nb = n // BLOCK
    qb = q.reshape(b, nb, BLOCK, h, d).swapaxes(0, 1)
    o = lax.map(lambda qblk: dense_attention(qblk, k, v, scale), qb)
    return o.swapaxes(0, 1).reshape(b, n, h, v.shape[-1])


def fourier_mix(u):
    b, n, _ = u.shape
    f = jnp.fft.fft2(u.reshape(b, n, F_GROUPS, F_GROUP_DIM).astype(jnp.float32), axes=(1, 3), norm='ortho')
    return f.real.reshape(b, n, F_WIDTH).astype(u.dtype)


def mla_q(q_a, q_norm, w_qb, rope):
    b, n = q_a.shape[:2]
    q = (rmsnorm(q_a, q_norm) @ w_qb).reshape(b, n, MLA_HEADS, MLA_NOPE + MLA_ROPE)
    q_nope, q_pe = q[..., :MLA_NOPE], q[..., MLA_NOPE:]
    if rope is not None:
        q_pe = apply_rope(q_pe, *rope)
    return jnp.concatenate([q_nope, q_pe], axis=-1)


def mla_kv(kv_a, k_pe, kv_norm, w_kvb, rope):
    b, n = kv_a.shape[:2]
    kv = (rmsnorm(kv_a, kv_norm) @ w_kvb).reshape(b, n, MLA_HEADS, MLA_NOPE + MLA_V)
    k_nope, v = kv[..., :MLA_NOPE], kv[..., MLA_NOPE:]
    k_pe = k_pe[:, :, None, :]
    if rope is not None:
        k_pe = apply_rope(k_pe, *rope)
    k = jnp.concatenate([k_nope, jnp.broadcast_to(k_pe, (b, n, MLA_HEADS, MLA_ROPE))], axis=-1)
    return k, v


def even_layer(h_lat, h_ctx, w_in, q_norm, w_qb, kv_norm, w_kvb, w_out, rope, ctx_out):
    b, n, _ = h_lat.shape
    f_in, f_gate, q_a, kv_a, k_pe, m_gate = split_cols(h_lat @ w_in, EVEN_SPLITS)
    if ctx_out:
        cf_in, cf_gate, cq_a, ckv_a, ck_pe, cm_gate = split_cols(h_ctx @ w_in, EVEN_SPLITS)
    else:
        ckv_a, ck_pe = split_cols(h_ctx @ w_in[:, KV_COL0:KV_COL1], (MLA_KV_RANK, MLA_ROPE))
    k_c, v_c = mla_kv(ckv_a, ck_pe, kv_norm, w_kvb, None)
    k_l, v_l = mla_kv(kv_a, k_pe, kv_norm, w_kvb, rope)
    q_l = mla_q(q_a, q_norm, w_qb, rope)
    a_l = blocked_dense_attention(q_l, jnp.concatenate([k_c, k_l], axis=1),
                                  jnp.concatenate([v_c, v_l], axis=1), MLA_SCALE)
    y_l = jnp.concatenate([fourier_mix(f_in) * jax.nn.silu(f_gate),
                           a_l.reshape(b, n, MLA_WIDTH) * jax.nn.silu(m_gate)], axis=-1) @ w_out
    if not ctx_out:
        return y_l, None
    cb, cn, _ = h_ctx.shape
    q_c = mla_q(cq_a, q_norm, w_qb, None)
    a_c = dense_attention(q_c, k_c, v_c, MLA_SCALE)
    y_c = jnp.concatenate([fourier_mix(cf_in) * jax.nn.silu(cf_gate),
                           a_c.reshape(cb, cn, MLA_WIDTH) * jax.nn.silu(cm_gate)], axis=-1) @ w_out
    return y_l, y_c


def windowed_gqa(q, k, v, kc, vc, sink):
    b, n = q.shape[:2]
    nb = n // BLOCK
    pad = ((0, 0), (BLOCK, BLOCK), (0, 0), (0, 0))

    def band(t):
        tp = jnp.pad(t, pad).reshape(b, nb + 2, BLOCK, *t.shape[2:])
        return jnp.concatenate([tp[:, :-2], tp[:, 1:-1], tp[:, 2:]], axis=2)

    kb, vb = band(k), band(v)
    qb = q.reshape(b, nb, BLOCK, GQA_KV_HEADS, GQA_GROUP, GQA_HEAD_DIM)
    s_ctx = jnp.einsum('bnqhgd,bchd->bnhgqc', qb, kc).astype(jnp.float32) * GQA_SCALE
    s_band = jnp.einsum('bnqhgd,bnkhd->bnhgqk', qb, kb).astype(jnp.float32) * GQA_SCALE
    blk = jnp.arange(nb)[:, None, None] * BLOCK
    qpos = blk + jnp.arange(BLOCK)[None, :, None]
    kpos = blk - BLOCK + jnp.arange(3 * BLOCK)[None, None, :]
    valid = (jnp.abs(qpos - kpos) <= WINDOW) & (kpos >= 0) & (kpos < n)
    s_band = jnp.where(valid[None, :, None, None], s_band, -jnp.inf)
    s_sink = jnp.broadcast_to(sink[None, None, :, :, None, None], s_ctx.shape[:-1] + (1,))
    p = jax.nn.softmax(jnp.concatenate([s_ctx, s_band, s_sink], axis=-1), axis=-1).astype(v.dtype)
    nc = kc.shape[1]
    o = (jnp.einsum('bnhgqc,bchd->bnqhgd', p[..., :nc], vc)
         + jnp.einsum('bnhgqk,bnkhd->bnqhgd', p[..., nc:nc + 3 * BLOCK], vb))
    return o.reshape(b, n, GQA_WIDTH)


def ctx_gqa(q, k, v, sink):
    b, n = q.shape[:2]
    s = jnp.einsum('bqhgd,bchd->bhgqc', q, k).astype(jnp.float32) * GQA_SCALE
    s_sink = jnp.broadcast_to(sink[None, :, :, None, None], s.shape[:-1] + (1,))
    p = jax.nn.softmax(jnp.concatenate([s, s_sink], axis=-1), axis=-1)[..., :-1].astype(v.dtype)
    return jnp.einsum('bhgqc,bchd->bqhgd', p, v).reshape(b, n, GQA_WIDTH)


def odd_layer(h_lat, h_ctx, w_in, sink, w_out, rope, ctx_out):
    b, n, _ = h_lat.shape
    cb, cn, _ = h_ctx.shape
    q, k, v, g = split_cols(h_lat @ w_in, ODD_SPLITS)
    q = apply_rope(q.reshape(b, n, GQA_KV_HEADS, GQA_GROUP, GQA_HEAD_DIM), *rope)
    k = apply_rope(k.reshape(b, n, GQA_KV_HEADS, GQA_HEAD_DIM), *rope)
    v = v.reshape(b, n, GQA_KV_HEADS, GQA_HEAD_DIM)
    if ctx_out:
        cq, ck, cv, cg = split_cols(h_ctx @ w_in, ODD_SPLITS)
    else:
        ck, cv = split_cols(h_ctx @ w_in[:, GQA_Q:GQA_Q + 2 * GQA_KV], (GQA_KV, GQA_KV))
    ck = ck.reshape(cb, cn, GQA_KV_HEADS, GQA_HEAD_DIM)
    cv = cv.reshape(cb, cn, GQA_KV_HEADS, GQA_HEAD_DIM)
    sink_hg = sink.reshape(GQA_KV_HEADS, GQA_GROUP).astype(jnp.float32)
    y_l = (windowed_gqa(q, k, v, ck, cv, sink_hg) * jax.nn.silu(g)) @ w_out
    if not ctx_out:
        return y_l, None
    cq = cq.reshape(cb, cn, GQA_KV_HEADS, GQA_GROUP, GQA_HEAD_DIM)
    y_c = (ctx_gqa(cq, ck, cv, sink_hg) * jax.nn.silu(cg)) @ w_out
    return y_l, y_c


def setup_inputs(seed: int = 0) -> dict:
    key = jax.random.key(seed)
    ks = jax.random.split(key, 17)

    def nrm(k, shape, scale):
        return jax.random.normal(k, shape, jnp.float32) * scale

    return {
        'x': nrm(ks[0], (BATCH, SEQ, D_MODEL), 1.0),
        'c': nrm(ks[1], (BATCH, D_MODEL), 1.0),
        'ctx': nrm(ks[2], (BATCH, CTX_LEN, D_MODEL), 1.0),
        'c_ctx': nrm(ks[3], (D_MODEL,), 1.0),
        'w_mod': nrm(ks[4], (DEPTH, D_MODEL, 3 * D_MODEL), 0.5 * D_MODEL ** -0.5),
        'b_mod': nrm(ks[5], (DEPTH, 3 * D_MODEL), 0.02),
        'norm_g': 1.0 + nrm(ks[6], (DEPTH, D_MODEL), 0.02),
        'e_w_in': nrm(ks[7], (N_EVEN, D_MODEL, EVEN_IN), D_MODEL ** -0.5),
        'e_q_norm': 1.0 + nrm(ks[8], (N_EVEN, MLA_Q_RANK), 0.02),
        'e_w_qb': nrm(ks[9], (N_EVEN, MLA_Q_RANK, MLA_HEADS * (MLA_NOPE + MLA_ROPE)), MLA_Q_RANK ** -0.5),
        'e_kv_norm': 1.0 + nrm(ks[10], (N_EVEN, MLA_KV_RANK), 0.02),
        'e_w_kvb': nrm(ks[11], (N_EVEN, MLA_KV_RANK, MLA_HEADS * (MLA_NOPE + MLA_V)), MLA_KV_RANK ** -0.5),
        'e_w_out': nrm(ks[12], (N_EVEN, EVEN_OUT, D_MODEL), EVEN_OUT ** -0.5),
        'o_w_in': nrm(ks[13], (N_ODD, D_MODEL, ODD_IN), D_MODEL ** -0.5),
        'o_sink': nrm(ks[14], (N_ODD, GQA_HEADS), 0.5),
        'o_w_out': nrm(ks[15], (N_ODD, GQA_WIDTH, D_MODEL), GQA_WIDTH ** -0.5),
        'final_g': 1.0 + nrm(ks[16], (D_MODEL,), 0.02),
    }


def reference(x, c, ctx, c_ctx, w_mod, b_mod, norm_g, e_w_in, e_q_norm, e_w_qb, e_kv_norm,
              e_w_kvb, e_w_out, o_w_in, o_sink, o_w_out, final_g):
    n = x.shape[1]
    rope_mla = axial_rope_tables(n, MLA_ROPE)
    rope_gqa = axial_rope_tables(n, GQA_HEAD_DIM)
    silu_c = jax.nn.silu(c)
    silu_cc = jax.nn.silu(c_ctx)
    for l in range(DEPTH):
        last = l == DEPTH - 1
        mod_l = silu_c @ w_mod[l] + b_mod[l]
        mod_c = silu_cc @ w_mod[l] + b_mod[l]
        sh_l, sc_l, g_l = jnp.split(mod_l, 3, axis=-1)
        sh_c, sc_c, g_c = jnp.split(mod_c, 3, axis=-1)
        h_l = rmsnorm(x, norm_g[l]) * (1 + sc_l[:, None]) + sh_l[:, None]
        h_c = rmsnorm(ctx, norm_g[l]) * (1 + sc_c) + sh_c
        i = l // 2
        if l % 2 == 0:
            y_l, y_c = even_layer(h_l, h_c, e_w_in[i], e_q_norm[i], e_w_qb[i], e_kv_norm[i],
                                  e_w_kvb[i], e_w_out[i], rope_mla, not last)
        else:
            y_l, y_c = odd_layer(h_l, h_c, o_w_in[i], o_sink[i], o_w_out[i], rope_gqa, not last)
        x = x + g_l[:, None] * y_l
        if not last:
            ctx = ctx + g_c * y_c
    return rmsnorm(x, final_g)
```

```python
import math
import numpy as np
import ml_dtypes
import concourse.bass as bass
import concourse.mybir as mybir
from concourse.bass_utils import run_bass_kernel_spmd

F32 = mybir.dt.float32
BF16 = mybir.dt.bfloat16
ALU = mybir.AluOpType
AF = mybir.ActivationFunctionType
AX = mybir.AxisListType


class Tile:
    __slots__ = ("t", "writers", "readers", "name")

    def __init__(self, t, name=""):
        self.t = t
        self.writers = []
        self.readers = []
        self.name = name

    def __getitem__(self, idx):
        return self.t[idx]


class Eng:
    def __init__(self, mk, name, eng, sem):
        self.mk, self.name, self.eng, self.sem = mk, name, eng, sem
        self.count = 0
        self.seen = {}


class MK:
    N_DMA_SEMS = 24

    def __init__(self, nc):
        self.nc = nc
        self.engs = {}
        for name, eng in (("pe", nc.tensor), ("act", nc.scalar), ("dve", nc.vector),
                          ("pool", nc.gpsimd), ("sp", nc.sync)):
            self.engs[name] = Eng(self, name, eng, nc.alloc_semaphore("s_" + name))
        self.dma_sems = [nc.alloc_semaphore("d%d" % i) for i in range(2 * self.N_DMA_SEMS)]
        self.dma_cnt = [0] * (2 * self.N_DMA_SEMS)
        self.dma_i = {"sw": 0, "hw": 0}
        self.ntile = 0

    def sb(self, shape, dt=BF16, name=None):
        self.ntile += 1
        name = name or "t%d" % self.ntile
        return Tile(self.nc.alloc_sbuf_tensor(name + "_%d" % self.ntile, list(shape), dt), name)

    def ps(self, shape, dt=F32, name=None):
        self.ntile += 1
        name = name or "p%d" % self.ntile
        return Tile(self.nc.alloc_psum_tensor(name + "_%d" % self.ntile, list(shape), dt), name)

    def dram(self, shape, dt, name, kind="Internal"):
        return Tile(self.nc.dram_tensor(name, list(shape), dt, kind=kind), name)

    def _wait(self, E, tok):
        sem, val = tok
        key = sem.num
        if E.seen.get(key, 0) >= val:
            return
        E.seen[key] = val
        E.eng.wait_ge(sem, val)

    def _deps(self, E, reads, writes):
        toks = []
        for t in reads:
            toks += t.writers
        for t in writes:
            toks += t.writers
            toks += t.readers
        best = {}
        for sem, val in toks:
            if best.get(sem.num, (None, 0))[1] < val:
                best[sem.num] = (sem, val)
        for sem, val in best.values():
            if E.name == "pe" and sem.num == E.sem.num:
                continue
            self._wait(E, (sem, val))

    def _mark(self, tok, reads, writes):
        for t in reads:
            t.readers.append(tok)
            if len(t.readers) > 64:
                t.readers = _compress(t.readers)
        for t in writes:
            t.writers = [tok]
            t.readers = []

    def op(self, eng, fn, reads=(), writes=(), inc=True):
        E = self.engs[eng]
        self._deps(E, reads, writes)
        ins = fn(E.eng)
        if inc:
            E.count += 1
            ins.then_inc(E.sem, 1)
            tok = (E.sem, E.count)
        else:
            assert eng == "pe"
            tok = (E.sem, E.count + 1)
        self._mark(tok, reads, writes)
        return tok

    def dma(self, eng, out, in_, reads=(), writes=(), **kw):
        E = self.engs[eng]
        self._deps(E, reads, writes)
        pool = "sw" if eng == "pool" else "hw"
        i = self.dma_i[pool] % self.N_DMA_SEMS + (self.N_DMA_SEMS if pool == "sw" else 0)
        self.dma_i[pool] += 1
        sem = self.dma_sems[i]
        if self.dma_cnt[i] > 0:
            self._wait(E, (sem, 16 * self.dma_cnt[i]))
        self.dma_cnt[i] += 1
        E.eng.dma_start(out=out, in_=in_, **kw).then_inc(sem, 16)
        tok = (sem, 16 * self.dma_cnt[i])
        self._mark(tok, reads, writes)
        return tok

    def barrier(self):
        for E in self.engs.values():
            for F in self.engs.values():
                if F is not E and F.count > 0:
                    self._wait(E, (F.sem, F.count))
            for i, sem in enumerate(self.dma_sems):
                if self.dma_cnt[i] > 0:
                    self._wait(E, (sem, 16 * self.dma_cnt[i]))

    def finish(self):
        E = self.engs["sp"]
        for F in self.engs.values():
            if F is not E and F.count > 0:
                self._wait(E, (F.sem, F.count))
        for i, sem in enumerate(self.dma_sems):
            if self.dma_cnt[i] > 0:
                self._wait(E, (sem, 16 * self.dma_cnt[i]))


def _compress(toks):
    best = {}
    for sem, val in toks:
        if best.get(sem.num, (None, 0))[1] < val:
            best[sem.num] = (sem, val)
    return list(best.values())


D = 1024
SEQ = 8192
NB = 2
CTX = 256
NOWN = 2304
NQ = NOWN + CTX
NKEY = SEQ + CTX
EPS = 1e-6
MLA_SCALE = 1.0 / math.sqrt(96.0)
GQA_SCALE = 1.0 / 8.0
NEG = -30000.0


def _bf(a):
    return np.ascontiguousarray(np.asarray(a, dtype=np.float32)).astype(ml_dtypes.bfloat16)


def _f32(a):
    return np.ascontiguousarray(np.asarray(a, dtype=np.float32))


def _rope_tab(tok, nf):
    tok = np.asarray(tok, dtype=np.float64)
    row = np.floor(tok / 64.0)
    col = tok - 64.0 * row
    inv = 10000.0 ** (-np.arange(nf, dtype=np.float64) / nf)
    ang = np.concatenate([row[:, None] * inv, col[:, None] * inv], axis=-1)
    return np.cos(ang).T, np.sin(ang).T


def host_tables(q):
    T = {}
    T["ident"] = _bf(np.eye(128))
    T["identf"] = _f32(np.eye(128))
    n1 = np.arange(128)[:, None, None]
    n2 = np.arange(64)[None, :, None]
    k1 = np.arange(128)[None, None, :]
    th = 2 * np.pi * ((k1 * (64 * n1 + n2)) % SEQ) / SEQ
    T["T1"] = _bf(np.stack([np.cos(th), -np.sin(th)], axis=2))
    c = np.arange(128)[:, None]
    j = np.arange(128)[None, :]
    ph = 2 * np.pi * ((c * j) % 128) / 128
    Cc, Sc = np.cos(ph), np.sin(ph)
    T["CS"] = _bf(np.stack([np.concatenate([Cc, -Sc], 1), np.concatenate([Sc, Cc], 1)], 1))
    k2 = (16 * q - 1 + np.arange(18)) % 64
    ps = 2 * np.pi * ((np.arange(64)[:, None] * k2[None, :]) % 64) / 64
    f2 = np.stack([np.cos(ps), np.sin(ps)], 1) / 1024.0
    f2b = np.zeros((128, 2, 36))
    f2b[0:64, :, 0:18] = f2
    f2b[64:128, :, 18:36] = f2
    T["F2"] = _bf(f2b)
    n = (np.arange(2)[None, :, None] * 128 + np.arange(128)[:, None, None])
    k = np.arange(256)[None, None, :]
    a = 2 * np.pi * ((n * k) % 256) / 256
    sc = 1.0 / math.sqrt(256.0 * 128.0)
    T["DC"] = _bf(np.stack([np.cos(a) * sc, -np.sin(a) * sc], 2))
    tk = np.zeros((66, 2, 32, 128), np.float32)
    tk[0:2, 0] = 1.0
    for t in range(64):
        cs, sn = _rope_tab(64 * np.arange(128) + t, 8)
        tk[2 + t, 0] = np.concatenate([cs, cs], 0)
        tk[2 + t, 1] = np.concatenate([-sn, sn], 0)
    T["tabK"] = tk
    t0 = 2048 * q - 128
    cs, sn = _rope_tab(t0 + np.arange(NOWN), 8)
    tq = np.zeros((2, 32, NQ), np.float32)
    tq[0, :, NOWN:] = 1.0
    tq[0, :, :NOWN] = np.concatenate([cs, cs], 0)
    tq[1, :, :NOWN] = np.concatenate([-sn, sn], 0)
    T["tabQ"] = tq
    cs, sn = _rope_tab(t0 + np.arange(NOWN), 16)
    t1 = np.zeros((2, 128, NQ), np.float32)
    t1[0, :, NOWN:] = 1.0
    t1[0, :, :NOWN] = np.concatenate([cs, cs, cs, cs], 0)
    t1[1, :, :NOWN] = np.concatenate([-sn, sn, -sn, sn], 0)
    T["tab1"] = t1
    kp = np.arange(128)[:, None]
    qp = np.arange(128)[None, :]
    mprev = np.where(kp >= qp, 0.0, NEG)
    mnext = np.where(kp <= qp, 0.0, NEG)
    full = np.full((128, 128), NEG)
    zero = np.zeros((128, 128))
    rel = {-1: full, 0: mprev, 1: zero, 2: mnext, 3: full}
    T["maskb"] = _bf(np.stack([np.stack([np.stack([rel[d - j] for j in range(2)], 1) for _ in range(2)], 1) for d in range(4)], 1))
    tok = t0 + np.arange(NOWN)
    kv = np.where((tok >= 0) & (tok < SEQ), 0.0, NEG).reshape(18, 128).T
    T["kvalid"] = _f32(np.concatenate([kv, np.zeros((128, 2))], 1))
    sel = np.zeros((2, 2, 128), np.float32)
    sel[0, 0] = 1.0
    sel[1, 1] = 1.0
    T["sel"] = sel
    return T


def pack_weights(inp):
    W = {}
    w = np.asarray(inp["e_w_in"][0], np.float32)
    kpe = w[:, 1664:1696]
    W["w0in"] = _f32(np.concatenate([w, kpe[:, 16:32], kpe[:, 0:16]], 1))
    wq = np.asarray(inp["e_w_qb"][0], np.float32)
    sw = []
    for h in range(8):
        pe = wq[:, h * 96 + 64:h * 96 + 96]
        sw += [pe[:, 16:32], pe[:, 0:16]]
    W["wqb"] = _f32(np.concatenate([wq] + sw, 1))
    W["wkvb"] = _f32(inp["e_w_kvb"][0])
    W["wout0"] = _f32(inp["e_w_out"][0])
    w1 = np.asarray(inp["o_w_in"][0], np.float32)
    qs, k2, k2s = [], [], []
    for h in range(16):
        qh = w1[:, h * 64:(h + 1) * 64]
        qs += [qh[:, 32:64], qh[:, 0:32]]
    for h in range(4):
        kh = w1[:, 1024 + h * 64:1024 + (h + 1) * 64]
        k2 += [kh, kh]
        k2s += [kh[:, 32:64], kh[:, 0:32]] * 2
    W["w1in"] = _f32(np.concatenate([w1] + qs + k2 + k2s, 1))
    W["wout1"] = _f32(inp["o_w_out"][0])
    W["wmod"] = _f32(inp["w_mod"])
    W["bmod"] = _f32(inp["b_mod"])
    W["normg"] = _f32(inp["norm_g"])
    W["finalg"] = _f32(np.asarray(inp["final_g"], np.float32)[None, :])
    W["qnorm"] = _f32(np.asarray(inp["e_q_norm"][0], np.float32).reshape(3, 128).T)
    W["kvnorm"] = _f32(np.asarray(inp["e_kv_norm"][0], np.float32).reshape(2, 128).T)
    W["sink"] = _f32(inp["o_sink"])
    return W


def core_inputs(inp, W, b, q):
    m = dict(W)
    m.update(host_tables(q))
    x = np.asarray(inp["x"], np.float32)[b]
    m["xa"] = _f32(x.reshape(128, 64, D).transpose(1, 0, 2))
    t0 = 2048 * q - 128
    xo = np.zeros((NOWN, D), np.float32)
    lo, hi = max(t0, 0), min(t0 + NOWN, SEQ)
    xo[lo - t0:hi - t0] = x[lo:hi]
    m["xo"] = xo.reshape(18, 128, D)
    m["ctx"] = _f32(np.asarray(inp["ctx"], np.float32)[b].reshape(2, 128, D))
    cc = np.stack([np.asarray(inp["c"], np.float32)[b], np.asarray(inp["c_ctx"], np.float32)], 0)
    m["cT"] = _f32(cc.reshape(2, 8, 128).transpose(2, 1, 0))
    return m


from contextlib import ExitStack

IN_SPECS = [
    ("xa", [64, 128, D], F32), ("xo", [18, 128, D], F32), ("ctx", [2, 128, D], F32), ("cT", [128, 8, 2], F32),
    ("w0in", [D, 2240], F32), ("wqb", [384, 1024], F32), ("wkvb", [256, 1024], F32), ("wout0", [D, D], F32),
    ("w1in", [D, 4608], F32), ("wout1", [D, D], F32), ("wmod", [2, D, 3072], F32), ("bmod", [2, 3072], F32),
    ("normg", [2, D], F32), ("finalg", [1, D], F32), ("qnorm", [128, 3], F32), ("kvnorm", [128, 2], F32),
    ("sink", [1, 16], F32),
    ("ident", [128, 128], BF16), ("identf", [128, 128], F32), ("T1", [128, 64, 2, 128], BF16),
    ("CS", [128, 2, 256], BF16), ("F2", [128, 2, 36], BF16), ("DC", [128, 2, 2, 256], BF16),
    ("tabK", [66, 2, 32, 128], F32), ("tabQ", [2, 32, NQ], F32), ("tab1", [2, 128, NQ], F32),
    ("maskb", [128, 4, 2, 2, 128], BF16), ("kvalid", [128, 20], F32), ("sel", [2, 2, 128], F32),
]

QBLK = [(0, 512), (512, 512), (1024, 512), (1536, 512), (2048, 512)]


class Prog:
    def __init__(self, debug=False, upto=99):
        self.debug = debug
        self.upto = upto
        nc = self.nc = bass.Bass("TRN2", target_bir_lowering=False)
        mk = self.mk = MK(nc)
        self.din = {n: mk.dram(s, dt, n, "ExternalInput") for n, s, dt in IN_SPECS}
        self.out = mk.dram([16, 128, D], F32, "out", "ExternalOutput")
        dk = "ExternalOutput" if debug else "Internal"
        self.qan_d = mk.dram([3, 128, NQ], BF16, "qan_d", dk)
        self.fg_d = mk.dram([4, 128, NQ], BF16, "fg_d", dk)
        self.mg_d = mk.dram([4, 128, NQ], BF16, "mg_d", dk)
        self.kvn_d = mk.dram([2, 128, NKEY], BF16, "kvn_d", dk)
        self.kpe_d = mk.dram([32, NKEY], BF16, "kpe_d", dk)
        self.cat_d = mk.dram([8, 128, NQ], BF16, "cat_d", dk)
        self.x1_d = mk.dram([20, 128, D], F32, "x1_d", dk)
        self.pTs = [mk.ps([128, 1024], BF16, "pT%d" % i) for i in range(2)]
        self.P = [mk.ps([128, 512], F32, "P%d" % i) for i in range(6)]
        self.pi = 0
        self.held = set()
        self.es = ExitStack()
        self.cnt = 0
        P_ = self.sbp
        self.ident = P_([128, 128], BF16); self.onesb = P_([128, 128], BF16); self.onesf = P_([128, 64], F32)
        self.epsb = P_([128, 1], F32)
        self.G0 = {k: P_([128, D], F32) for k in ("Gl", "Gc")}
        self.bcs = {}
        mk.dma("sp", self.ident[:], self.din["ident"][:, :], writes=[self.ident])
        mk.op("pool", lambda e: e.memset(self.onesb[:], 1.0), writes=[self.onesb])
        mk.op("pool", lambda e: e.memset(self.onesf[:], 1.0), writes=[self.onesf])
        mk.op("pool", lambda e: e.memset(self.epsb[:], EPS), writes=[self.epsb])

    def sbp(self, shape, dt=BF16):
        self.cnt += 1
        return Tile(self.nc.alloc_sbuf_tensor("pers%d" % self.cnt, list(shape), dt))

    def sb(self, es, shape, dt=BF16):
        self.cnt += 1
        return Tile(es.enter_context(self.nc.sbuf_tensor("s%d" % self.cnt, list(shape), dt)))

    def bank(self, hold=False):
        while True:
            self.pi = (self.pi + 1) % 6
            if self.pi not in self.held:
                break
        if hold:
            self.held.add(self.pi)
        return self.P[self.pi]

    def release(self, p):
        self.held.discard(self.P.index(p))

    def wload(self, dst, dcol, src, r0, nk, c0, n):
        v = src.t.ap()[r0:r0 + nk * 128, c0:c0 + n].rearrange("(c p) n -> p c n", p=128)
        self.mk.dma("pool", dst[:, 0:nk, dcol:dcol + n], v, writes=[dst], max_dma_last_dim=4096)

    def alloc_bc(self, l, es_ab):
        bc = self.bcs[l] = {}
        for k in ("Al", "Bl", "Ac", "Bc", "Gl", "Gc"):
            if l == 0 and k[0] == "G":
                bc[k] = self.G0[k]
            else:
                bc[k] = self.sb(es_ab, [128, D], F32)

    def mod_steps(self, l, es_ab, es):
        mk, din = self.mk, self.din
        if l not in self.bcs:
            self.alloc_bc(l, es_ab)
        bc = self.bcs[l]
        wms = [self.sb(es, [128, 8, 512], F32) for _ in range(2)]
        cT = self.sb(es, [128, 16], F32); scT = self.sb(es, [128, 16], F32)
        bm = self.sb(es, [2, 3072], F32); g2 = self.sb(es, [2, D], F32); sel = self.sb(es, [2, 2, 128], F32)
        mrow = self.sb(es, [2, 3072], F32); arow = self.sb(es, [2, D], F32)
        steps = []

        def first():
            mk.dma("sp", cT[:], din["cT"].t.ap().rearrange("p c v -> p (c v)"), writes=[cT])
            mk.op("act", lambda e: e.activation(out=scT[:], in_=cT[:], func=AF.Silu), reads=[cT], writes=[scT])
            for r in range(2):
                mk.dma("sp", bm[r:r + 1, :], din["bmod"][l:l + 1, :], writes=[bm])
                mk.dma("sp", g2[r:r + 1, :], din["normg"][l:l + 1, :], writes=[g2])
            mk.dma("sp", sel[:], din["sel"].t.ap(), writes=[sel])
        steps.append(first)
        for nb in range(6):
            def ld(nb=nb):
                wm = wms[nb % 2]
                mk.dma("sp", wm[:], din["wmod"].t.ap()[l, :, nb * 512:(nb + 1) * 512].rearrange("(c p) n -> p c n", p=128), writes=[wm])
            def blk(nb=nb):
                wm = wms[nb % 2]
                p = self.bank()
                for kc in range(8):
                    mk.op("pe", lambda e: e.matmul(out=p[0:2, :], lhsT=scT[:, 2 * kc:2 * kc + 2],
                                                   rhs=wm[:, kc, :], start=(kc == 0), stop=(kc == 7)),
                          reads=[scT, wm], writes=[p], inc=(kc == 7))
                mk.op("dve", lambda e: e.tensor_tensor(out=mrow[:, nb * 512:(nb + 1) * 512], in0=p[0:2, :],
                                                       in1=bm[:, nb * 512:(nb + 1) * 512], op=ALU.add),
                      reads=[p, bm], writes=[mrow])
            steps.append(ld)
            steps.append(blk)
        order = [steps[0], steps[1]]
        for nb in range(6):
            if nb + 1 < 6:
                order.append(steps[1 + 2 * (nb + 1)])
            order.append(steps[2 + 2 * nb])

        def arow_f():
            mk.op("dve", lambda e: e.scalar_tensor_tensor(out=arow[:], in0=mrow[:, D:2 * D], scalar=1.0, in1=g2[:],
                                                          op0=ALU.add, op1=ALU.mult), reads=[mrow, g2], writes=[arow])
        order.append(arow_f)
        for which, sfx in ((0, "l"), (1, "c")):
            for key, src, off in (("A", arow, 0), ("B", mrow, 0), ("G", mrow, 2 * D)):
                def bcf(which=which, sfx=sfx, key=key, src=src, off=off):
                    dst = bc[key + sfx]
                    for half in range(2):
                        p = self.bank()
                        mk.op("pe", lambda e: e.matmul(out=p[:], lhsT=sel[:, which, :],
                                                       rhs=src[:, off + half * 512:off + (half + 1) * 512],
                                                       start=True, stop=True), reads=[sel, src], writes=[p])
                        mk.op("dve", lambda e: e.tensor_copy(out=dst[:, half * 512:(half + 1) * 512], in_=p[:]),
                              reads=[p], writes=[dst])
                order.append(bcf)
        return order

    def make_norm_scratch(self, es, n=3, nx=5):
        self.xts = [self.sb(es, [128, D], F32) for _ in range(nx)]
        self.ns = [dict(junk=self.sb(es, [128, D], BF16), ss=self.sb(es, [128, 1], F32),
                        rstd=self.sb(es, [128, 1], F32), tmp=self.sb(es, [128, D], F32), hb=self.sb(es, [128, D], BF16))
                   for _ in range(n)]

    def norm_s0(self, i, src_ap):
        xt = self.xts[i % len(self.xts)]
        self.mk.dma("sp", xt[:], src_ap, writes=[xt])

    def norm_s1(self, i, src_ap, A, B):
        mk = self.mk
        s = self.ns[i % len(self.ns)]
        xt = self.xts[i % len(self.xts)]
        junk, ss, rstd, tmp, hb = s["junk"], s["ss"], s["rstd"], s["tmp"], s["hb"]
        mk.op("act", lambda e: e.activation(out=junk[:], in_=xt[:], func=AF.Square, scale=1.0 / 32.0, accum_out=ss[:]),
              reads=[xt], writes=[junk, ss])
        mk.op("act", lambda e: e.activation(out=rstd[:], in_=ss[:], func=AF.Ln, bias=self.epsb[:], scale=1.0),
              reads=[ss, self.epsb], writes=[rstd])
        mk.op("act", lambda e: e.activation(out=rstd[:], in_=rstd[:], func=AF.Exp, scale=-0.5), reads=[rstd], writes=[rstd])
        mk.op("dve", lambda e: e.scalar_tensor_tensor(out=tmp[:], in0=xt[:], scalar=rstd[:, 0:1], in1=A[:],
                                                      op0=ALU.mult, op1=ALU.mult), reads=[xt, rstd, A], writes=[tmp])
        mk.op("pool", lambda e: e.tensor_tensor(out=hb[:], in0=tmp[:], in1=B[:], op=ALU.add), reads=[tmp, B], writes=[hb])

    def norm_s2(self, i, dst_tile, dst_ap):
        mk = self.mk
        hb = self.ns[i % len(self.ns)]["hb"]
        pT = self.pTs[i % 2]
        for c in range(8):
            mk.op("pe", lambda e: e.transpose(out=pT[:, c * 128:(c + 1) * 128], in_=hb[:, c * 128:(c + 1) * 128],
                                              identity=self.ident[:]), reads=[hb, self.ident], writes=[pT], inc=(c == 7))
        mk.op("act", lambda e: e.activation(out=dst_ap, in_=pT[:, :].rearrange("p (c n) -> p c n", c=8), func=AF.Copy),
              reads=[pT], writes=[dst_tile])

    def norm_pipe(self, n, src_fn, ab_fn, dst_fn, s3=None):
        for i in range(min(2, n)):
            self.norm_s0(i, src_fn(i))
        for t in range(n + 2):
            if t + 2 < n:
                self.norm_s0(t + 2, src_fn(t + 2))
            if t < n:
                A, B = ab_fn(t)
                self.norm_s1(t, src_fn(t), A, B)
            if 0 <= t - 1 < n:
                dt_, dap = dst_fn(t - 1)
                self.norm_s2(t - 1, dt_, dap)
            if s3 is not None and 0 <= t - 2 < n:
                s3(t - 2)

    def proj(self, p_ap, ptile, w, wcols, h, hcols, n_c=8):
        for c in range(n_c):
            self.mk.op("pe", lambda e: e.matmul(out=p_ap, lhsT=w[:, c, wcols[0]:wcols[1]], rhs=h[:, c, hcols[0]:hcols[1]],
                                                start=(c == 0), stop=(c == n_c - 1)), reads=[w, h], writes=[ptile], inc=(c == n_c - 1))

    def rstd_bcast(self, es_tiles, pss, n, dim, out_tile):
        mk = self.mk
        mk.op("act", lambda e: e.activation(out=out_tile[:, 0:n], in_=pss[:, 0:n], func=AF.Ln, bias=self.epsb[:], scale=1.0 / dim),
              reads=[pss, self.epsb], writes=[out_tile])
        mk.op("act", lambda e: e.activation(out=out_tile[:, 0:n], in_=out_tile[:, 0:n], func=AF.Exp, scale=-0.5),
              reads=[out_tile], writes=[out_tile])

    def phase_O(self):
        mk, din = self.mk, self.din
        with ExitStack() as es:
            hT = self.sb(es, [128, 8, NQ], BF16)
            wps = [self.sb(es, [128, 8, 512], BF16) for _ in range(3)]
            self.wload(wps[0], 0, din["w0in"], 0, 8, 512, 512)
            self.wload(wps[1], 0, din["w0in"], 0, 8, 1696, 512)
            self.wload(wps[2], 0, din["w0in"], 0, 8, 1024, 384)
            with ExitStack() as esn:
                self.make_norm_scratch(esn)
                self.norm_pipe(20, lambda i: din["xo"].t.ap()[i] if i < 18 else din["ctx"].t.ap()[i - 18],
                               lambda i: (self.bc["Al" if i < 18 else "Ac"], self.bc["Bl" if i < 18 else "Bc"]),
                               lambda i: (hT, hT[:, :, i * 128:(i + 1) * 128]))
                mk.barrier()
            stage = [self.sb(es, [128, 512], BF16) for _ in range(3)]
            si = 0
            for wp, dst in ((wps[0], self.fg_d), (wps[1], self.mg_d)):
                for j in range(4):
                    for b0, n in QBLK:
                        p = self.bank()
                        self.proj(p[:, 0:n], p, wp, (j * 128, (j + 1) * 128), hT, (b0, b0 + n))
                        st = stage[si % 3]; si += 1
                        mk.op("act", lambda e: e.activation(out=st[:, 0:n], in_=p[:, 0:n], func=AF.Silu), reads=[p], writes=[st])
                        mk.dma("sp", dst.t.ap()[j, :, b0:b0 + n], st[:, 0:n], reads=[st], writes=[dst])
            wp = wps[2]
            sq = [self.sb(es, [128, 512], BF16) for _ in range(3)]
            rq = self.sb(es, [128, 512], F32)
            for b0, n in QBLK:
                ps = [self.bank() for _ in range(3)]
                pss = self.bank()
                for j in range(3):
                    self.proj(ps[j][:, 0:n], ps[j], wp, (j * 128, (j + 1) * 128), hT, (b0, b0 + n))
                    mk.op("act", lambda e: e.activation(out=sq[j][:, 0:n], in_=ps[j][:, 0:n], func=AF.Square), reads=[ps[j]], writes=[sq[j]])
                for j in range(3):
                    mk.op("pe", lambda e: e.matmul(out=pss[:, 0:n], lhsT=self.onesb[:], rhs=sq[j][:, 0:n], start=(j == 0), stop=(j == 2)),
                          reads=[self.onesb, sq[j]], writes=[pss])
                self.rstd_bcast(None, pss, n, 384.0, rq)
                for j in range(3):
                    st = stage[si % 3]; si += 1
                    mk.op("dve", lambda e: e.tensor_tensor(out=st[:, 0:n], in0=ps[j][:, 0:n], in1=rq[:, 0:n], op=ALU.mult),
                          reads=[ps[j], rq], writes=[st])
                    mk.dma("sp", self.qan_d.t.ap()[j, :, b0:b0 + n], st[:, 0:n], reads=[st], writes=[self.qan_d])
            mk.barrier()

    def phase_A(self, es_outer):
        mk, din = self.mk, self.din
        with ExitStack() as es:
            self.make_norm_scratch(es)
            wA = self.sb(es, [128, 8, 832], BF16)
            self.wload(wA, 0, din["w0in"], 0, 8, 0, 512)
            self.wload(wA, 512, din["w0in"], 0, 8, 1408, 288)
            self.wload(wA, 800, din["w0in"], 0, 8, 2208, 32)
            hTs = [self.sb(es, [128, 8, 128], BF16) for _ in range(3)]
            sqkv = [self.sb(es, [128, 256], BF16) for _ in range(2)]
            rk = [self.sb(es, [128, 128], F32) for _ in range(2)]
            kst = [self.sb(es, [128, 2, 128], BF16) for _ in range(2)]
            tabk = [self.sb(es, [128, 2, 128], F32) for _ in range(3)]
            t1 = [self.sb(es, [128, 128], F32) for _ in range(2)]
            t2 = [self.sb(es, [128, 128], F32) for _ in range(2)]
            pst = [self.sb(es, [128, 128], BF16) for _ in range(2)]

            def s3(i):
                lat = i >= 2
                hT = hTs[i % 3]
                tb = tabk[i % 3]
                mk.dma("sp", tb[64:96, :, :], din["tabK"].t.ap()[i].rearrange("t p n -> p t n"), writes=[tb])
                pkv = self.bank()
                for j in range(2):
                    self.proj(pkv[:, j * 128:(j + 1) * 128], pkv, wA, (512 + j * 128, 640 + j * 128), hT, (0, 128))
                sq = sqkv[i % 2]
                mk.op("act", lambda e: e.activation(out=sq[:], in_=pkv[:, 0:256], func=AF.Square), reads=[pkv], writes=[sq])
                p = self.bank()
                self.mk_tokmajor(p, hT, wA, 0, 512)
                udst, uap = (self.u_all, self.u_all[:, i - 2, :]) if lat else (self.u_ctx, self.u_ctx[:, i, :])
                mk.op("dve", lambda e: e.tensor_copy(out=uap, in_=p[:]), reads=[p], writes=[udst])
                pk = self.bank()
                self.proj(pk[64:96, 0:128], pk, wA, (768, 800), hT, (0, 128))
                self.proj(pk[64:96, 128:256], pk, wA, (800, 832), hT, (0, 128))
                a1, a2 = t1[i % 2], t2[i % 2]
                mk.op("dve", lambda e: e.tensor_tensor(out=a1[64:96, :], in0=pk[64:96, 0:128], in1=tb[64:96, 0, :], op=ALU.mult),
                      reads=[pk, tb], writes=[a1])
                mk.op("dve", lambda e: e.tensor_tensor(out=a2[64:96, :], in0=pk[64:96, 128:256], in1=tb[64:96, 1, :], op=ALU.mult),
                      reads=[pk, tb], writes=[a2])
                ps_ = pst[i % 2]
                mk.op("pool", lambda e: e.tensor_tensor(out=ps_[64:96, :], in0=a1[64:96, :], in1=a2[64:96, :], op=ALU.add),
                      reads=[a1, a2], writes=[ps_])
                mk.dma("sp", self.kpe_d.t.ap()[:, i * 128:(i + 1) * 128], ps_[64:96, :], reads=[ps_], writes=[self.kpe_d])
                pss = self.bank()
                for j in range(2):
                    mk.op("pe", lambda e: e.matmul(out=pss[:, 0:128], lhsT=self.onesb[:], rhs=sq[:, j * 128:(j + 1) * 128],
                                                   start=(j == 0), stop=(j == 1)), reads=[self.onesb, sq], writes=[pss], inc=(j == 1))
                r_ = rk[i % 2]
                self.rstd_bcast(None, pss, 128, 256.0, r_)
                ks = kst[i % 2]
                for j in range(2):
                    mk.op("dve", lambda e: e.tensor_tensor(out=ks[:, j, :], in0=pkv[:, j * 128:(j + 1) * 128], in1=r_[:], op=ALU.mult),
                          reads=[pkv, r_], writes=[ks])
                mk.dma("sp", self.kvn_d.t.ap()[:, :, i * 128:(i + 1) * 128].rearrange("j p n -> p j n"), ks[:], reads=[ks], writes=[self.kvn_d])

            self.norm_pipe(66, lambda i: din["xa"].t.ap()[i - 2] if i >= 2 else din["ctx"].t.ap()[i],
                           lambda i: (self.bc["Al" if i >= 2 else "Ac"], self.bc["Bl" if i >= 2 else "Bc"]),
                           lambda i: (hTs[i % 3], hTs[i % 3][:, :, :]), s3=s3)
            mk.barrier()

    def mk_tokmajor(self, p, hT, w, c0, n):
        for c in range(8):
            self.mk.op("pe", lambda e: e.matmul(out=p[:, 0:n], lhsT=hT[:, c, :], rhs=w[:, c, c0:c0 + n], start=(c == 0), stop=(c == 7)),
                       reads=[hT, w], writes=[p], inc=(c == 7))

    def phase_F(self):
        mk, din = self.mk, self.din
        with ExitStack() as es:
            T1 = self.sb(es, [128, 64, 256], BF16)
            for k in range(4):
                mk.dma("sp", T1[:, k * 16:(k + 1) * 16, :], din["T1"].t.ap()[:, k * 16:(k + 1) * 16].rearrange("p a r k -> p a (r k)"), writes=[T1])
            CS = self.sb(es, [128, 2, 256], BF16); F2 = self.sb(es, [128, 2, 36], BF16); DC = self.sb(es, [128, 2, 512], BF16)
            mk.dma("sp", CS[:], din["CS"].t.ap(), writes=[CS])
            mk.dma("sp", F2[:], din["F2"].t.ap(), writes=[F2])
            mk.dma("sp", DC[:], din["DC"].t.ap().rearrange("p t r k -> p t (r k)"), writes=[DC])
            fg = self.sb(es, [128, 4, NQ], BF16)
            mk.dma("sp", fg[:], self.fg_d.t.ap().rearrange("j p n -> p j n"), reads=[self.fg_d], writes=[fg])
            A1 = self.sb(es, [128, 128, 128], BF16)
            A1w = A1[:, :, :].rearrange("p (a n) (r k) -> p n a r k", a=2, r=2)
            Gb = [self.sb(es, [128, 2, 256], BF16) for _ in range(3)]
            catF = [self.sb(es, [128, NOWN], BF16) for _ in range(2)]
            XT = self.sb(es, [128, 512], BF16); cst = self.sb(es, [128, 256], BF16)
            for g in range(4):
                for pr in range(32):
                    p = self.bank()
                    for s in range(2):
                        n2 = 2 * pr + s
                        mk.op("pe", lambda e: e.matmul(out=p[:, s * 256:(s + 1) * 256], lhsT=self.u_all[:, n2, g * 128:(g + 1) * 128],
                                                       rhs=T1[:, n2, :], start=True, stop=True), reads=[self.u_all, T1], writes=[p], inc=(s == 1))
                    eng = "dve" if pr % 2 == 0 else "act"
                    for s_ in range(2):
                        oap = A1w[:, 2 * pr + s_]
                        iap = p[:, s_ * 256:(s_ + 1) * 256].rearrange("p (r k a) -> p a r k", r=2, a=2)
                        if s_ == 0:
                            mk.op("dve", lambda e: e.tensor_copy(out=oap, in_=iap), reads=[p], writes=[A1])
                        else:
                            mk.op("act", lambda e: e.activation(out=oap, in_=iap, func=AF.Copy), reads=[p], writes=[A1])
                cf = catF[g % 2]
                cfv = cf[:, :].rearrange("p (k2 k1) -> p k1 k2", k1=128)
                fgv = fg[:, g, 0:NOWN].rearrange("p (k2 k1) -> p k1 k2", k1=128)
                fst = {"pacc": None, "a0": 0}

                def s3(pr, gb, fst=fst, cf=cf, cfv=cfv, fgv=fgv):
                    for s_ in range(2):
                        kp = 2 * pr + s_
                        k1 = 2 * kp
                        if k1 % 28 == 0:
                            fst["pacc"] = self.bank(hold=True); fst["a0"] = k1
                        pacc, a0 = fst["pacc"], fst["a0"]
                        sl = (k1 - a0) * 18
                        mk.op("pe", lambda e: e.matmul(out=pacc[:, sl:sl + 36], lhsT=gb[:, s_, 0:128], rhs=F2[:, 0, :], start=True, stop=False),
                              reads=[gb, F2], writes=[pacc], inc=False)
                        mk.op("pe", lambda e: e.matmul(out=pacc[:, sl:sl + 36], lhsT=gb[:, s_, 128:256], rhs=F2[:, 1, :], start=False, stop=True),
                              reads=[gb, F2], writes=[pacc])
                        if (k1 + 1) % 28 == 27 or k1 + 1 == 127:
                            cnt = k1 + 2 - a0
                            mk.op("dve", lambda e: e.tensor_tensor(out=cfv[:, a0:a0 + cnt, :], in0=pacc[:, 0:cnt * 18].rearrange("p (a b) -> p a b", b=18),
                                                                   in1=fgv[:, a0:a0 + cnt, :], op=ALU.mult), reads=[pacc, fg], writes=[cf])
                            self.release(pacc)

                prev = None
                for pr in range(32):
                    p = self.bank()
                    for s_ in range(2):
                        kp = 2 * pr + s_
                        mk.op("pe", lambda e: e.matmul(out=p[:, s_ * 256:(s_ + 1) * 256], lhsT=A1[:, :, kp], rhs=CS[:, 0, :], start=True, stop=False),
                              reads=[A1, CS], writes=[p], inc=False)
                        mk.op("pe", lambda e: e.matmul(out=p[:, s_ * 256:(s_ + 1) * 256], lhsT=A1[:, :, 64 + kp], rhs=CS[:, 1, :], start=False, stop=True),
                              reads=[A1, CS], writes=[p], inc=(s_ == 1))
                    gb = Gb[pr % 3]
                    if pr % 2 == 0:
                        mk.op("dve", lambda e: e.tensor_copy(out=gb[:, :, :], in_=p[:, :].rearrange("p (a b) -> p a b", a=2)), reads=[p], writes=[gb])
                    else:
                        mk.op("act", lambda e: e.activation(out=gb[:, :, :], in_=p[:, :].rearrange("p (a b) -> p a b", a=2), func=AF.Copy), reads=[p], writes=[gb])
                    if prev is not None:
                        s3(*prev)
                    prev = (pr, gb)
                s3(*prev)
                mk.dma("sp", self.cat_d.t.ap()[g, :, 0:NOWN], cf[:, :], reads=[cf], writes=[self.cat_d])
                px = self.bank()
                for nt in range(2):
                    mk.op("pe", lambda e: e.matmul(out=px[:, :], lhsT=self.u_ctx[:, nt, g * 128:(g + 1) * 128], rhs=DC[:, nt, :], start=(nt == 0), stop=(nt == 1)),
                          reads=[self.u_ctx, DC], writes=[px], inc=(nt == 1))
                mk.op("dve", lambda e: e.tensor_copy(out=XT[:], in_=px[:]), reads=[px], writes=[XT])
                py = self.bank()
                mk.op("pe", lambda e: e.matmul(out=py[:, 0:256], lhsT=CS[:, 0, 0:128], rhs=XT[:, 0:256], start=True, stop=False), reads=[CS, XT], writes=[py], inc=False)
                mk.op("pe", lambda e: e.matmul(out=py[:, 0:256], lhsT=CS[:, 1, 0:128], rhs=XT[:, 256:512], start=False, stop=True), reads=[CS, XT], writes=[py])
                mk.op("dve", lambda e: e.tensor_tensor(out=cst[:], in0=py[:, 0:256], in1=fg[:, g, NOWN:NQ], op=ALU.mult), reads=[py, fg], writes=[cst])
                mk.dma("sp", self.cat_d.t.ap()[g, :, NOWN:NQ], cst[:], reads=[cst], writes=[self.cat_d])
            mk.barrier()

    def wscaled(self, es, src, nk, ncols, normname):
        mk = self.mk
        w = self.sb(es, [128, nk, ncols], BF16)
        g = self.sb(es, [128, nk], F32)
        self.wload(w, 0, src, 0, nk, 0, ncols)
        mk.dma("sp", g[:], self.din[normname].t.ap(), writes=[g])
        for c in range(nk):
            mk.op("dve", lambda e: e.tensor_scalar(out=w[:, c, :], in0=w[:, c, :], scalar1=g[:, c:c + 1], scalar2=None, op0=ALU.mult),
                  reads=[w, g], writes=[w])
        return w

    def phase_ATT(self):
        mk, din = self.mk, self.din
        with ExitStack() as es:
            kvn = self.sb(es, [128, 2, NKEY], BF16)
            KTs = [self.sb(es, [128, NKEY], BF16) for _ in range(2)]
            VAs = [self.sb(es, [128, 66, 128], BF16) for _ in range(2)]
            qan = self.sb(es, [128, 3, NQ], BF16)
            QTs = [self.sb(es, [128, NQ], BF16) for _ in range(2)]
            tabq = self.sb(es, [128, 2, NQ], F32)
            mgs = [self.sb(es, [128, NQ], BF16) for _ in range(2)]
            ATs = [self.sb(es, [128, NQ], BF16)] * 2
            for j in range(2):
                mk.dma("sp", kvn[:, j, :], self.kvn_d.t.ap()[j], reads=[self.kvn_d], writes=[kvn])
            for KT in KTs:
                mk.dma("sp", KT[64:96, :], self.kpe_d.t.ap(), reads=[self.kpe_d], writes=[KT])
            mk.dma("sp", qan[:], self.qan_d.t.ap().rearrange("j p n -> p j n"), reads=[self.qan_d], writes=[qan])
            mk.dma("sp", tabq[64:96, :, :], din["tabQ"].t.ap().rearrange("t p n -> p t n"), writes=[tabq])
            wq = self.wscaled(es, din["wqb"], 3, 1024, "qnorm")
            wkv = self.wscaled(es, din["wkvb"], 2, 1024, "kvnorm")
            mk.op("dve", lambda e: e.memset(VAs[0][:, :, 64:128], 0.0), writes=[VAs[0]])
            mk.op("dve", lambda e: e.memset(VAs[0][:, :, 64:65], 1.0), writes=[VAs[0]])
            mk.op("dve", lambda e: e.memset(VAs[1][:, :, 0:64], 0.0), writes=[VAs[1]])
            mk.op("dve", lambda e: e.memset(VAs[1][:, :, 0:1], 1.0), writes=[VAs[1]])
            PTs = [self.sb(es, [128, 512], BF16) for _ in range(4)]
            t1 = self.sb(es, [128, 512], F32); t2 = self.sb(es, [128, 512], F32)
            rec = self.sb(es, [128, 512], F32); tmp = self.sb(es, [128, 512], F32)

            def gen_steps(h):
                odd = h % 2
                o0 = 64 if odd else 0
                KT, VA, QT = KTs[odd], VAs[odd], QTs[odd]
                steps = []
                if not odd:
                    mgt = mgs[(h // 2) % 2]
                    steps.append(lambda: mk.dma("sp", mgt[:], self.mg_d.t.ap()[h // 2], reads=[self.mg_d], writes=[mgt]))
                for kb in range(17):
                    def f(kb=kb):
                        k0 = kb * 512
                        n = min(512, NKEY - k0)
                        p = self.bank()
                        self.proj(p[0:64, 0:n], p, wkv, (h * 128, h * 128 + 64), kvn, (k0, k0 + n), n_c=2)
                        mk.op("dve", lambda e: e.tensor_copy(out=KT[0:64, k0:k0 + n], in_=p[0:64, 0:n]), reads=[p], writes=[KT])
                    steps.append(f)
                for g0 in range(0, 66, 8):
                    def f(g0=g0):
                        cnt = min(8, 66 - g0)
                        p = self.bank()
                        for t in range(cnt):
                            kt = g0 + t
                            for j in range(2):
                                mk.op("pe", lambda e: e.matmul(out=p[:, t * 64:(t + 1) * 64], lhsT=kvn[:, j, kt * 128:(kt + 1) * 128],
                                                               rhs=wkv[:, j, h * 128 + 64:h * 128 + 128], start=(j == 0), stop=(j == 1)),
                                      reads=[kvn, wkv], writes=[p], inc=(j == 1 and t == cnt - 1))
                        mk.op("dve", lambda e: e.tensor_copy(out=VA[:, g0:g0 + cnt, o0:o0 + 64], in_=p[:, 0:cnt * 64].rearrange("p (a b) -> p a b", b=64)),
                              reads=[p], writes=[VA])
                    steps.append(f)
                for b0, n in QBLK:
                    def f(b0=b0, n=n):
                        pa = self.bank(); pb = self.bank()
                        self.proj(pa[0:96, 0:n], pa, wq, (h * 96, h * 96 + 96), qan, (b0, b0 + n), n_c=3)
                        self.proj(pb[64:96, 0:n], pb, wq, (768 + h * 32, 800 + h * 32), qan, (b0, b0 + n), n_c=3)
                        mk.op("dve", lambda e: e.tensor_copy(out=QT[0:64, b0:b0 + n], in_=pa[0:64, 0:n]), reads=[pa], writes=[QT])
                        mk.op("dve", lambda e: e.tensor_tensor(out=t1[64:96, 0:n], in0=pa[64:96, 0:n], in1=tabq[64:96, 0, b0:b0 + n], op=ALU.mult),
                              reads=[pa, tabq], writes=[t1])
                        mk.op("dve", lambda e: e.tensor_tensor(out=t2[64:96, 0:n], in0=pb[64:96, 0:n], in1=tabq[64:96, 1, b0:b0 + n], op=ALU.mult),
                              reads=[pb, tabq], writes=[t2])
                        mk.op("pool", lambda e: e.tensor_tensor(out=QT[64:96, b0:b0 + n], in0=t1[64:96, 0:n], in1=t2[64:96, 0:n], op=ALU.add),
                              reads=[t1, t2], writes=[QT])
                    steps.append(f)
                return steps

            for f in gen_steps(0):
                f()
            items = []
            for h in range(8):
                for bi, (b0, n, kts) in enumerate([(0, 512, 66), (512, 512, 66), (1024, 512, 66), (1536, 512, 66), (2048, 256, 66), (2304, 256, 2)]):
                    for kt in range(kts):
                        items.append((h, b0, n, kt, kts))
            pending = []
            state = {}
            LAG = 2
            nit = len(items)
            per_head = nit // 8
            for t in range(nit + LAG):
                if t < nit:
                    h, b0, n, kt, kts = items[t]
                    odd = h % 2
                    if kt == 0:
                        state[(h, b0)] = self.bank(hold=True)
                    if t % per_head == 0 and h < 7:
                        pending = gen_steps(h + 1)
                        every = max(1, (per_head - 40) // len(pending))
                    if pending and (t % per_head) % every == 0:
                        pending.pop(0)()
                    ps = self.bank()
                    mk.op("pe", lambda e: e.matmul(out=ps[:, 0:n], lhsT=KTs[odd][0:96, kt * 128:(kt + 1) * 128], rhs=QTs[odd][0:96, b0:b0 + n], start=True, stop=True),
                          reads=[KTs[odd], QTs[odd]], writes=[ps])
                    pt = PTs[t % 4]
                    mk.op("act", lambda e: e.activation(out=pt[:, 0:n], in_=ps[:, 0:n], func=AF.Exp, scale=MLA_SCALE), reads=[ps], writes=[pt])
                if t >= LAG:
                    h, b0, n, kt, kts = items[t - LAG]
                    odd = h % 2
                    o0, r0 = (64, 0) if odd else (0, 64)
                    po = state[(h, b0)]
                    pt = PTs[(t - LAG) % 4]
                    mk.op("pe", lambda e: e.matmul(out=po[:, 0:n], lhsT=VAs[odd][:, kt, :], rhs=pt[:, 0:n], start=(kt == 0), stop=(kt == kts - 1)),
                          reads=[VAs[odd], pt], writes=[po], inc=(kt == kts - 1))
                    if kt == kts - 1:
                        mgt = mgs[(h // 2) % 2]; AT = ATs[(h // 2) % 2]
                        self.att_finish(po, n, o0, r0, mgt, mgt[o0:o0 + 64, b0:b0 + n], AT, AT[o0:o0 + 64, b0:b0 + n], rec, tmp)
                        self.release(po)
                        if odd and b0 == 2304:
                            mk.dma("sp", self.cat_d.t.ap()[4 + h // 2], AT[:], reads=[AT], writes=[self.cat_d])
            while pending:
                pending.pop(0)()
            mk.barrier()

    def att_finish(self, po, n, o0, r0, gate_tile, gate_ap, dst_tile, dst_ap, rec, tmp, extra=None):
        mk = self.mk
        if extra is not None:
            mk.op("dve", lambda e: e.tensor_tensor(out=rec[r0:r0 + 1, 0:n], in0=po[r0:r0 + 1, 0:n], in1=extra, op=ALU.add), reads=[po], writes=[rec])
            mk.op("act", lambda e: e.activation(out=rec[r0:r0 + 1, 0:n], in_=rec[r0:r0 + 1, 0:n], func=AF.Ln), reads=[rec], writes=[rec])
        else:
            mk.op("act", lambda e: e.activation(out=rec[r0:r0 + 1, 0:n], in_=po[r0:r0 + 1, 0:n], func=AF.Ln), reads=[po], writes=[rec])
        mk.op("act", lambda e: e.activation(out=rec[r0:r0 + 1, 0:n], in_=rec[r0:r0 + 1, 0:n], func=AF.Exp, scale=-1.0), reads=[rec], writes=[rec])
        pb = self.bank()
        mk.op("pe", lambda e: e.matmul(out=pb[o0:o0 + 64, 0:n], lhsT=self.onesf[r0:r0 + 1, 0:64], rhs=rec[r0:r0 + 1, 0:n], start=True, stop=True),
              reads=[self.onesf, rec], writes=[pb])
        mk.op("dve", lambda e: e.tensor_tensor(out=tmp[o0:o0 + 64, 0:n], in0=po[o0:o0 + 64, 0:n], in1=gate_ap, op=ALU.mult),
              reads=[po, gate_tile], writes=[tmp])
        mk.op("dve", lambda e: e.tensor_tensor(out=dst_ap, in0=tmp[o0:o0 + 64, 0:n], in1=pb[o0:o0 + 64, 0:n], op=ALU.mult),
              reads=[tmp, pb], writes=[dst_tile])

    def phase_OUT(self, wname, xsrc, ntiles, tile0, final=False, G=None, side=None):
        mk, din = self.mk, self.din
        with ExitStack() as es:
            wo = self.sb(es, [128, 8, D], BF16)
            self.wload(wo, 0, din[wname], 0, 8, 0, 512)
            self.wload(wo, 512, din[wname], 0, 8, 512, 512)
            catb = [self.sb(es, [128, 8, 512], BF16) for _ in range(3)]
            xts = [self.sb(es, [128, D], F32) for _ in range(8)]
            tmp = [self.sb(es, [128, D], F32) for _ in range(2)]
            xn = [self.sb(es, [128, D], F32) for _ in range(2)]
            junk = self.sb(es, [128, D], BF16); ss = self.sb(es, [128, 1], F32); rstd = self.sb(es, [128, 1], F32)
            fgb = None
            if final:
                fgb = self.sb(es, [128, D], F32)
                fr = self.sb(es, [1, D], F32)
                mk.dma("sp", fr[:], din["finalg"].t.ap(), writes=[fr])
                self.bcast_row(fr, fgb)
            side_steps = side(es) if side is not None else []
            loaded = set()

            def load(ti):
                gi = tile0 + ti
                blk = gi // 4
                if blk not in loaded:
                    loaded.add(blk)
                    cb = catb[blk % 3]
                    mk.dma("sp", cb[:], self.cat_d.t.ap()[:, :, blk * 512:(blk + 1) * 512].rearrange("c p n -> p c n"), reads=[self.cat_d], writes=[cb])
                xt = xts[ti % 8]
                mk.dma("sp", xt[:], xsrc(gi), reads=[self.x1_d], writes=[xt])

            PF = 6
            for ti in range(min(PF, ntiles)):
                load(ti)
            for ti in range(ntiles):
                if ti + PF < ntiles:
                    load(ti + PF)
                if side_steps:
                    side_steps.pop(0)()
                gi = tile0 + ti
                blk, t = divmod(gi, 4)
                cb = catb[blk % 3]
                xt = xts[ti % 8]; tm = tmp[ti % 2]; xo = xn[ti % 2]
                Gt = G["Gl" if gi < 18 else "Gc"]
                for half in range(2):
                    p = self.bank()
                    for c in range(8):
                        mk.op("pe", lambda e: e.matmul(out=p[:], lhsT=cb[:, c, t * 128:(t + 1) * 128], rhs=wo[:, c, half * 512:(half + 1) * 512],
                                                       start=(c == 0), stop=(c == 7)), reads=[cb, wo], writes=[p], inc=(c == 7))
                    mk.op("dve", lambda e: e.tensor_tensor(out=tm[:, half * 512:(half + 1) * 512], in0=p[:], in1=Gt[:, half * 512:(half + 1) * 512], op=ALU.mult),
                          reads=[p, Gt], writes=[tm])
                mk.op("pool", lambda e: e.tensor_tensor(out=xo[:], in0=tm[:], in1=xt[:], op=ALU.add), reads=[tm, xt], writes=[xo])
                if not final:
                    mk.dma("sp", self.x1_d.t.ap()[gi], xo[:], reads=[xo], writes=[self.x1_d])
                else:
                    mk.op("act", lambda e: e.activation(out=junk[:], in_=xo[:], func=AF.Square, scale=1.0 / 32.0, accum_out=ss[:]), reads=[xo], writes=[junk, ss])
                    mk.op("act", lambda e: e.activation(out=rstd[:], in_=ss[:], func=AF.Ln, bias=self.epsb[:], scale=1.0), reads=[ss, self.epsb], writes=[rstd])
                    mk.op("act", lambda e: e.activation(out=rstd[:], in_=rstd[:], func=AF.Exp, scale=-0.5), reads=[rstd], writes=[rstd])
                    mk.op("dve", lambda e: e.scalar_tensor_tensor(out=tm[:], in0=xo[:], scalar=rstd[:, 0:1], in1=fgb[:], op0=ALU.mult, op1=ALU.mult),
                          reads=[xo, rstd, fgb], writes=[tm])
                    mk.dma("sp", self.out.t.ap()[ti], tm[:], reads=[tm], writes=[self.out])
            while side_steps:
                side_steps.pop(0)()
            mk.barrier()

    def bcast_row(self, row, dst):
        mk = self.mk
        for half in range(2):
            p = self.bank()
            mk.op("pe", lambda e: e.matmul(out=p[0:64, :], lhsT=self.onesf[0:1, 0:64], rhs=row[0:1, half * 512:(half + 1) * 512], start=True, stop=True),
                  reads=[self.onesf, row], writes=[p])
            mk.op("pe", lambda e: e.matmul(out=p[64:128, :], lhsT=self.onesf[0:1, 0:64], rhs=row[0:1, half * 512:(half + 1) * 512], start=True, stop=True),
                  reads=[self.onesf, row], writes=[p])
            mk.op("dve", lambda e: e.tensor_copy(out=dst[:, half * 512:(half + 1) * 512], in_=p[:]), reads=[p], writes=[dst])

    def layer0(self):
        with ExitStack() as es:
            self.u_all = self.sb(es, [128, 64, 512], BF16)
            self.u_ctx = self.sb(es, [128, 2, 512], BF16)
            with ExitStack() as es_ab:
                with ExitStack() as es_tmp:
                    for f in self.mod_steps(0, es_ab, es_tmp):
                        f()
                    self.mk.barrier()
                self.bc = self.bcs[0]
                self.phase_O()
                self.phase_A(es)
            self.phase_F()
        self.mk.barrier()
        self.phase_ATT()
        din = self.din
        self.es_ab1 = ExitStack()
        self.alloc_bc(1, self.es_ab1)
        self.phase_OUT("wout0", lambda gi: din["xo"].t.ap()[gi] if gi < 18 else din["ctx"].t.ap()[gi - 18], 20, 0,
                       G=self.bcs[0], side=lambda es_tmp: self.mod_steps(1, self.es_ab1, es_tmp))

    def phase_L1(self):
        mk, din = self.mk, self.din
        g1_d = self.g1_d
        with ExitStack() as es:
            QT1 = self.sb(es, [128, 8, NOWN], BF16)
            KT1 = self.sb(es, [128, 4, NQ], BF16)
            VA1 = self.sb(es, [128, 20, 4, 2, 128], BF16)
            mk.op("pool", lambda e: e.memset(VA1[:].rearrange("p a b c d -> p (a b c d)"), 0.0), writes=[VA1])
            mk.op("pool", lambda e: e.memset(VA1[:, :, :, 0, 64:65], 1.0), writes=[VA1])
            mk.op("pool", lambda e: e.memset(VA1[:, :, :, 1, 0:1], 1.0), writes=[VA1])
            self.bc = self.bcs[1]
            with ExitStack() as es2:
                hT = self.sb(es2, [128, 8, NQ], BF16)
                with ExitStack() as es3:
                    self.make_norm_scratch(es3, n=2, nx=3)
                    self.norm_pipe(20, lambda i: self.x1_d.t.ap()[i],
                                   lambda i: (self.bc["Al" if i < 18 else "Ac"], self.bc["Bl" if i < 18 else "Bc"]),
                                   lambda i: (hT, hT[:, :, i * 128:(i + 1) * 128]))
                    mk.barrier()
                tab1 = self.sb(es2, [128, 2, NQ], F32)
                mk.dma("sp", tab1[:], din["tab1"].t.ap().rearrange("t p n -> p t n"), writes=[tab1])
                wps = [self.sb(es2, [128, 8, 256], BF16) for _ in range(2)]
                g0l, g0c = self.G0["Gl"], self.G0["Gc"]
                st = [self.sb(es2, [128, 512], BF16) for _ in range(2)]
                jobs = [("q", hp, [(0, hp * 128, 128), (128, 2560 + hp * 128, 128)]) for hp in range(8)]
                jobs += [("k", kh, [(0, 3584 + kh * 128, 128), (128, 4096 + kh * 128, 128)]) for kh in range(4)]
                jobs += [("g", hp, [(0, 1536 + hp * 128, 128)]) for hp in range(8)]
                jobs += [("v", 0, [(0, 1280, 256)])]

                def jload(ji):
                    for dcol, c0, n in jobs[ji][2]:
                        self.wload(wps[ji % 2], dcol, din["w1in"], 0, 8, c0, n)

                jload(0)
                si = 0
                ri = 0
                for ji, (kind, idx, _) in enumerate(jobs):
                    wp = wps[ji % 2]
                    first = True
                    if kind in ("q", "k"):
                        for b0, n in QBLK:
                            if kind == "q" and b0 >= NOWN:
                                continue
                            n_ = min(n, NOWN - b0) if kind == "q" else n
                            t1 = g0l[:, (ri % 2) * 512:(ri % 2) * 512 + n_]; t2 = g0c[:, (ri % 2) * 512:(ri % 2) * 512 + n_]
                            ri += 1
                            pa = self.bank(); pb = self.bank()
                            self.proj(pa[:, 0:n_], pa, wp, (0, 128), hT, (b0, b0 + n_))
                            self.proj(pb[:, 0:n_], pb, wp, (128, 256), hT, (b0, b0 + n_))
                            if first and ji + 1 < len(jobs):
                                jload(ji + 1); first = False
                            mk.op("dve", lambda e: e.tensor_tensor(out=t1, in0=pa[:, 0:n_], in1=tab1[:, 0, b0:b0 + n_], op=ALU.mult), reads=[pa, tab1], writes=[g0l])
                            mk.op("dve", lambda e: e.tensor_tensor(out=t2, in0=pb[:, 0:n_], in1=tab1[:, 1, b0:b0 + n_], op=ALU.mult), reads=[pb, tab1], writes=[g0c])
                            dst, dap = (QT1, QT1[:, idx, b0:b0 + n_]) if kind == "q" else (KT1, KT1[:, idx, b0:b0 + n_])
                            mk.op("pool", lambda e: e.tensor_tensor(out=dap, in0=t1, in1=t2, op=ALU.add), reads=[g0l, g0c], writes=[dst])
                    elif kind == "g":
                        for b0, n in QBLK:
                            p = self.bank()
                            self.proj(p[:, 0:n], p, wp, (0, 128), hT, (b0, b0 + n))
                            if first and ji + 1 < len(jobs):
                                jload(ji + 1); first = False
                            s_ = st[si % 2]; si += 1
                            mk.op("act", lambda e: e.activation(out=s_[:, 0:n], in_=p[:, 0:n], func=AF.Silu), reads=[p], writes=[s_])
                            mk.dma("sp", g1_d.t.ap()[idx, :, b0:b0 + n], s_[:, 0:n], reads=[s_], writes=[g1_d])
                    else:
                        for i in range(20):
                            p = self.bank()
                            for c in range(8):
                                mk.op("pe", lambda e: e.matmul(out=p[:, 0:256], lhsT=hT[:, c, i * 128:(i + 1) * 128], rhs=wp[:, c, 0:256], start=(c == 0), stop=(c == 7)),
                                      reads=[hT, wp], writes=[p], inc=(c == 7))
                            pv = p[:, 0:256].rearrange("p (a b) -> p a b", b=64)
                            mk.op("dve", lambda e: e.tensor_copy(out=VA1[:, i, :, 0, 0:64], in_=pv), reads=[p], writes=[VA1])
                            mk.op("act", lambda e: e.activation(out=VA1[:, i, :, 1, 64:128], in_=pv, func=AF.Copy), reads=[p], writes=[VA1])
                mk.barrier()
            maskb = self.sb(es, [128, 4, 512], BF16); kval = self.sb(es, [128, 20], F32)
            mk.dma("sp", maskb[:], din["maskb"].t.ap().rearrange("p d s j q -> p d (s j q)"), writes=[maskb])
            mk.dma("sp", kval[:], din["kvalid"].t.ap(), writes=[kval])
            snk = self.sb(es, [128, 16], F32); esk = self.sb(es, [128, 16], F32)
            for r in (0, 64):
                mk.dma("sp", snk[r:r + 1, :], din["sink"].t.ap(), writes=[snk])
            mk.op("act", lambda e: e.activation(out=esk[0:1, :], in_=snk[0:1, :], func=AF.Exp), reads=[snk], writes=[esk])
            mk.op("act", lambda e: e.activation(out=esk[64:65, :], in_=snk[64:65, :], func=AF.Exp), reads=[snk], writes=[esk])
            esr = self.sb(es, [128, 8, 512], F32)
            mk.op("pool", lambda e: e.memset(esr[:].rearrange("p a b -> p (a b)"), 0.0), writes=[esr])
            for kh in range(4):
                for par in range(2):
                    for k_ in range(2):
                        hq = 4 * kh + 2 * k_ + par
                        for r in (0, 64):
                            sl = esr[r:r + 1, kh * 2 + par, k_ * 256:(k_ + 1) * 256]
                            mk.op("dve", lambda e: e.tensor_scalar(out=sl, in0=sl, scalar1=esk[r:r + 1, hq:hq + 1], scalar2=None, op0=ALU.add),
                                  reads=[esk, esr], writes=[esr])
            gt = [self.sb(es, [128, 2, NQ], BF16) for _ in range(2)]
            AT = [self.sb(es, [128, 2, 2048], BF16) for _ in range(2)]
            PTs = [self.sb(es, [128, 512], BF16) for _ in range(18)]
            rec = self.sb(es, [128, 512], F32); tmp = self.sb(es, [128, 512], F32)
            items = [(kh, par, nb) for kh in range(4) for par in range(2) for nb in range(0, 16, 2)]
            st = {}
            LAG = 1
            for t in range(len(items) + LAG):
                if t < len(items):
                    kh, par, nb = items[t]
                    if par == 0 and nb == 0:
                        g_ = gt[kh % 2]
                        mk.dma("sp", g_[:], g1_d.t.ap()[2 * kh:2 * kh + 2].rearrange("c p n -> p c n"), reads=[g1_d], writes=[g_])
                    p0 = par * 64
                    q0 = 128 + 128 * nb
                    st[t] = self.bank(hold=True)
                    kts = [(18, None), (19, None), (nb, 0), (nb + 1, 1), (nb + 2, 2), (nb + 3, 3)]
                    for ki, (kt, mi) in enumerate(kts):
                        ps = self.bank()
                        mk.op("pe", lambda e: e.matmul(out=ps[:, :], lhsT=KT1[p0:p0 + 64, kh, kt * 128:(kt + 1) * 128],
                                                       rhs=QT1[p0:p0 + 64, 2 * kh:2 * kh + 2, q0:q0 + 256], start=True, stop=(mi is None)),
                              reads=[KT1, QT1], writes=[ps], inc=(mi is None))
                        if mi is not None:
                            mk.op("pe", lambda e: e.matmul(out=ps[:, :], lhsT=self.ident[:], rhs=maskb[:, mi, :], start=False, stop=True),
                                  reads=[self.ident, maskb], writes=[ps])
                        pt = PTs[(t % 3) * 6 + ki]
                        mk.op("act", lambda e: e.activation(out=pt[:], in_=ps[:, :], func=AF.Exp, scale=GQA_SCALE, bias=kval[:, kt:kt + 1]),
                              reads=[ps, kval], writes=[pt])
                if t >= LAG:
                    u = t - LAG
                    kh, par, nb = items[u]
                    g_ = gt[kh % 2]; at = AT[kh % 2]
                    o0, r0 = (64, 0) if par else (0, 64)
                    q0 = 128 + 128 * nb
                    po = st.pop(u)
                    kts = [18, 19, nb, nb + 1, nb + 2, nb + 3]
                    for ki, kt in enumerate(kts):
                        pt = PTs[(u % 3) * 6 + ki]
                        mk.op("pe", lambda e: e.matmul(out=po[:, :], lhsT=VA1[:, kt, kh, par, :], rhs=pt[:], start=(ki == 0), stop=(ki == 5)),
                              reads=[VA1, pt], writes=[po], inc=(ki == 5))
                    self.att_finish(po, 512, o0, r0, g_, g_[o0:o0 + 64, :, q0:q0 + 256], at, at[o0:o0 + 64, :, nb * 128:(nb + 2) * 128], rec, tmp,
                                    extra=esr[r0:r0 + 1, kh * 2 + par, :])
                    self.release(po)
                    if par == 1 and nb == 14:
                        mk.dma("sp", self.cat_d.t.ap()[2 * kh:2 * kh + 2, :, 128:2176].rearrange("c p n -> p c n"), at[:], reads=[at], writes=[self.cat_d])
            mk.barrier()

    def layer1(self):
        self.phase_L1()
        x1 = self.x1_d
        self.phase_OUT("wout1", lambda gi: x1.t.ap()[gi], 16, 1, final=True, G=self.bcs[1])
        self.es_ab1.close()


def build_program(debug=False, upto=99):
    pr = Prog(debug=debug, upto=upto)
    pr.g1_d = pr.mk.dram([8, 128, NQ], BF16, "g1_d", "Internal")
    pr.layer0()
    if upto >= 1:
        pr.layer1()
    pr.mk.finish()
    return pr


def kernel(**inputs):
    W = pack_weights(inputs)
    in_maps = []
    for core in range(8):
        b, q = divmod(core, 4)
        m = core_inputs(inputs, W, b, q)
        in_maps.append({n: m[n] for n, _, _ in IN_SPECS})
    pr = build_program()
    res = run_bass_kernel_spmd(pr.nc, in_maps, core_ids=list(range(8)))
    out = np.zeros((NB, SEQ, D), np.float32)
    for core in range(8):
        b, q = divmod(core, 4)
        out[b, 2048 * q:2048 * (q + 1)] = np.asarray(res.results[core]["out"], np.float32).reshape(2048, D)
    return out
```

```python
import math
import numpy as np
import ml_dtypes
import concourse.bass as bass
import concourse.mybir as mybir
from concourse.bass_utils import run_bass_kernel_spmd

F32 = mybir.dt.float32
BF16 = mybir.dt.bfloat16
ALU = mybir.AluOpType
AF = mybir.ActivationFunctionType
AX = mybir.AxisListType


class Tile:
    __slots__ = ("t", "writers", "readers", "name")

    def __init__(self, t, name=""):
        self.t = t
        self.writers = []
        self.readers = []
        self.name = name

    def __getitem__(self, idx):
        return self.t[idx]


class Eng:
    def __init__(self, mk, name, eng, sem):
        self.mk, self.name, self.eng, self.sem = mk, name, eng, sem
        self.count = 0
        self.seen = {}


class MK:
    N_DMA_SEMS = 24

    def __init__(self, nc):
        self.nc = nc
        self.engs = {}
        for name, eng in (("pe", nc.tensor), ("act", nc.scalar), ("dve", nc.vector),
                          ("pool", nc.gpsimd), ("sp", nc.sync)):
            self.engs[name] = Eng(self, name, eng, nc.alloc_semaphore("s_" + name))
        self.dma_sems = [nc.alloc_semaphore("d%d" % i) for i in range(2 * self.N_DMA_SEMS)]
        self.dma_cnt = [0] * (2 * self.N_DMA_SEMS)
        self.dma_i = {"sw": 0, "hw": 0}
        self.ntile = 0

    def sb(self, shape, dt=BF16, name=None):
        self.ntile += 1
        name = name or "t%d" % self.ntile
        return Tile(self.nc.alloc_sbuf_tensor(name + "_%d" % self.ntile, list(shape), dt), name)

    def ps(self, shape, dt=F32, name=None):
        self.ntile += 1
        name = name or "p%d" % self.ntile
        return Tile(self.nc.alloc_psum_tensor(name + "_%d" % self.ntile, list(shape), dt), name)

    def dram(self, shape, dt, name, kind="Internal"):
        return Tile(self.nc.dram_tensor(name, list(shape), dt, kind=kind), name)

    def _wait(self, E, tok):
        sem, val = tok
        key = sem.num
        if E.seen.get(key, 0) >= val:
            return
        E.seen[key] = val
        E.eng.wait_ge(sem, val)

    def _deps(self, E, reads, writes):
        toks = []
        for t in reads:
            toks += t.writers
        for t in writes:
            toks += t.writers
            toks += t.readers
        best = {}
        for sem, val in toks:
            if best.get(sem.num, (None, 0))[1] < val:
                best[sem.num] = (sem, val)
        for sem, val in best.values():
            if E.name == "pe" and sem.num == E.sem.num:
                continue
            self._wait(E, (sem, val))

    def _mark(self, tok, reads, writes):
        for t in reads:
            t.readers.append(tok)
            if len(t.readers) > 64:
                t.readers = _compress(t.readers)
        for t in writes:
            t.writers = [tok]
            t.readers = []

    def op(self, eng, fn, reads=(), writes=(), inc=True):
        E = self.engs[eng]
        self._deps(E, reads, writes)
        ins = fn(E.eng)
        if inc:
            E.count += 1
            ins.then_inc(E.sem, 1)
            tok = (E.sem, E.count)
        else:
            assert eng == "pe"
            tok = (E.sem, E.count + 1)
        self._mark(tok, reads, writes)
        return tok

    def dma(self, eng, out, in_, reads=(), writes=(), **kw):
        E = self.engs[eng]
        self._deps(E, reads, writes)
        pool = "sw" if eng == "pool" else "hw"
        i = self.dma_i[pool] % self.N_DMA_SEMS + (self.N_DMA_SEMS if pool == "sw" else 0)
        self.dma_i[pool] += 1
        sem = self.dma_sems[i]
        if self.dma_cnt[i] > 0:
            self._wait(E, (sem, 16 * self.dma_cnt[i]))
        self.dma_cnt[i] += 1
        E.eng.dma_start(out=out, in_=in_, **kw).then_inc(sem, 16)
        tok = (sem, 16 * self.dma_cnt[i])
        self._mark(tok, reads, writes)
        return tok

    def barrier(self):
        for E in self.engs.values():
            for F in self.engs.values():
                if F is not E and F.count > 0:
                    self._wait(E, (F.sem, F.count))
            for i, sem in enumerate(self.dma_sems):
                if self.dma_cnt[i] > 0:
                    self._wait(E, (sem, 16 * self.dma_cnt[i]))

    def finish(self):
        E = self.engs["sp"]
        for F in self.engs.values():
            if F is not E and F.count > 0:
                self._wait(E, (F.sem, F.count))
        for i, sem in enumerate(self.dma_sems):
            if self.dma_cnt[i] > 0:
                self._wait(E, (sem, 16 * self.dma_cnt[i]))


def _compress(toks):
    best = {}
    for sem, val in toks:
        if best.get(sem.num, (None, 0))[1] < val:
            best[sem.num] = (sem, val)
    return list(best.values())


D = 1024
SEQ = 8192
NB = 2
CTX = 256
NOWN = 2304
NQ = NOWN + CTX
NKEY = SEQ + CTX
EPS = 1e-6
MLA_SCALE = 1.0 / math.sqrt(96.0)
GQA_SCALE = 1.0 / 8.0
NEG = -30000.0


def _bf(a):
    return np.ascontiguousarray(np.asarray(a, dtype=np.float32)).astype(ml_dtypes.bfloat16)


def _f32(a):
    return np.ascontiguousarray(np.asarray(a, dtype=np.float32))


def _rope_tab(tok, nf):
    tok = np.asarray(tok, dtype=np.float64)
    row = np.floor(tok / 64.0)
    col = tok - 64.0 * row
    inv = 10000.0 ** (-np.arange(nf, dtype=np.float64) / nf)
    ang = np.concatenate([row[:, None] * inv, col[:, None] * inv], axis=-1)
    return np.cos(ang).T, np.sin(ang).T


def host_tables(q):
    T = {}
    T["ident"] = _bf(np.eye(128))
    T["identf"] = _f32(np.eye(128))
    n1 = np.arange(128)[:, None, None]
    n2 = np.arange(64)[None, :, None]
    k1 = np.arange(128)[None, None, :]
    th = 2 * np.pi * ((k1 * (64 * n1 + n2)) % SEQ) / SEQ
    T["T1"] = _bf(np.stack([np.cos(th), -np.sin(th)], axis=2))
    c = np.arange(128)[:, None]
    j = np.arange(128)[None, :]
    ph = 2 * np.pi * ((c * j) % 128) / 128
    Cc, Sc = np.cos(ph), np.sin(ph)
    T["CS"] = _bf(np.stack([np.concatenate([Cc, -Sc], 1), np.concatenate([Sc, Cc], 1)], 1))
    k2 = (16 * q - 1 + np.arange(18)) % 64
    ps = 2 * np.pi * ((np.arange(64)[:, None] * k2[None, :]) % 64) / 64
    f2 = np.stack([np.cos(ps), np.sin(ps)], 1) / 1024.0
    f2b = np.zeros((128, 2, 36))
    f2b[0:64, :, 0:18] = f2
    f2b[64:128, :, 18:36] = f2
    T["F2"] = _bf(f2b)
    n = (np.arange(2)[None, :, None] * 128 + np.arange(128)[:, None, None])
    k = np.arange(256)[None, None, :]
    a = 2 * np.pi * ((n * k) % 256) / 256
    sc = 1.0 / math.sqrt(256.0 * 128.0)
    T["DC"] = _bf(np.stack([np.cos(a) * sc, -np.sin(a) * sc], 2))
    tk = np.zeros((66, 2, 32, 128), np.float32)
    tk[0:2, 0] = 1.0
    for t in range(64):
        cs, sn = _rope_tab(64 * np.arange(128) + t, 8)
        tk[2 + t, 0] = np.concatenate([cs, cs], 0)
        tk[2 + t, 1] = np.concatenate([-sn, sn], 0)
    T["tabK"] = tk
    t0 = 2048 * q - 128
    cs, sn = _rope_tab(t0 + np.arange(NOWN), 8)
    tq = np.zeros((2, 32, NQ), np.float32)
    tq[0, :, NOWN:] = 1.0
    tq[0, :, :NOWN] = np.concatenate([cs, cs], 0)
    tq[1, :, :NOWN] = np.concatenate([-sn, sn], 0)
    T["tabQ"] = tq
    cs, sn = _rope_tab(t0 + np.arange(NOWN), 16)
    t1 = np.zeros((2, 128, NQ), np.float32)
    t1[0, :, NOWN:] = 1.0
    t1[0, :, :NOWN] = np.concatenate([cs, cs, cs, cs], 0)
    t1[1, :, :NOWN] = np.concatenate([-sn, sn, -sn, sn], 0)
    T["tab1"] = t1
    kp = np.arange(128)[:, None]
    qp = np.arange(128)[None, :]
    mprev = np.where(kp >= qp, 0.0, NEG)
    mnext = np.where(kp <= qp, 0.0, NEG)
    full = np.full((128, 128), NEG)
    zero = np.zeros((128, 128))
    rel = {-1: full, 0: mprev, 1: zero, 2: mnext, 3: full}
    T["maskb"] = _bf(np.stack([np.stack([np.stack([rel[d - j] for j in range(2)], 1) for _ in range(2)], 1) for d in range(4)], 1))
    tok = t0 + np.arange(NOWN)
    kv = np.where((tok >= 0) & (tok < SEQ), 0.0, NEG).reshape(18, 128).T
    T["kvalid"] = _f32(np.concatenate([kv, np.zeros((128, 2))], 1))
    sel = np.zeros((2, 2, 128), np.float32)
    sel[0, 0] = 1.0
    sel[1, 1] = 1.0
    T["sel"] = sel
    return T


def pack_weights(inp):
    W = {}
    w = np.asarray(inp["e_w_in"][0], np.float32)
    kpe = w[:, 1664:1696]
    W["w0in"] = _f32(np.concatenate([w, kpe[:, 16:32], kpe[:, 0:16]], 1))
    wq = np.asarray(inp["e_w_qb"][0], np.float32)
    sw = []
    for h in range(8):
        pe = wq[:, h * 96 + 64:h * 96 + 96]
        sw += [pe[:, 16:32], pe[:, 0:16]]
    W["wqb"] = _f32(np.concatenate([wq] + sw, 1))
    W["wkvb"] = _f32(inp["e_w_kvb"][0])
    W["wout0"] = _f32(inp["e_w_out"][0])
    w1 = np.asarray(inp["o_w_in"][0], np.float32)
    qs, k2, k2s = [], [], []
    for h in range(16):
        qh = w1[:, h * 64:(h + 1) * 64]
        qs += [qh[:, 32:64], qh[:, 0:32]]
    for h in range(4):
        kh = w1[:, 1024 + h * 64:1024 + (h + 1) * 64]
        k2 += [kh, kh]
        k2s += [kh[:, 32:64], kh[:, 0:32]] * 2
    W["w1in"] = _f32(np.concatenate([w1] + qs + k2 + k2s, 1))
    W["wout1"] = _f32(inp["o_w_out"][0])
    W["wmod"] = _f32(inp["w_mod"])
    W["bmod"] = _f32(inp["b_mod"])
    W["normg"] = _f32(inp["norm_g"])
    W["finalg"] = _f32(np.asarray(inp["final_g"], np.float32)[None, :])
    W["qnorm"] = _f32(np.asarray(inp["e_q_norm"][0], np.float32).reshape(3, 128).T)
    W["kvnorm"] = _f32(np.asarray(inp["e_kv_norm"][0], np.float32).reshape(2, 128).T)
    W["sink"] = _f32(inp["o_sink"])
    return W


def core_inputs(inp, W, b, q):
    m = dict(W)
    m.update(host_tables(q))
    x = np.asarray(inp["x"], np.float32)[b]
    m["xa"] = _f32(x.reshape(128, 64, D).transpose(1, 0, 2))
    t0 = 2048 * q - 128
    xo = np.zeros((NOWN, D), np.float32)
    lo, hi = max(t0, 0), min(t0 + NOWN, SEQ)
    xo[lo - t0:hi - t0] = x[lo:hi]
    m["xo"] = xo.reshape(18, 128, D)
    m["ctx"] = _f32(np.asarray(inp["ctx"], np.float32)[b].reshape(2, 128, D))
    cc = np.stack([np.asarray(inp["c"], np.float32)[b], np.asarray(inp["c_ctx"], np.float32)], 0)
    m["cT"] = _f32(cc.reshape(2, 8, 128).transpose(2, 1, 0))
    return m


from contextlib import ExitStack

IN_SPECS = [
    ("xa", [64, 128, D], F32), ("xo", [18, 128, D], F32), ("ctx", [2, 128, D], F32), ("cT", [128, 8, 2], F32),
    ("w0in", [D, 2240], F32), ("wqb", [384, 1024], F32), ("wkvb", [256, 1024], F32), ("wout0", [D, D], F32),
    ("w1in", [D, 4608], F32), ("wout1", [D, D], F32), ("wmod", [2, D, 3072], F32), ("bmod", [2, 3072], F32),
    ("normg", [2, D], F32), ("finalg", [1, D], F32), ("qnorm", [128, 3], F32), ("kvnorm", [128, 2], F32),
    ("sink", [1, 16], F32),
    ("ident", [128, 128], BF16), ("identf", [128, 128], F32), ("T1", [128, 64, 2, 128], BF16),
    ("CS", [128, 2, 256], BF16), ("F2", [128, 2, 36], BF16), ("DC", [128, 2, 2, 256], BF16),
    ("tabK", [66, 2, 32, 128], F32), ("tabQ", [2, 32, NQ], F32), ("tab1", [2, 128, NQ], F32),
    ("maskb", [128, 4, 2, 2, 128], BF16), ("kvalid", [128, 20], F32), ("sel", [2, 2, 128], F32),
]

QBLK = [(0, 512), (512, 512), (1024, 512), (1536, 512), (2048, 512)]


class Prog:
    def __init__(self, debug=False, upto=99):
        self.debug = debug
        self.upto = upto
        nc = self.nc = bass.Bass("TRN2", target_bir_lowering=False)
        mk = self.mk = MK(nc)
        self.din = {n: mk.dram(s, dt, n, "ExternalInput") for n, s, dt in IN_SPECS}
        self.out = mk.dram([16, 128, D], F32, "out", "ExternalOutput")
        dk = "ExternalOutput" if debug else "Internal"
        self.qan_d = mk.dram([3, 128, NQ], BF16, "qan_d", dk)
        self.fg_d = mk.dram([4, 128, NQ], BF16, "fg_d", dk)
        self.mg_d = mk.dram([4, 128, NQ], BF16, "mg_d", dk)
        self.kvn_d = mk.dram([2, 128, NKEY], BF16, "kvn_d", dk)
        self.kpe_d = mk.dram([32, NKEY], BF16, "kpe_d", dk)
        self.cat_d = mk.dram([8, 128, NQ], BF16, "cat_d", dk)
        self.x1_d = mk.dram([20, 128, D], F32, "x1_d", dk)
        self.pTs = [mk.ps([128, 1024], BF16, "pT%d" % i) for i in range(2)]
        self.P = [mk.ps([128, 512], F32, "P%d" % i) for i in range(6)]
        self.pi = 0
        self.held = set()
        self.es = ExitStack()
        self.cnt = 0
        P_ = self.sbp
        self.ident = P_([128, 128], BF16); self.onesb = P_([128, 128], BF16); self.onesf = P_([128, 64], F32)
        self.epsb = P_([128, 1], F32)
        self.G0 = {k: P_([128, D], F32) for k in ("Gl", "Gc")}
        self.bcs = {}
        mk.dma("sp", self.ident[:], self.din["ident"][:, :], writes=[self.ident])
        mk.op("pool", lambda e: e.memset(self.onesb[:], 1.0), writes=[self.onesb])
        mk.op("pool", lambda e: e.memset(self.onesf[:], 1.0), writes=[self.onesf])
        mk.op("pool", lambda e: e.memset(self.epsb[:], EPS), writes=[self.epsb])

    def sbp(self, shape, dt=BF16):
        self.cnt += 1
        return Tile(self.nc.alloc_sbuf_tensor("pers%d" % self.cnt, list(shape), dt))

    def sb(self, es, shape, dt=BF16):
        self.cnt += 1
        return Tile(es.enter_context(self.nc.sbuf_tensor("s%d" % self.cnt, list(shape), dt)))

    def bank(self, hold=False):
        while True:
            self.pi = (self.pi + 1) % 6
            if self.pi not in self.held:
                break
        if hold:
            self.held.add(self.pi)
        return self.P[self.pi]

    def release(self, p):
        self.held.discard(self.P.index(p))

    def wload(self, dst, dcol, src, r0, nk, c0, n):
        v = src.t.ap()[r0:r0 + nk * 128, c0:c0 + n].rearrange("(c p) n -> p c n", p=128)
        self.mk.dma("pool", dst[:, 0:nk, dcol:dcol + n], v, writes=[dst], max_dma_last_dim=4096)

    def alloc_bc(self, l, es_ab):
        bc = self.bcs[l] = {}
        for k in ("Al", "Bl", "Ac", "Bc", "Gl", "Gc"):
            if l == 0 and k[0] == "G":
                bc[k] = self.G0[k]
            else:
                bc[k] = self.sb(es_ab, [128, D], F32)

    def mod_steps(self, l, es_ab, es):
        mk, din = self.mk, self.din
        if l not in self.bcs:
            self.alloc_bc(l, es_ab)
        bc = self.bcs[l]
        wms = [self.sb(es, [128, 8, 512], F32) for _ in range(2)]
        cT = self.sb(es, [128, 16], F32); scT = self.sb(es, [128, 16], F32)
        bm = self.sb(es, [2, 3072], F32); g2 = self.sb(es, [2, D], F32); sel = self.sb(es, [2, 2, 128], F32)
        mrow = self.sb(es, [2, 3072], F32); arow = self.sb(es, [2, D], F32)
        steps = []

        def first():
            mk.dma("sp", cT[:], din["cT"].t.ap().rearrange("p c v -> p (c v)"), writes=[cT])
            mk.op("act", lambda e: e.activation(out=scT[:], in_=cT[:], func=AF.Silu), reads=[cT], writes=[scT])
            for r in range(2):
                mk.dma("sp", bm[r:r + 1, :], din["bmod"][l:l + 1, :], writes=[bm])
                mk.dma("sp", g2[r:r + 1, :], din["normg"][l:l + 1, :], writes=[g2])
            mk.dma("sp", sel[:], din["sel"].t.ap(), writes=[sel])
        steps.append(first)
        for nb in range(6):
            def ld(nb=nb):
                wm = wms[nb % 2]
                mk.dma("sp", wm[:], din["wmod"].t.ap()[l, :, nb * 512:(nb + 1) * 512].rearrange("(c p) n -> p c n", p=128), writes=[wm])
            def blk(nb=nb):
                wm = wms[nb % 2]
                p = self.bank()
                for kc in range(8):
                    mk.op("pe", lambda e: e.matmul(out=p[0:2, :], lhsT=scT[:, 2 * kc:2 * kc + 2],
                                                   rhs=wm[:, kc, :], start=(kc == 0), stop=(kc == 7)),
                          reads=[scT, wm], writes=[p], inc=(kc == 7))
                mk.op("dve", lambda e: e.tensor_tensor(out=mrow[:, nb * 512:(nb + 1) * 512], in0=p[0:2, :],
                                                       in1=bm[:, nb * 512:(nb + 1) * 512], op=ALU.add),
                      reads=[p, bm], writes=[mrow])
            steps.append(ld)
            steps.append(blk)
        order = [steps[0], steps[1]]
        for nb in range(6):
            if nb + 1 < 6:
                order.append(steps[1 + 2 * (nb + 1)])
            order.append(steps[2 + 2 * nb])

        def arow_f():
            mk.op("dve", lambda e: e.scalar_tensor_tensor(out=arow[:], in0=mrow[:, D:2 * D], scalar=1.0, in1=g2[:],
                                                          op0=ALU.add, op1=ALU.mult), reads=[mrow, g2], writes=[arow])
        order.append(arow_f)
        for which, sfx in ((0, "l"), (1, "c")):
            for key, src, off in (("A", arow, 0), ("B", mrow, 0), ("G", mrow, 2 * D)):
                def bcf(which=which, sfx=sfx, key=key, src=src, off=off):
                    dst = bc[key + sfx]
                    for half in range(2):
                        p = self.bank()
                        mk.op("pe", lambda e: e.matmul(out=p[:], lhsT=sel[:, which, :],
                                                       rhs=src[:, off + half * 512:off + (half + 1) * 512],
                                                       start=True, stop=True), reads=[sel, src], writes=[p])
                        mk.op("dve", lambda e: e.tensor_copy(out=dst[:, half * 512:(half + 1) * 512], in_=p[:]),
                              reads=[p], writes=[dst])
                order.append(bcf)
        return order

    def make_norm_scratch(self, es, n=3, nx=5):
        self.xts = [self.sb(es, [128, D], F32) for _ in range(nx)]
        self.ns = [dict(junk=self.sb(es, [128, D], BF16), ss=self.sb(es, [128, 1], F32),
                        rstd=self.sb(es, [128, 1], F32), tmp=self.sb(es, [128, D], F32), hb=self.sb(es, [128, D], BF16))
                   for _ in range(n)]

    def norm_s0(self, i, src_ap):
        xt = self.xts[i % len(self.xts)]
        self.mk.dma("sp", xt[:], src_ap, writes=[xt])

    def norm_s1(self, i, src_ap, A, B):
        mk = self.mk
        s = self.ns[i % len(self.ns)]
        xt = self.xts[i % len(self.xts)]
        junk, ss, rstd, tmp, hb = s["junk"], s["ss"], s["rstd"], s["tmp"], s["hb"]
        mk.op("act", lambda e: e.activation(out=junk[:], in_=xt[:], func=AF.Square, scale=1.0 / 32.0, accum_out=ss[:]),
              reads=[xt], writes=[junk, ss])
        mk.op("act", lambda e: e.activation(out=rstd[:], in_=ss[:], func=AF.Ln, bias=self.epsb[:], scale=1.0),
              reads=[ss, self.epsb], writes=[rstd])
        mk.op("act", lambda e: e.activation(out=rstd[:], in_=rstd[:], func=AF.Exp, scale=-0.5), reads=[rstd], writes=[rstd])
        mk.op("dve", lambda e: e.scalar_tensor_tensor(out=tmp[:], in0=xt[:], scalar=rstd[:, 0:1], in1=A[:],
                                                      op0=ALU.mult, op1=ALU.mult), reads=[xt, rstd, A], writes=[tmp])
        mk.op("pool", lambda e: e.tensor_tensor(out=hb[:], in0=tmp[:], in1=B[:], op=ALU.add), reads=[tmp, B], writes=[hb])

    def norm_s2(self, i, dst_tile, dst_ap):
        mk = self.mk
        hb = self.ns[i % len(self.ns)]["hb"]
        pT = self.pTs[i % 2]
        for c in range(8):
            mk.op("pe", lambda e: e.transpose(out=pT[:, c * 128:(c + 1) * 128], in_=hb[:, c * 128:(c + 1) * 128],
                                              identity=self.ident[:]), reads=[hb, self.ident], writes=[pT], inc=(c == 7))
        mk.op("act", lambda e: e.activation(out=dst_ap, in_=pT[:, :].rearrange("p (c n) -> p c n", c=8), func=AF.Copy),
              reads=[pT], writes=[dst_tile])

    def norm_pipe(self, n, src_fn, ab_fn, dst_fn, s3=None):
        for i in range(min(2, n)):
            self.norm_s0(i, src_fn(i))
        for t in range(n + 2):
            if t + 2 < n:
                self.norm_s0(t + 2, src_fn(t + 2))
            if t < n:
                A, B = ab_fn(t)
                self.norm_s1(t, src_fn(t), A, B)
            if 0 <= t - 1 < n:
                dt_, dap = dst_fn(t - 1)
                self.norm_s2(t - 1, dt_, dap)
            if s3 is not None and 0 <= t - 2 < n:
                s3(t - 2)

    def proj(self, p_ap, ptile, w, wcols, h, hcols, n_c=8):
        for c in range(n_c):
            self.mk.op("pe", lambda e: e.matmul(out=p_ap, lhsT=w[:, c, wcols[0]:wcols[1]], rhs=h[:, c, hcols[0]:hcols[1]],
                                                start=(c == 0), stop=(c == n_c - 1)), reads=[w, h], writes=[ptile], inc=(c == n_c - 1))

    def rstd_bcast(self, es_tiles, pss, n, dim, out_tile):
        mk = self.mk
        mk.op("act", lambda e: e.activation(out=out_tile[:, 0:n], in_=pss[:, 0:n], func=AF.Ln, bias=self.epsb[:], scale=1.0 / dim),
              reads=[pss, self.epsb], writes=[out_tile])
        mk.op("act", lambda e: e.activation(out=out_tile[:, 0:n], in_=out_tile[:, 0:n], func=AF.Exp, scale=-0.5),
              reads=[out_tile], writes=[out_tile])

    def phase_O(self):
        mk, din = self.mk, self.din
        with ExitStack() as es:
            hT = self.sb(es, [128, 8, NQ], BF16)
            wps = [self.sb(es, [128, 8, 512], BF16) for _ in range(3)]
            self.wload(wps[0], 0, din["w0in"], 0, 8, 512, 512)
            self.wload(wps[1], 0, din["w0in"], 0, 8, 1696, 512)
            self.wload(wps[2], 0, din["w0in"], 0, 8, 1024, 384)
            with ExitStack() as esn:
                self.make_norm_scratch(esn)
                self.norm_pipe(20, lambda i: din["xo"].t.ap()[i] if i < 18 else din["ctx"].t.ap()[i - 18],
                               lambda i: (self.bc["Al" if i < 18 else "Ac"], self.bc["Bl" if i < 18 else "Bc"]),
                               lambda i: (hT, hT[:, :, i * 128:(i + 1) * 128]))
                mk.barrier()
            stage = [self.sb(es, [128, 512], BF16) for _ in range(3)]
            si = 0
            for wp, dst in ((wps[0], self.fg_d), (wps[1], self.mg_d)):
                for j in range(4):
                    for b0, n in QBLK:
                        p = self.bank()
                        self.proj(p[:, 0:n], p, wp, (j * 128, (j + 1) * 128), hT, (b0, b0 + n))
                        st = stage[si % 3]; si += 1
                        mk.op("act", lambda e: e.activation(out=st[:, 0:n], in_=p[:, 0:n], func=AF.Silu), reads=[p], writes=[st])
                        mk.dma("sp", dst.t.ap()[j, :, b0:b0 + n], st[:, 0:n], reads=[st], writes=[dst])
            wp = wps[2]
            sq = [self.sb(es, [128, 512], BF16) for _ in range(3)]
            rq = self.sb(es, [128, 512], F32)
            for b0, n in QBLK:
                ps = [self.bank() for _ in range(3)]
                pss = self.bank()
                for j in range(3):
                    self.proj(ps[j][:, 0:n], ps[j], wp, (j * 128, (j + 1) * 128), hT, (b0, b0 + n))
                    mk.op("act", lambda e: e.activation(out=sq[j][:, 0:n], in_=ps[j][:, 0:n], func=AF.Square), reads=[ps[j]], writes=[sq[j]])
                for j in range(3):
                    mk.op("pe", lambda e: e.matmul(out=pss[:, 0:n], lhsT=self.onesb[:], rhs=sq[j][:, 0:n], start=(j == 0), stop=(j == 2)),
                          reads=[self.onesb, sq[j]], writes=[pss])
                self.rstd_bcast(None, pss, n, 384.0, rq)
                for j in range(3):
                    st = stage[si % 3]; si += 1
                    mk.op("dve", lambda e: e.tensor_tensor(out=st[:, 0:n], in0=ps[j][:, 0:n], in1=rq[:, 0:n], op=ALU.mult),
                          reads=[ps[j], rq], writes=[st])
                    mk.dma("sp", self.qan_d.t.ap()[j, :, b0:b0 + n], st[:, 0:n], reads=[st], writes=[self.qan_d])
            mk.barrier()

    def phase_A(self, es_outer):
        mk, din = self.mk, self.din
        with ExitStack() as es:
            self.make_norm_scratch(es)
            wA = self.sb(es, [128, 8, 832], BF16)
            self.wload(wA, 0, din["w0in"], 0, 8, 0, 512)
            self.wload(wA, 512, din["w0in"], 0, 8, 1408, 288)
            self.wload(wA, 800, din["w0in"], 0, 8, 2208, 32)
            hTs = [self.sb(es, [128, 8, 128], BF16) for _ in range(3)]
            sqkv = [self.sb(es, [128, 256], BF16) for _ in range(2)]
            rk = [self.sb(es, [128, 128], F32) for _ in range(2)]
            kst = [self.sb(es, [128, 2, 128], BF16) for _ in range(2)]
            tabk = [self.sb(es, [128, 2, 128], F32) for _ in range(3)]
            t1 = [self.sb(es, [128, 128], F32) for _ in range(2)]
            t2 = [self.sb(es, [128, 128], F32) for _ in range(2)]
            pst = [self.sb(es, [128, 128], BF16) for _ in range(2)]

            def s3(i):
                lat = i >= 2
                hT = hTs[i % 3]
                tb = tabk[i % 3]
                mk.dma("sp", tb[64:96, :, :], din["tabK"].t.ap()[i].rearrange("t p n -> p t n"), writes=[tb])
                pkv = self.bank()
                for j in range(2):
                    self.proj(pkv[:, j * 128:(j + 1) * 128], pkv, wA, (512 + j * 128, 640 + j * 128), hT, (0, 128))
                sq = sqkv[i % 2]
                mk.op("act", lambda e: e.activation(out=sq[:], in_=pkv[:, 0:256], func=AF.Square), reads=[pkv], writes=[sq])
                p = self.bank()
                self.mk_tokmajor(p, hT, wA, 0, 512)
                udst, uap = (self.u_all, self.u_all[:, i - 2, :]) if lat else (self.u_ctx, self.u_ctx[:, i, :])
                mk.op("dve", lambda e: e.tensor_copy(out=uap, in_=p[:]), reads=[p], writes=[udst])
                pk = self.bank()
                self.proj(pk[64:96, 0:128], pk, wA, (768, 800), hT, (0, 128))
                self.proj(pk[64:96, 128:256], pk, wA, (800, 832), hT, (0, 128))
                a1, a2 = t1[i % 2], t2[i % 2]
                mk.op("dve", lambda e: e.tensor_tensor(out=a1[64:96, :], in0=pk[64:96, 0:128], in1=tb[64:96, 0, :], op=ALU.mult),
                      reads=[pk, tb], writes=[a1])
                mk.op("dve", lambda e: e.tensor_tensor(out=a2[64:96, :], in0=pk[64:96, 128:256], in1=tb[64:96, 1, :], op=ALU.mult),
                      reads=[pk, tb], writes=[a2])
                ps_ = pst[i % 2]
                mk.op("pool", lambda e: e.tensor_tensor(out=ps_[64:96, :], in0=a1[64:96, :], in1=a2[64:96, :], op=ALU.add),
                      reads=[a1, a2], writes=[ps_])
                mk.dma("sp", self.kpe_d.t.ap()[:, i * 128:(i + 1) * 128], ps_[64:96, :], reads=[ps_], writes=[self.kpe_d])
                pss = self.bank()
                for j in range(2):
                    mk.op("pe", lambda e: e.matmul(out=pss[:, 0:128], lhsT=self.onesb[:], rhs=sq[:, j * 128:(j + 1) * 128],
                                                   start=(j == 0), stop=(j == 1)), reads=[self.onesb, sq], writes=[pss], inc=(j == 1))
                r_ = rk[i % 2]
                self.rstd_bcast(None, pss, 128, 256.0, r_)
                ks = kst[i % 2]
                for j in range(2):
                    mk.op("dve", lambda e: e.tensor_tensor(out=ks[:, j, :], in0=pkv[:, j * 128:(j + 1) * 128], in1=r_[:], op=ALU.mult),
                          reads=[pkv, r_], writes=[ks])
                mk.dma("sp", self.kvn_d.t.ap()[:, :, i * 128:(i + 1) * 128].rearrange("j p n -> p j n"), ks[:], reads=[ks], writes=[self.kvn_d])

            self.norm_pipe(66, lambda i: din["xa"].t.ap()[i - 2] if i >= 2 else din["ctx"].t.ap()[i],
                           lambda i: (self.bc["Al" if i >= 2 else "Ac"], self.bc["Bl" if i >= 2 else "Bc"]),
                           lambda i: (hTs[i % 3], hTs[i % 3][:, :, :]), s3=s3)
            mk.barrier()

    def mk_tokmajor(self, p, hT, w, c0, n):
        for c in range(8):
            self.mk.op("pe", lambda e: e.matmul(out=p[:, 0:n], lhsT=hT[:, c, :], rhs=w[:, c, c0:c0 + n], start=(c == 0), stop=(c == 7)),
                       reads=[hT, w], writes=[p], inc=(c == 7))

    def phase_F(self):
        mk, din = self.mk, self.din
        with ExitStack() as es:
            T1 = self.sb(es, [128, 64, 256], BF16)
            for k in range(4):
                mk.dma("sp", T1[:, k * 16:(k + 1) * 16, :], din["T1"].t.ap()[:, k * 16:(k + 1) * 16].rearrange("p a r k -> p a (r k)"), writes=[T1])
            CS = self.sb(es, [128, 2, 256], BF16); F2 = self.sb(es, [128, 2, 36], BF16); DC = self.sb(es, [128, 2, 512], BF16)
            mk.dma("sp", CS[:], din["CS"].t.ap(), writes=[CS])
            mk.dma("sp", F2[:], din["F2"].t.ap(), writes=[F2])
            mk.dma("sp", DC[:], din["DC"].t.ap().rearrange("p t r k -> p t (r k)"), writes=[DC])
            fg = self.sb(es, [128, 4, NQ], BF16)
            mk.dma("sp", fg[:], self.fg_d.t.ap().rearrange("j p n -> p j n"), reads=[self.fg_d], writes=[fg])
            A1 = self.sb(es, [128, 128, 128], BF16)
            A1w = A1[:, :, :].rearrange("p (a n) (r k) -> p n a r k", a=2, r=2)
            Gb = [self.sb(es, [128, 2, 256], BF16) for _ in range(3)]
            catF = [self.sb(es, [128, NOWN], BF16) for _ in range(2)]
            XT = self.sb(es, [128, 512], BF16); cst = self.sb(es, [128, 256], BF16)
            for g in range(4):
                for pr in range(32):
                    p = self.bank()
                    for s in range(2):
                        n2 = 2 * pr + s
                        mk.op("pe", lambda e: e.matmul(out=p[:, s * 256:(s + 1) * 256], lhsT=self.u_all[:, n2, g * 128:(g + 1) * 128],
                                                       rhs=T1[:, n2, :], start=True, stop=True), reads=[self.u_all, T1], writes=[p], inc=(s == 1))
                    eng = "dve" if pr % 2 == 0 else "act"
                    for s_ in range(2):
                        oap = A1w[:, 2 * pr + s_]
                        iap = p[:, s_ * 256:(s_ + 1) * 256].rearrange("p (r k a) -> p a r k", r=2, a=2)
                        if s_ == 0:
                            mk.op("dve", lambda e: e.tensor_copy(out=oap, in_=iap), reads=[p], writes=[A1])
                        else:
                            mk.op("act", lambda e: e.activation(out=oap, in_=iap, func=AF.Copy), reads=[p], writes=[A1])
                cf = catF[g % 2]
                cfv = cf[:, :].rearrange("p (k2 k1) -> p k1 k2", k1=128)
                fgv = fg[:, g, 0:NOWN].rearrange("p (k2 k1) -> p k1 k2", k1=128)
                fst = {"pacc": None, "a0": 0}

                def s3(pr, gb, fst=fst, cf=cf, cfv=cfv, fgv=fgv):
                    for s_ in range(2):
                        kp = 2 * pr + s_
                        k1 = 2 * kp
                        if k1 % 28 == 0:
                            fst["pacc"] = self.bank(hold=True); fst["a0"] = k1
                        pacc, a0 = fst["pacc"], fst["a0"]
                        sl = (k1 - a0) * 18
                        mk.op("pe", lambda e: e.matmul(out=pacc[:, sl:sl + 36], lhsT=gb[:, s_, 0:128], rhs=F2[:, 0, :], start=True, stop=False),
                              reads=[gb, F2], writes=[pacc], inc=False)
                        mk.op("pe", lambda e: e.matmul(out=pacc[:, sl:sl + 36], lhsT=gb[:, s_, 128:256], rhs=F2[:, 1, :], start=False, stop=True),
                              reads=[gb, F2], writes=[pacc])
                        if (k1 + 1) % 28 == 27 or k1 + 1 == 127:
                            cnt = k1 + 2 - a0
                            mk.op("dve", lambda e: e.tensor_tensor(out=cfv[:, a0:a0 + cnt, :], in0=pacc[:, 0:cnt * 18].rearrange("p (a b) -> p a b", b=18),
                                                                   in1=fgv[:, a0:a0 + cnt, :], op=ALU.mult), reads=[pacc, fg], writes=[cf])
                            self.release(pacc)

                prev = None
                for pr in range(32):
                    p = self.bank()
                    for s_ in range(2):
                        kp = 2 * pr + s_
                        mk.op("pe", lambda e: e.matmul(out=p[:, s_ * 256:(s_ + 1) * 256], lhsT=A1[:, :, kp], rhs=CS[:, 0, :], start=True, stop=False),
                              reads=[A1, CS], writes=[p], inc=False)
                        mk.op("pe", lambda e: e.matmul(out=p[:, s_ * 256:(s_ + 1) * 256], lhsT=A1[:, :, 64 + kp], rhs=CS[:, 1, :], start=False, stop=True),
                              reads=[A1, CS], writes=[p], inc=(s_ == 1))
                    gb = Gb[pr % 3]
                    if pr % 2 == 0:
                        mk.op("dve", lambda e: e.tensor_copy(out=gb[:, :, :], in_=p[:, :].rearrange("p (a b) -> p a b", a=2)), reads=[p], writes=[gb])
                    else:
                        mk.op("act", lambda e: e.activation(out=gb[:, :, :], in_=p[:, :].rearrange("p (a b) -> p a b", a=2), func=AF.Copy), reads=[p], writes=[gb])
                    if prev is not None:
                        s3(*prev)
                    prev = (pr, gb)
                s3(*prev)
                mk.dma("sp", self.cat_d.t.ap()[g, :, 0:NOWN], cf[:, :], reads=[cf], writes=[self.cat_d])
                px = self.bank()
                for nt in range(2):
                    mk.op("pe", lambda e: e.matmul(out=px[:, :], lhsT=self.u_ctx[:, nt, g * 128:(g + 1) * 128], rhs=DC[:, nt, :], start=(nt == 0), stop=(nt == 1)),
                          reads=[self.u_ctx, DC], writes=[px], inc=(nt == 1))
                mk.op("dve", lambda e: e.tensor_copy(out=XT[:], in_=px[:]), reads=[px], writes=[XT])
                py = self.bank()
                mk.op("pe", lambda e: e.matmul(out=py[:, 0:256], lhsT=CS[:, 0, 0:128], rhs=XT[:, 0:256], start=True, stop=False), reads=[CS, XT], writes=[py], inc=False)
                mk.op("pe", lambda e: e.matmul(out=py[:, 0:256], lhsT=CS[:, 1, 0:128], rhs=XT[:, 256:512], start=False, stop=True), reads=[CS, XT], writes=[py])
                mk.op("dve", lambda e: e.tensor_tensor(out=cst[:], in0=py[:, 0:256], in1=fg[:, g, NOWN:NQ], op=ALU.mult), reads=[py, fg], writes=[cst])
                mk.dma("sp", self.cat_d.t.ap()[g, :, NOWN:NQ], cst[:], reads=[cst], writes=[self.cat_d])
            mk.barrier()

    def wscaled(self, es, src, nk, ncols, normname):
        mk = self.mk
        w = self.sb(es, [128, nk, ncols], BF16)
        g = self.sb(es, [128, nk], F32)
        self.wload(w, 0, src, 0, nk, 0, ncols)
        mk.dma("sp", g[:], self.din[normname].t.ap(), writes=[g])
        for c in range(nk):
            mk.op("dve", lambda e: e.tensor_scalar(out=w[:, c, :], in0=w[:, c, :], scalar1=g[:, c:c + 1], scalar2=None, op0=ALU.mult),
                  reads=[w, g], writes=[w])
        return w

    def phase_ATT(self):
        mk, din = self.mk, self.din
        with ExitStack() as es:
            kvn = self.sb(es, [128, 2, NKEY], BF16)
            KTs = [self.sb(es, [128, NKEY], BF16) for _ in range(2)]
            VAs = [self.sb(es, [128, 66, 128], BF16) for _ in range(2)]
            qan = self.sb(es, [128, 3, NQ], BF16)
            QTs = [self.sb(es, [128, NQ], BF16) for _ in range(2)]
            tabq = self.sb(es, [128, 2, NQ], F32)
            mgs = [self.sb(es, [128, NQ], BF16) for _ in range(2)]
            ATs = [self.sb(es, [128, NQ], BF16)] * 2
            for j in range(2):
                mk.dma("sp", kvn[:, j, :], self.kvn_d.t.ap()[j], reads=[self.kvn_d], writes=[kvn])
            for KT in KTs:
                mk.dma("sp", KT[64:96, :], self.kpe_d.t.ap(), reads=[self.kpe_d], writes=[KT])
            mk.dma("sp", qan[:], self.qan_d.t.ap().rearrange("j p n -> p j n"), reads=[self.qan_d], writes=[qan])
            mk.dma("sp", tabq[64:96, :, :], din["tabQ"].t.ap().rearrange("t p n -> p t n"), writes=[tabq])
            wq = self.wscaled(es, din["wqb"], 3, 1024, "qnorm")
            wkv = self.wscaled(es, din["wkvb"], 2, 1024, "kvnorm")
            mk.op("dve", lambda e: e.memset(VAs[0][:, :, 64:128], 0.0), writes=[VAs[0]])
            mk.op("dve", lambda e: e.memset(VAs[0][:, :, 64:65], 1.0), writes=[VAs[0]])
            mk.op("dve", lambda e: e.memset(VAs[1][:, :, 0:64], 0.0), writes=[VAs[1]])
            mk.op("dve", lambda e: e.memset(VAs[1][:, :, 0:1], 1.0), writes=[VAs[1]])
            PTs = [self.sb(es, [128, 512], BF16) for _ in range(4)]
            t1 = self.sb(es, [128, 512], F32); t2 = self.sb(es, [128, 512], F32)
            rec = self.sb(es, [128, 512], F32); tmp = self.sb(es, [128, 512], F32)

            def gen_steps(h):
                odd = h % 2
                o0 = 64 if odd else 0
                KT, VA, QT = KTs[odd], VAs[odd], QTs[odd]
                steps = []
                if not odd:
                    mgt = mgs[(h // 2) % 2]
                    steps.append(lambda: mk.dma("sp", mgt[:], self.mg_d.t.ap()[h // 2], reads=[self.mg_d], writes=[mgt]))
                for kb in range(17):
                    def f(kb=kb):
                        k0 = kb * 512
                        n = min(512, NKEY - k0)
                        p = self.bank()
                        self.proj(p[0:64, 0:n], p, wkv, (h * 128, h * 128 + 64), kvn, (k0, k0 + n), n_c=2)
                        mk.op("dve", lambda e: e.tensor_copy(out=KT[0:64, k0:k0 + n], in_=p[0:64, 0:n]), reads=[p], writes=[KT])
                    steps.append(f)
                for g0 in range(0, 66, 8):
                    def f(g0=g0):
                        cnt = min(8, 66 - g0)
                        p = self.bank()
                        for t in range(cnt):
                            kt = g0 + t
                            for j in range(2):
                                mk.op("pe", lambda e: e.matmul(out=p[:, t * 64:(t + 1) * 64], lhsT=kvn[:, j, kt * 128:(kt + 1) * 128],
                                                               rhs=wkv[:, j, h * 128 + 64:h * 128 + 128], start=(j == 0), stop=(j == 1)),
                                      reads=[kvn, wkv], writes=[p], inc=(j == 1 and t == cnt - 1))
                        mk.op("dve", lambda e: e.tensor_copy(out=VA[:, g0:g0 + cnt, o0:o0 + 64], in_=p[:, 0:cnt * 64].rearrange("p (a b) -> p a b", b=64)),
                              reads=[p], writes=[VA])
                    steps.append(f)
                for b0, n in QBLK:
                    def f(b0=b0, n=n):
                        pa = self.bank(); pb = self.bank()
                        self.proj(pa[0:96, 0:n], pa, wq, (h * 96, h * 96 + 96), qan, (b0, b0 + n), n_c=3)
                        self.proj(pb[64:96, 0:n], pb, wq, (768 + h * 32, 800 + h * 32), qan, (b0, b0 + n), n_c=3)
                        mk.op("dve", lambda e: e.tensor_copy(out=QT[0:64, b0:b0 + n], in_=pa[0:64, 0:n]), reads=[pa], writes=[QT])
                        mk.op("dve", lambda e: e.tensor_tensor(out=t1[64:96, 0:n], in0=pa[64:96, 0:n], in1=tabq[64:96, 0, b0:b0 + n], op=ALU.mult),
                              reads=[pa, tabq], writes=[t1])
                        mk.op("dve", lambda e: e.tensor_tensor(out=t2[64:96, 0:n], in0=pb[64:96, 0:n], in1=tabq[64:96, 1, b0:b0 + n], op=ALU.mult),
                              reads=[pb, tabq], writes=[t2])
                        mk.op("pool", lambda e: e.tensor_tensor(out=QT[64:96, b0:b0 + n], in0=t1[64:96, 0:n], in1=t2[64:96, 0:n], op=ALU.add),
                              reads=[t1, t2], writes=[QT])
                    steps.append(f)
                return steps

            for f in gen_steps(0):
                f()
            items = []
            for h in range(8):
                for bi, (b0, n, kts) in enumerate([(0, 512, 66), (512, 512, 66), (1024, 512, 66), (1536, 512, 66), (2048, 256, 66), (2304, 256, 2)]):
                    for kt in range(kts):
                        items.append((h, b0, n, kt, kts))
            pending = []
            state = {}
            LAG = 2
            nit = len(items)
            per_head = nit // 8
            for t in range(nit + LAG):
                if t < nit:
                    h, b0, n, kt, kts = items[t]
                    odd = h % 2
                    if kt == 0:
                        state[(h, b0)] = self.bank(hold=True)
                    if t % per_head == 0 and h < 7:
                        pending = gen_steps(h + 1)
                        every = max(1, (per_head - 40) // len(pending))
                    if pending and (t % per_head) % every == 0:
                        pending.pop(0)()
                    ps = self.bank()
                    mk.op("pe", lambda e: e.matmul(out=ps[:, 0:n], lhsT=KTs[odd][0:96, kt * 128:(kt + 1) * 128], rhs=QTs[odd][0:96, b0:b0 + n], start=True, stop=True),
                          reads=[KTs[odd], QTs[odd]], writes=[ps])
                    pt = PTs[t % 4]
                    mk.op("act", lambda e: e.activation(out=pt[:, 0:n], in_=ps[:, 0:n], func=AF.Exp, scale=MLA_SCALE), reads=[ps], writes=[pt])
                if t >= LAG:
                    h, b0, n, kt, kts = items[t - LAG]
                    odd = h % 2
                    o0, r0 = (64, 0) if odd else (0, 64)
                    po = state[(h, b0)]
                    pt = PTs[(t - LAG) % 4]
                    mk.op("pe", lambda e: e.matmul(out=po[:, 0:n], lhsT=VAs[odd][:, kt, :], rhs=pt[:, 0:n], start=(kt == 0), stop=(kt == kts - 1)),
                          reads=[VAs[odd], pt], writes=[po], inc=(kt == kts - 1))
                    if kt == kts - 1:
                        mgt = mgs[(h // 2) % 2]; AT = ATs[(h // 2) % 2]
                        self.att_finish(po, n, o0, r0, mgt, mgt[o0:o0 + 64, b0:b0 + n], AT, AT[o0:o0 + 64, b0:b0 + n], rec, tmp)
                        self.release(po)
                        if odd and b0 == 2304:
                            mk.dma("sp", self.cat_d.t.ap()[4 + h // 2], AT[:], reads=[AT], writes=[self.cat_d])
            while pending:
                pending.pop(0)()
            mk.barrier()

    def att_finish(self, po, n, o0, r0, gate_tile, gate_ap, dst_tile, dst_ap, rec, tmp, extra=None, part=None):
        mk = self.mk
        if part in (None, "A"):
            self._att_finish_a(po, n, r0, rec, extra)
        if part in (None, "B"):
            self._att_finish_b(po, n, o0, r0, gate_tile, gate_ap, dst_tile, dst_ap, rec, tmp)

    def _att_finish_a(self, po, n, r0, rec, extra):
        mk = self.mk
        if extra is not None:
            mk.op("dve", lambda e: e.tensor_tensor(out=rec[r0:r0 + 1, 0:n], in0=po[r0:r0 + 1, 0:n], in1=extra, op=ALU.add), reads=[po], writes=[rec])
            mk.op("act", lambda e: e.activation(out=rec[r0:r0 + 1, 0:n], in_=rec[r0:r0 + 1, 0:n], func=AF.Ln), reads=[rec], writes=[rec])
        else:
            mk.op("act", lambda e: e.activation(out=rec[r0:r0 + 1, 0:n], in_=po[r0:r0 + 1, 0:n], func=AF.Ln), reads=[po], writes=[rec])
        mk.op("act", lambda e: e.activation(out=rec[r0:r0 + 1, 0:n], in_=rec[r0:r0 + 1, 0:n], func=AF.Exp, scale=-1.0), reads=[rec], writes=[rec])

    def _att_finish_b(self, po, n, o0, r0, gate_tile, gate_ap, dst_tile, dst_ap, rec, tmp):
        mk = self.mk
        pb = self.bank()
        mk.op("pe", lambda e: e.matmul(out=pb[o0:o0 + 64, 0:n], lhsT=self.onesf[r0:r0 + 1, 0:64], rhs=rec[r0:r0 + 1, 0:n], start=True, stop=True),
              reads=[self.onesf, rec], writes=[pb])
        mk.op("dve", lambda e: e.tensor_tensor(out=tmp[o0:o0 + 64, 0:n], in0=po[o0:o0 + 64, 0:n], in1=gate_ap, op=ALU.mult),
              reads=[po, gate_tile], writes=[tmp])
        mk.op("dve", lambda e: e.tensor_tensor(out=dst_ap, in0=tmp[o0:o0 + 64, 0:n], in1=pb[o0:o0 + 64, 0:n], op=ALU.mult),
              reads=[tmp, pb], writes=[dst_tile])

    def phase_OUT(self, wname, xsrc, ntiles, tile0, final=False, G=None, side=None):
        mk, din = self.mk, self.din
        with ExitStack() as es:
            wo = self.sb(es, [128, 8, D], BF16)
            self.wload(wo, 0, din[wname], 0, 8, 0, 512)
            self.wload(wo, 512, din[wname], 0, 8, 512, 512)
            catb = [self.sb(es, [128, 8, 512], BF16) for _ in range(3)]
            xts = [self.sb(es, [128, D], F32) for _ in range(4)]
            tmp = [self.sb(es, [128, D], F32) for _ in range(2)]
            xn = [self.sb(es, [128, D], F32) for _ in range(2)]
            junk = self.sb(es, [128, D], BF16); ss = self.sb(es, [128, 1], F32); rstd = self.sb(es, [128, 1], F32)
            fgb = None
            if final:
                fgb = self.sb(es, [128, D], F32)
                fr = self.sb(es, [1, D], F32)
                mk.dma("sp", fr[:], din["finalg"].t.ap(), writes=[fr])
                self.bcast_row(fr, fgb)
            side_steps = side(es) if side is not None else []
            loaded = set()

            def load(ti):
                gi = tile0 + ti
                blk = gi // 4
                if blk not in loaded:
                    loaded.add(blk)
                    cb = catb[blk % 3]
                    mk.dma("sp", cb[:], self.cat_d.t.ap()[:, :, blk * 512:(blk + 1) * 512].rearrange("c p n -> p c n"), reads=[self.cat_d], writes=[cb])
                xt = xts[ti % 4]
                mk.dma("sp", xt[:], xsrc(gi), reads=[self.x1_d], writes=[xt])

            for ti in range(min(2, ntiles)):
                load(ti)
            for ti in range(ntiles):
                if ti + 2 < ntiles:
                    load(ti + 2)
                if side_steps:
                    side_steps.pop(0)()
                gi = tile0 + ti
                blk, t = divmod(gi, 4)
                cb = catb[blk % 3]
                xt = xts[ti % 4]; tm = tmp[ti % 2]; xo = xn[ti % 2]
                Gt = G["Gl" if gi < 18 else "Gc"]
                for half in range(2):
                    p = self.bank()
                    for c in range(8):
                        mk.op("pe", lambda e: e.matmul(out=p[:], lhsT=cb[:, c, t * 128:(t + 1) * 128], rhs=wo[:, c, half * 512:(half + 1) * 512],
                                                       start=(c == 0), stop=(c == 7)), reads=[cb, wo], writes=[p], inc=(c == 7))
                    mk.op("dve", lambda e: e.tensor_tensor(out=tm[:, half * 512:(half + 1) * 512], in0=p[:], in1=Gt[:, half * 512:(half + 1) * 512], op=ALU.mult),
                          reads=[p, Gt], writes=[tm])
                mk.op("pool", lambda e: e.tensor_tensor(out=xo[:], in0=tm[:], in1=xt[:], op=ALU.add), reads=[tm, xt], writes=[xo])
                if not final:
                    mk.dma("sp", self.x1_d.t.ap()[gi], xo[:], reads=[xo], writes=[self.x1_d])
                else:
                    mk.op("act", lambda e: e.activation(out=junk[:], in_=xo[:], func=AF.Square, scale=1.0 / 32.0, accum_out=ss[:]), reads=[xo], writes=[junk, ss])
                    mk.op("act", lambda e: e.activation(out=rstd[:], in_=ss[:], func=AF.Ln, bias=self.epsb[:], scale=1.0), reads=[ss, self.epsb], writes=[rstd])
                    mk.op("act", lambda e: e.activation(out=rstd[:], in_=rstd[:], func=AF.Exp, scale=-0.5), reads=[rstd], writes=[rstd])
                    mk.op("dve", lambda e: e.scalar_tensor_tensor(out=tm[:], in0=xo[:], scalar=rstd[:, 0:1], in1=fgb[:], op0=ALU.mult, op1=ALU.mult),
                          reads=[xo, rstd, fgb], writes=[tm])
                    mk.dma("sp", self.out.t.ap()[ti], tm[:], reads=[tm], writes=[self.out])
            while side_steps:
                side_steps.pop(0)()
            mk.barrier()

    def bcast_row(self, row, dst):
        mk = self.mk
        for half in range(2):
            p = self.bank()
            mk.op("pe", lambda e: e.matmul(out=p[0:64, :], lhsT=self.onesf[0:1, 0:64], rhs=row[0:1, half * 512:(half + 1) * 512], start=True, stop=True),
                  reads=[self.onesf, row], writes=[p])
            mk.op("pe", lambda e: e.matmul(out=p[64:128, :], lhsT=self.onesf[0:1, 0:64], rhs=row[0:1, half * 512:(half + 1) * 512], start=True, stop=True),
                  reads=[self.onesf, row], writes=[p])
            mk.op("dve", lambda e: e.tensor_copy(out=dst[:, half * 512:(half + 1) * 512], in_=p[:]), reads=[p], writes=[dst])

    def layer0(self):
        with ExitStack() as es:
            self.u_all = self.sb(es, [128, 64, 512], BF16)
            self.u_ctx = self.sb(es, [128, 2, 512], BF16)
            with ExitStack() as es_ab:
                with ExitStack() as es_tmp:
                    for f in self.mod_steps(0, es_ab, es_tmp):
                        f()
                    self.mk.barrier()
                self.bc = self.bcs[0]
                self.phase_O()
                self.phase_A(es)
            self.phase_F()
        self.mk.barrier()
        self.phase_ATT()
        din = self.din
        self.es_ab1 = ExitStack()
        self.alloc_bc(1, self.es_ab1)
        self.phase_OUT("wout0", lambda gi: din["xo"].t.ap()[gi] if gi < 18 else din["ctx"].t.ap()[gi - 18], 20, 0,
                       G=self.bcs[0], side=lambda es_tmp: self.mod_steps(1, self.es_ab1, es_tmp))

    def phase_L1(self):
        mk, din = self.mk, self.din
        g1_d = self.g1_d
        with ExitStack() as es:
            QT1 = self.sb(es, [128, 8, NOWN], BF16)
            KT1 = self.sb(es, [128, 4, NQ], BF16)
            VA1 = self.sb(es, [128, 20, 4, 2, 128], BF16)
            mk.op("pool", lambda e: e.memset(VA1[:].rearrange("p a b c d -> p (a b c d)"), 0.0), writes=[VA1])
            mk.op("pool", lambda e: e.memset(VA1[:, :, :, 0, 64:65], 1.0), writes=[VA1])
            mk.op("pool", lambda e: e.memset(VA1[:, :, :, 1, 0:1], 1.0), writes=[VA1])
            self.bc = self.bcs[1]
            with ExitStack() as es2:
                hT = self.sb(es2, [128, 8, NQ], BF16)
                with ExitStack() as es3:
                    self.make_norm_scratch(es3, n=2, nx=3)
                    self.norm_pipe(20, lambda i: self.x1_d.t.ap()[i],
                                   lambda i: (self.bc["Al" if i < 18 else "Ac"], self.bc["Bl" if i < 18 else "Bc"]),
                                   lambda i: (hT, hT[:, :, i * 128:(i + 1) * 128]))
                    mk.barrier()
                tab1 = self.sb(es2, [128, 2, NQ], F32)
                mk.dma("sp", tab1[:], din["tab1"].t.ap().rearrange("t p n -> p t n"), writes=[tab1])
                wps = [self.sb(es2, [128, 8, 256], BF16) for _ in range(2)]
                g0l, g0c = self.G0["Gl"], self.G0["Gc"]
                st = [self.sb(es2, [128, 512], BF16) for _ in range(2)]
                jobs = [("q", hp, [(0, hp * 128, 128), (128, 2560 + hp * 128, 128)]) for hp in range(8)]
                jobs += [("k", kh, [(0, 3584 + kh * 128, 128), (128, 4096 + kh * 128, 128)]) for kh in range(4)]
                jobs += [("g", hp, [(0, 1536 + hp * 128, 128)]) for hp in range(8)]
                jobs += [("v", 0, [(0, 1280, 256)])]

                def jload(ji):
                    for dcol, c0, n in jobs[ji][2]:
                        self.wload(wps[ji % 2], dcol, din["w1in"], 0, 8, c0, n)

                jload(0)
                si = 0
                ri = 0
                for ji, (kind, idx, _) in enumerate(jobs):
                    wp = wps[ji % 2]
                    first = True
                    if kind in ("q", "k"):
                        for b0, n in QBLK:
                            if kind == "q" and b0 >= NOWN:
                                continue
                            n_ = min(n, NOWN - b0) if kind == "q" else n
                            t1 = g0l[:, (ri % 2) * 512:(ri % 2) * 512 + n_]; t2 = g0c[:, (ri % 2) * 512:(ri % 2) * 512 + n_]
                            ri += 1
                            pa = self.bank(); pb = self.bank()
                            self.proj(pa[:, 0:n_], pa, wp, (0, 128), hT, (b0, b0 + n_))
                            self.proj(pb[:, 0:n_], pb, wp, (128, 256), hT, (b0, b0 + n_))
                            if first and ji + 1 < len(jobs):
                                jload(ji + 1); first = False
                            mk.op("dve", lambda e: e.tensor_tensor(out=t1, in0=pa[:, 0:n_], in1=tab1[:, 0, b0:b0 + n_], op=ALU.mult), reads=[pa, tab1], writes=[g0l])
                            mk.op("dve", lambda e: e.tensor_tensor(out=t2, in0=pb[:, 0:n_], in1=tab1[:, 1, b0:b0 + n_], op=ALU.mult), reads=[pb, tab1], writes=[g0c])
                            dst, dap = (QT1, QT1[:, idx, b0:b0 + n_]) if kind == "q" else (KT1, KT1[:, idx, b0:b0 + n_])
                            mk.op("pool", lambda e: e.tensor_tensor(out=dap, in0=t1, in1=t2, op=ALU.add), reads=[g0l, g0c], writes=[dst])
                    elif kind == "g":
                        for b0, n in QBLK:
                            p = self.bank()
                            self.proj(p[:, 0:n], p, wp, (0, 128), hT, (b0, b0 + n))
                            if first and ji + 1 < len(jobs):
                                jload(ji + 1); first = False
                            s_ = st[si % 2]; si += 1
                            mk.op("act", lambda e: e.activation(out=s_[:, 0:n], in_=p[:, 0:n], func=AF.Silu), reads=[p], writes=[s_])
                            mk.dma("sp", g1_d.t.ap()[idx, :, b0:b0 + n], s_[:, 0:n], reads=[s_], writes=[g1_d])
                    else:
                        for i in range(20):
                            p = self.bank()
                            for c in range(8):
                                mk.op("pe", lambda e: e.matmul(out=p[:, 0:256], lhsT=hT[:, c, i * 128:(i + 1) * 128], rhs=wp[:, c, 0:256], start=(c == 0), stop=(c == 7)),
                                      reads=[hT, wp], writes=[p], inc=(c == 7))
                            pv = p[:, 0:256].rearrange("p (a b) -> p a b", b=64)
                            mk.op("dve", lambda e: e.tensor_copy(out=VA1[:, i, :, 0, 0:64], in_=pv), reads=[p], writes=[VA1])
                            mk.op("act", lambda e: e.activation(out=VA1[:, i, :, 1, 64:128], in_=pv, func=AF.Copy), reads=[p], writes=[VA1])
                mk.barrier()
            maskb = self.sb(es, [128, 4, 512], BF16); kval = self.sb(es, [128, 20], F32)
            mk.dma("sp", maskb[:], din["maskb"].t.ap().rearrange("p d s j q -> p d (s j q)"), writes=[maskb])
            mk.dma("sp", kval[:], din["kvalid"].t.ap(), writes=[kval])
            snk = self.sb(es, [128, 16], F32); esk = self.sb(es, [128, 16], F32)
            for r in (0, 64):
                mk.dma("sp", snk[r:r + 1, :], din["sink"].t.ap(), writes=[snk])
            mk.op("act", lambda e: e.activation(out=esk[0:1, :], in_=snk[0:1, :], func=AF.Exp), reads=[snk], writes=[esk])
            mk.op("act", lambda e: e.activation(out=esk[64:65, :], in_=snk[64:65, :], func=AF.Exp), reads=[snk], writes=[esk])
            esr = self.sb(es, [128, 8, 512], F32)
            mk.op("pool", lambda e: e.memset(esr[:].rearrange("p a b -> p (a b)"), 0.0), writes=[esr])
            for kh in range(4):
                for par in range(2):
                    for k_ in range(2):
                        hq = 4 * kh + 2 * k_ + par
                        for r in (0, 64):
                            sl = esr[r:r + 1, kh * 2 + par, k_ * 256:(k_ + 1) * 256]
                            mk.op("dve", lambda e: e.tensor_scalar(out=sl, in0=sl, scalar1=esk[r:r + 1, hq:hq + 1], scalar2=None, op0=ALU.add),
                                  reads=[esk, esr], writes=[esr])
            gt = [self.sb(es, [128, 2, NQ], BF16) for _ in range(2)]
            AT = [self.sb(es, [128, 2, 2048], BF16) for _ in range(2)]
            PTs = [self.sb(es, [128, 512], BF16) for _ in range(18)]
            rec = self.sb(es, [128, 512], F32); tmp = self.sb(es, [128, 512], F32)
            items = [(kh, par, nb) for kh in range(4) for par in range(2) for nb in range(0, 16, 2)]
            st = {}
            ni = len(items)

            def fin_args(u):
                kh, par, nb = items[u]
                g_ = gt[kh % 2]; at = AT[kh % 2]
                o0, r0 = (64, 0) if par else (0, 64)
                q0 = 128 + 128 * nb
                return dict(po=st[u], n=512, o0=o0, r0=r0, gate_tile=g_, gate_ap=g_[o0:o0 + 64, :, q0:q0 + 256], dst_tile=at,
                            dst_ap=at[o0:o0 + 64, :, nb * 128:(nb + 2) * 128], rec=rec, tmp=tmp, extra=esr[r0:r0 + 1, kh * 2 + par, :])

            for t in range(ni + 2):
                if 0 <= t - 2 < ni:
                    self.att_finish(part="A", **fin_args(t - 2))
                if t < ni:
                    kh, par, nb = items[t]
                    if par == 0 and nb == 0:
                        g_ = gt[kh % 2]
                        mk.dma("sp", g_[:], g1_d.t.ap()[2 * kh:2 * kh + 2].rearrange("c p n -> p c n"), reads=[g1_d], writes=[g_])
                    p0 = par * 64
                    q0 = 128 + 128 * nb
                    kts = [(18, None), (19, None), (nb, 0), (nb + 1, 1), (nb + 2, 2), (nb + 3, 3)]
                    for ki, (kt, mi) in enumerate(kts):
                        ps = self.bank()
                        mk.op("pe", lambda e: e.matmul(out=ps[:, :], lhsT=KT1[p0:p0 + 64, kh, kt * 128:(kt + 1) * 128],
                                                       rhs=QT1[p0:p0 + 64, 2 * kh:2 * kh + 2, q0:q0 + 256], start=True, stop=(mi is None)),
                              reads=[KT1, QT1], writes=[ps], inc=(mi is None))
                        if mi is not None:
                            mk.op("pe", lambda e: e.matmul(out=ps[:, :], lhsT=self.ident[:], rhs=maskb[:, mi, :], start=False, stop=True),
                                  reads=[self.ident, maskb], writes=[ps])
                        pt = PTs[(t % 3) * 6 + ki]
                        mk.op("act", lambda e: e.activation(out=pt[:], in_=ps[:, :], func=AF.Exp, scale=GQA_SCALE, bias=kval[:, kt:kt + 1]),
                              reads=[ps, kval], writes=[pt])
                if 0 <= t - 2 < ni:
                    u = t - 2
                    self.att_finish(part="B", **fin_args(u))
                    self.release(st.pop(u))
                    kh, par, nb = items[u]
                    if par == 1 and nb == 14:
                        at = AT[kh % 2]
                        mk.dma("sp", self.cat_d.t.ap()[2 * kh:2 * kh + 2, :, 128:2176].rearrange("c p n -> p c n"), at[:], reads=[at], writes=[self.cat_d])
                if 0 <= t - 1 < ni:
                    u = t - 1
                    kh, par, nb = items[u]
                    po = st[u] = self.bank(hold=True)
                    kts = [18, 19, nb, nb + 1, nb + 2, nb + 3]
                    for ki, kt in enumerate(kts):
                        pt = PTs[(u % 3) * 6 + ki]
                        mk.op("pe", lambda e: e.matmul(out=po[:, :], lhsT=VA1[:, kt, kh, par, :], rhs=pt[:], start=(ki == 0), stop=(ki == 5)),
                              reads=[VA1, pt], writes=[po], inc=(ki == 5))
            mk.barrier()

    def layer1(self):
        self.phase_L1()
        x1 = self.x1_d
        self.phase_OUT("wout1", lambda gi: x1.t.ap()[gi], 16, 1, final=True, G=self.bcs[1])
        self.es_ab1.close()


def build_program(debug=False, upto=99):
    pr = Prog(debug=debug, upto=upto)
    pr.g1_d = pr.mk.dram([8, 128, NQ], BF16, "g1_d", "Internal")
    pr.layer0()
    if upto >= 1:
        pr.layer1()
    pr.mk.finish()
    return pr


def kernel(**inputs):
    W = pack_weights(inputs)
    in_maps = []
    for core in range(8):
        b, q = divmod(core, 4)
        m = core_inputs(inputs, W, b, q)
        in_maps.append({n: m[n] for n, _, _ in IN_SPECS})
    pr = build_program()
    res = run_bass_kernel_spmd(pr.nc, in_maps, core_ids=list(range(8)))
    out = np.zeros((NB, SEQ, D), np.float32)
    for core in range(8):
        b, q = divmod(core, 4)
        out[b, 2048 * q:2048 * (q + 1)] = np.asarray(res.results[core]["out"], np.float32).reshape(2048, D)
    return out
```

```python
import math
import numpy as np
import ml_dtypes
import concourse.bass as bass
import concourse.mybir as mybir
from concourse.bass_utils import run_bass_kernel_spmd

F32 = mybir.dt.float32
BF16 = mybir.dt.bfloat16
ALU = mybir.AluOpType
AF = mybir.ActivationFunctionType
AX = mybir.AxisListType


class Tile:
    __slots__ = ("t", "writers", "readers", "name")

    def __init__(self, t, name=""):
        self.t = t
        self.writers = []
        self.readers = []
        self.name = name

    def __getitem__(self, idx):
        return self.t[idx]


class Eng:
    def __init__(self, mk, name, eng, sem):
        self.mk, self.name, self.eng, self.sem = mk, name, eng, sem
        self.count = 0
        self.seen = {}


class MK:
    N_DMA_SEMS = 24

    def __init__(self, nc):
        self.nc = nc
        self.engs = {}
        for name, eng in (("pe", nc.tensor), ("act", nc.scalar), ("dve", nc.vector),
                          ("pool", nc.gpsimd), ("sp", nc.sync)):
            self.engs[name] = Eng(self, name, eng, nc.alloc_semaphore("s_" + name))
        self.dma_sems = [nc.alloc_semaphore("d%d" % i) for i in range(2 * self.N_DMA_SEMS)]
        self.dma_cnt = [0] * (2 * self.N_DMA_SEMS)
        self.dma_i = {"sw": 0, "hw": 0}
        self.ntile = 0

    def sb(self, shape, dt=BF16, name=None):
        self.ntile += 1
        name = name or "t%d" % self.ntile
        return Tile(self.nc.alloc_sbuf_tensor(name + "_%d" % self.ntile, list(shape), dt), name)

    def ps(self, shape, dt=F32, name=None):
        self.ntile += 1
        name = name or "p%d" % self.ntile
        return Tile(self.nc.alloc_psum_tensor(name + "_%d" % self.ntile, list(shape), dt), name)

    def dram(self, shape, dt, name, kind="Internal"):
        return Tile(self.nc.dram_tensor(name, list(shape), dt, kind=kind), name)

    def _wait(self, E, tok):
        sem, val = tok
        key = sem.num
        if E.seen.get(key, 0) >= val:
            return
        E.seen[key] = val
        E.eng.wait_ge(sem, val)

    def _deps(self, E, reads, writes):
        toks = []
        for t in reads:
            toks += t.writers
        for t in writes:
            toks += t.writers
            toks += t.readers
        best = {}
        for sem, val in toks:
            if best.get(sem.num, (None, 0))[1] < val:
                best[sem.num] = (sem, val)
        for sem, val in best.values():
            if E.name == "pe" and sem.num == E.sem.num:
                continue
            self._wait(E, (sem, val))

    def _mark(self, tok, reads, writes):
        for t in reads:
            t.readers.append(tok)
            if len(t.readers) > 64:
                t.readers = _compress(t.readers)
        for t in writes:
            t.writers = [tok]
            t.readers = []

    def op(self, eng, fn, reads=(), writes=(), inc=True):
        E = self.engs[eng]
        self._deps(E, reads, writes)
        ins = fn(E.eng)
        if inc:
            E.count += 1
            ins.then_inc(E.sem, 1)
            tok = (E.sem, E.count)
        else:
            assert eng == "pe"
            tok = (E.sem, E.count + 1)
        self._mark(tok, reads, writes)
        return tok

    def dma(self, eng, out, in_, reads=(), writes=(), **kw):
        E = self.engs[eng]
        self._deps(E, reads, writes)
        pool = "sw" if eng == "pool" else "hw"
        i = self.dma_i[pool] % self.N_DMA_SEMS + (self.N_DMA_SEMS if pool == "sw" else 0)
        self.dma_i[pool] += 1
        sem = self.dma_sems[i]
        if self.dma_cnt[i] > 0:
            self._wait(E, (sem, 16 * self.dma_cnt[i]))
        self.dma_cnt[i] += 1
        E.eng.dma_start(out=out, in_=in_, **kw).then_inc(sem, 16)
        tok = (sem, 16 * self.dma_cnt[i])
        self._mark(tok, reads, writes)
        return tok

    def barrier(self):
        for E in self.engs.values():
            for F in self.engs.values():
                if F is not E and F.count > 0:
                    self._wait(E, (F.sem, F.count))
            for i, sem in enumerate(self.dma_sems):
                if self.dma_cnt[i] > 0:
                    self._wait(E, (sem, 16 * self.dma_cnt[i]))

    def finish(self):
        E = self.engs["sp"]
        for F in self.engs.values():
            if F is not E and F.count > 0:
                self._wait(E, (F.sem, F.count))
        for i, sem in enumerate(self.dma_sems):
            if self.dma_cnt[i] > 0:
                self._wait(E, (sem, 16 * self.dma_cnt[i]))


def _compress(toks):
    best = {}
    for sem, val in toks:
        if best.get(sem.num, (None, 0))[1] < val:
            best[sem.num] = (sem, val)
    return list(best.values())


D = 1024
SEQ = 8192
NB = 2
CTX = 256
NOWN = 2304
NQ = NOWN + CTX
NKEY = SEQ + CTX
EPS = 1e-6
MLA_SCALE = 1.0 / math.sqrt(96.0)
GQA_SCALE = 1.0 / 8.0
NEG = -30000.0


def _bf(a):
    return np.ascontiguousarray(np.asarray(a, dtype=np.float32)).astype(ml_dtypes.bfloat16)


def _f32(a):
    return np.ascontiguousarray(np.asarray(a, dtype=np.float32))


def _rope_tab(tok, nf):
    tok = np.asarray(tok, dtype=np.float64)
    row = np.floor(tok / 64.0)
    col = tok - 64.0 * row
    inv = 10000.0 ** (-np.arange(nf, dtype=np.float64) / nf)
    ang = np.concatenate([row[:, None] * inv, col[:, None] * inv], axis=-1)
    return np.cos(ang).T, np.sin(ang).T


def host_tables(q):
    T = {}
    T["ident"] = _bf(np.eye(128))
    T["identf"] = _f32(np.eye(128))
    n1 = np.arange(128)[:, None, None]
    n2 = np.arange(64)[None, :, None]
    k1 = np.arange(128)[None, None, :]
    th = 2 * np.pi * ((k1 * (64 * n1 + n2)) % SEQ) / SEQ
    T["T1"] = _bf(np.stack([np.cos(th), -np.sin(th)], axis=2))
    c = np.arange(128)[:, None]
    j = np.arange(128)[None, :]
    ph = 2 * np.pi * ((c * j) % 128) / 128
    Cc, Sc = np.cos(ph), np.sin(ph)
    T["CS"] = _bf(np.stack([np.concatenate([Cc, -Sc], 1), np.concatenate([Sc, Cc], 1)], 1))
    k2 = (16 * q - 1 + np.arange(18)) % 64
    ps = 2 * np.pi * ((np.arange(64)[:, None] * k2[None, :]) % 64) / 64
    f2 = np.stack([np.cos(ps), np.sin(ps)], 1) / 1024.0
    f2b = np.zeros((128, 2, 36))
    f2b[0:64, :, 0:18] = f2
    f2b[64:128, :, 18:36] = f2
    T["F2"] = _bf(f2b)
    n = (np.arange(2)[None, :, None] * 128 + np.arange(128)[:, None, None])
    k = np.arange(256)[None, None, :]
    a = 2 * np.pi * ((n * k) % 256) / 256
    sc = 1.0 / math.sqrt(256.0 * 128.0)
    T["DC"] = _bf(np.stack([np.cos(a) * sc, -np.sin(a) * sc], 2))
    tk = np.zeros((66, 2, 32, 128), np.float32)
    tk[0:2, 0] = 1.0
    for t in range(64):
        cs, sn = _rope_tab(64 * np.arange(128) + t, 8)
        tk[2 + t, 0] = np.concatenate([cs, cs], 0)
        tk[2 + t, 1] = np.concatenate([-sn, sn], 0)
    T["tabK"] = tk
    t0 = 2048 * q - 128
    cs, sn = _rope_tab(t0 + np.arange(NOWN), 8)
    tq = np.zeros((2, 32, NQ), np.float32)
    tq[0, :, NOWN:] = 1.0
    tq[0, :, :NOWN] = np.concatenate([cs, cs], 0)
    tq[1, :, :NOWN] = np.concatenate([-sn, sn], 0)
    T["tabQ"] = tq
    cs, sn = _rope_tab(t0 + np.arange(NOWN), 16)
    t1 = np.zeros((2, 128, NQ), np.float32)
    t1[0, :, NOWN:] = 1.0
    t1[0, :, :NOWN] = np.concatenate([cs, cs, cs, cs], 0)
    t1[1, :, :NOWN] = np.concatenate([-sn, sn, -sn, sn], 0)
    T["tab1"] = t1
    kp = np.arange(128)[:, None]
    qp = np.arange(128)[None, :]
    mprev = np.where(kp >= qp, 0.0, NEG)
    mnext = np.where(kp <= qp, 0.0, NEG)
    full = np.full((128, 128), NEG)
    zero = np.zeros((128, 128))
    rel = {-1: full, 0: mprev, 1: zero, 2: mnext, 3: full}
    T["maskb"] = _bf(np.stack([np.stack([np.stack([rel[d - j] for j in range(2)], 1) for _ in range(2)], 1) for d in range(4)], 1))
    tok = t0 + np.arange(NOWN)
    kv = np.where((tok >= 0) & (tok < SEQ), 0.0, NEG).reshape(18, 128).T
    T["kvalid"] = _f32(np.concatenate([kv, np.zeros((128, 2))], 1))
    sel = np.zeros((2, 2, 128), np.float32)
    sel[0, 0] = 1.0
    sel[1, 1] = 1.0
    T["sel"] = sel
    return T


def pack_weights(inp):
    W = {}
    w = np.asarray(inp["e_w_in"][0], np.float32)
    kpe = w[:, 1664:1696]
    W["w0in"] = _f32(np.concatenate([w, kpe[:, 16:32], kpe[:, 0:16]], 1))
    wq = np.asarray(inp["e_w_qb"][0], np.float32)
    sw = []
    for h in range(8):
        pe = wq[:, h * 96 + 64:h * 96 + 96]
        sw += [pe[:, 16:32], pe[:, 0:16]]
    W["wqb"] = _f32(np.concatenate([wq] + sw, 1))
    W["wkvb"] = _f32(inp["e_w_kvb"][0])
    W["wout0"] = _f32(inp["e_w_out"][0])
    w1 = np.asarray(inp["o_w_in"][0], np.float32)
    qs, k2, k2s = [], [], []
    for h in range(16):
        qh = w1[:, h * 64:(h + 1) * 64]
        qs += [qh[:, 32:64], qh[:, 0:32]]
    for h in range(4):
        kh = w1[:, 1024 + h * 64:1024 + (h + 1) * 64]
        k2 += [kh, kh]
        k2s += [kh[:, 32:64], kh[:, 0:32]] * 2
    W["w1in"] = _f32(np.concatenate([w1] + qs + k2 + k2s, 1))
    W["wout1"] = _f32(inp["o_w_out"][0])
    W["wmod"] = _f32(inp["w_mod"])
    W["bmod"] = _f32(inp["b_mod"])
    W["normg"] = _f32(inp["norm_g"])
    W["finalg"] = _f32(np.asarray(inp["final_g"], np.float32)[None, :])
    W["qnorm"] = _f32(np.asarray(inp["e_q_norm"][0], np.float32).reshape(3, 128).T)
    W["kvnorm"] = _f32(np.asarray(inp["e_kv_norm"][0], np.float32).reshape(2, 128).T)
    W["sink"] = _f32(inp["o_sink"])
    return W


def core_inputs(inp, W, b, q):
    m = dict(W)
    m.update(host_tables(q))
    x = np.asarray(inp["x"], np.float32)[b]
    m["xa"] = _f32(x.reshape(128, 64, D).transpose(1, 0, 2))
    t0 = 2048 * q - 128
    xo = np.zeros((NOWN, D), np.float32)
    lo, hi = max(t0, 0), min(t0 + NOWN, SEQ)
    xo[lo - t0:hi - t0] = x[lo:hi]
    m["xo"] = xo.reshape(18, 128, D)
    m["ctx"] = _f32(np.asarray(inp["ctx"], np.float32)[b].reshape(2, 128, D))
    cc = np.stack([np.asarray(inp["c"], np.float32)[b], np.asarray(inp["c_ctx"], np.float32)], 0)
    m["cT"] = _f32(cc.reshape(2, 8, 128).transpose(2, 1, 0))
    return m


from contextlib import ExitStack

IN_SPECS = [
    ("xa", [64, 128, D], F32), ("xo", [18, 128, D], F32), ("ctx", [2, 128, D], F32), ("cT", [128, 8, 2], F32),
    ("w0in", [D, 2240], F32), ("wqb", [384, 1024], F32), ("wkvb", [256, 1024], F32), ("wout0", [D, D], F32),
    ("w1in", [D, 4608], F32), ("wout1", [D, D], F32), ("wmod", [2, D, 3072], F32), ("bmod", [2, 3072], F32),
    ("normg", [2, D], F32), ("finalg", [1, D], F32), ("qnorm", [128, 3], F32), ("kvnorm", [128, 2], F32),
    ("sink", [1, 16], F32),
    ("ident", [128, 128], BF16), ("identf", [128, 128], F32), ("T1", [128, 64, 2, 128], BF16),
    ("CS", [128, 2, 256], BF16), ("F2", [128, 2, 36], BF16), ("DC", [128, 2, 2, 256], BF16),
    ("tabK", [66, 2, 32, 128], F32), ("tabQ", [2, 32, NQ], F32), ("tab1", [2, 128, NQ], F32),
    ("maskb", [128, 4, 2, 2, 128], BF16), ("kvalid", [128, 20], F32), ("sel", [2, 2, 128], F32),
]

QBLK = [(0, 512), (512, 512), (1024, 512), (1536, 512), (2048, 512)]


class Prog:
    def __init__(self, debug=False, upto=99):
        self.debug = debug
        self.upto = upto
        nc = self.nc = bass.Bass("TRN2", target_bir_lowering=False)
        mk = self.mk = MK(nc)
        self.din = {n: mk.dram(s, dt, n, "ExternalInput") for n, s, dt in IN_SPECS}
        self.out = mk.dram([16, 128, D], F32, "out", "ExternalOutput")
        dk = "ExternalOutput" if debug else "Internal"
        self.qan_d = mk.dram([3, 128, NQ], BF16, "qan_d", dk)
        self.fg_d = mk.dram([4, 128, NQ], BF16, "fg_d", dk)
        self.mg_d = mk.dram([4, 128, NQ], BF16, "mg_d", dk)
        self.kvn_d = mk.dram([2, 128, NKEY], BF16, "kvn_d", dk)
        self.kpe_d = mk.dram([32, NKEY], BF16, "kpe_d", dk)
        self.cat_d = mk.dram([8, 128, NQ], BF16, "cat_d", dk)
        self.x1_d = mk.dram([20, 128, D], F32, "x1_d", dk)
        self.pTs = [mk.ps([128, 1024], BF16, "pT%d" % i) for i in range(2)]
        self.P = [mk.ps([128, 512], F32, "P%d" % i) for i in range(6)]
        self.pi = 0
        self.held = set()
        self.es = ExitStack()
        self.cnt = 0
        P_ = self.sbp
        self.ident = P_([128, 128], BF16); self.onesb = P_([128, 128], BF16); self.onesf = P_([128, 64], F32)
        self.epsb = P_([128, 1], F32)
        self.G0 = {k: P_([128, D], F32) for k in ("Gl", "Gc")}
        self.bcs = {}
        mk.dma("sp", self.ident[:], self.din["ident"][:, :], writes=[self.ident])
        mk.op("pool", lambda e: e.memset(self.onesb[:], 1.0), writes=[self.onesb])
        mk.op("pool", lambda e: e.memset(self.onesf[:], 1.0), writes=[self.onesf])
        mk.op("pool", lambda e: e.memset(self.epsb[:], EPS), writes=[self.epsb])

    def sbp(self, shape, dt=BF16):
        self.cnt += 1
        return Tile(self.nc.alloc_sbuf_tensor("pers%d" % self.cnt, list(shape), dt))

    def sb(self, es, shape, dt=BF16):
        self.cnt += 1
        return Tile(es.enter_context(self.nc.sbuf_tensor("s%d" % self.cnt, list(shape), dt)))

    def bank(self, hold=False):
        while True:
            self.pi = (self.pi + 1) % 6
            if self.pi not in self.held:
                break
        if hold:
            self.held.add(self.pi)
        return self.P[self.pi]

    def release(self, p):
        self.held.discard(self.P.index(p))

    def wload(self, dst, dcol, src, r0, nk, c0, n):
        v = src.t.ap()[r0:r0 + nk * 128, c0:c0 + n].rearrange("(c p) n -> p c n", p=128)
        self.mk.dma("pool", dst[:, 0:nk, dcol:dcol + n], v, writes=[dst], max_dma_last_dim=4096)

    def alloc_bc(self, l, es_ab):
        bc = self.bcs[l] = {}
        for k in ("Al", "Bl", "Ac", "Bc", "Gl", "Gc"):
            if l == 0 and k[0] == "G":
                bc[k] = self.G0[k]
            else:
                bc[k] = self.sb(es_ab, [128, D], F32)

    def mod_steps(self, l, es_ab, es):
        mk, din = self.mk, self.din
        if l not in self.bcs:
            self.alloc_bc(l, es_ab)
        bc = self.bcs[l]
        wms = [self.sb(es, [128, 8, 512], F32) for _ in range(2)]
        cT = self.sb(es, [128, 16], F32); scT = self.sb(es, [128, 16], F32)
        bm = self.sb(es, [2, 3072], F32); g2 = self.sb(es, [2, D], F32); sel = self.sb(es, [2, 2, 128], F32)
        mrow = self.sb(es, [2, 3072], F32); arow = self.sb(es, [2, D], F32)
        steps = []

        def first():
            mk.dma("sp", cT[:], din["cT"].t.ap().rearrange("p c v -> p (c v)"), writes=[cT])
            mk.op("act", lambda e: e.activation(out=scT[:], in_=cT[:], func=AF.Silu), reads=[cT], writes=[scT])
            for r in range(2):
                mk.dma("sp", bm[r:r + 1, :], din["bmod"][l:l + 1, :], writes=[bm])
                mk.dma("sp", g2[r:r + 1, :], din["normg"][l:l + 1, :], writes=[g2])
            mk.dma("sp", sel[:], din["sel"].t.ap(), writes=[sel])
        steps.append(first)
        for nb in range(6):
            def ld(nb=nb):
                wm = wms[nb % 2]
                mk.dma("sp", wm[:], din["wmod"].t.ap()[l, :, nb * 512:(nb + 1) * 512].rearrange("(c p) n -> p c n", p=128), writes=[wm])
            def blk(nb=nb):
                wm = wms[nb % 2]
                p = self.bank()
                for kc in range(8):
                    mk.op("pe", lambda e: e.matmul(out=p[0:2, :], lhsT=scT[:, 2 * kc:2 * kc + 2],
                                                   rhs=wm[:, kc, :], start=(kc == 0), stop=(kc == 7)),
                          reads=[scT, wm], writes=[p], inc=(kc == 7))
                mk.op("dve", lambda e: e.tensor_tensor(out=mrow[:, nb * 512:(nb + 1) * 512], in0=p[0:2, :],
                                                       in1=bm[:, nb * 512:(nb + 1) * 512], op=ALU.add),
                      reads=[p, bm], writes=[mrow])
            steps.append(ld)
            steps.append(blk)
        order = [steps[0], steps[1]]
        for nb in range(6):
            if nb + 1 < 6:
                order.append(steps[1 + 2 * (nb + 1)])
            order.append(steps[2 + 2 * nb])

        def arow_f():
            mk.op("dve", lambda e: e.scalar_tensor_tensor(out=arow[:], in0=mrow[:, D:2 * D], scalar=1.0, in1=g2[:],
                                                          op0=ALU.add, op1=ALU.mult), reads=[mrow, g2], writes=[arow])
        order.append(arow_f)
        for which, sfx in ((0, "l"), (1, "c")):
            for key, src, off in (("A", arow, 0), ("B", mrow, 0), ("G", mrow, 2 * D)):
                def bcf(which=which, sfx=sfx, key=key, src=src, off=off):
                    dst = bc[key + sfx]
                    for half in range(2):
                        p = self.bank()
                        mk.op("pe", lambda e: e.matmul(out=p[:], lhsT=sel[:, which, :],
                                                       rhs=src[:, off + half * 512:off + (half + 1) * 512],
                                                       start=True, stop=True), reads=[sel, src], writes=[p])
                        mk.op("dve", lambda e: e.tensor_copy(out=dst[:, half * 512:(half + 1) * 512], in_=p[:]),
                              reads=[p], writes=[dst])
                order.append(bcf)
        return order

    def make_norm_scratch(self, es, n=3, nx=5):
        self.xts = [self.sb(es, [128, D], F32) for _ in range(nx)]
        self.ns = [dict(junk=self.sb(es, [128, D], BF16), ss=self.sb(es, [128, 1], F32),
                        rstd=self.sb(es, [128, 1], F32), tmp=self.sb(es, [128, D], F32), hb=self.sb(es, [128, D], BF16))
                   for _ in range(n)]

    def norm_s0(self, i, src_ap):
        xt = self.xts[i % len(self.xts)]
        self.mk.dma("sp", xt[:], src_ap, writes=[xt])

    def norm_s1(self, i, src_ap, A, B):
        mk = self.mk
        s = self.ns[i % len(self.ns)]
        xt = self.xts[i % len(self.xts)]
        junk, ss, rstd, tmp, hb = s["junk"], s["ss"], s["rstd"], s["tmp"], s["hb"]
        mk.op("act", lambda e: e.activation(out=junk[:], in_=xt[:], func=AF.Square, scale=1.0 / 32.0, accum_out=ss[:]),
              reads=[xt], writes=[junk, ss])
        mk.op("act", lambda e: e.activation(out=rstd[:], in_=ss[:], func=AF.Ln, bias=self.epsb[:], scale=1.0),
              reads=[ss, self.epsb], writes=[rstd])
        mk.op("act", lambda e: e.activation(out=rstd[:], in_=rstd[:], func=AF.Exp, scale=-0.5), reads=[rstd], writes=[rstd])
        mk.op("dve", lambda e: e.scalar_tensor_tensor(out=tmp[:], in0=xt[:], scalar=rstd[:, 0:1], in1=A[:],
                                                      op0=ALU.mult, op1=ALU.mult), reads=[xt, rstd, A], writes=[tmp])
        mk.op("pool", lambda e: e.tensor_tensor(out=hb[:], in0=tmp[:], in1=B[:], op=ALU.add), reads=[tmp, B], writes=[hb])

    def norm_s2(self, i, dst_tile, dst_ap):
        mk = self.mk
        hb = self.ns[i % len(self.ns)]["hb"]
        pT = self.pTs[i % 2]
        for c in range(8):
            mk.op("pe", lambda e: e.transpose(out=pT[:, c * 128:(c + 1) * 128], in_=hb[:, c * 128:(c + 1) * 128],
                                              identity=self.ident[:]), reads=[hb, self.ident], writes=[pT], inc=(c == 7))
        mk.op("act", lambda e: e.activation(out=dst_ap, in_=pT[:, :].rearrange("p (c n) -> p c n", c=8), func=AF.Copy),
              reads=[pT], writes=[dst_tile])

    def norm_pipe(self, n, src_fn, ab_fn, dst_fn, s3=None):
        for i in range(min(2, n)):
            self.norm_s0(i, src_fn(i))
        for t in range(n + 2):
            if t + 2 < n:
                self.norm_s0(t + 2, src_fn(t + 2))
            if t < n:
                A, B = ab_fn(t)
                self.norm_s1(t, src_fn(t), A, B)
            if 0 <= t - 1 < n:
                dt_, dap = dst_fn(t - 1)
                self.norm_s2(t - 1, dt_, dap)
            if s3 is not None and 0 <= t - 2 < n:
                s3(t - 2)

    def proj(self, p_ap, ptile, w, wcols, h, hcols, n_c=8):
        for c in range(n_c):
            self.mk.op("pe", lambda e: e.matmul(out=p_ap, lhsT=w[:, c, wcols[0]:wcols[1]], rhs=h[:, c, hcols[0]:hcols[1]],
                                                start=(c == 0), stop=(c == n_c - 1)), reads=[w, h], writes=[ptile], inc=(c == n_c - 1))

    def rstd_bcast(self, es_tiles, pss, n, dim, out_tile):
        mk = self.mk
        mk.op("act", lambda e: e.activation(out=out_tile[:, 0:n], in_=pss[:, 0:n], func=AF.Ln, bias=self.epsb[:], scale=1.0 / dim),
              reads=[pss, self.epsb], writes=[out_tile])
        mk.op("act", lambda e: e.activation(out=out_tile[:, 0:n], in_=out_tile[:, 0:n], func=AF.Exp, scale=-0.5),
              reads=[out_tile], writes=[out_tile])

    def phase_O(self):
        mk, din = self.mk, self.din
        with ExitStack() as es:
            hT = self.sb(es, [128, 8, NQ], BF16)
            wps = [self.sb(es, [128, 8, 512], BF16) for _ in range(3)]
            self.wload(wps[0], 0, din["w0in"], 0, 8, 512, 512)
            self.wload(wps[1], 0, din["w0in"], 0, 8, 1696, 512)
            self.wload(wps[2], 0, din["w0in"], 0, 8, 1024, 384)
            with ExitStack() as esn:
                self.make_norm_scratch(esn)
                self.norm_pipe(20, lambda i: din["xo"].t.ap()[i] if i < 18 else din["ctx"].t.ap()[i - 18],
                               lambda i: (self.bc["Al" if i < 18 else "Ac"], self.bc["Bl" if i < 18 else "Bc"]),
                               lambda i: (hT, hT[:, :, i * 128:(i + 1) * 128]))
                mk.barrier()
            stage = [self.sb(es, [128, 512], BF16) for _ in range(3)]
            si = 0
            for wp, dst in ((wps[0], self.fg_d), (wps[1], self.mg_d)):
                for j in range(4):
                    for b0, n in QBLK:
                        p = self.bank()
                        self.proj(p[:, 0:n], p, wp, (j * 128, (j + 1) * 128), hT, (b0, b0 + n))
                        st = stage[si % 3]; si += 1
                        mk.op("act", lambda e: e.activation(out=st[:, 0:n], in_=p[:, 0:n], func=AF.Silu), reads=[p], writes=[st])
                        mk.dma("sp", dst.t.ap()[j, :, b0:b0 + n], st[:, 0:n], reads=[st], writes=[dst])
            wp = wps[2]
            sq = [self.sb(es, [128, 512], BF16) for _ in range(3)]
            rq = self.sb(es, [128, 512], F32)
            for b0, n in QBLK:
                ps = [self.bank() for _ in range(3)]
                pss = self.bank()
                for j in range(3):
                    self.proj(ps[j][:, 0:n], ps[j], wp, (j * 128, (j + 1) * 128), hT, (b0, b0 + n))
                    mk.op("act", lambda e: e.activation(out=sq[j][:, 0:n], in_=ps[j][:, 0:n], func=AF.Square), reads=[ps[j]], writes=[sq[j]])
                for j in range(3):
                    mk.op("pe", lambda e: e.matmul(out=pss[:, 0:n], lhsT=self.onesb[:], rhs=sq[j][:, 0:n], start=(j == 0), stop=(j == 2)),
                          reads=[self.onesb, sq[j]], writes=[pss])
                self.rstd_bcast(None, pss, n, 384.0, rq)
                for j in range(3):
                    st = stage[si % 3]; si += 1
                    mk.op("dve", lambda e: e.tensor_tensor(out=st[:, 0:n], in0=ps[j][:, 0:n], in1=rq[:, 0:n], op=ALU.mult),
                          reads=[ps[j], rq], writes=[st])
                    mk.dma("sp", self.qan_d.t.ap()[j, :, b0:b0 + n], st[:, 0:n], reads=[st], writes=[self.qan_d])
            mk.barrier()

    def phase_A(self, es_outer):
        mk, din = self.mk, self.din
        with ExitStack() as es:
            self.make_norm_scratch(es)
            wA = self.sb(es, [128, 8, 832], BF16)
            self.wload(wA, 0, din["w0in"], 0, 8, 0, 512)
            self.wload(wA, 512, din["w0in"], 0, 8, 1408, 288)
            self.wload(wA, 800, din["w0in"], 0, 8, 2208, 32)
            hTs = [self.sb(es, [128, 8, 128], BF16) for _ in range(3)]
            sqkv = [self.sb(es, [128, 256], BF16) for _ in range(2)]
            rk = [self.sb(es, [128, 128], F32) for _ in range(2)]
            kst = [self.sb(es, [128, 2, 128], BF16) for _ in range(2)]
            tabk = [self.sb(es, [128, 2, 128], F32) for _ in range(3)]
            t1 = [self.sb(es, [128, 128], F32) for _ in range(2)]
            t2 = [self.sb(es, [128, 128], F32) for _ in range(2)]
            pst = [self.sb(es, [128, 128], BF16) for _ in range(2)]

            def s3(i):
                lat = i >= 2
                hT = hTs[i % 3]
                tb = tabk[i % 3]
                mk.dma("sp", tb[64:96, :, :], din["tabK"].t.ap()[i].rearrange("t p n -> p t n"), writes=[tb])
                pkv = self.bank()
                for j in range(2):
                    self.proj(pkv[:, j * 128:(j + 1) * 128], pkv, wA, (512 + j * 128, 640 + j * 128), hT, (0, 128))
                sq = sqkv[i % 2]
                mk.op("act", lambda e: e.activation(out=sq[:], in_=pkv[:, 0:256], func=AF.Square), reads=[pkv], writes=[sq])
                p = self.bank()
                self.mk_tokmajor(p, hT, wA, 0, 512)
                udst, uap = (self.u_all, self.u_all[:, i - 2, :]) if lat else (self.u_ctx, self.u_ctx[:, i, :])
                mk.op("dve", lambda e: e.tensor_copy(out=uap, in_=p[:]), reads=[p], writes=[udst])
                pk = self.bank()
                self.proj(pk[64:96, 0:128], pk, wA, (768, 800), hT, (0, 128))
                self.proj(pk[64:96, 128:256], pk, wA, (800, 832), hT, (0, 128))
                a1, a2 = t1[i % 2], t2[i % 2]
                mk.op("dve", lambda e: e.tensor_tensor(out=a1[64:96, :], in0=pk[64:96, 0:128], in1=tb[64:96, 0, :], op=ALU.mult),
                      reads=[pk, tb], writes=[a1])
                mk.op("dve", lambda e: e.tensor_tensor(out=a2[64:96, :], in0=pk[64:96, 128:256], in1=tb[64:96, 1, :], op=ALU.mult),
                      reads=[pk, tb], writes=[a2])
                ps_ = pst[i % 2]
                mk.op("pool", lambda e: e.tensor_tensor(out=ps_[64:96, :], in0=a1[64:96, :], in1=a2[64:96, :], op=ALU.add),
                      reads=[a1, a2], writes=[ps_])
                mk.dma("sp", self.kpe_d.t.ap()[:, i * 128:(i + 1) * 128], ps_[64:96, :], reads=[ps_], writes=[self.kpe_d])
                pss = self.bank()
                for j in range(2):
                    mk.op("pe", lambda e: e.matmul(out=pss[:, 0:128], lhsT=self.onesb[:], rhs=sq[:, j * 128:(j + 1) * 128],
                                                   start=(j == 0), stop=(j == 1)), reads=[self.onesb, sq], writes=[pss], inc=(j == 1))
                r_ = rk[i % 2]
                self.rstd_bcast(None, pss, 128, 256.0, r_)
                ks = kst[i % 2]
                for j in range(2):
                    mk.op("dve", lambda e: e.tensor_tensor(out=ks[:, j, :], in0=pkv[:, j * 128:(j + 1) * 128], in1=r_[:], op=ALU.mult),
                          reads=[pkv, r_], writes=[ks])
                mk.dma("sp", self.kvn_d.t.ap()[:, :, i * 128:(i + 1) * 128].rearrange("j p n -> p j n"), ks[:], reads=[ks], writes=[self.kvn_d])

            self.norm_pipe(66, lambda i: din["xa"].t.ap()[i - 2] if i >= 2 else din["ctx"].t.ap()[i],
                           lambda i: (self.bc["Al" if i >= 2 else "Ac"], self.bc["Bl" if i >= 2 else "Bc"]),
                           lambda i: (hTs[i % 3], hTs[i % 3][:, :, :]), s3=s3)
            mk.barrier()

    def mk_tokmajor(self, p, hT, w, c0, n):
        for c in range(8):
            self.mk.op("pe", lambda e: e.matmul(out=p[:, 0:n], lhsT=hT[:, c, :], rhs=w[:, c, c0:c0 + n], start=(c == 0), stop=(c == 7)),
                       reads=[hT, w], writes=[p], inc=(c == 7))

    def phase_F(self):
        mk, din = self.mk, self.din
        with ExitStack() as es:
            T1 = self.sb(es, [128, 64, 256], BF16)
            for k in range(4):
                mk.dma("sp", T1[:, k * 16:(k + 1) * 16, :], din["T1"].t.ap()[:, k * 16:(k + 1) * 16].rearrange("p a r k -> p a (r k)"), writes=[T1])
            CS = self.sb(es, [128, 2, 256], BF16); F2 = self.sb(es, [128, 2, 36], BF16); DC = self.sb(es, [128, 2, 512], BF16)
            mk.dma("sp", CS[:], din["CS"].t.ap(), writes=[CS])
            mk.dma("sp", F2[:], din["F2"].t.ap(), writes=[F2])
            mk.dma("sp", DC[:], din["DC"].t.ap().rearrange("p t r k -> p t (r k)"), writes=[DC])
            fg = self.sb(es, [128, 4, NQ], BF16)
            mk.dma("sp", fg[:], self.fg_d.t.ap().rearrange("j p n -> p j n"), reads=[self.fg_d], writes=[fg])
            A1 = self.sb(es, [128, 128, 128], BF16)
            A1w = A1[:, :, :].rearrange("p (a n) (r k) -> p n a r k", a=2, r=2)
            Gb = [self.sb(es, [128, 2, 256], BF16) for _ in range(3)]
            catF = [self.sb(es, [128, NOWN], BF16) for _ in range(2)]
            XT = self.sb(es, [128, 512], BF16); cst = self.sb(es, [128, 256], BF16)
            for g in range(4):
                for pr in range(32):
                    p = self.bank()
                    for s in range(2):
                        n2 = 2 * pr + s
                        mk.op("pe", lambda e: e.matmul(out=p[:, s * 256:(s + 1) * 256], lhsT=self.u_all[:, n2, g * 128:(g + 1) * 128],
                                                       rhs=T1[:, n2, :], start=True, stop=True), reads=[self.u_all, T1], writes=[p], inc=(s == 1))
                    eng = "dve" if pr % 2 == 0 else "act"
                    for s_ in range(2):
                        oap = A1w[:, 2 * pr + s_]
                        iap = p[:, s_ * 256:(s_ + 1) * 256].rearrange("p (r k a) -> p a r k", r=2, a=2)
                        if s_ == 0:
                            mk.op("dve", lambda e: e.tensor_copy(out=oap, in_=iap), reads=[p], writes=[A1])
                        else:
                            mk.op("act", lambda e: e.activation(out=oap, in_=iap, func=AF.Copy), reads=[p], writes=[A1])
                cf = catF[g % 2]
                cfv = cf[:, :].rearrange("p (k2 k1) -> p k1 k2", k1=128)
                fgv = fg[:, g, 0:NOWN].rearrange("p (k2 k1) -> p k1 k2", k1=128)
                fst = {"pacc": None, "a0": 0}

                def s3(pr, gb, fst=fst, cf=cf, cfv=cfv, fgv=fgv):
                    for s_ in range(2):
                        kp = 2 * pr + s_
                        k1 = 2 * kp
                        if k1 % 28 == 0:
                            fst["pacc"] = self.bank(hold=True); fst["a0"] = k1
                        pacc, a0 = fst["pacc"], fst["a0"]
                        sl = (k1 - a0) * 18
                        mk.op("pe", lambda e: e.matmul(out=pacc[:, sl:sl + 36], lhsT=gb[:, s_, 0:128], rhs=F2[:, 0, :], start=True, stop=False),
                              reads=[gb, F2], writes=[pacc], inc=False)
                        mk.op("pe", lambda e: e.matmul(out=pacc[:, sl:sl + 36], lhsT=gb[:, s_, 128:256], rhs=F2[:, 1, :], start=False, stop=True),
                              reads=[gb, F2], writes=[pacc])
                        if (k1 + 1) % 28 == 27 or k1 + 1 == 127:
                            cnt = k1 + 2 - a0
                            mk.op("dve", lambda e: e.tensor_tensor(out=cfv[:, a0:a0 + cnt, :], in0=pacc[:, 0:cnt * 18].rearrange("p (a b) -> p a b", b=18),
                                                                   in1=fgv[:, a0:a0 + cnt, :], op=ALU.mult), reads=[pacc, fg], writes=[cf])
                            self.release(pacc)

                prev = None
                for pr in range(32):
                    p = self.bank()
                    for s_ in range(2):
                        kp = 2 * pr + s_
                        mk.op("pe", lambda e: e.matmul(out=p[:, s_ * 256:(s_ + 1) * 256], lhsT=A1[:, :, kp], rhs=CS[:, 0, :], start=True, stop=False),
                              reads=[A1, CS], writes=[p], inc=False)
                        mk.op("pe", lambda e: e.matmul(out=p[:, s_ * 256:(s_ + 1) * 256], lhsT=A1[:, :, 64 + kp], rhs=CS[:, 1, :], start=False, stop=True),
                              reads=[A1, CS], writes=[p], inc=(s_ == 1))
                    gb = Gb[pr % 3]
                    if pr % 2 == 0:
                        mk.op("dve", lambda e: e.tensor_copy(out=gb[:, :, :], in_=p[:, :].rearrange("p (a b) -> p a b", a=2)), reads=[p], writes=[gb])
                    else:
                        mk.op("act", lambda e: e.activation(out=gb[:, :, :], in_=p[:, :].rearrange("p (a b) -> p a b", a=2), func=AF.Copy), reads=[p], writes=[gb])
                    if prev is not None:
                        s3(*prev)
                    prev = (pr, gb)
                s3(*prev)
                mk.dma("sp", self.cat_d.t.ap()[g, :, 0:NOWN], cf[:, :], reads=[cf], writes=[self.cat_d])
                px = self.bank()
                for nt in range(2):
                    mk.op("pe", lambda e: e.matmul(out=px[:, :], lhsT=self.u_ctx[:, nt, g * 128:(g + 1) * 128], rhs=DC[:, nt, :], start=(nt == 0), stop=(nt == 1)),
                          reads=[self.u_ctx, DC], writes=[px], inc=(nt == 1))
                mk.op("dve", lambda e: e.tensor_copy(out=XT[:], in_=px[:]), reads=[px], writes=[XT])
                py = self.bank()
                mk.op("pe", lambda e: e.matmul(out=py[:, 0:256], lhsT=CS[:, 0, 0:128], rhs=XT[:, 0:256], start=True, stop=False), reads=[CS, XT], writes=[py], inc=False)
                mk.op("pe", lambda e: e.matmul(out=py[:, 0:256], lhsT=CS[:, 1, 0:128], rhs=XT[:, 256:512], start=False, stop=True), reads=[CS, XT], writes=[py])
                mk.op("dve", lambda e: e.tensor_tensor(out=cst[:], in0=py[:, 0:256], in1=fg[:, g, NOWN:NQ], op=ALU.mult), reads=[py, fg], writes=[cst])
                mk.dma("sp", self.cat_d.t.ap()[g, :, NOWN:NQ], cst[:], reads=[cst], writes=[self.cat_d])
            mk.barrier()

    def wscaled(self, es, src, nk, ncols, normname):
        mk = self.mk
        w = self.sb(es, [128, nk, ncols], BF16)
        g = self.sb(es, [128, nk], F32)
        self.wload(w, 0, src, 0, nk, 0, ncols)
        mk.dma("sp", g[:], self.din[normname].t.ap(), writes=[g])
        for c in range(nk):
            mk.op("dve", lambda e: e.tensor_scalar(out=w[:, c, :], in0=w[:, c, :], scalar1=g[:, c:c + 1], scalar2=None, op0=ALU.mult),
                  reads=[w, g], writes=[w])
        return w

    def phase_ATT(self):
        mk, din = self.mk, self.din
        with ExitStack() as es:
            kvn = self.sb(es, [128, 2, NKEY], BF16)
            KTs = [self.sb(es, [128, NKEY], BF16) for _ in range(2)]
            VAs = [self.sb(es, [128, 66, 128], BF16) for _ in range(2)]
            qan = self.sb(es, [128, 3, NQ], BF16)
            QTs = [self.sb(es, [128, NQ], BF16) for _ in range(2)]
            tabq = self.sb(es, [128, 2, NQ], F32)
            mgs = [self.sb(es, [128, NQ], BF16) for _ in range(2)]
            ATs = [self.sb(es, [128, NQ], BF16)] * 2
            for j in range(2):
                mk.dma("sp", kvn[:, j, :], self.kvn_d.t.ap()[j], reads=[self.kvn_d], writes=[kvn])
            for KT in KTs:
                mk.dma("sp", KT[64:96, :], self.kpe_d.t.ap(), reads=[self.kpe_d], writes=[KT])
            mk.dma("sp", qan[:], self.qan_d.t.ap().rearrange("j p n -> p j n"), reads=[self.qan_d], writes=[qan])
            mk.dma("sp", tabq[64:96, :, :], din["tabQ"].t.ap().rearrange("t p n -> p t n"), writes=[tabq])
            wq = self.wscaled(es, din["wqb"], 3, 1024, "qnorm")
            wkv = self.wscaled(es, din["wkvb"], 2, 1024, "kvnorm")
            mk.op("dve", lambda e: e.memset(VAs[0][:, :, 64:128], 0.0), writes=[VAs[0]])
            mk.op("dve", lambda e: e.memset(VAs[0][:, :, 64:65], 1.0), writes=[VAs[0]])
            mk.op("dve", lambda e: e.memset(VAs[1][:, :, 0:64], 0.0), writes=[VAs[1]])
            mk.op("dve", lambda e: e.memset(VAs[1][:, :, 0:1], 1.0), writes=[VAs[1]])
            PTs = [self.sb(es, [128, 512], BF16) for _ in range(4)]
            t1 = self.sb(es, [128, 512], F32); t2 = self.sb(es, [128, 512], F32)
            rec = self.sb(es, [128, 512], F32); tmp = self.sb(es, [128, 512], F32)

            def gen_steps(h):
                odd = h % 2
                o0 = 64 if odd else 0
                KT, VA, QT = KTs[odd], VAs[odd], QTs[odd]
                steps = []
                if not odd:
                    mgt = mgs[(h // 2) % 2]
                    steps.append(lambda: mk.dma("sp", mgt[:], self.mg_d.t.ap()[h // 2], reads=[self.mg_d], writes=[mgt]))
                for kb in range(17):
                    def f(kb=kb):
                        k0 = kb * 512
                        n = min(512, NKEY - k0)
                        p = self.bank()
                        self.proj(p[0:64, 0:n], p, wkv, (h * 128, h * 128 + 64), kvn, (k0, k0 + n), n_c=2)
                        mk.op("dve", lambda e: e.tensor_copy(out=KT[0:64, k0:k0 + n], in_=p[0:64, 0:n]), reads=[p], writes=[KT])
                    steps.append(f)
                for g0 in range(0, 66, 8):
                    def f(g0=g0):
                        cnt = min(8, 66 - g0)
                        p = self.bank()
                        for t in range(cnt):
                            kt = g0 + t
                            for j in range(2):
                                mk.op("pe", lambda e: e.matmul(out=p[:, t * 64:(t + 1) * 64], lhsT=kvn[:, j, kt * 128:(kt + 1) * 128],
                                                               rhs=wkv[:, j, h * 128 + 64:h * 128 + 128], start=(j == 0), stop=(j == 1)),
                                      reads=[kvn, wkv], writes=[p], inc=(j == 1 and t == cnt - 1))
                        mk.op("dve", lambda e: e.tensor_copy(out=VA[:, g0:g0 + cnt, o0:o0 + 64], in_=p[:, 0:cnt * 64].rearrange("p (a b) -> p a b", b=64)),
                              reads=[p], writes=[VA])
                    steps.append(f)
                for b0, n in QBLK:
                    def f(b0=b0, n=n):
                        pa = self.bank(); pb = self.bank()
                        self.proj(pa[0:96, 0:n], pa, wq, (h * 96, h * 96 + 96), qan, (b0, b0 + n), n_c=3)
                        self.proj(pb[64:96, 0:n], pb, wq, (768 + h * 32, 800 + h * 32), qan, (b0, b0 + n), n_c=3)
                        mk.op("dve", lambda e: e.tensor_copy(out=QT[0:64, b0:b0 + n], in_=pa[0:64, 0:n]), reads=[pa], writes=[QT])
                        mk.op("dve", lambda e: e.tensor_tensor(out=t1[64:96, 0:n], in0=pa[64:96, 0:n], in1=tabq[64:96, 0, b0:b0 + n], op=ALU.mult),
                              reads=[pa, tabq], writes=[t1])
                        mk.op("dve", lambda e: e.tensor_tensor(out=t2[64:96, 0:n], in0=pb[64:96, 0:n], in1=tabq[64:96, 1, b0:b0 + n], op=ALU.mult),
                              reads=[pb, tabq], writes=[t2])
                        mk.op("pool", lambda e: e.tensor_tensor(out=QT[64:96, b0:b0 + n], in0=t1[64:96, 0:n], in1=t2[64:96, 0:n], op=ALU.add),
                              reads=[t1, t2], writes=[QT])
                    steps.append(f)
                return steps

            for f in gen_steps(0):
                f()
            items = []
            for h in range(8):
                for bi, (b0, n, kts) in enumerate([(0, 512, 66), (512, 512, 66), (1024, 512, 66), (1536, 512, 66), (2048, 256, 66), (2304, 256, 2)]):
                    for kt in range(kts):
                        items.append((h, b0, n, kt, kts))
            pending = []
            deferred = []
            recs = [rec, self.sb(es, [128, 512], F32)]
            nfin = 0
            state = {}
            LAG = 2
            nit = len(items)
            per_head = nit // 8
            for t in range(nit + LAG):
                if t < nit:
                    h, b0, n, kt, kts = items[t]
                    odd = h % 2
                    if kt == 0:
                        state[(h, b0)] = self.bank(hold=True)
                    if t % per_head == 0 and h < 7:
                        pending = gen_steps(h + 1)
                        every = max(1, (per_head - 40) // len(pending))
                    if pending and (t % per_head) % every == 0:
                        pending.pop(0)()
                    ps = self.bank()
                    mk.op("pe", lambda e: e.matmul(out=ps[:, 0:n], lhsT=KTs[odd][0:96, kt * 128:(kt + 1) * 128], rhs=QTs[odd][0:96, b0:b0 + n], start=True, stop=True),
                          reads=[KTs[odd], QTs[odd]], writes=[ps])
                    pt = PTs[t % 4]
                    mk.op("act", lambda e: e.activation(out=pt[:, 0:n], in_=ps[:, 0:n], func=AF.Exp, scale=MLA_SCALE), reads=[ps], writes=[pt])
                while deferred:
                    deferred.pop(0)()
                if t >= LAG:
                    h, b0, n, kt, kts = items[t - LAG]
                    odd = h % 2
                    o0, r0 = (64, 0) if odd else (0, 64)
                    po = state[(h, b0)]
                    pt = PTs[(t - LAG) % 4]
                    mk.op("pe", lambda e: e.matmul(out=po[:, 0:n], lhsT=VAs[odd][:, kt, :], rhs=pt[:, 0:n], start=(kt == 0), stop=(kt == kts - 1)),
                          reads=[VAs[odd], pt], writes=[po], inc=(kt == kts - 1))
                    if kt == kts - 1:
                        mgt = mgs[(h // 2) % 2]; AT = ATs[(h // 2) % 2]
                        fa = dict(po=po, n=n, o0=o0, r0=r0, gate_tile=mgt, gate_ap=mgt[o0:o0 + 64, b0:b0 + n], dst_tile=AT,
                                  dst_ap=AT[o0:o0 + 64, b0:b0 + n], rec=recs[nfin % 2], tmp=tmp)
                        nfin += 1
                        self.att_finish(part="A", **fa)

                        def fb(fa=fa, po=po, odd=odd, b0=b0, h=h, AT=AT):
                            self.att_finish(part="B", **fa)
                            self.release(po)
                            if odd and b0 == 2304:
                                mk.dma("sp", self.cat_d.t.ap()[4 + h // 2], AT[:], reads=[AT], writes=[self.cat_d])
                        deferred.append(fb)
            while deferred:
                deferred.pop(0)()
            while pending:
                pending.pop(0)()
            mk.barrier()

    def att_finish(self, po, n, o0, r0, gate_tile, gate_ap, dst_tile, dst_ap, rec, tmp, extra=None, part=None):
        mk = self.mk
        if part in (None, "A"):
            self._att_finish_a(po, n, r0, rec, extra)
        if part in (None, "B"):
            self._att_finish_b(po, n, o0, r0, gate_tile, gate_ap, dst_tile, dst_ap, rec, tmp)

    def _att_finish_a(self, po, n, r0, rec, extra):
        mk = self.mk
        if extra is not None:
            mk.op("dve", lambda e: e.tensor_tensor(out=rec[r0:r0 + 1, 0:n], in0=po[r0:r0 + 1, 0:n], in1=extra, op=ALU.add), reads=[po], writes=[rec])
            mk.op("act", lambda e: e.activation(out=rec[r0:r0 + 1, 0:n], in_=rec[r0:r0 + 1, 0:n], func=AF.Ln), reads=[rec], writes=[rec])
        else:
            mk.op("act", lambda e: e.activation(out=rec[r0:r0 + 1, 0:n], in_=po[r0:r0 + 1, 0:n], func=AF.Ln), reads=[po], writes=[rec])
        mk.op("act", lambda e: e.activation(out=rec[r0:r0 + 1, 0:n], in_=rec[r0:r0 + 1, 0:n], func=AF.Exp, scale=-1.0), reads=[rec], writes=[rec])

    def _att_finish_b(self, po, n, o0, r0, gate_tile, gate_ap, dst_tile, dst_ap, rec, tmp):
        mk = self.mk
        pb = self.bank()
        mk.op("pe", lambda e: e.matmul(out=pb[o0:o0 + 64, 0:n], lhsT=self.onesf[r0:r0 + 1, 0:64], rhs=rec[r0:r0 + 1, 0:n], start=True, stop=True),
              reads=[self.onesf, rec], writes=[pb])
        mk.op("dve", lambda e: e.tensor_tensor(out=tmp[o0:o0 + 64, 0:n], in0=po[o0:o0 + 64, 0:n], in1=gate_ap, op=ALU.mult),
              reads=[po, gate_tile], writes=[tmp])
        mk.op("dve", lambda e: e.tensor_tensor(out=dst_ap, in0=tmp[o0:o0 + 64, 0:n], in1=pb[o0:o0 + 64, 0:n], op=ALU.mult),
              reads=[tmp, pb], writes=[dst_tile])

    def phase_OUT(self, wname, xsrc, ntiles, tile0, final=False, G=None, side=None):
        mk, din = self.mk, self.din
        with ExitStack() as es:
            wo = self.sb(es, [128, 8, D], BF16)
            self.wload(wo, 0, din[wname], 0, 8, 0, 512)
            self.wload(wo, 512, din[wname], 0, 8, 512, 512)
            catb = [self.sb(es, [128, 8, 512], BF16) for _ in range(3)]
            xts = [self.sb(es, [128, D], F32) for _ in range(4)]
            tmp = [self.sb(es, [128, D], F32) for _ in range(2)]
            xn = [self.sb(es, [128, D], F32) for _ in range(2)]
            junk = self.sb(es, [128, D], BF16); ss = self.sb(es, [128, 1], F32); rstd = self.sb(es, [128, 1], F32)
            fgb = None
            if final:
                fgb = self.sb(es, [128, D], F32)
                fr = self.sb(es, [1, D], F32)
                mk.dma("sp", fr[:], din["finalg"].t.ap(), writes=[fr])
                self.bcast_row(fr, fgb)
            side_steps = side(es) if side is not None else []
            loaded = set()

            def load(ti):
                gi = tile0 + ti
                blk = gi // 4
                if blk not in loaded:
                    loaded.add(blk)
                    cb = catb[blk % 3]
                    mk.dma("sp", cb[:], self.cat_d.t.ap()[:, :, blk * 512:(blk + 1) * 512].rearrange("c p n -> p c n"), reads=[self.cat_d], writes=[cb])
                xt = xts[ti % 4]
                mk.dma("sp", xt[:], xsrc(gi), reads=[self.x1_d], writes=[xt])

            for ti in range(min(2, ntiles)):
                load(ti)
            for ti in range(ntiles):
                if ti + 2 < ntiles:
                    load(ti + 2)
                if side_steps:
                    side_steps.pop(0)()
                gi = tile0 + ti
                blk, t = divmod(gi, 4)
                cb = catb[blk % 3]
                xt = xts[ti % 4]; tm = tmp[ti % 2]; xo = xn[ti % 2]
                Gt = G["Gl" if gi < 18 else "Gc"]
                for half in range(2):
                    p = self.bank()
                    for c in range(8):
                        mk.op("pe", lambda e: e.matmul(out=p[:], lhsT=cb[:, c, t * 128:(t + 1) * 128], rhs=wo[:, c, half * 512:(half + 1) * 512],
                                                       start=(c == 0), stop=(c == 7)), reads=[cb, wo], writes=[p], inc=(c == 7))
                    mk.op("dve", lambda e: e.tensor_tensor(out=tm[:, half * 512:(half + 1) * 512], in0=p[:], in1=Gt[:, half * 512:(half + 1) * 512], op=ALU.mult),
                          reads=[p, Gt], writes=[tm])
                mk.op("pool", lambda e: e.tensor_tensor(out=xo[:], in0=tm[:], in1=xt[:], op=ALU.add), reads=[tm, xt], writes=[xo])
                if not final:
                    mk.dma("sp", self.x1_d.t.ap()[gi], xo[:], reads=[xo], writes=[self.x1_d])
                else:
                    mk.op("act", lambda e: e.activation(out=junk[:], in_=xo[:], func=AF.Square, scale=1.0 / 32.0, accum_out=ss[:]), reads=[xo], writes=[junk, ss])
                    mk.op("act", lambda e: e.activation(out=rstd[:], in_=ss[:], func=AF.Ln, bias=self.epsb[:], scale=1.0), reads=[ss, self.epsb], writes=[rstd])
                    mk.op("act", lambda e: e.activation(out=rstd[:], in_=rstd[:], func=AF.Exp, scale=-0.5), reads=[rstd], writes=[rstd])
                    mk.op("dve", lambda e: e.scalar_tensor_tensor(out=tm[:], in0=xo[:], scalar=rstd[:, 0:1], in1=fgb[:], op0=ALU.mult, op1=ALU.mult),
                          reads=[xo, rstd, fgb], writes=[tm])
                    mk.dma("sp", self.out.t.ap()[ti], tm[:], reads=[tm], writes=[self.out])
            while side_steps:
                side_steps.pop(0)()
            mk.barrier()

    def bcast_row(self, row, dst):
        mk = self.mk
        for half in range(2):
            p = self.bank()
            mk.op("pe", lambda e: e.matmul(out=p[0:64, :], lhsT=self.onesf[0:1, 0:64], rhs=row[0:1, half * 512:(half + 1) * 512], start=True, stop=True),
                  reads=[self.onesf, row], writes=[p])
            mk.op("pe", lambda e: e.matmul(out=p[64:128, :], lhsT=self.onesf[0:1, 0:64], rhs=row[0:1, half * 512:(half + 1) * 512], start=True, stop=True),
                  reads=[self.onesf, row], writes=[p])
            mk.op("dve", lambda e: e.tensor_copy(out=dst[:, half * 512:(half + 1) * 512], in_=p[:]), reads=[p], writes=[dst])

    def layer0(self):
        with ExitStack() as es:
            self.u_all = self.sb(es, [128, 64, 512], BF16)
            self.u_ctx = self.sb(es, [128, 2, 512], BF16)
            with ExitStack() as es_ab:
                with ExitStack() as es_tmp:
                    for f in self.mod_steps(0, es_ab, es_tmp):
                        f()
                    self.mk.barrier()
                self.bc = self.bcs[0]
                self.phase_O()
                self.phase_A(es)
            self.phase_F()
        self.mk.barrier()
        self.phase_ATT()
        din = self.din
        self.es_ab1 = ExitStack()
        self.alloc_bc(1, self.es_ab1)
        self.phase_OUT("wout0", lambda gi: din["xo"].t.ap()[gi] if gi < 18 else din["ctx"].t.ap()[gi - 18], 20, 0,
                       G=self.bcs[0], side=lambda es_tmp: self.mod_steps(1, self.es_ab1, es_tmp))

    def phase_L1(self):
        mk, din = self.mk, self.din
        g1_d = self.g1_d
        with ExitStack() as es:
            QT1 = self.sb(es, [128, 8, NOWN], BF16)
            KT1 = self.sb(es, [128, 4, NQ], BF16)
            VA1 = self.sb(es, [128, 20, 4, 2, 128], BF16)
            mk.op("pool", lambda e: e.memset(VA1[:].rearrange("p a b c d -> p (a b c d)"), 0.0), writes=[VA1])
            mk.op("pool", lambda e: e.memset(VA1[:, :, :, 0, 64:65], 1.0), writes=[VA1])
            mk.op("pool", lambda e: e.memset(VA1[:, :, :, 1, 0:1], 1.0), writes=[VA1])
            self.bc = self.bcs[1]
            with ExitStack() as es2:
                hT = self.sb(es2, [128, 8, NQ], BF16)
                with ExitStack() as es3:
                    self.make_norm_scratch(es3, n=2, nx=3)
                    self.norm_pipe(20, lambda i: self.x1_d.t.ap()[i],
                                   lambda i: (self.bc["Al" if i < 18 else "Ac"], self.bc["Bl" if i < 18 else "Bc"]),
                                   lambda i: (hT, hT[:, :, i * 128:(i + 1) * 128]))
                    mk.barrier()
                tab1 = self.sb(es2, [128, 2, NQ], F32)
                mk.dma("sp", tab1[:], din["tab1"].t.ap().rearrange("t p n -> p t n"), writes=[tab1])
                wps = [self.sb(es2, [128, 8, 256], BF16) for _ in range(2)]
                g0l, g0c = self.G0["Gl"], self.G0["Gc"]
                st = [self.sb(es2, [128, 512], BF16) for _ in range(2)]
                jobs = [("q", hp, [(0, hp * 128, 128), (128, 2560 + hp * 128, 128)]) for hp in range(8)]
                jobs += [("k", kh, [(0, 3584 + kh * 128, 128), (128, 4096 + kh * 128, 128)]) for kh in range(4)]
                jobs += [("g", hp, [(0, 1536 + hp * 128, 128)]) for hp in range(8)]
                jobs += [("v", 0, [(0, 1280, 256)])]

                def jload(ji):
                    for dcol, c0, n in jobs[ji][2]:
                        self.wload(wps[ji % 2], dcol, din["w1in"], 0, 8, c0, n)

                jload(0)
                si = 0
                ri = 0
                for ji, (kind, idx, _) in enumerate(jobs):
                    wp = wps[ji % 2]
                    first = True
                    if kind in ("q", "k"):
                        for b0, n in QBLK:
                            if kind == "q" and b0 >= NOWN:
                                continue
                            n_ = min(n, NOWN - b0) if kind == "q" else n
                            t1 = g0l[:, (ri % 2) * 512:(ri % 2) * 512 + n_]; t2 = g0c[:, (ri % 2) * 512:(ri % 2) * 512 + n_]
                            ri += 1
                            pa = self.bank(); pb = self.bank()
                            self.proj(pa[:, 0:n_], pa, wp, (0, 128), hT, (b0, b0 + n_))
                            self.proj(pb[:, 0:n_], pb, wp, (128, 256), hT, (b0, b0 + n_))
                            if first and ji + 1 < len(jobs):
                                jload(ji + 1); first = False
                            mk.op("dve", lambda e: e.tensor_tensor(out=t1, in0=pa[:, 0:n_], in1=tab1[:, 0, b0:b0 + n_], op=ALU.mult), reads=[pa, tab1], writes=[g0l])
                            mk.op("dve", lambda e: e.tensor_tensor(out=t2, in0=pb[:, 0:n_], in1=tab1[:, 1, b0:b0 + n_], op=ALU.mult), reads=[pb, tab1], writes=[g0c])
                            dst, dap = (QT1, QT1[:, idx, b0:b0 + n_]) if kind == "q" else (KT1, KT1[:, idx, b0:b0 + n_])
                            mk.op("pool", lambda e: e.tensor_tensor(out=dap, in0=t1, in1=t2, op=ALU.add), reads=[g0l, g0c], writes=[dst])
                    elif kind == "g":
                        for b0, n in QBLK:
                            p = self.bank()
                            self.proj(p[:, 0:n], p, wp, (0, 128), hT, (b0, b0 + n))
                            if first and ji + 1 < len(jobs):
                                jload(ji + 1); first = False
                            s_ = st[si % 2]; si += 1
                            mk.op("act", lambda e: e.activation(out=s_[:, 0:n], in_=p[:, 0:n], func=AF.Silu), reads=[p], writes=[s_])
                            mk.dma("sp", g1_d.t.ap()[idx, :, b0:b0 + n], s_[:, 0:n], reads=[s_], writes=[g1_d])
                    else:
                        for i in range(20):
                            p = self.bank()
                            for c in range(8):
                                mk.op("pe", lambda e: e.matmul(out=p[:, 0:256], lhsT=hT[:, c, i * 128:(i + 1) * 128], rhs=wp[:, c, 0:256], start=(c == 0), stop=(c == 7)),
                                      reads=[hT, wp], writes=[p], inc=(c == 7))
                            pv = p[:, 0:256].rearrange("p (a b) -> p a b", b=64)
                            mk.op("dve", lambda e: e.tensor_copy(out=VA1[:, i, :, 0, 0:64], in_=pv), reads=[p], writes=[VA1])
                            mk.op("act", lambda e: e.activation(out=VA1[:, i, :, 1, 64:128], in_=pv, func=AF.Copy), reads=[p], writes=[VA1])
                mk.barrier()
            maskb = self.sb(es, [128, 4, 512], BF16); kval = self.sb(es, [128, 20], F32)
            mk.dma("sp", maskb[:], din["maskb"].t.ap().rearrange("p d s j q -> p d (s j q)"), writes=[maskb])
            mk.dma("sp", kval[:], din["kvalid"].t.ap(), writes=[kval])
            snk = self.sb(es, [128, 16], F32); esk = self.sb(es, [128, 16], F32)
            for r in (0, 64):
                mk.dma("sp", snk[r:r + 1, :], din["sink"].t.ap(), writes=[snk])
            mk.op("act", lambda e: e.activation(out=esk[0:1, :], in_=snk[0:1, :], func=AF.Exp), reads=[snk], writes=[esk])
            mk.op("act", lambda e: e.activation(out=esk[64:65, :], in_=snk[64:65, :], func=AF.Exp), reads=[snk], writes=[esk])
            esr = self.sb(es, [128, 8, 512], F32)
            mk.op("pool", lambda e: e.memset(esr[:].rearrange("p a b -> p (a b)"), 0.0), writes=[esr])
            for kh in range(4):
                for par in range(2):
                    for k_ in range(2):
                        hq = 4 * kh + 2 * k_ + par
                        for r in (0, 64):
                            sl = esr[r:r + 1, kh * 2 + par, k_ * 256:(k_ + 1) * 256]
                            mk.op("dve", lambda e: e.tensor_scalar(out=sl, in0=sl, scalar1=esk[r:r + 1, hq:hq + 1], scalar2=None, op0=ALU.add),
                                  reads=[esk, esr], writes=[esr])
            gt = [self.sb(es, [128, 2, NQ], BF16) for _ in range(2)]
            AT = [self.sb(es, [128, 2, 2048], BF16) for _ in range(2)]
            PTs = [self.sb(es, [128, 512], BF16) for _ in range(18)]
            rec = self.sb(es, [128, 512], F32); tmp = self.sb(es, [128, 512], F32)
            items = [(kh, par, nb) for kh in range(4) for par in range(2) for nb in range(0, 16, 2)]
            st = {}
            ni = len(items)

            def fin_args(u):
                kh, par, nb = items[u]
                g_ = gt[kh % 2]; at = AT[kh % 2]
                o0, r0 = (64, 0) if par else (0, 64)
                q0 = 128 + 128 * nb
                return dict(po=st[u], n=512, o0=o0, r0=r0, gate_tile=g_, gate_ap=g_[o0:o0 + 64, :, q0:q0 + 256], dst_tile=at,
                            dst_ap=at[o0:o0 + 64, :, nb * 128:(nb + 2) * 128], rec=rec, tmp=tmp, extra=esr[r0:r0 + 1, kh * 2 + par, :])

            for t in range(ni + 2):
                if 0 <= t - 2 < ni:
                    self.att_finish(part="A", **fin_args(t - 2))
                if t < ni:
                    kh, par, nb = items[t]
                    if par == 0 and nb == 0:
                        g_ = gt[kh % 2]
                        mk.dma("sp", g_[:], g1_d.t.ap()[2 * kh:2 * kh + 2].rearrange("c p n -> p c n"), reads=[g1_d], writes=[g_])
                    p0 = par * 64
                    q0 = 128 + 128 * nb
                    kts = [(18, None), (19, None), (nb, 0), (nb + 1, 1), (nb + 2, 2), (nb + 3, 3)]
                    for ki, (kt, mi) in enumerate(kts):
                        ps = self.bank()
                        mk.op("pe", lambda e: e.matmul(out=ps[:, :], lhsT=KT1[p0:p0 + 64, kh, kt * 128:(kt + 1) * 128],
                                                       rhs=QT1[p0:p0 + 64, 2 * kh:2 * kh + 2, q0:q0 + 256], start=True, stop=(mi is None)),
                              reads=[KT1, QT1], writes=[ps], inc=(mi is None))
                        if mi is not None:
                            mk.op("pe", lambda e: e.matmul(out=ps[:, :], lhsT=self.ident[:], rhs=maskb[:, mi, :], start=False, stop=True),
                                  reads=[self.ident, maskb], writes=[ps])
                        pt = PTs[(t % 3) * 6 + ki]
                        mk.op("act", lambda e: e.activation(out=pt[:], in_=ps[:, :], func=AF.Exp, scale=GQA_SCALE, bias=kval[:, kt:kt + 1]),
                              reads=[ps, kval], writes=[pt])
                if 0 <= t - 2 < ni:
                    u = t - 2
                    self.att_finish(part="B", **fin_args(u))
                    self.release(st.pop(u))
                    kh, par, nb = items[u]
                    if par == 1 and nb == 14:
                        at = AT[kh % 2]
                        mk.dma("sp", self.cat_d.t.ap()[2 * kh:2 * kh + 2, :, 128:2176].rearrange("c p n -> p c n"), at[:], reads=[at], writes=[self.cat_d])
                if 0 <= t - 1 < ni:
                    u = t - 1
                    kh, par, nb = items[u]
                    po = st[u] = self.bank(hold=True)
                    kts = [18, 19, nb, nb + 1, nb + 2, nb + 3]
                    for ki, kt in enumerate(kts):
                        pt = PTs[(u % 3) * 6 + ki]
                        mk.op("pe", lambda e: e.matmul(out=po[:, :], lhsT=VA1[:, kt, kh, par, :], rhs=pt[:], start=(ki == 0), stop=(ki == 5)),
                              reads=[VA1, pt], writes=[po], inc=(ki == 5))
            mk.barrier()

    def layer1(self):
        self.phase_L1()
        x1 = self.x1_d
        self.phase_OUT("wout1", lambda gi: x1.t.ap()[gi], 16, 1, final=True, G=self.bcs[1])
        self.es_ab1.close()


def build_program(debug=False, upto=99):
    pr = Prog(debug=debug, upto=upto)
    pr.g1_d = pr.mk.dram([8, 128, NQ], BF16, "g1_d", "Internal")
    pr.layer0()
    if upto >= 1:
        pr.layer1()
    pr.mk.finish()
    return pr


def kernel(**inputs):
    W = pack_weights(inputs)
    in_maps = []
    for core in range(8):
        b, q = divmod(core, 4)
        m = core_inputs(inputs, W, b, q)
        in_maps.append({n: m[n] for n, _, _ in IN_SPECS})
    pr = build_program()
    res = run_bass_kernel_spmd(pr.nc, in_maps, core_ids=list(range(8)))
    out = np.zeros((NB, SEQ, D), np.float32)
    for core in range(8):
        b, q = divmod(core, 4)
        out[b, 2048 * q:2048 * (q + 1)] = np.asarray(res.results[core]["out"], np.float32).reshape(2048, D)
    return out
```

```python
import math
import numpy as np
import ml_dtypes
import concourse.bass as bass
import concourse.mybir as mybir
from concourse.bass_utils import run_bass_kernel_spmd

F32 = mybir.dt.float32
BF16 = mybir.dt.bfloat16
ALU = mybir.AluOpType
AF = mybir.ActivationFunctionType
AX = mybir.AxisListType


class Tile:
    __slots__ = ("t", "writers", "readers", "name")

    def __init__(self, t, name=""):
        self.t = t
        self.writers = []
        self.readers = []
        self.name = name

    def __getitem__(self, idx):
        return self.t[idx]


class Eng:
    def __init__(self, mk, name, eng, sem):
        self.mk, self.name, self.eng, self.sem = mk, name, eng, sem
        self.count = 0
        self.seen = {}


class MK:
    N_DMA_SEMS = 24

    def __init__(self, nc):
        self.nc = nc
        self.engs = {}
        for name, eng in (("pe", nc.tensor), ("act", nc.scalar), ("dve", nc.vector),
                          ("pool", nc.gpsimd), ("sp", nc.sync)):
            self.engs[name] = Eng(self, name, eng, nc.alloc_semaphore("s_" + name))
        self.dma_sems = [nc.alloc_semaphore("d%d" % i) for i in range(2 * self.N_DMA_SEMS)]
        self.dma_cnt = [0] * (2 * self.N_DMA_SEMS)
        self.dma_i = {"sw": 0, "hw": 0}
        self.ntile = 0

    def sb(self, shape, dt=BF16, name=None):
        self.ntile += 1
        name = name or "t%d" % self.ntile
        return Tile(self.nc.alloc_sbuf_tensor(name + "_%d" % self.ntile, list(shape), dt), name)

    def ps(self, shape, dt=F32, name=None):
        self.ntile += 1
        name = name or "p%d" % self.ntile
        return Tile(self.nc.alloc_psum_tensor(name + "_%d" % self.ntile, list(shape), dt), name)

    def dram(self, shape, dt, name, kind="Internal"):
        return Tile(self.nc.dram_tensor(name, list(shape), dt, kind=kind), name)

    def _wait(self, E, tok):
        sem, val = tok
        key = sem.num
        if E.seen.get(key, 0) >= val:
            return
        E.seen[key] = val
        E.eng.wait_ge(sem, val)

    def _deps(self, E, reads, writes):
        toks = []
        for t in reads:
            toks += t.writers
        for t in writes:
            toks += t.writers
            toks += t.readers
        best = {}
        for sem, val in toks:
            if best.get(sem.num, (None, 0))[1] < val:
                best[sem.num] = (sem, val)
        for sem, val in best.values():
            if E.name == "pe" and sem.num == E.sem.num:
                continue
            self._wait(E, (sem, val))

    def _mark(self, tok, reads, writes):
        for t in reads:
            t.readers.append(tok)
            if len(t.readers) > 64:
                t.readers = _compress(t.readers)
        for t in writes:
            t.writers = [tok]
            t.readers = []

    def op(self, eng, fn, reads=(), writes=(), inc=True):
        E = self.engs[eng]
        self._deps(E, reads, writes)
        ins = fn(E.eng)
        if inc:
            E.count += 1
            ins.then_inc(E.sem, 1)
            tok = (E.sem, E.count)
        else:
            assert eng == "pe"
            tok = (E.sem, E.count + 1)
        self._mark(tok, reads, writes)
        return tok

    def dma(self, eng, out, in_, reads=(), writes=(), **kw):
        E = self.engs[eng]
        self._deps(E, reads, writes)
        pool = "sw" if eng == "pool" else "hw"
        i = self.dma_i[pool] % self.N_DMA_SEMS + (self.N_DMA_SEMS if pool == "sw" else 0)
        self.dma_i[pool] += 1
        sem = self.dma_sems[i]
        if self.dma_cnt[i] > 0:
            self._wait(E, (sem, 16 * self.dma_cnt[i]))
        self.dma_cnt[i] += 1
        E.eng.dma_start(out=out, in_=in_, **kw).then_inc(sem, 16)
        tok = (sem, 16 * self.dma_cnt[i])
        self._mark(tok, reads, writes)
        return tok

    def barrier(self):
        for E in self.engs.values():
            for F in self.engs.values():
                if F is not E and F.count > 0:
                    self._wait(E, (F.sem, F.count))
            for i, sem in enumerate(self.dma_sems):
                if self.dma_cnt[i] > 0:
                    self._wait(E, (sem, 16 * self.dma_cnt[i]))

    def finish(self):
        E = self.engs["sp"]
        for F in self.engs.values():
            if F is not E and F.count > 0:
                self._wait(E, (F.sem, F.count))
        for i, sem in enumerate(self.dma_sems):
            if self.dma_cnt[i] > 0:
                self._wait(E, (sem, 16 * self.dma_cnt[i]))


def _compress(toks):
    best = {}
    for sem, val in toks:
        if best.get(sem.num, (None, 0))[1] < val:
            best[sem.num] = (sem, val)
    return list(best.values())


D = 1024
SEQ = 8192
NB = 2
CTX = 256
NOWN = 2304
NQ = NOWN + CTX
NKEY = SEQ + CTX
EPS = 1e-6
MLA_SCALE = 1.0 / math.sqrt(96.0)
GQA_SCALE = 1.0 / 8.0
NEG = -30000.0


def _bf(a):
    return np.ascontiguousarray(np.asarray(a, dtype=np.float32)).astype(ml_dtypes.bfloat16)


def _f32(a):
    return np.ascontiguousarray(np.asarray(a, dtype=np.float32))


def _rope_tab(tok, nf):
    tok = np.asarray(tok, dtype=np.float64)
    row = np.floor(tok / 64.0)
    col = tok - 64.0 * row
    inv = 10000.0 ** (-np.arange(nf, dtype=np.float64) / nf)
    ang = np.concatenate([row[:, None] * inv, col[:, None] * inv], axis=-1)
    return np.cos(ang).T, np.sin(ang).T


def host_tables(q):
    T = {}
    T["ident"] = _bf(np.eye(128))
    T["identf"] = _f32(np.eye(128))
    n1 = np.arange(128)[:, None, None]
    n2 = np.arange(64)[None, :, None]
    k1 = np.arange(128)[None, None, :]
    th = 2 * np.pi * ((k1 * (64 * n1 + n2)) % SEQ) / SEQ
    T["T1"] = _bf(np.stack([np.cos(th), -np.sin(th)], axis=2))
    c = np.arange(128)[:, None]
    j = np.arange(128)[None, :]
    ph = 2 * np.pi * ((c * j) % 128) / 128
    Cc, Sc = np.cos(ph), np.sin(ph)
    T["CS"] = _bf(np.stack([np.concatenate([Cc, -Sc], 1), np.concatenate([Sc, Cc], 1)], 1))
    k2 = (16 * q - 1 + np.arange(18)) % 64
    ps = 2 * np.pi * ((np.arange(64)[:, None] * k2[None, :]) % 64) / 64
    f2 = np.stack([np.cos(ps), np.sin(ps)], 1) / 1024.0
    f2b = np.zeros((128, 2, 36))
    f2b[0:64, :, 0:18] = f2
    f2b[64:128, :, 18:36] = f2
    T["F2"] = _bf(f2b)
    n = (np.arange(2)[None, :, None] * 128 + np.arange(128)[:, None, None])
    k = np.arange(256)[None, None, :]
    a = 2 * np.pi * ((n * k) % 256) / 256
    sc = 1.0 / math.sqrt(256.0 * 128.0)
    T["DC"] = _bf(np.stack([np.cos(a) * sc, -np.sin(a) * sc], 2))
    tk = np.zeros((66, 2, 32, 128), np.float32)
    tk[0:2, 0] = 1.0
    for t in range(64):
        cs, sn = _rope_tab(64 * np.arange(128) + t, 8)
        tk[2 + t, 0] = np.concatenate([cs, cs], 0)
        tk[2 + t, 1] = np.concatenate([-sn, sn], 0)
    T["tabK"] = tk
    t0 = 2048 * q - 128
    cs, sn = _rope_tab(t0 + np.arange(NOWN), 8)
    tq = np.zeros((2, 32, NQ), np.float32)
    tq[0, :, NOWN:] = 1.0
    tq[0, :, :NOWN] = np.concatenate([cs, cs], 0)
    tq[1, :, :NOWN] = np.concatenate([-sn, sn], 0)
    T["tabQ"] = tq
    cs, sn = _rope_tab(t0 + np.arange(NOWN), 16)
    t1 = np.zeros((2, 128, NQ), np.float32)
    t1[0, :, NOWN:] = 1.0
    t1[0, :, :NOWN] = np.concatenate([cs, cs, cs, cs], 0)
    t1[1, :, :NOWN] = np.concatenate([-sn, sn, -sn, sn], 0)
    T["tab1"] = t1
    kp = np.arange(128)[:, None]
    qp = np.arange(128)[None, :]
    mprev = np.where(kp >= qp, 0.0, NEG)
    mnext = np.where(kp <= qp, 0.0, NEG)
    full = np.full((128, 128), NEG)
    zero = np.zeros((128, 128))
    rel = {-1: full, 0: mprev, 1: zero, 2: mnext, 3: full}
    T["maskb"] = _bf(np.stack([np.stack([np.stack([rel[d - j] for j in range(2)], 1) for _ in range(2)], 1) for d in range(4)], 1))
    tok = t0 + np.arange(NOWN)
    kv = np.where((tok >= 0) & (tok < SEQ), 0.0, NEG).reshape(18, 128).T
    T["kvalid"] = _f32(np.concatenate([kv, np.zeros((128, 2))], 1))
    sel = np.zeros((2, 2, 128), np.float32)
    sel[0, 0] = 1.0
    sel[1, 1] = 1.0
    T["sel"] = sel
    return T


def pack_weights(inp):
    W = {}
    w = np.asarray(inp["e_w_in"][0], np.float32)
    kpe = w[:, 1664:1696]
    W["w0in"] = _f32(np.concatenate([w, kpe[:, 16:32], kpe[:, 0:16]], 1))
    wq = np.asarray(inp["e_w_qb"][0], np.float32)
    sw = []
    for h in range(8):
        pe = wq[:, h * 96 + 64:h * 96 + 96]
        sw += [pe[:, 16:32], pe[:, 0:16]]
    W["wqb"] = _f32(np.concatenate([wq] + sw, 1))
    W["wkvb"] = _f32(inp["e_w_kvb"][0])
    W["wout0"] = _f32(inp["e_w_out"][0])
    w1 = np.asarray(inp["o_w_in"][0], np.float32)
    qs, k2, k2s = [], [], []
    for h in range(16):
        qh = w1[:, h * 64:(h + 1) * 64]
        qs += [qh[:, 32:64], qh[:, 0:32]]
    for h in range(4):
        kh = w1[:, 1024 + h * 64:1024 + (h + 1) * 64]
        k2 += [kh, kh]
        k2s += [kh[:, 32:64], kh[:, 0:32]] * 2
    W["w1in"] = _f32(np.concatenate([w1] + qs + k2 + k2s, 1))
    W["wout1"] = _f32(inp["o_w_out"][0])
    W["wmod"] = _f32(inp["w_mod"])
    W["bmod"] = _f32(inp["b_mod"])
    W["normg"] = _f32(inp["norm_g"])
    W["finalg"] = _f32(np.asarray(inp["final_g"], np.float32)[None, :])
    W["qnorm"] = _f32(np.asarray(inp["e_q_norm"][0], np.float32).reshape(3, 128).T)
    W["kvnorm"] = _f32(np.asarray(inp["e_kv_norm"][0], np.float32).reshape(2, 128).T)
    W["sink"] = _f32(inp["o_sink"])
    return W


def core_inputs(inp, W, b, q):
    m = dict(W)
    m.update(host_tables(q))
    x = np.asarray(inp["x"], np.float32)[b]
    m["xa"] = _f32(x.reshape(128, 64, D).transpose(1, 0, 2))
    t0 = 2048 * q - 128
    xo = np.zeros((NOWN, D), np.float32)
    lo, hi = max(t0, 0), min(t0 + NOWN, SEQ)
    xo[lo - t0:hi - t0] = x[lo:hi]
    m["xo"] = xo.reshape(18, 128, D)
    m["ctx"] = _f32(np.asarray(inp["ctx"], np.float32)[b].reshape(2, 128, D))
    cc = np.stack([np.asarray(inp["c"], np.float32)[b], np.asarray(inp["c_ctx"], np.float32)], 0)
    m["cT"] = _f32(cc.reshape(2, 8, 128).transpose(2, 1, 0))
    return m


from contextlib import ExitStack

IN_SPECS = [
    ("xa", [64, 128, D], F32), ("xo", [18, 128, D], F32), ("ctx", [2, 128, D], F32), ("cT", [128, 8, 2], F32),
    ("w0in", [D, 2240], F32), ("wqb", [384, 1024], F32), ("wkvb", [256, 1024], F32), ("wout0", [D, D], F32),
    ("w1in", [D, 4608], F32), ("wout1", [D, D], F32), ("wmod", [2, D, 3072], F32), ("bmod", [2, 3072], F32),
    ("normg", [2, D], F32), ("finalg", [1, D], F32), ("qnorm", [128, 3], F32), ("kvnorm", [128, 2], F32),
    ("sink", [1, 16], F32),
    ("ident", [128, 128], BF16), ("identf", [128, 128], F32), ("T1", [128, 64, 2, 128], BF16),
    ("CS", [128, 2, 256], BF16), ("F2", [128, 2, 36], BF16), ("DC", [128, 2, 2, 256], BF16),
    ("tabK", [66, 2, 32, 128], F32), ("tabQ", [2, 32, NQ], F32), ("tab1", [2, 128, NQ], F32),
    ("maskb", [128, 4, 2, 2, 128], BF16), ("kvalid", [128, 20], F32), ("sel", [2, 2, 128], F32),
]

QBLK = [(0, 512), (512, 512), (1024, 512), (1536, 512), (2048, 512)]


class Prog:
    def __init__(self, debug=False, upto=99):
        self.debug = debug
        self.upto = upto
        nc = self.nc = bass.Bass("TRN2", target_bir_lowering=False)
        mk = self.mk = MK(nc)
        self.din = {n: mk.dram(s, dt, n, "ExternalInput") for n, s, dt in IN_SPECS}
        self.out = mk.dram([16, 128, D], F32, "out", "ExternalOutput")
        dk = "ExternalOutput" if debug else "Internal"
        self.qan_d = mk.dram([3, 128, NQ], BF16, "qan_d", dk)
        self.fg_d = mk.dram([4, 128, NQ], BF16, "fg_d", dk)
        self.mg_d = mk.dram([4, 128, NQ], BF16, "mg_d", dk)
        self.kvn_d = mk.dram([2, 128, NKEY], BF16, "kvn_d", dk)
        self.kpe_d = mk.dram([32, NKEY], BF16, "kpe_d", dk)
        self.cat_d = mk.dram([8, 128, NQ], BF16, "cat_d", dk)
        self.x1_d = mk.dram([20, 128, D], F32, "x1_d", dk)
        self.pTs = [mk.ps([128, 1024], BF16, "pT%d" % i) for i in range(2)]
        self.P = [mk.ps([128, 512], F32, "P%d" % i) for i in range(6)]
        self.pi = 0
        self.held = set()
        self.es = ExitStack()
        self.cnt = 0
        P_ = self.sbp
        self.ident = P_([128, 128], BF16); self.onesb = P_([128, 128], BF16); self.onesf = P_([128, 64], F32)
        self.epsb = P_([128, 1], F32)
        self.G0 = {k: P_([128, D], F32) for k in ("Gl", "Gc")}
        self.bcs = {}
        mk.dma("sp", self.ident[:], self.din["ident"][:, :], writes=[self.ident])
        mk.op("pool", lambda e: e.memset(self.onesb[:], 1.0), writes=[self.onesb])
        mk.op("pool", lambda e: e.memset(self.onesf[:], 1.0), writes=[self.onesf])
        mk.op("pool", lambda e: e.memset(self.epsb[:], EPS), writes=[self.epsb])

    def sbp(self, shape, dt=BF16):
        self.cnt += 1
        return Tile(self.nc.alloc_sbuf_tensor("pers%d" % self.cnt, list(shape), dt))

    def sb(self, es, shape, dt=BF16):
        self.cnt += 1
        return Tile(es.enter_context(self.nc.sbuf_tensor("s%d" % self.cnt, list(shape), dt)))

    def bank(self, hold=False):
        while True:
            self.pi = (self.pi + 1) % 6
            if self.pi not in self.held:
                break
        if hold:
            self.held.add(self.pi)
        return self.P[self.pi]

    def release(self, p):
        self.held.discard(self.P.index(p))

    def wload(self, dst, dcol, src, r0, nk, c0, n):
        v = src.t.ap()[r0:r0 + nk * 128, c0:c0 + n].rearrange("(c p) n -> p c n", p=128)
        self.mk.dma("pool", dst[:, 0:nk, dcol:dcol + n], v, writes=[dst], max_dma_last_dim=4096)

    def alloc_bc(self, l, es_ab):
        bc = self.bcs[l] = {}
        for k in ("Al", "Bl", "Ac", "Bc", "Gl", "Gc"):
            if l == 0 and k[0] == "G":
                bc[k] = self.G0[k]
            else:
                bc[k] = self.sb(es_ab, [128, D], F32)

    def mod_steps(self, l, es_ab, es):
        mk, din = self.mk, self.din
        if l not in self.bcs:
            self.alloc_bc(l, es_ab)
        bc = self.bcs[l]
        wms = [self.sb(es, [128, 8, 512], F32) for _ in range(2)]
        cT = self.sb(es, [128, 16], F32); scT = self.sb(es, [128, 16], F32)
        bm = self.sb(es, [2, 3072], F32); g2 = self.sb(es, [2, D], F32); sel = self.sb(es, [2, 2, 128], F32)
        mrow = self.sb(es, [2, 3072], F32); arow = self.sb(es, [2, D], F32)
        steps = []

        def first():
            mk.dma("sp", cT[:], din["cT"].t.ap().rearrange("p c v -> p (c v)"), writes=[cT])
            mk.op("act", lambda e: e.activation(out=scT[:], in_=cT[:], func=AF.Silu), reads=[cT], writes=[scT])
            for r in range(2):
                mk.dma("sp", bm[r:r + 1, :], din["bmod"][l:l + 1, :], writes=[bm])
                mk.dma("sp", g2[r:r + 1, :], din["normg"][l:l + 1, :], writes=[g2])
            mk.dma("sp", sel[:], din["sel"].t.ap(), writes=[sel])
        steps.append(first)
        for nb in range(6):
            def ld(nb=nb):
                wm = wms[nb % 2]
                mk.dma("sp", wm[:], din["wmod"].t.ap()[l, :, nb * 512:(nb + 1) * 512].rearrange("(c p) n -> p c n", p=128), writes=[wm])
            def blk(nb=nb):
                wm = wms[nb % 2]
                p = self.bank()
                for kc in range(8):
                    mk.op("pe", lambda e: e.matmul(out=p[0:2, :], lhsT=scT[:, 2 * kc:2 * kc + 2],
                                                   rhs=wm[:, kc, :], start=(kc == 0), stop=(kc == 7)),
                          reads=[scT, wm], writes=[p], inc=(kc == 7))
                mk.op("dve", lambda e: e.tensor_tensor(out=mrow[:, nb * 512:(nb + 1) * 512], in0=p[0:2, :],
                                                       in1=bm[:, nb * 512:(nb + 1) * 512], op=ALU.add),
                      reads=[p, bm], writes=[mrow])
            steps.append(ld)
            steps.append(blk)
        order = [steps[0], steps[1]]
        for nb in range(6):
            if nb + 1 < 6:
                order.append(steps[1 + 2 * (nb + 1)])
            order.append(steps[2 + 2 * nb])

        def arow_f():
            mk.op("dve", lambda e: e.scalar_tensor_tensor(out=arow[:], in0=mrow[:, D:2 * D], scalar=1.0, in1=g2[:],
                                                          op0=ALU.add, op1=ALU.mult), reads=[mrow, g2], writes=[arow])
        order.append(arow_f)
        for which, sfx in ((0, "l"), (1, "c")):
            for key, src, off in (("A", arow, 0), ("B", mrow, 0), ("G", mrow, 2 * D)):
                def bcf(which=which, sfx=sfx, key=key, src=src, off=off):
                    dst = bc[key + sfx]
                    for half in range(2):
                        p = self.bank()
                        mk.op("pe", lambda e: e.matmul(out=p[:], lhsT=sel[:, which, :],
                                                       rhs=src[:, off + half * 512:off + (half + 1) * 512],
                                                       start=True, stop=True), reads=[sel, src], writes=[p])
                        mk.op("dve", lambda e: e.tensor_copy(out=dst[:, half * 512:(half + 1) * 512], in_=p[:]),
                              reads=[p], writes=[dst])
                order.append(bcf)
        return order

    def make_norm_scratch(self, es, n=3, nx=5):
        self.xts = [self.sb(es, [128, D], F32) for _ in range(nx)]
        self.ns = [dict(junk=self.sb(es, [128, D], BF16), ss=self.sb(es, [128, 1], F32),
                        rstd=self.sb(es, [128, 1], F32), tmp=self.sb(es, [128, D], F32), hb=self.sb(es, [128, D], BF16))
                   for _ in range(n)]

    def norm_s0(self, i, src_ap):
        xt = self.xts[i % len(self.xts)]
        self.mk.dma("sp", xt[:], src_ap, writes=[xt])

    def norm_s1(self, i, src_ap, A, B):
        mk = self.mk
        s = self.ns[i % len(self.ns)]
        xt = self.xts[i % len(self.xts)]
        junk, ss, rstd, tmp, hb = s["junk"], s["ss"], s["rstd"], s["tmp"], s["hb"]
        mk.op("act", lambda e: e.activation(out=junk[:], in_=xt[:], func=AF.Square, scale=1.0 / 32.0, accum_out=ss[:]),
              reads=[xt], writes=[junk, ss])
        mk.op("act", lambda e: e.activation(out=rstd[:], in_=ss[:], func=AF.Ln, bias=self.epsb[:], scale=1.0),
              reads=[ss, self.epsb], writes=[rstd])
        mk.op("act", lambda e: e.activation(out=rstd[:], in_=rstd[:], func=AF.Exp, scale=-0.5), reads=[rstd], writes=[rstd])
        mk.op("dve", lambda e: e.scalar_tensor_tensor(out=tmp[:], in0=xt[:], scalar=rstd[:, 0:1], in1=A[:],
                                                      op0=ALU.mult, op1=ALU.mult), reads=[xt, rstd, A], writes=[tmp])
        mk.op("pool", lambda e: e.tensor_tensor(out=hb[:], in0=tmp[:], in1=B[:], op=ALU.add), reads=[tmp, B], writes=[hb])

    def norm_s2(self, i, dst_tile, dst_ap):
        mk = self.mk
        hb = self.ns[i % len(self.ns)]["hb"]
        pT = self.pTs[i % 2]
        for c in range(8):
            mk.op("pe", lambda e: e.transpose(out=pT[:, c * 128:(c + 1) * 128], in_=hb[:, c * 128:(c + 1) * 128],
                                              identity=self.ident[:]), reads=[hb, self.ident], writes=[pT], inc=(c == 7))
        mk.op("act", lambda e: e.activation(out=dst_ap, in_=pT[:, :].rearrange("p (c n) -> p c n", c=8), func=AF.Copy),
              reads=[pT], writes=[dst_tile])

    def norm_pipe(self, n, src_fn, ab_fn, dst_fn, s3=None):
        for i in range(min(2, n)):
            self.norm_s0(i, src_fn(i))
        pre = getattr(self, "pre_step", None)
        for t in range(n + 2):
            if pre is not None:
                pre()
            if t + 2 < n:
                self.norm_s0(t + 2, src_fn(t + 2))
            if t < n:
                A, B = ab_fn(t)
                self.norm_s1(t, src_fn(t), A, B)
            if 0 <= t - 1 < n:
                dt_, dap = dst_fn(t - 1)
                self.norm_s2(t - 1, dt_, dap)
            if s3 is not None and 0 <= t - 2 < n:
                s3(t - 2)
        if pre is not None:
            pre()
            self.pre_step = None

    def proj(self, p_ap, ptile, w, wcols, h, hcols, n_c=8):
        for c in range(n_c):
            self.mk.op("pe", lambda e: e.matmul(out=p_ap, lhsT=w[:, c, wcols[0]:wcols[1]], rhs=h[:, c, hcols[0]:hcols[1]],
                                                start=(c == 0), stop=(c == n_c - 1)), reads=[w, h], writes=[ptile], inc=(c == n_c - 1))

    def rstd_bcast(self, es_tiles, pss, n, dim, out_tile):
        mk = self.mk
        mk.op("act", lambda e: e.activation(out=out_tile[:, 0:n], in_=pss[:, 0:n], func=AF.Ln, bias=self.epsb[:], scale=1.0 / dim),
              reads=[pss, self.epsb], writes=[out_tile])
        mk.op("act", lambda e: e.activation(out=out_tile[:, 0:n], in_=out_tile[:, 0:n], func=AF.Exp, scale=-0.5),
              reads=[out_tile], writes=[out_tile])

    def phase_O(self):
        mk, din = self.mk, self.din
        with ExitStack() as es:
            hT = self.sb(es, [128, 8, NQ], BF16)
            wps = [self.sb(es, [128, 8, 512], BF16) for _ in range(3)]
            self.wload(wps[0], 0, din["w0in"], 0, 8, 512, 512)
            self.wload(wps[1], 0, din["w0in"], 0, 8, 1696, 512)
            self.wload(wps[2], 0, din["w0in"], 0, 8, 1024, 384)
            with ExitStack() as esn:
                self.make_norm_scratch(esn)
                self.norm_pipe(20, lambda i: din["xo"].t.ap()[i] if i < 18 else din["ctx"].t.ap()[i - 18],
                               lambda i: (self.bc["Al" if i < 18 else "Ac"], self.bc["Bl" if i < 18 else "Bc"]),
                               lambda i: (hT, hT[:, :, i * 128:(i + 1) * 128]))
                mk.barrier()
            stage = [self.sb(es, [128, 512], BF16) for _ in range(3)]
            si = 0
            for wp, dst in ((wps[0], self.fg_d), (wps[1], self.mg_d)):
                for j in range(4):
                    for b0, n in QBLK:
                        p = self.bank()
                        self.proj(p[:, 0:n], p, wp, (j * 128, (j + 1) * 128), hT, (b0, b0 + n))
                        st = stage[si % 3]; si += 1
                        mk.op("act", lambda e: e.activation(out=st[:, 0:n], in_=p[:, 0:n], func=AF.Silu), reads=[p], writes=[st])
                        mk.dma("sp", dst.t.ap()[j, :, b0:b0 + n], st[:, 0:n], reads=[st], writes=[dst])
            wp = wps[2]
            sq = [self.sb(es, [128, 512], BF16) for _ in range(3)]
            rq = self.sb(es, [128, 512], F32)
            for b0, n in QBLK:
                ps = [self.bank() for _ in range(3)]
                pss = self.bank()
                for j in range(3):
                    self.proj(ps[j][:, 0:n], ps[j], wp, (j * 128, (j + 1) * 128), hT, (b0, b0 + n))
                    mk.op("act", lambda e: e.activation(out=sq[j][:, 0:n], in_=ps[j][:, 0:n], func=AF.Square), reads=[ps[j]], writes=[sq[j]])
                for j in range(3):
                    mk.op("pe", lambda e: e.matmul(out=pss[:, 0:n], lhsT=self.onesb[:], rhs=sq[j][:, 0:n], start=(j == 0), stop=(j == 2)),
                          reads=[self.onesb, sq[j]], writes=[pss])
                self.rstd_bcast(None, pss, n, 384.0, rq)
                for j in range(3):
                    st = stage[si % 3]; si += 1
                    mk.op("dve", lambda e: e.tensor_tensor(out=st[:, 0:n], in0=ps[j][:, 0:n], in1=rq[:, 0:n], op=ALU.mult),
                          reads=[ps[j], rq], writes=[st])
                    mk.dma("sp", self.qan_d.t.ap()[j, :, b0:b0 + n], st[:, 0:n], reads=[st], writes=[self.qan_d])
            mk.barrier()

    def phase_A(self, es_outer):
        mk, din = self.mk, self.din
        with ExitStack() as es:
            self.make_norm_scratch(es)
            wA = self.sb(es, [128, 8, 832], BF16)
            self.wload(wA, 0, din["w0in"], 0, 8, 0, 512)
            self.wload(wA, 512, din["w0in"], 0, 8, 1408, 288)
            self.wload(wA, 800, din["w0in"], 0, 8, 2208, 32)
            hTs = [self.sb(es, [128, 8, 128], BF16) for _ in range(3)]
            sqkv = [self.sb(es, [128, 256], BF16) for _ in range(2)]
            rk = [self.sb(es, [128, 128], F32) for _ in range(2)]
            kst = [self.sb(es, [128, 2, 128], BF16) for _ in range(2)]
            tabk = [self.sb(es, [128, 2, 128], F32) for _ in range(3)]
            t1 = [self.sb(es, [128, 128], F32) for _ in range(2)]
            t2 = [self.sb(es, [128, 128], F32) for _ in range(2)]
            pst = [self.sb(es, [128, 128], BF16) for _ in range(2)]

            def s3(i):
                lat = i >= 2
                hT = hTs[i % 3]
                tb = tabk[i % 3]
                mk.dma("sp", tb[64:96, :, :], din["tabK"].t.ap()[i].rearrange("t p n -> p t n"), writes=[tb])
                pkv = self.bank(hold=True)
                for j in range(2):
                    self.proj(pkv[:, j * 128:(j + 1) * 128], pkv, wA, (512 + j * 128, 640 + j * 128), hT, (0, 128))
                sq = sqkv[i % 2]
                mk.op("act", lambda e: e.activation(out=sq[:], in_=pkv[:, 0:256], func=AF.Square), reads=[pkv], writes=[sq])
                p = self.bank()
                self.mk_tokmajor(p, hT, wA, 0, 512)
                udst, uap = (self.u_all, self.u_all[:, i - 2, :]) if lat else (self.u_ctx, self.u_ctx[:, i, :])
                mk.op("dve", lambda e: e.tensor_copy(out=uap, in_=p[:]), reads=[p], writes=[udst])
                pk = self.bank()
                self.proj(pk[64:96, 0:128], pk, wA, (768, 800), hT, (0, 128))
                self.proj(pk[64:96, 128:256], pk, wA, (800, 832), hT, (0, 128))
                a1, a2 = t1[i % 2], t2[i % 2]
                mk.op("dve", lambda e: e.tensor_tensor(out=a1[64:96, :], in0=pk[64:96, 0:128], in1=tb[64:96, 0, :], op=ALU.mult),
                      reads=[pk, tb], writes=[a1])
                mk.op("dve", lambda e: e.tensor_tensor(out=a2[64:96, :], in0=pk[64:96, 128:256], in1=tb[64:96, 1, :], op=ALU.mult),
                      reads=[pk, tb], writes=[a2])
                ps_ = pst[i % 2]
                mk.op("pool", lambda e: e.tensor_tensor(out=ps_[64:96, :], in0=a1[64:96, :], in1=a2[64:96, :], op=ALU.add),
                      reads=[a1, a2], writes=[ps_])
                mk.dma("sp", self.kpe_d.t.ap()[:, i * 128:(i + 1) * 128], ps_[64:96, :], reads=[ps_], writes=[self.kpe_d])
                pss = self.bank(hold=True)
                for j in range(2):
                    mk.op("pe", lambda e: e.matmul(out=pss[:, 0:128], lhsT=self.onesb[:], rhs=sq[:, j * 128:(j + 1) * 128],
                                                   start=(j == 0), stop=(j == 1)), reads=[self.onesb, sq], writes=[pss], inc=(j == 1))
                late.append((i, pkv, pss))

            late = []

            def s3b():
                while late:
                    i, pkv, pss = late.pop(0)
                    r_ = rk[i % 2]
                    self.rstd_bcast(None, pss, 128, 256.0, r_)
                    self.release(pss)
                    ks = kst[i % 2]
                    for j in range(2):
                        mk.op("dve", lambda e: e.tensor_tensor(out=ks[:, j, :], in0=pkv[:, j * 128:(j + 1) * 128], in1=r_[:], op=ALU.mult),
                              reads=[pkv, r_], writes=[ks])
                    self.release(pkv)
                    mk.dma("sp", self.kvn_d.t.ap()[:, :, i * 128:(i + 1) * 128].rearrange("j p n -> p j n"), ks[:], reads=[ks], writes=[self.kvn_d])

            self.pre_step = s3b
            self.norm_pipe(66, lambda i: din["xa"].t.ap()[i - 2] if i >= 2 else din["ctx"].t.ap()[i],
                           lambda i: (self.bc["Al" if i >= 2 else "Ac"], self.bc["Bl" if i >= 2 else "Bc"]),
                           lambda i: (hTs[i % 3], hTs[i % 3][:, :, :]), s3=s3)
            mk.barrier()

    def mk_tokmajor(self, p, hT, w, c0, n):
        for c in range(8):
            self.mk.op("pe", lambda e: e.matmul(out=p[:, 0:n], lhsT=hT[:, c, :], rhs=w[:, c, c0:c0 + n], start=(c == 0), stop=(c == 7)),
                       reads=[hT, w], writes=[p], inc=(c == 7))

    def phase_F(self):
        mk, din = self.mk, self.din
        with ExitStack() as es:
            T1 = self.sb(es, [128, 64, 256], BF16)
            for k in range(4):
                mk.dma("sp", T1[:, k * 16:(k + 1) * 16, :], din["T1"].t.ap()[:, k * 16:(k + 1) * 16].rearrange("p a r k -> p a (r k)"), writes=[T1])
            CS = self.sb(es, [128, 2, 256], BF16); F2 = self.sb(es, [128, 2, 36], BF16); DC = self.sb(es, [128, 2, 512], BF16)
            mk.dma("sp", CS[:], din["CS"].t.ap(), writes=[CS])
            mk.dma("sp", F2[:], din["F2"].t.ap(), writes=[F2])
            mk.dma("sp", DC[:], din["DC"].t.ap().rearrange("p t r k -> p t (r k)"), writes=[DC])
            fg = self.sb(es, [128, 4, NQ], BF16)
            mk.dma("sp", fg[:], self.fg_d.t.ap().rearrange("j p n -> p j n"), reads=[self.fg_d], writes=[fg])
            A1 = self.sb(es, [128, 128, 128], BF16)
            A1w = A1[:, :, :].rearrange("p (a n) (r k) -> p n a r k", a=2, r=2)
            Gb = [self.sb(es, [128, 2, 256], BF16) for _ in range(3)]
            catF = [self.sb(es, [128, NOWN], BF16) for _ in range(2)]
            XT = self.sb(es, [128, 512], BF16); cst = self.sb(es, [128, 256], BF16)
            for g in range(4):
                for pr in range(32):
                    p = self.bank()
                    for s in range(2):
                        n2 = 2 * pr + s
                        mk.op("pe", lambda e: e.matmul(out=p[:, s * 256:(s + 1) * 256], lhsT=self.u_all[:, n2, g * 128:(g + 1) * 128],
                                                       rhs=T1[:, n2, :], start=True, stop=True), reads=[self.u_all, T1], writes=[p], inc=(s == 1))
                    eng = "dve" if pr % 2 == 0 else "act"
                    for s_ in range(2):
                        oap = A1w[:, 2 * pr + s_]
                        iap = p[:, s_ * 256:(s_ + 1) * 256].rearrange("p (r k a) -> p a r k", r=2, a=2)
                        if s_ == 0:
                            mk.op("dve", lambda e: e.tensor_copy(out=oap, in_=iap), reads=[p], writes=[A1])
                        else:
                            mk.op("act", lambda e: e.activation(out=oap, in_=iap, func=AF.Copy), reads=[p], writes=[A1])
                cf = catF[g % 2]
                cfv = cf[:, :].rearrange("p (k2 k1) -> p k1 k2", k1=128)
                fgv = fg[:, g, 0:NOWN].rearrange("p (k2 k1) -> p k1 k2", k1=128)
                fst = {"pacc": None, "a0": 0}

                def s3(pr, gb, fst=fst, cf=cf, cfv=cfv, fgv=fgv):
                    for s_ in range(2):
                        kp = 2 * pr + s_
                        k1 = 2 * kp
                        if k1 % 28 == 0:
                            fst["pacc"] = self.bank(hold=True); fst["a0"] = k1
                        pacc, a0 = fst["pacc"], fst["a0"]
                        sl = (k1 - a0) * 18
                        mk.op("pe", lambda e: e.matmul(out=pacc[:, sl:sl + 36], lhsT=gb[:, s_, 0:128], rhs=F2[:, 0, :], start=True, stop=False),
                              reads=[gb, F2], writes=[pacc], inc=False)
                        mk.op("pe", lambda e: e.matmul(out=pacc[:, sl:sl + 36], lhsT=gb[:, s_, 128:256], rhs=F2[:, 1, :], start=False, stop=True),
                              reads=[gb, F2], writes=[pacc])
                        if (k1 + 1) % 28 == 27 or k1 + 1 == 127:
                            cnt = k1 + 2 - a0
                            mk.op("dve", lambda e: e.tensor_tensor(out=cfv[:, a0:a0 + cnt, :], in0=pacc[:, 0:cnt * 18].rearrange("p (a b) -> p a b", b=18),
                                                                   in1=fgv[:, a0:a0 + cnt, :], op=ALU.mult), reads=[pacc, fg], writes=[cf])
                            self.release(pacc)

                prev = None
                for pr in range(32):
                    p = self.bank()
                    for s_ in range(2):
                        kp = 2 * pr + s_
                        mk.op("pe", lambda e: e.matmul(out=p[:, s_ * 256:(s_ + 1) * 256], lhsT=A1[:, :, kp], rhs=CS[:, 0, :], start=True, stop=False),
                              reads=[A1, CS], writes=[p], inc=False)
                        mk.op("pe", lambda e: e.matmul(out=p[:, s_ * 256:(s_ + 1) * 256], lhsT=A1[:, :, 64 + kp], rhs=CS[:, 1, :], start=False, stop=True),
                              reads=[A1, CS], writes=[p], inc=(s_ == 1))
                    gb = Gb[pr % 3]
                    if pr % 2 == 0:
                        mk.op("dve", lambda e: e.tensor_copy(out=gb[:, :, :], in_=p[:, :].rearrange("p (a b) -> p a b", a=2)), reads=[p], writes=[gb])
                    else:
                        mk.op("act", lambda e: e.activation(out=gb[:, :, :], in_=p[:, :].rearrange("p (a b) -> p a b", a=2), func=AF.Copy), reads=[p], writes=[gb])
                    if prev is not None:
                        s3(*prev)
                    prev = (pr, gb)
                s3(*prev)
                mk.dma("sp", self.cat_d.t.ap()[g, :, 0:NOWN], cf[:, :], reads=[cf], writes=[self.cat_d])
                px = self.bank()
                for nt in range(2):
                    mk.op("pe", lambda e: e.matmul(out=px[:, :], lhsT=self.u_ctx[:, nt, g * 128:(g + 1) * 128], rhs=DC[:, nt, :], start=(nt == 0), stop=(nt == 1)),
                          reads=[self.u_ctx, DC], writes=[px], inc=(nt == 1))
                mk.op("dve", lambda e: e.tensor_copy(out=XT[:], in_=px[:]), reads=[px], writes=[XT])
                py = self.bank()
                mk.op("pe", lambda e: e.matmul(out=py[:, 0:256], lhsT=CS[:, 0, 0:128], rhs=XT[:, 0:256], start=True, stop=False), reads=[CS, XT], writes=[py], inc=False)
                mk.op("pe", lambda e: e.matmul(out=py[:, 0:256], lhsT=CS[:, 1, 0:128], rhs=XT[:, 256:512], start=False, stop=True), reads=[CS, XT], writes=[py])
                mk.op("dve", lambda e: e.tensor_tensor(out=cst[:], in0=py[:, 0:256], in1=fg[:, g, NOWN:NQ], op=ALU.mult), reads=[py, fg], writes=[cst])
                mk.dma("sp", self.cat_d.t.ap()[g, :, NOWN:NQ], cst[:], reads=[cst], writes=[self.cat_d])
            mk.barrier()

    def wscaled(self, es, src, nk, ncols, normname):
        mk = self.mk
        w = self.sb(es, [128, nk, ncols], BF16)
        g = self.sb(es, [128, nk], F32)
        self.wload(w, 0, src, 0, nk, 0, ncols)
        mk.dma("sp", g[:], self.din[normname].t.ap(), writes=[g])
        for c in range(nk):
            mk.op("dve", lambda e: e.tensor_scalar(out=w[:, c, :], in0=w[:, c, :], scalar1=g[:, c:c + 1], scalar2=None, op0=ALU.mult),
                  reads=[w, g], writes=[w])
        return w

    def phase_ATT(self):
        mk, din = self.mk, self.din
        with ExitStack() as es:
            kvn = self.sb(es, [128, 2, NKEY], BF16)
            KTs = [self.sb(es, [128, NKEY], BF16) for _ in range(2)]
            VAs = [self.sb(es, [128, 66, 128], BF16) for _ in range(2)]
            qan = self.sb(es, [128, 3, NQ], BF16)
            QTs = [self.sb(es, [128, NQ], BF16) for _ in range(2)]
            tabq = self.sb(es, [128, 2, NQ], F32)
            mgs = [self.sb(es, [128, NQ], BF16) for _ in range(2)]
            ATs = [self.sb(es, [128, NQ], BF16)] * 2
            for j in range(2):
                mk.dma("sp", kvn[:, j, :], self.kvn_d.t.ap()[j], reads=[self.kvn_d], writes=[kvn])
            for KT in KTs:
                mk.dma("sp", KT[64:96, :], self.kpe_d.t.ap(), reads=[self.kpe_d], writes=[KT])
            mk.dma("sp", qan[:], self.qan_d.t.ap().rearrange("j p n -> p j n"), reads=[self.qan_d], writes=[qan])
            mk.dma("sp", tabq[64:96, :, :], din["tabQ"].t.ap().rearrange("t p n -> p t n"), writes=[tabq])
            wq = self.wscaled(es, din["wqb"], 3, 1024, "qnorm")
            wkv = self.wscaled(es, din["wkvb"], 2, 1024, "kvnorm")
            mk.op("dve", lambda e: e.memset(VAs[0][:, :, 64:128], 0.0), writes=[VAs[0]])
            mk.op("dve", lambda e: e.memset(VAs[0][:, :, 64:65], 1.0), writes=[VAs[0]])
            mk.op("dve", lambda e: e.memset(VAs[1][:, :, 0:64], 0.0), writes=[VAs[1]])
            mk.op("dve", lambda e: e.memset(VAs[1][:, :, 0:1], 1.0), writes=[VAs[1]])
            PTs = [self.sb(es, [128, 512], BF16) for _ in range(4)]
            t1 = self.sb(es, [128, 512], F32); t2 = self.sb(es, [128, 512], F32)
            rec = self.sb(es, [128, 512], F32); tmp = self.sb(es, [128, 512], F32)

            def gen_steps(h):
                odd = h % 2
                o0 = 64 if odd else 0
                KT, VA, QT = KTs[odd], VAs[odd], QTs[odd]
                steps = []
                if not odd:
                    mgt = mgs[(h // 2) % 2]
                    steps.append(lambda: mk.dma("sp", mgt[:], self.mg_d.t.ap()[h // 2], reads=[self.mg_d], writes=[mgt]))
                for kb in range(17):
                    def f(kb=kb):
                        k0 = kb * 512
                        n = min(512, NKEY - k0)
                        p = self.bank()
                        self.proj(p[0:64, 0:n], p, wkv, (h * 128, h * 128 + 64), kvn, (k0, k0 + n), n_c=2)
                        mk.op("dve", lambda e: e.tensor_copy(out=KT[0:64, k0:k0 + n], in_=p[0:64, 0:n]), reads=[p], writes=[KT])
                    steps.append(f)
                for g0 in range(0, 66, 8):
                    def f(g0=g0):
                        cnt = min(8, 66 - g0)
                        p = self.bank()
                        for t in range(cnt):
                            kt = g0 + t
                            for j in range(2):
                                mk.op("pe", lambda e: e.matmul(out=p[:, t * 64:(t + 1) * 64], lhsT=kvn[:, j, kt * 128:(kt + 1) * 128],
                                                               rhs=wkv[:, j, h * 128 + 64:h * 128 + 128], start=(j == 0), stop=(j == 1)),
                                      reads=[kvn, wkv], writes=[p], inc=(j == 1 and t == cnt - 1))
                        mk.op("dve", lambda e: e.tensor_copy(out=VA[:, g0:g0 + cnt, o0:o0 + 64], in_=p[:, 0:cnt * 64].rearrange("p (a b) -> p a b", b=64)),
                              reads=[p], writes=[VA])
                    steps.append(f)
                for b0, n in QBLK:
                    def f(b0=b0, n=n):
                        pa = self.bank(); pb = self.bank()
                        self.proj(pa[0:96, 0:n], pa, wq, (h * 96, h * 96 + 96), qan, (b0, b0 + n), n_c=3)
                        self.proj(pb[64:96, 0:n], pb, wq, (768 + h * 32, 800 + h * 32), qan, (b0, b0 + n), n_c=3)
                        mk.op("dve", lambda e: e.tensor_copy(out=QT[0:64, b0:b0 + n], in_=pa[0:64, 0:n]), reads=[pa], writes=[QT])
                        mk.op("dve", lambda e: e.tensor_tensor(out=t1[64:96, 0:n], in0=pa[64:96, 0:n], in1=tabq[64:96, 0, b0:b0 + n], op=ALU.mult),
                              reads=[pa, tabq], writes=[t1])
                        mk.op("dve", lambda e: e.tensor_tensor(out=t2[64:96, 0:n], in0=pb[64:96, 0:n], in1=tabq[64:96, 1, b0:b0 + n], op=ALU.mult),
                              reads=[pb, tabq], writes=[t2])
                        mk.op("pool", lambda e: e.tensor_tensor(out=QT[64:96, b0:b0 + n], in0=t1[64:96, 0:n], in1=t2[64:96, 0:n], op=ALU.add),
                              reads=[t1, t2], writes=[QT])
                    steps.append(f)
                return steps

            for f in gen_steps(0):
                f()
            items = []
            for h in range(8):
                for bi, (b0, n, kts) in enumerate([(0, 512, 66), (512, 512, 66), (1024, 512, 66), (1536, 512, 66), (2048, 256, 66), (2304, 256, 2)]):
                    for kt in range(kts):
                        items.append((h, b0, n, kt, kts))
            pending = []
            deferred = []
            recs = [rec, self.sb(es, [128, 512], F32)]
            nfin = 0
            state = {}
            LAG = 2
            nit = len(items)
            per_head = nit // 8
            for t in range(nit + LAG):
                if t < nit:
                    h, b0, n, kt, kts = items[t]
                    odd = h % 2
                    if kt == 0:
                        state[(h, b0)] = self.bank(hold=True)
                    if t % per_head == 0 and h < 7:
                        pending = gen_steps(h + 1)
                        every = max(1, (per_head - 40) // len(pending))
                    if pending and (t % per_head) % every == 0:
                        pending.pop(0)()
                    ps = self.bank()
                    mk.op("pe", lambda e: e.matmul(out=ps[:, 0:n], lhsT=KTs[odd][0:96, kt * 128:(kt + 1) * 128], rhs=QTs[odd][0:96, b0:b0 + n], start=True, stop=True),
                          reads=[KTs[odd], QTs[odd]], writes=[ps])
                    pt = PTs[t % 4]
                    mk.op("act", lambda e: e.activation(out=pt[:, 0:n], in_=ps[:, 0:n], func=AF.Exp, scale=MLA_SCALE), reads=[ps], writes=[pt])
                while deferred:
                    deferred.pop(0)()
                if t >= LAG:
                    h, b0, n, kt, kts = items[t - LAG]
                    odd = h % 2
                    o0, r0 = (64, 0) if odd else (0, 64)
                    po = state[(h, b0)]
                    pt = PTs[(t - LAG) % 4]
                    mk.op("pe", lambda e: e.matmul(out=po[:, 0:n], lhsT=VAs[odd][:, kt, :], rhs=pt[:, 0:n], start=(kt == 0), stop=(kt == kts - 1)),
                          reads=[VAs[odd], pt], writes=[po], inc=(kt == kts - 1))
                    if kt == kts - 1:
                        mgt = mgs[(h // 2) % 2]; AT = ATs[(h // 2) % 2]
                        fa = dict(po=po, n=n, o0=o0, r0=r0, gate_tile=mgt, gate_ap=mgt[o0:o0 + 64, b0:b0 + n], dst_tile=AT,
                                  dst_ap=AT[o0:o0 + 64, b0:b0 + n], rec=recs[nfin % 2], tmp=tmp)
                        nfin += 1
                        self.att_finish(part="A", **fa)

                        def fb(fa=fa, po=po, odd=odd, b0=b0, h=h, AT=AT):
                            self.att_finish(part="B", **fa)
                            self.release(po)
                            if odd and b0 == 2304:
                                mk.dma("sp", self.cat_d.t.ap()[4 + h // 2], AT[:], reads=[AT], writes=[self.cat_d])
                        deferred.append(fb)
            while deferred:
                deferred.pop(0)()
            while pending:
                pending.pop(0)()
            mk.barrier()

    def att_finish(self, po, n, o0, r0, gate_tile, gate_ap, dst_tile, dst_ap, rec, tmp, extra=None, part=None):
        mk = self.mk
        if part in (None, "A"):
            self._att_finish_a(po, n, r0, rec, extra)
        if part in (None, "B"):
            self._att_finish_b(po, n, o0, r0, gate_tile, gate_ap, dst_tile, dst_ap, rec, tmp)

    def _att_finish_a(self, po, n, r0, rec, extra):
        mk = self.mk
        if extra is not None:
            mk.op("dve", lambda e: e.tensor_tensor(out=rec[r0:r0 + 1, 0:n], in0=po[r0:r0 + 1, 0:n], in1=extra, op=ALU.add), reads=[po], writes=[rec])
            mk.op("act", lambda e: e.activation(out=rec[r0:r0 + 1, 0:n], in_=rec[r0:r0 + 1, 0:n], func=AF.Ln), reads=[rec], writes=[rec])
        else:
            mk.op("act", lambda e: e.activation(out=rec[r0:r0 + 1, 0:n], in_=po[r0:r0 + 1, 0:n], func=AF.Ln), reads=[po], writes=[rec])
        mk.op("act", lambda e: e.activation(out=rec[r0:r0 + 1, 0:n], in_=rec[r0:r0 + 1, 0:n], func=AF.Exp, scale=-1.0), reads=[rec], writes=[rec])

    def _att_finish_b(self, po, n, o0, r0, gate_tile, gate_ap, dst_tile, dst_ap, rec, tmp):
        mk = self.mk
        pb = self.bank()
        mk.op("pe", lambda e: e.matmul(out=pb[o0:o0 + 64, 0:n], lhsT=self.onesf[r0:r0 + 1, 0:64], rhs=rec[r0:r0 + 1, 0:n], start=True, stop=True),
              reads=[self.onesf, rec], writes=[pb])
        mk.op("dve", lambda e: e.tensor_tensor(out=tmp[o0:o0 + 64, 0:n], in0=po[o0:o0 + 64, 0:n], in1=gate_ap, op=ALU.mult),
              reads=[po, gate_tile], writes=[tmp])
        mk.op("dve", lambda e: e.tensor_tensor(out=dst_ap, in0=tmp[o0:o0 + 64, 0:n], in1=pb[o0:o0 + 64, 0:n], op=ALU.mult),
              reads=[tmp, pb], writes=[dst_tile])

    def phase_OUT(self, wname, xsrc, ntiles, tile0, final=False, G=None, side=None):
        mk, din = self.mk, self.din
        with ExitStack() as es:
            wo = self.sb(es, [128, 8, D], BF16)
            self.wload(wo, 0, din[wname], 0, 8, 0, 512)
            self.wload(wo, 512, din[wname], 0, 8, 512, 512)
            catb = [self.sb(es, [128, 8, 512], BF16) for _ in range(3)]
            xts = [self.sb(es, [128, D], F32) for _ in range(4)]
            tmp = [self.sb(es, [128, D], F32) for _ in range(2)]
            xn = [self.sb(es, [128, D], F32) for _ in range(2)]
            junk = self.sb(es, [128, D], BF16); ss = self.sb(es, [128, 1], F32); rstd = self.sb(es, [128, 1], F32)
            fgb = None
            if final:
                fgb = self.sb(es, [128, D], F32)
                fr = self.sb(es, [1, D], F32)
                mk.dma("sp", fr[:], din["finalg"].t.ap(), writes=[fr])
                self.bcast_row(fr, fgb)
            side_steps = side(es) if side is not None else []
            loaded = set()

            def load(ti):
                gi = tile0 + ti
                blk = gi // 4
                if blk not in loaded:
                    loaded.add(blk)
                    cb = catb[blk % 3]
                    mk.dma("sp", cb[:], self.cat_d.t.ap()[:, :, blk * 512:(blk + 1) * 512].rearrange("c p n -> p c n"), reads=[self.cat_d], writes=[cb])
                xt = xts[ti % 4]
                mk.dma("sp", xt[:], xsrc(gi), reads=[self.x1_d], writes=[xt])

            for ti in range(min(2, ntiles)):
                load(ti)
            for ti in range(ntiles):
                if ti + 2 < ntiles:
                    load(ti + 2)
                if side_steps:
                    side_steps.pop(0)()
                gi = tile0 + ti
                blk, t = divmod(gi, 4)
                cb = catb[blk % 3]
                xt = xts[ti % 4]; tm = tmp[ti % 2]; xo = xn[ti % 2]
                Gt = G["Gl" if gi < 18 else "Gc"]
                for half in range(2):
                    p = self.bank()
                    for c in range(8):
                        mk.op("pe", lambda e: e.matmul(out=p[:], lhsT=cb[:, c, t * 128:(t + 1) * 128], rhs=wo[:, c, half * 512:(half + 1) * 512],
                                                       start=(c == 0), stop=(c == 7)), reads=[cb, wo], writes=[p], inc=(c == 7))
                    mk.op("dve", lambda e: e.tensor_tensor(out=tm[:, half * 512:(half + 1) * 512], in0=p[:], in1=Gt[:, half * 512:(half + 1) * 512], op=ALU.mult),
                          reads=[p, Gt], writes=[tm])
                mk.op("pool", lambda e: e.tensor_tensor(out=xo[:], in0=tm[:], in1=xt[:], op=ALU.add), reads=[tm, xt], writes=[xo])
                if not final:
                    mk.dma("sp", self.x1_d.t.ap()[gi], xo[:], reads=[xo], writes=[self.x1_d])
                else:
                    mk.op("act", lambda e: e.activation(out=junk[:], in_=xo[:], func=AF.Square, scale=1.0 / 32.0, accum_out=ss[:]), reads=[xo], writes=[junk, ss])
                    mk.op("act", lambda e: e.activation(out=rstd[:], in_=ss[:], func=AF.Ln, bias=self.epsb[:], scale=1.0), reads=[ss, self.epsb], writes=[rstd])
                    mk.op("act", lambda e: e.activation(out=rstd[:], in_=rstd[:], func=AF.Exp, scale=-0.5), reads=[rstd], writes=[rstd])
                    mk.op("dve", lambda e: e.scalar_tensor_tensor(out=tm[:], in0=xo[:], scalar=rstd[:, 0:1], in1=fgb[:], op0=ALU.mult, op1=ALU.mult),
                          reads=[xo, rstd, fgb], writes=[tm])
                    mk.dma("sp", self.out.t.ap()[ti], tm[:], reads=[tm], writes=[self.out])
            while side_steps:
                side_steps.pop(0)()
            mk.barrier()

    def bcast_row(self, row, dst):
        mk = self.mk
        for half in range(2):
            p = self.bank()
            mk.op("pe", lambda e: e.matmul(out=p[0:64, :], lhsT=self.onesf[0:1, 0:64], rhs=row[0:1, half * 512:(half + 1) * 512], start=True, stop=True),
                  reads=[self.onesf, row], writes=[p])
            mk.op("pe", lambda e: e.matmul(out=p[64:128, :], lhsT=self.onesf[0:1, 0:64], rhs=row[0:1, half * 512:(half + 1) * 512], start=True, stop=True),
                  reads=[self.onesf, row], writes=[p])
            mk.op("dve", lambda e: e.tensor_copy(out=dst[:, half * 512:(half + 1) * 512], in_=p[:]), reads=[p], writes=[dst])

    def layer0(self):
        with ExitStack() as es:
            self.u_all = self.sb(es, [128, 64, 512], BF16)
            self.u_ctx = self.sb(es, [128, 2, 512], BF16)
            with ExitStack() as es_ab:
                with ExitStack() as es_tmp:
                    for f in self.mod_steps(0, es_ab, es_tmp):
                        f()
                    self.mk.barrier()
                self.bc = self.bcs[0]
                self.phase_O()
                self.phase_A(es)
            self.phase_F()
        self.mk.barrier()
        self.phase_ATT()
        din = self.din
        self.es_ab1 = ExitStack()
        self.alloc_bc(1, self.es_ab1)
        self.phase_OUT("wout0", lambda gi: din["xo"].t.ap()[gi] if gi < 18 else din["ctx"].t.ap()[gi - 18], 20, 0,
                       G=self.bcs[0], side=lambda es_tmp: self.mod_steps(1, self.es_ab1, es_tmp))

    def phase_L1(self):
        mk, din = self.mk, self.din
        g1_d = self.g1_d
        with ExitStack() as es:
            QT1 = self.sb(es, [128, 8, NOWN], BF16)
            KT1 = self.sb(es, [128, 4, NQ], BF16)
            VA1 = self.sb(es, [128, 20, 4, 2, 128], BF16)
            mk.op("pool", lambda e: e.memset(VA1[:].rearrange("p a b c d -> p (a b c d)"), 0.0), writes=[VA1])
            mk.op("pool", lambda e: e.memset(VA1[:, :, :, 0, 64:65], 1.0), writes=[VA1])
            mk.op("pool", lambda e: e.memset(VA1[:, :, :, 1, 0:1], 1.0), writes=[VA1])
            self.bc = self.bcs[1]
            with ExitStack() as es2:
                hT = self.sb(es2, [128, 8, NQ], BF16)
                with ExitStack() as es3:
                    self.make_norm_scratch(es3, n=2, nx=3)
                    self.norm_pipe(20, lambda i: self.x1_d.t.ap()[i],
                                   lambda i: (self.bc["Al" if i < 18 else "Ac"], self.bc["Bl" if i < 18 else "Bc"]),
                                   lambda i: (hT, hT[:, :, i * 128:(i + 1) * 128]))
                    mk.barrier()
                tab1 = self.sb(es2, [128, 2, NQ], F32)
                mk.dma("sp", tab1[:], din["tab1"].t.ap().rearrange("t p n -> p t n"), writes=[tab1])
                wps = [self.sb(es2, [128, 8, 256], BF16) for _ in range(2)]
                g0l, g0c = self.G0["Gl"], self.G0["Gc"]
                st = [self.sb(es2, [128, 512], BF16) for _ in range(2)]
                jobs = [("q", hp, [(0, hp * 128, 128), (128, 2560 + hp * 128, 128)]) for hp in range(8)]
                jobs += [("k", kh, [(0, 3584 + kh * 128, 128), (128, 4096 + kh * 128, 128)]) for kh in range(4)]
                jobs += [("g", hp, [(0, 1536 + hp * 128, 128)]) for hp in range(8)]
                jobs += [("v", 0, [(0, 1280, 256)])]

                def jload(ji):
                    for dcol, c0, n in jobs[ji][2]:
                        self.wload(wps[ji % 2], dcol, din["w1in"], 0, 8, c0, n)

                jload(0)
                si = 0
                ri = 0
                for ji, (kind, idx, _) in enumerate(jobs):
                    wp = wps[ji % 2]
                    first = True
                    if kind in ("q", "k"):
                        for b0, n in QBLK:
                            if kind == "q" and b0 >= NOWN:
                                continue
                            n_ = min(n, NOWN - b0) if kind == "q" else n
                            t1 = g0l[:, (ri % 2) * 512:(ri % 2) * 512 + n_]; t2 = g0c[:, (ri % 2) * 512:(ri % 2) * 512 + n_]
                            ri += 1
                            pa = self.bank(); pb = self.bank()
                            self.proj(pa[:, 0:n_], pa, wp, (0, 128), hT, (b0, b0 + n_))
                            self.proj(pb[:, 0:n_], pb, wp, (128, 256), hT, (b0, b0 + n_))
                            if first and ji + 1 < len(jobs):
                                jload(ji + 1); first = False
                            mk.op("dve", lambda e: e.tensor_tensor(out=t1, in0=pa[:, 0:n_], in1=tab1[:, 0, b0:b0 + n_], op=ALU.mult), reads=[pa, tab1], writes=[g0l])
                            mk.op("dve", lambda e: e.tensor_tensor(out=t2, in0=pb[:, 0:n_], in1=tab1[:, 1, b0:b0 + n_], op=ALU.mult), reads=[pb, tab1], writes=[g0c])
                            dst, dap = (QT1, QT1[:, idx, b0:b0 + n_]) if kind == "q" else (KT1, KT1[:, idx, b0:b0 + n_])
                            mk.op("pool", lambda e: e.tensor_tensor(out=dap, in0=t1, in1=t2, op=ALU.add), reads=[g0l, g0c], writes=[dst])
                    elif kind == "g":
                        for b0, n in QBLK:
                            p = self.bank()
                            self.proj(p[:, 0:n], p, wp, (0, 128), hT, (b0, b0 + n))
                            if first and ji + 1 < len(jobs):
                                jload(ji + 1); first = False
                            s_ = st[si % 2]; si += 1
                            mk.op("act", lambda e: e.activation(out=s_[:, 0:n], in_=p[:, 0:n], func=AF.Silu), reads=[p], writes=[s_])
                            mk.dma("sp", g1_d.t.ap()[idx, :, b0:b0 + n], s_[:, 0:n], reads=[s_], writes=[g1_d])
                    else:
                        for i in range(20):
                            p = self.bank()
                            for c in range(8):
                                mk.op("pe", lambda e: e.matmul(out=p[:, 0:256], lhsT=hT[:, c, i * 128:(i + 1) * 128], rhs=wp[:, c, 0:256], start=(c == 0), stop=(c == 7)),
                                      reads=[hT, wp], writes=[p], inc=(c == 7))
                            pv = p[:, 0:256].rearrange("p (a b) -> p a b", b=64)
                            mk.op("dve", lambda e: e.tensor_copy(out=VA1[:, i, :, 0, 0:64], in_=pv), reads=[p], writes=[VA1])
                            mk.op("act", lambda e: e.activation(out=VA1[:, i, :, 1, 64:128], in_=pv, func=AF.Copy), reads=[p], writes=[VA1])
                mk.barrier()
            maskb = self.sb(es, [128, 4, 512], BF16); kval = self.sb(es, [128, 20], F32)
            mk.dma("sp", maskb[:], din["maskb"].t.ap().rearrange("p d s j q -> p d (s j q)"), writes=[maskb])
            mk.dma("sp", kval[:], din["kvalid"].t.ap(), writes=[kval])
            snk = self.sb(es, [128, 16], F32); esk = self.sb(es, [128, 16], F32)
            for r in (0, 64):
                mk.dma("sp", snk[r:r + 1, :], din["sink"].t.ap(), writes=[snk])
            mk.op("act", lambda e: e.activation(out=esk[0:1, :], in_=snk[0:1, :], func=AF.Exp), reads=[snk], writes=[esk])
            mk.op("act", lambda e: e.activation(out=esk[64:65, :], in_=snk[64:65, :], func=AF.Exp), reads=[snk], writes=[esk])
            esr = self.sb(es, [128, 8, 512], F32)
            mk.op("pool", lambda e: e.memset(esr[:].rearrange("p a b -> p (a b)"), 0.0), writes=[esr])
            for kh in range(4):
                for par in range(2):
                    for k_ in range(2):
                        hq = 4 * kh + 2 * k_ + par
                        for r in (0, 64):
                            sl = esr[r:r + 1, kh * 2 + par, k_ * 256:(k_ + 1) * 256]
                            mk.op("dve", lambda e: e.tensor_scalar(out=sl, in0=sl, scalar1=esk[r:r + 1, hq:hq + 1], scalar2=None, op0=ALU.add),
                                  reads=[esk, esr], writes=[esr])
            gt = [self.sb(es, [128, 2, NQ], BF16) for _ in range(2)]
            AT = [self.sb(es, [128, 2, 2048], BF16) for _ in range(2)]
            PTs = [self.sb(es, [128, 512], BF16) for _ in range(18)]
            rec = self.sb(es, [128, 512], F32); tmp = self.sb(es, [128, 512], F32)
            items = [(kh, par, nb) for kh in range(4) for par in range(2) for nb in range(0, 16, 2)]
            st = {}
            ni = len(items)

            def fin_args(u):
                kh, par, nb = items[u]
                g_ = gt[kh % 2]; at = AT[kh % 2]
                o0, r0 = (64, 0) if par else (0, 64)
                q0 = 128 + 128 * nb
                return dict(po=st[u], n=512, o0=o0, r0=r0, gate_tile=g_, gate_ap=g_[o0:o0 + 64, :, q0:q0 + 256], dst_tile=at,
                            dst_ap=at[o0:o0 + 64, :, nb * 128:(nb + 2) * 128], rec=rec, tmp=tmp, extra=esr[r0:r0 + 1, kh * 2 + par, :])

            for t in range(ni + 2):
                if 0 <= t - 2 < ni:
                    self.att_finish(part="A", **fin_args(t - 2))
                if t < ni:
                    kh, par, nb = items[t]
                    if par == 0 and nb == 0:
                        g_ = gt[kh % 2]
                        mk.dma("sp", g_[:], g1_d.t.ap()[2 * kh:2 * kh + 2].rearrange("c p n -> p c n"), reads=[g1_d], writes=[g_])
                    p0 = par * 64
                    q0 = 128 + 128 * nb
                    kts = [(18, None), (19, None), (nb, 0), (nb + 1, 1), (nb + 2, 2), (nb + 3, 3)]
                    for ki, (kt, mi) in enumerate(kts):
                        ps = self.bank()
                        mk.op("pe", lambda e: e.matmul(out=ps[:, :], lhsT=KT1[p0:p0 + 64, kh, kt * 128:(kt + 1) * 128],
                                                       rhs=QT1[p0:p0 + 64, 2 * kh:2 * kh + 2, q0:q0 + 256], start=True, stop=(mi is None)),
                              reads=[KT1, QT1], writes=[ps], inc=(mi is None))
                        if mi is not None:
                            mk.op("pe", lambda e: e.matmul(out=ps[:, :], lhsT=self.ident[:], rhs=maskb[:, mi, :], start=False, stop=True),
                                  reads=[self.ident, maskb], writes=[ps])
                        pt = PTs[(t % 3) * 6 + ki]
                        mk.op("act", lambda e: e.activation(out=pt[:], in_=ps[:, :], func=AF.Exp, scale=GQA_SCALE, bias=kval[:, kt:kt + 1]),
                              reads=[ps, kval], writes=[pt])
                if 0 <= t - 2 < ni:
                    u = t - 2
                    self.att_finish(part="B", **fin_args(u))
                    self.release(st.pop(u))
                    kh, par, nb = items[u]
                    if par == 1 and nb == 14:
                        at = AT[kh % 2]
                        mk.dma("sp", self.cat_d.t.ap()[2 * kh:2 * kh + 2, :, 128:2176].rearrange("c p n -> p c n"), at[:], reads=[at], writes=[self.cat_d])
                if 0 <= t - 1 < ni:
                    u = t - 1
                    kh, par, nb = items[u]
                    po = st[u] = self.bank(hold=True)
                    kts = [18, 19, nb, nb + 1, nb + 2, nb + 3]
                    for ki, kt in enumerate(kts):
                        pt = PTs[(u % 3) * 6 + ki]
                        mk.op("pe", lambda e: e.matmul(out=po[:, :], lhsT=VA1[:, kt, kh, par, :], rhs=pt[:], start=(ki == 0), stop=(ki == 5)),
                              reads=[VA1, pt], writes=[po], inc=(ki == 5))
            mk.barrier()

    def layer1(self):
        self.phase_L1()
        x1 = self.x1_d
        self.phase_OUT("wout1", lambda gi: x1.t.ap()[gi], 16, 1, final=True, G=self.bcs[1])
        self.es_ab1.close()


def build_program(debug=False, upto=99):
    pr = Prog(debug=debug, upto=upto)
    pr.g1_d = pr.mk.dram([8, 128, NQ], BF16, "g1_d", "Internal")
    pr.layer0()
    if upto >= 1:
        pr.layer1()
    pr.mk.finish()
    return pr


def kernel(**inputs):
    W = pack_weights(inputs)
    in_maps = []
    for core in range(8):
        b, q = divmod(core, 4)
        m = core_inputs(inputs, W, b, q)
        in_maps.append({n: m[n] for n, _, _ in IN_SPECS})
    pr = build_program()
    res = run_bass_kernel_spmd(pr.nc, in_maps, core_ids=list(range(8)))
    out = np.zeros((NB, SEQ, D), np.float32)
    for core in range(8):
        b, q = divmod(core, 4)
        out[b, 2048 * q:2048 * (q + 1)] = np.asarray(res.results[core]["out"], np.float32).reshape(2048, D)
    return out
```

```python
import math
import numpy as np
import ml_dtypes
import concourse.bass as bass
import concourse.mybir as mybir
from concourse.bass_utils import run_bass_kernel_spmd

F32 = mybir.dt.float32
BF16 = mybir.dt.bfloat16
ALU = mybir.AluOpType
AF = mybir.ActivationFunctionType
AX = mybir.AxisListType


class Tile:
    __slots__ = ("t", "writers", "readers", "name")

    def __init__(self, t, name=""):
        self.t = t
        self.writers = []
        self.readers = []
        self.name = name

    def __getitem__(self, idx):
        return self.t[idx]


class Eng:
    def __init__(self, mk, name, eng, sem):
        self.mk, self.name, self.eng, self.sem = mk, name, eng, sem
        self.count = 0
        self.seen = {}


class MK:
    N_DMA_SEMS = 24

    def __init__(self, nc):
        self.nc = nc
        self.engs = {}
        for name, eng in (("pe", nc.tensor), ("act", nc.scalar), ("dve", nc.vector),
                          ("pool", nc.gpsimd), ("sp", nc.sync)):
            self.engs[name] = Eng(self, name, eng, nc.alloc_semaphore("s_" + name))
        self.dma_sems = [nc.alloc_semaphore("d%d" % i) for i in range(2 * self.N_DMA_SEMS)]
        self.dma_cnt = [0] * (2 * self.N_DMA_SEMS)
        self.dma_i = {"sw": 0, "hw": 0}
        self.ntile = 0

    def sb(self, shape, dt=BF16, name=None):
        self.ntile += 1
        name = name or "t%d" % self.ntile
        return Tile(self.nc.alloc_sbuf_tensor(name + "_%d" % self.ntile, list(shape), dt), name)

    def ps(self, shape, dt=F32, name=None):
        self.ntile += 1
        name = name or "p%d" % self.ntile
        return Tile(self.nc.alloc_psum_tensor(name + "_%d" % self.ntile, list(shape), dt), name)

    def dram(self, shape, dt, name, kind="Internal"):
        return Tile(self.nc.dram_tensor(name, list(shape), dt, kind=kind), name)

    def _wait(self, E, tok):
        sem, val = tok
        key = sem.num
        if E.seen.get(key, 0) >= val:
            return
        E.seen[key] = val
        E.eng.wait_ge(sem, val)

    def _deps(self, E, reads, writes):
        toks = []
        for t in reads:
            toks += t.writers
        for t in writes:
            toks += t.writers
            toks += t.readers
        best = {}
        for sem, val in toks:
            if best.get(sem.num, (None, 0))[1] < val:
                best[sem.num] = (sem, val)
        for sem, val in best.values():
            if E.name == "pe" and sem.num == E.sem.num:
                continue
            self._wait(E, (sem, val))

    def _mark(self, tok, reads, writes):
        for t in reads:
            t.readers.append(tok)
            if len(t.readers) > 64:
                t.readers = _compress(t.readers)
        for t in writes:
            t.writers = [tok]
            t.readers = []

    def op(self, eng, fn, reads=(), writes=(), inc=True):
        E = self.engs[eng]
        self._deps(E, reads, writes)
        ins = fn(E.eng)
        if inc:
            E.count += 1
            ins.then_inc(E.sem, 1)
            tok = (E.sem, E.count)
        else:
            assert eng == "pe"
            tok = (E.sem, E.count + 1)
        self._mark(tok, reads, writes)
        return tok

    def dma(self, eng, out, in_, reads=(), writes=(), **kw):
        E = self.engs[eng]
        self._deps(E, reads, writes)
        pool = "sw" if eng == "pool" else "hw"
        i = self.dma_i[pool] % self.N_DMA_SEMS + (self.N_DMA_SEMS if pool == "sw" else 0)
        self.dma_i[pool] += 1
        sem = self.dma_sems[i]
        if self.dma_cnt[i] > 0:
            self._wait(E, (sem, 16 * self.dma_cnt[i]))
        self.dma_cnt[i] += 1
        E.eng.dma_start(out=out, in_=in_, **kw).then_inc(sem, 16)
        tok = (sem, 16 * self.dma_cnt[i])
        self._mark(tok, reads, writes)
        return tok

    def barrier(self):
        for E in self.engs.values():
            for F in self.engs.values():
                if F is not E and F.count > 0:
                    self._wait(E, (F.sem, F.count))
            for i, sem in enumerate(self.dma_sems):
                if self.dma_cnt[i] > 0:
                    self._wait(E, (sem, 16 * self.dma_cnt[i]))

    def finish(self):
        E = self.engs["sp"]
        for F in self.engs.values():
            if F is not E and F.count > 0:
                self._wait(E, (F.sem, F.count))
        for i, sem in enumerate(self.dma_sems):
            if self.dma_cnt[i] > 0:
                self._wait(E, (sem, 16 * self.dma_cnt[i]))


def _compress(toks):
    best = {}
    for sem, val in toks:
        if best.get(sem.num, (None, 0))[1] < val:
            best[sem.num] = (sem, val)
    return list(best.values())


D = 1024
SEQ = 8192
NB = 2
CTX = 256
NOWN = 2304
NQ = NOWN + CTX
NKEY = SEQ + CTX
EPS = 1e-6
MLA_SCALE = 1.0 / math.sqrt(96.0)
GQA_SCALE = 1.0 / 8.0
NEG = -30000.0


def _bf(a):
    return np.ascontiguousarray(np.asarray(a, dtype=np.float32)).astype(ml_dtypes.bfloat16)


def _f32(a):
    return np.ascontiguousarray(np.asarray(a, dtype=np.float32))


def _rope_tab(tok, nf):
    tok = np.asarray(tok, dtype=np.float64)
    row = np.floor(tok / 64.0)
    col = tok - 64.0 * row
    inv = 10000.0 ** (-np.arange(nf, dtype=np.float64) / nf)
    ang = np.concatenate([row[:, None] * inv, col[:, None] * inv], axis=-1)
    return np.cos(ang).T, np.sin(ang).T


def host_tables(q):
    T = {}
    T["ident"] = _bf(np.eye(128))
    T["identf"] = _f32(np.eye(128))
    n1 = np.arange(128)[:, None, None]
    n2 = np.arange(64)[None, :, None]
    k1 = np.arange(128)[None, None, :]
    th = 2 * np.pi * ((k1 * (64 * n1 + n2)) % SEQ) / SEQ
    T["T1"] = _bf(np.stack([np.cos(th), -np.sin(th)], axis=2))
    c = np.arange(128)[:, None]
    j = np.arange(128)[None, :]
    ph = 2 * np.pi * ((c * j) % 128) / 128
    Cc, Sc = np.cos(ph), np.sin(ph)
    T["CS"] = _bf(np.stack([np.concatenate([Cc, -Sc], 1), np.concatenate([Sc, Cc], 1)], 1))
    k2 = (16 * q - 1 + np.arange(18)) % 64
    ps = 2 * np.pi * ((np.arange(64)[:, None] * k2[None, :]) % 64) / 64
    f2 = np.stack([np.cos(ps), np.sin(ps)], 1) / 1024.0
    f2b = np.zeros((128, 2, 36))
    f2b[0:64, :, 0:18] = f2
    f2b[64:128, :, 18:36] = f2
    T["F2"] = _bf(f2b)
    n = (np.arange(2)[None, :, None] * 128 + np.arange(128)[:, None, None])
    k = np.arange(256)[None, None, :]
    a = 2 * np.pi * ((n * k) % 256) / 256
    sc = 1.0 / math.sqrt(256.0 * 128.0)
    T["DC"] = _bf(np.stack([np.cos(a) * sc, -np.sin(a) * sc], 2))
    tk = np.zeros((66, 2, 32, 128), np.float32)
    tk[0:2, 0] = 1.0
    for t in range(64):
        cs, sn = _rope_tab(64 * np.arange(128) + t, 8)
        tk[2 + t, 0] = np.concatenate([cs, cs], 0)
        tk[2 + t, 1] = np.concatenate([-sn, sn], 0)
    T["tabK"] = tk
    t0 = 2048 * q - 128
    cs, sn = _rope_tab(t0 + np.arange(NOWN), 8)
    tq = np.zeros((2, 32, NQ), np.float32)
    tq[0, :, NOWN:] = 1.0
    tq[0, :, :NOWN] = np.concatenate([cs, cs], 0)
    tq[1, :, :NOWN] = np.concatenate([-sn, sn], 0)
    T["tabQ"] = tq
    cs, sn = _rope_tab(t0 + np.arange(NOWN), 16)
    t1 = np.zeros((2, 128, NQ), np.float32)
    t1[0, :, NOWN:] = 1.0
    t1[0, :, :NOWN] = np.concatenate([cs, cs, cs, cs], 0)
    t1[1, :, :NOWN] = np.concatenate([-sn, sn, -sn, sn], 0)
    T["tab1"] = t1
    kp = np.arange(128)[:, None]
    qp = np.arange(128)[None, :]
    mprev = np.where(kp >= qp, 0.0, NEG)
    mnext = np.where(kp <= qp, 0.0, NEG)
    full = np.full((128, 128), NEG)
    zero = np.zeros((128, 128))
    rel = {-1: full, 0: mprev, 1: zero, 2: mnext, 3: full}
    T["maskb"] = _bf(np.stack([np.stack([np.stack([rel[d - j] for j in range(2)], 1) for _ in range(2)], 1) for d in range(4)], 1))
    tok = t0 + np.arange(NOWN)
    kv = np.where((tok >= 0) & (tok < SEQ), 0.0, NEG).reshape(18, 128).T
    T["kvalid"] = _f32(np.concatenate([kv, np.zeros((128, 2))], 1))
    sel = np.zeros((2, 2, 128), np.float32)
    sel[0, 0] = 1.0
    sel[1, 1] = 1.0
    T["sel"] = sel
    return T


def pack_weights(inp):
    W = {}
    w = np.asarray(inp["e_w_in"][0], np.float32)
    kpe = w[:, 1664:1696]
    W["w0in"] = _f32(np.concatenate([w, kpe[:, 16:32], kpe[:, 0:16]], 1))
    wq = np.asarray(inp["e_w_qb"][0], np.float32)
    sw = []
    for h in range(8):
        pe = wq[:, h * 96 + 64:h * 96 + 96]
        sw += [pe[:, 16:32], pe[:, 0:16]]
    W["wqb"] = _f32(np.concatenate([wq] + sw, 1))
    W["wkvb"] = _f32(inp["e_w_kvb"][0])
    W["wout0"] = _f32(inp["e_w_out"][0])
    w1 = np.asarray(inp["o_w_in"][0], np.float32)
    qs, k2, k2s = [], [], []
    for h in range(16):
        qh = w1[:, h * 64:(h + 1) * 64]
        qs += [qh[:, 32:64], qh[:, 0:32]]
    for h in range(4):
        kh = w1[:, 1024 + h * 64:1024 + (h + 1) * 64]
        k2 += [kh, kh]
        k2s += [kh[:, 32:64], kh[:, 0:32]] * 2
    W["w1in"] = _f32(np.concatenate([w1] + qs + k2 + k2s, 1))
    W["wout1"] = _f32(inp["o_w_out"][0])
    W["wmod"] = _f32(inp["w_mod"])
    W["bmod"] = _f32(inp["b_mod"])
    W["normg"] = _f32(inp["norm_g"])
    W["finalg"] = _f32(np.asarray(inp["final_g"], np.float32)[None, :])
    W["qnorm"] = _f32(np.asarray(inp["e_q_norm"][0], np.float32).reshape(3, 128).T)
    W["kvnorm"] = _f32(np.asarray(inp["e_kv_norm"][0], np.float32).reshape(2, 128).T)
    W["sink"] = _f32(inp["o_sink"])
    return W


def core_inputs(inp, W, b, q):
    m = dict(W)
    m.update(host_tables(q))
    x = np.asarray(inp["x"], np.float32)[b]
    m["xa"] = _f32(x.reshape(128, 64, D).transpose(1, 0, 2))
    t0 = 2048 * q - 128
    xo = np.zeros((NOWN, D), np.float32)
    lo, hi = max(t0, 0), min(t0 + NOWN, SEQ)
    xo[lo - t0:hi - t0] = x[lo:hi]
    m["xo"] = xo.reshape(18, 128, D)
    m["ctx"] = _f32(np.asarray(inp["ctx"], np.float32)[b].reshape(2, 128, D))
    cc = np.stack([np.asarray(inp["c"], np.float32)[b], np.asarray(inp["c_ctx"], np.float32)], 0)
    m["cT"] = _f32(cc.reshape(2, 8, 128).transpose(2, 1, 0))
    return m


from contextlib import ExitStack

IN_SPECS = [
    ("xa", [64, 128, D], F32), ("xo", [18, 128, D], F32), ("ctx", [2, 128, D], F32), ("cT", [128, 8, 2], F32),
    ("w0in", [D, 2240], F32), ("wqb", [384, 1024], F32), ("wkvb", [256, 1024], F32), ("wout0", [D, D], F32),
    ("w1in", [D, 4608], F32), ("wout1", [D, D], F32), ("wmod", [2, D, 3072], F32), ("bmod", [2, 3072], F32),
    ("normg", [2, D], F32), ("finalg", [1, D], F32), ("qnorm", [128, 3], F32), ("kvnorm", [128, 2], F32),
    ("sink", [1, 16], F32),
    ("ident", [128, 128], BF16), ("identf", [128, 128], F32), ("T1", [128, 64, 2, 128], BF16),
    ("CS", [128, 2, 256], BF16), ("F2", [128, 2, 36], BF16), ("DC", [128, 2, 2, 256], BF16),
    ("tabK", [66, 2, 32, 128], F32), ("tabQ", [2, 32, NQ], F32), ("tab1", [2, 128, NQ], F32),
    ("maskb", [128, 4, 2, 2, 128], BF16), ("kvalid", [128, 20], F32), ("sel", [2, 2, 128], F32),
]

QBLK = [(0, 512), (512, 512), (1024, 512), (1536, 512), (2048, 512)]


class Prog:
    def __init__(self, debug=False, upto=99):
        self.debug = debug
        self.upto = upto
        nc = self.nc = bass.Bass("TRN2", target_bir_lowering=False)
        mk = self.mk = MK(nc)
        self.din = {n: mk.dram(s, dt, n, "ExternalInput") for n, s, dt in IN_SPECS}
        self.out = mk.dram([16, 128, D], F32, "out", "ExternalOutput")
        dk = "ExternalOutput" if debug else "Internal"
        self.qan_d = mk.dram([3, 128, NQ], BF16, "qan_d", dk)
        self.fg_d = mk.dram([4, 128, NQ], BF16, "fg_d", dk)
        self.mg_d = mk.dram([4, 128, NQ], BF16, "mg_d", dk)
        self.kvn_d = mk.dram([2, 128, NKEY], BF16, "kvn_d", dk)
        self.kpe_d = mk.dram([32, NKEY], BF16, "kpe_d", dk)
        self.cat_d = mk.dram([8, 128, NQ], BF16, "cat_d", dk)
        self.x1_d = mk.dram([20, 128, D], F32, "x1_d", dk)
        self.pTs = [mk.ps([128, 1024], BF16, "pT%d" % i) for i in range(2)]
        self.P = [mk.ps([128, 512], F32, "P%d" % i) for i in range(6)]
        self.pi = 0
        self.held = set()
        self.es = ExitStack()
        self.cnt = 0
        P_ = self.sbp
        self.ident = P_([128, 128], BF16); self.onesb = P_([128, 128], BF16); self.onesf = P_([128, 64], F32)
        self.epsb = P_([128, 1], F32)
        self.G0 = {k: P_([128, D], F32) for k in ("Gl", "Gc")}
        self.bcs = {}
        mk.dma("sp", self.ident[:], self.din["ident"][:, :], writes=[self.ident])
        mk.op("pool", lambda e: e.memset(self.onesb[:], 1.0), writes=[self.onesb])
        mk.op("pool", lambda e: e.memset(self.onesf[:], 1.0), writes=[self.onesf])
        mk.op("pool", lambda e: e.memset(self.epsb[:], EPS), writes=[self.epsb])

    def sbp(self, shape, dt=BF16):
        self.cnt += 1
        return Tile(self.nc.alloc_sbuf_tensor("pers%d" % self.cnt, list(shape), dt))

    def sb(self, es, shape, dt=BF16):
        self.cnt += 1
        return Tile(es.enter_context(self.nc.sbuf_tensor("s%d" % self.cnt, list(shape), dt)))

    def bank(self, hold=False):
        while True:
            self.pi = (self.pi + 1) % 6
            if self.pi not in self.held:
                break
        if hold:
            self.held.add(self.pi)
        return self.P[self.pi]

    def release(self, p):
        self.held.discard(self.P.index(p))

    def wload(self, dst, dcol, src, r0, nk, c0, n):
        v = src.t.ap()[r0:r0 + nk * 128, c0:c0 + n].rearrange("(c p) n -> p c n", p=128)
        self.mk.dma("pool", dst[:, 0:nk, dcol:dcol + n], v, writes=[dst], max_dma_last_dim=4096)

    def alloc_bc(self, l, es_ab):
        bc = self.bcs[l] = {}
        for k in ("Al", "Bl", "Ac", "Bc", "Gl", "Gc"):
            if l == 0 and k[0] == "G":
                bc[k] = self.G0[k]
            else:
                bc[k] = self.sb(es_ab, [128, D], F32)

    def mod_steps(self, l, es_ab, es):
        mk, din = self.mk, self.din
        if l not in self.bcs:
            self.alloc_bc(l, es_ab)
        bc = self.bcs[l]
        wms = [self.sb(es, [128, 8, 512], F32) for _ in range(2)]
        cT = self.sb(es, [128, 16], F32); scT = self.sb(es, [128, 16], F32)
        bm = self.sb(es, [2, 3072], F32); g2 = self.sb(es, [2, D], F32); sel = self.sb(es, [2, 2, 128], F32)
        mrow = self.sb(es, [2, 3072], F32); arow = self.sb(es, [2, D], F32)
        steps = []

        def first():
            mk.dma("sp", cT[:], din["cT"].t.ap().rearrange("p c v -> p (c v)"), writes=[cT])
            mk.op("act", lambda e: e.activation(out=scT[:], in_=cT[:], func=AF.Silu), reads=[cT], writes=[scT])
            for r in range(2):
                mk.dma("sp", bm[r:r + 1, :], din["bmod"][l:l + 1, :], writes=[bm])
                mk.dma("sp", g2[r:r + 1, :], din["normg"][l:l + 1, :], writes=[g2])
            mk.dma("sp", sel[:], din["sel"].t.ap(), writes=[sel])
        steps.append(first)
        for nb in range(6):
            def ld(nb=nb):
                wm = wms[nb % 2]
                mk.dma("sp", wm[:], din["wmod"].t.ap()[l, :, nb * 512:(nb + 1) * 512].rearrange("(c p) n -> p c n", p=128), writes=[wm])
            def blk(nb=nb):
                wm = wms[nb % 2]
                p = self.bank()
                for kc in range(8):
                    mk.op("pe", lambda e: e.matmul(out=p[0:2, :], lhsT=scT[:, 2 * kc:2 * kc + 2],
                                                   rhs=wm[:, kc, :], start=(kc == 0), stop=(kc == 7)),
                          reads=[scT, wm], writes=[p], inc=(kc == 7))
                mk.op("dve", lambda e: e.tensor_tensor(out=mrow[:, nb * 512:(nb + 1) * 512], in0=p[0:2, :],
                                                       in1=bm[:, nb * 512:(nb + 1) * 512], op=ALU.add),
                      reads=[p, bm], writes=[mrow])
            steps.append(ld)
            steps.append(blk)
        order = [steps[0], steps[1]]
        for nb in range(6):
            if nb + 1 < 6:
                order.append(steps[1 + 2 * (nb + 1)])
            order.append(steps[2 + 2 * nb])

        def arow_f():
            mk.op("dve", lambda e: e.scalar_tensor_tensor(out=arow[:], in0=mrow[:, D:2 * D], scalar=1.0, in1=g2[:],
                                                          op0=ALU.add, op1=ALU.mult), reads=[mrow, g2], writes=[arow])
        order.append(arow_f)
        for which, sfx in ((0, "l"), (1, "c")):
            for key, src, off in (("A", arow, 0), ("B", mrow, 0), ("G", mrow, 2 * D)):
                def bcf(which=which, sfx=sfx, key=key, src=src, off=off):
                    dst = bc[key + sfx]
                    for half in range(2):
                        p = self.bank()
                        mk.op("pe", lambda e: e.matmul(out=p[:], lhsT=sel[:, which, :],
                                                       rhs=src[:, off + half * 512:off + (half + 1) * 512],
                                                       start=True, stop=True), reads=[sel, src], writes=[p])
                        mk.op("dve", lambda e: e.tensor_copy(out=dst[:, half * 512:(half + 1) * 512], in_=p[:]),
                              reads=[p], writes=[dst])
                order.append(bcf)
        return order

    def make_norm_scratch(self, es, n=3, nx=5):
        self.xts = [self.sb(es, [128, D], F32) for _ in range(nx)]
        self.ns = [dict(junk=self.sb(es, [128, D], BF16), ss=self.sb(es, [128, 1], F32),
                        rstd=self.sb(es, [128, 1], F32), tmp=self.sb(es, [128, D], F32), hb=self.sb(es, [128, D], BF16))
                   for _ in range(n)]

    def norm_s0(self, i, src_ap):
        xt = self.xts[i % len(self.xts)]
        self.mk.dma("sp", xt[:], src_ap, writes=[xt])

    def norm_s1(self, i, src_ap, A, B):
        mk = self.mk
        s = self.ns[i % len(self.ns)]
        xt = self.xts[i % len(self.xts)]
        junk, ss, rstd, tmp, hb = s["junk"], s["ss"], s["rstd"], s["tmp"], s["hb"]
        mk.op("act", lambda e: e.activation(out=junk[:], in_=xt[:], func=AF.Square, scale=1.0 / 32.0, accum_out=ss[:]),
              reads=[xt], writes=[junk, ss])
        mk.op("act", lambda e: e.activation(out=rstd[:], in_=ss[:], func=AF.Ln, bias=self.epsb[:], scale=1.0),
              reads=[ss, self.epsb], writes=[rstd])
        mk.op("act", lambda e: e.activation(out=rstd[:], in_=rstd[:], func=AF.Exp, scale=-0.5), reads=[rstd], writes=[rstd])
        mk.op("dve", lambda e: e.scalar_tensor_tensor(out=tmp[:], in0=xt[:], scalar=rstd[:, 0:1], in1=A[:],
                                                      op0=ALU.mult, op1=ALU.mult), reads=[xt, rstd, A], writes=[tmp])
        mk.op("pool", lambda e: e.tensor_tensor(out=hb[:], in0=tmp[:], in1=B[:], op=ALU.add), reads=[tmp, B], writes=[hb])

    def norm_s2(self, i, dst_tile, dst_ap):
        mk = self.mk
        hb = self.ns[i % len(self.ns)]["hb"]
        pT = self.pTs[i % 2]
        for c in range(8):
            mk.op("pe", lambda e: e.transpose(out=pT[:, c * 128:(c + 1) * 128], in_=hb[:, c * 128:(c + 1) * 128],
                                              identity=self.ident[:]), reads=[hb, self.ident], writes=[pT], inc=(c == 7))
        mk.op("act", lambda e: e.activation(out=dst_ap, in_=pT[:, :].rearrange("p (c n) -> p c n", c=8), func=AF.Copy),
              reads=[pT], writes=[dst_tile])

    def norm_pipe(self, n, src_fn, ab_fn, dst_fn, s3=None):
        for i in range(min(2, n)):
            self.norm_s0(i, src_fn(i))
        for t in range(n + 2):
            if t + 2 < n:
                self.norm_s0(t + 2, src_fn(t + 2))
            if t < n:
                A, B = ab_fn(t)
                self.norm_s1(t, src_fn(t), A, B)
            if 0 <= t - 1 < n:
                dt_, dap = dst_fn(t - 1)
                self.norm_s2(t - 1, dt_, dap)
            if s3 is not None and 0 <= t - 2 < n:
                s3(t - 2)

    def proj(self, p_ap, ptile, w, wcols, h, hcols, n_c=8):
        for c in range(n_c):
            self.mk.op("pe", lambda e: e.matmul(out=p_ap, lhsT=w[:, c, wcols[0]:wcols[1]], rhs=h[:, c, hcols[0]:hcols[1]],
                                                start=(c == 0), stop=(c == n_c - 1)), reads=[w, h], writes=[ptile], inc=(c == n_c - 1))

    def rstd_bcast(self, es_tiles, pss, n, dim, out_tile):
        mk = self.mk
        mk.op("act", lambda e: e.activation(out=out_tile[:, 0:n], in_=pss[:, 0:n], func=AF.Ln, bias=self.epsb[:], scale=1.0 / dim),
              reads=[pss, self.epsb], writes=[out_tile])
        mk.op("act", lambda e: e.activation(out=out_tile[:, 0:n], in_=out_tile[:, 0:n], func=AF.Exp, scale=-0.5),
              reads=[out_tile], writes=[out_tile])

    def phase_O(self):
        mk, din = self.mk, self.din
        with ExitStack() as es:
            hT = self.sb(es, [128, 8, NQ], BF16)
            wps = [self.sb(es, [128, 8, 512], BF16) for _ in range(3)]
            self.wload(wps[0], 0, din["w0in"], 0, 8, 512, 512)
            self.wload(wps[1], 0, din["w0in"], 0, 8, 1696, 512)
            self.wload(wps[2], 0, din["w0in"], 0, 8, 1024, 384)
            with ExitStack() as esn:
                self.make_norm_scratch(esn)
                self.norm_pipe(20, lambda i: din["xo"].t.ap()[i] if i < 18 else din["ctx"].t.ap()[i - 18],
                               lambda i: (self.bc["Al" if i < 18 else "Ac"], self.bc["Bl" if i < 18 else "Bc"]),
                               lambda i: (hT, hT[:, :, i * 128:(i + 1) * 128]))
                mk.barrier()
            stage = [self.sb(es, [128, 512], BF16) for _ in range(3)]
            si = 0
            for wp, dst in ((wps[0], self.fg_d), (wps[1], self.mg_d)):
                for j in range(4):
                    for b0, n in QBLK:
                        p = self.bank()
                        self.proj(p[:, 0:n], p, wp, (j * 128, (j + 1) * 128), hT, (b0, b0 + n))
                        st = stage[si % 3]; si += 1
                        mk.op("act", lambda e: e.activation(out=st[:, 0:n], in_=p[:, 0:n], func=AF.Silu), reads=[p], writes=[st])
                        mk.dma("sp", dst.t.ap()[j, :, b0:b0 + n], st[:, 0:n], reads=[st], writes=[dst])
            wp = wps[2]
            sq = [self.sb(es, [128, 512], BF16) for _ in range(3)]
            rq = self.sb(es, [128, 512], F32)
            for b0, n in QBLK:
                ps = [self.bank() for _ in range(3)]
                pss = self.bank()
                for j in range(3):
                    self.proj(ps[j][:, 0:n], ps[j], wp, (j * 128, (j + 1) * 128), hT, (b0, b0 + n))
                    mk.op("act", lambda e: e.activation(out=sq[j][:, 0:n], in_=ps[j][:, 0:n], func=AF.Square), reads=[ps[j]], writes=[sq[j]])
                for j in range(3):
                    mk.op("pe", lambda e: e.matmul(out=pss[:, 0:n], lhsT=self.onesb[:], rhs=sq[j][:, 0:n], start=(j == 0), stop=(j == 2)),
                          reads=[self.onesb, sq[j]], writes=[pss])
                self.rstd_bcast(None, pss, n, 384.0, rq)
                for j in range(3):
                    st = stage[si % 3]; si += 1
                    mk.op("dve", lambda e: e.tensor_tensor(out=st[:, 0:n], in0=ps[j][:, 0:n], in1=rq[:, 0:n], op=ALU.mult),
                          reads=[ps[j], rq], writes=[st])
                    mk.dma("sp", self.qan_d.t.ap()[j, :, b0:b0 + n], st[:, 0:n], reads=[st], writes=[self.qan_d])
            mk.barrier()

    def phase_A(self, es_outer):
        mk, din = self.mk, self.din
        with ExitStack() as es:
            self.make_norm_scratch(es)
            wA = self.sb(es, [128, 8, 832], BF16)
            self.wload(wA, 0, din["w0in"], 0, 8, 0, 512)
            self.wload(wA, 512, din["w0in"], 0, 8, 1408, 288)
            self.wload(wA, 800, din["w0in"], 0, 8, 2208, 32)
            hTs = [self.sb(es, [128, 8, 128], BF16) for _ in range(3)]
            sqkv = [self.sb(es, [128, 256], BF16) for _ in range(2)]
            rk = [self.sb(es, [128, 128], F32) for _ in range(2)]
            kst = [self.sb(es, [128, 2, 128], BF16) for _ in range(2)]
            tabk = [self.sb(es, [128, 2, 128], F32) for _ in range(3)]
            t1 = [self.sb(es, [128, 128], F32) for _ in range(2)]
            t2 = [self.sb(es, [128, 128], F32) for _ in range(2)]
            pst = [self.sb(es, [128, 128], BF16) for _ in range(2)]

            def s3(i):
                lat = i >= 2
                hT = hTs[i % 3]
                tb = tabk[i % 3]
                mk.dma("sp", tb[64:96, :, :], din["tabK"].t.ap()[i].rearrange("t p n -> p t n"), writes=[tb])
                pkv = self.bank()
                for j in range(2):
                    self.proj(pkv[:, j * 128:(j + 1) * 128], pkv, wA, (512 + j * 128, 640 + j * 128), hT, (0, 128))
                sq = sqkv[i % 2]
                mk.op("act", lambda e: e.activation(out=sq[:], in_=pkv[:, 0:256], func=AF.Square), reads=[pkv], writes=[sq])
                p = self.bank()
                self.mk_tokmajor(p, hT, wA, 0, 512)
                udst, uap = (self.u_all, self.u_all[:, i - 2, :]) if lat else (self.u_ctx, self.u_ctx[:, i, :])
                mk.op("dve", lambda e: e.tensor_copy(out=uap, in_=p[:]), reads=[p], writes=[udst])
                pk = self.bank()
                self.proj(pk[64:96, 0:128], pk, wA, (768, 800), hT, (0, 128))
                self.proj(pk[64:96, 128:256], pk, wA, (800, 832), hT, (0, 128))
                a1, a2 = t1[i % 2], t2[i % 2]
                mk.op("dve", lambda e: e.tensor_tensor(out=a1[64:96, :], in0=pk[64:96, 0:128], in1=tb[64:96, 0, :], op=ALU.mult),
                      reads=[pk, tb], writes=[a1])
                mk.op("dve", lambda e: e.tensor_tensor(out=a2[64:96, :], in0=pk[64:96, 128:256], in1=tb[64:96, 1, :], op=ALU.mult),
                      reads=[pk, tb], writes=[a2])
                ps_ = pst[i % 2]
                mk.op("pool", lambda e: e.tensor_tensor(out=ps_[64:96, :], in0=a1[64:96, :], in1=a2[64:96, :], op=ALU.add),
                      reads=[a1, a2], writes=[ps_])
                mk.dma("sp", self.kpe_d.t.ap()[:, i * 128:(i + 1) * 128], ps_[64:96, :], reads=[ps_], writes=[self.kpe_d])
                pss = self.bank()
                for j in range(2):
                    mk.op("pe", lambda e: e.matmul(out=pss[:, 0:128], lhsT=self.onesb[:], rhs=sq[:, j * 128:(j + 1) * 128],
                                                   start=(j == 0), stop=(j == 1)), reads=[self.onesb, sq], writes=[pss], inc=(j == 1))
                r_ = rk[i % 2]
                self.rstd_bcast(None, pss, 128, 256.0, r_)
                ks = kst[i % 2]
                for j in range(2):
                    mk.op("dve", lambda e: e.tensor_tensor(out=ks[:, j, :], in0=pkv[:, j * 128:(j + 1) * 128], in1=r_[:], op=ALU.mult),
                          reads=[pkv, r_], writes=[ks])
                mk.dma("sp", self.kvn_d.t.ap()[:, :, i * 128:(i + 1) * 128].rearrange("j p n -> p j n"), ks[:], reads=[ks], writes=[self.kvn_d])

            self.norm_pipe(66, lambda i: din["xa"].t.ap()[i - 2] if i >= 2 else din["ctx"].t.ap()[i],
                           lambda i: (self.bc["Al" if i >= 2 else "Ac"], self.bc["Bl" if i >= 2 else "Bc"]),
                           lambda i: (hTs[i % 3], hTs[i % 3][:, :, :]), s3=s3)
            mk.barrier()

    def mk_tokmajor(self, p, hT, w, c0, n):
        for c in range(8):
            self.mk.op("pe", lambda e: e.matmul(out=p[:, 0:n], lhsT=hT[:, c, :], rhs=w[:, c, c0:c0 + n], start=(c == 0), stop=(c == 7)),
                       reads=[hT, w], writes=[p], inc=(c == 7))

    def phase_F(self):
        mk, din = self.mk, self.din
        with ExitStack() as es:
            T1 = self.sb(es, [128, 64, 256], BF16)
            for k in range(4):
                mk.dma("sp", T1[:, k * 16:(k + 1) * 16, :], din["T1"].t.ap()[:, k * 16:(k + 1) * 16].rearrange("p a r k -> p a (r k)"), writes=[T1])
            CS = self.sb(es, [128, 2, 256], BF16); F2 = self.sb(es, [128, 2, 36], BF16); DC = self.sb(es, [128, 2, 512], BF16)
            mk.dma("sp", CS[:], din["CS"].t.ap(), writes=[CS])
            mk.dma("sp", F2[:], din["F2"].t.ap(), writes=[F2])
            mk.dma("sp", DC[:], din["DC"].t.ap().rearrange("p t r k -> p t (r k)"), writes=[DC])
            fg = self.sb(es, [128, 4, NQ], BF16)
            mk.dma("sp", fg[:], self.fg_d.t.ap().rearrange("j p n -> p j n"), reads=[self.fg_d], writes=[fg])
            A1 = self.sb(es, [128, 128, 128], BF16)
            A1w = A1[:, :, :].rearrange("p (a n) (r k) -> p n a r k", a=2, r=2)
            Gb = [self.sb(es, [128, 2, 256], BF16) for _ in range(3)]
            catF = [self.sb(es, [128, NOWN], BF16) for _ in range(2)]
            XT = self.sb(es, [128, 512], BF16); cst = self.sb(es, [128, 256], BF16)
            for g in range(4):
                for pr in range(32):
                    p = self.bank()
                    for s in range(2):
                        n2 = 2 * pr + s
                        mk.op("pe", lambda e: e.matmul(out=p[:, s * 256:(s + 1) * 256], lhsT=self.u_all[:, n2, g * 128:(g + 1) * 128],
                                                       rhs=T1[:, n2, :], start=True, stop=True), reads=[self.u_all, T1], writes=[p], inc=(s == 1))
                    eng = "dve" if pr % 2 == 0 else "act"
                    for s_ in range(2):
                        oap = A1w[:, 2 * pr + s_]
                        iap = p[:, s_ * 256:(s_ + 1) * 256].rearrange("p (r k a) -> p a r k", r=2, a=2)
                        if s_ == 0:
                            mk.op("dve", lambda e: e.tensor_copy(out=oap, in_=iap), reads=[p], writes=[A1])
                        else:
                            mk.op("act", lambda e: e.activation(out=oap, in_=iap, func=AF.Copy), reads=[p], writes=[A1])
                cf = catF[g % 2]
                cfv = cf[:, :].rearrange("p (k2 k1) -> p k1 k2", k1=128)
                fgv = fg[:, g, 0:NOWN].rearrange("p (k2 k1) -> p k1 k2", k1=128)
                fst = {"pacc": None, "a0": 0}

                def s3(pr, gb, fst=fst, cf=cf, cfv=cfv, fgv=fgv):
                    for s_ in range(2):
                        kp = 2 * pr + s_
                        k1 = 2 * kp
                        if k1 % 28 == 0:
                            fst["pacc"] = self.bank(hold=True); fst["a0"] = k1
                        pacc, a0 = fst["pacc"], fst["a0"]
                        sl = (k1 - a0) * 18
                        mk.op("pe", lambda e: e.matmul(out=pacc[:, sl:sl + 36], lhsT=gb[:, s_, 0:128], rhs=F2[:, 0, :], start=True, stop=False),
                              reads=[gb, F2], writes=[pacc], inc=False)
                        mk.op("pe", lambda e: e.matmul(out=pacc[:, sl:sl + 36], lhsT=gb[:, s_, 128:256], rhs=F2[:, 1, :], start=False, stop=True),
                              reads=[gb, F2], writes=[pacc])
                        if (k1 + 1) % 28 == 27 or k1 + 1 == 127:
                            cnt = k1 + 2 - a0
                            mk.op("dve", lambda e: e.tensor_tensor(out=cfv[:, a0:a0 + cnt, :], in0=pacc[:, 0:cnt * 18].rearrange("p (a b) -> p a b", b=18),
                                                                   in1=fgv[:, a0:a0 + cnt, :], op=ALU.mult), reads=[pacc, fg], writes=[cf])
                            self.release(pacc)

                prev = None
                for pr in range(32):
                    p = self.bank()
                    for s_ in range(2):
                        kp = 2 * pr + s_
                        mk.op("pe", lambda e: e.matmul(out=p[:, s_ * 256:(s_ + 1) * 256], lhsT=A1[:, :, kp], rhs=CS[:, 0, :], start=True, stop=False),
                              reads=[A1, CS], writes=[p], inc=False)
                        mk.op("pe", lambda e: e.matmul(out=p[:, s_ * 256:(s_ + 1) * 256], lhsT=A1[:, :, 64 + kp], rhs=CS[:, 1, :], start=False, stop=True),
                              reads=[A1, CS], writes=[p], inc=(s_ == 1))
                    gb = Gb[pr % 3]
                    if pr % 2 == 0:
                        mk.op("dve", lambda e: e.tensor_copy(out=gb[:, :, :], in_=p[:, :].rearrange("p (a b) -> p a b", a=2)), reads=[p], writes=[gb])
                    else:
                        mk.op("act", lambda e: e.activation(out=gb[:, :, :], in_=p[:, :].rearrange("p (a b) -> p a b", a=2), func=AF.Copy), reads=[p], writes=[gb])
                    if prev is not None:
                        s3(*prev)
                    prev = (pr, gb)
                s3(*prev)
                mk.dma("sp", self.cat_d.t.ap()[g, :, 0:NOWN], cf[:, :], reads=[cf], writes=[self.cat_d])
                px = self.bank()
                for nt in range(2):
                    mk.op("pe", lambda e: e.matmul(out=px[:, :], lhsT=self.u_ctx[:, nt, g * 128:(g + 1) * 128], rhs=DC[:, nt, :], start=(nt == 0), stop=(nt == 1)),
                          reads=[self.u_ctx, DC], writes=[px], inc=(nt == 1))
                mk.op("dve", lambda e: e.tensor_copy(out=XT[:], in_=px[:]), reads=[px], writes=[XT])
                py = self.bank()
                mk.op("pe", lambda e: e.matmul(out=py[:, 0:256], lhsT=CS[:, 0, 0:128], rhs=XT[:, 0:256], start=True, stop=False), reads=[CS, XT], writes=[py], inc=False)
                mk.op("pe", lambda e: e.matmul(out=py[:, 0:256], lhsT=CS[:, 1, 0:128], rhs=XT[:, 256:512], start=False, stop=True), reads=[CS, XT], writes=[py])
                mk.op("dve", lambda e: e.tensor_tensor(out=cst[:], in0=py[:, 0:256], in1=fg[:, g, NOWN:NQ], op=ALU.mult), reads=[py, fg], writes=[cst])
                mk.dma("sp", self.cat_d.t.ap()[g, :, NOWN:NQ], cst[:], reads=[cst], writes=[self.cat_d])
            mk.barrier()

    def wscaled(self, es, src, nk, ncols, normname):
        mk = self.mk
        w = self.sb(es, [128, nk, ncols], BF16)
        g = self.sb(es, [128, nk], F32)
        self.wload(w, 0, src, 0, nk, 0, ncols)
        mk.dma("sp", g[:], self.din[normname].t.ap(), writes=[g])
        for c in range(nk):
            mk.op("dve", lambda e: e.tensor_scalar(out=w[:, c, :], in0=w[:, c, :], scalar1=g[:, c:c + 1], scalar2=None, op0=ALU.mult),
                  reads=[w, g], writes=[w])
        return w

    def phase_ATT(self):
        mk, din = self.mk, self.din
        with ExitStack() as es:
            kvn = self.sb(es, [128, 2, NKEY], BF16)
            KTs = [self.sb(es, [128, NKEY], BF16) for _ in range(2)]
            VAs = [self.sb(es, [128, 66, 128], BF16) for _ in range(2)]
            qan = self.sb(es, [128, 3, NQ], BF16)
            QTs = [self.sb(es, [128, NQ], BF16) for _ in range(2)]
            tabq = self.sb(es, [128, 2, NQ], F32)
            mgs = [self.sb(es, [128, NQ], BF16) for _ in range(2)]
            ATs = [self.sb(es, [128, NQ], BF16)] * 2
            for j in range(2):
                mk.dma("sp", kvn[:, j, :], self.kvn_d.t.ap()[j], reads=[self.kvn_d], writes=[kvn])
            for KT in KTs:
                mk.dma("sp", KT[64:96, :], self.kpe_d.t.ap(), reads=[self.kpe_d], writes=[KT])
            mk.dma("sp", qan[:], self.qan_d.t.ap().rearrange("j p n -> p j n"), reads=[self.qan_d], writes=[qan])
            mk.dma("sp", tabq[64:96, :, :], din["tabQ"].t.ap().rearrange("t p n -> p t n"), writes=[tabq])
            wq = self.wscaled(es, din["wqb"], 3, 1024, "qnorm")
            wkv = self.wscaled(es, din["wkvb"], 2, 1024, "kvnorm")
            mk.op("dve", lambda e: e.memset(VAs[0][:, :, 64:128], 0.0), writes=[VAs[0]])
            mk.op("dve", lambda e: e.memset(VAs[0][:, :, 64:65], 1.0), writes=[VAs[0]])
            mk.op("dve", lambda e: e.memset(VAs[1][:, :, 0:64], 0.0), writes=[VAs[1]])
            mk.op("dve", lambda e: e.memset(VAs[1][:, :, 0:1], 1.0), writes=[VAs[1]])
            PTs = [self.sb(es, [128, 512], BF16) for _ in range(4)]
            t1 = self.sb(es, [128, 512], F32); t2 = self.sb(es, [128, 512], F32)
            rec = self.sb(es, [128, 512], F32); tmp = self.sb(es, [128, 512], F32)

            def gen_steps(h):
                odd = h % 2
                o0 = 64 if odd else 0
                KT, VA, QT = KTs[odd], VAs[odd], QTs[odd]
                steps = []
                if not odd:
                    mgt = mgs[(h // 2) % 2]
                    steps.append(lambda: mk.dma("sp", mgt[:], self.mg_d.t.ap()[h // 2], reads=[self.mg_d], writes=[mgt]))
                for kb in range(17):
                    def f(kb=kb):
                        k0 = kb * 512
                        n = min(512, NKEY - k0)
                        p = self.bank()
                        self.proj(p[0:64, 0:n], p, wkv, (h * 128, h * 128 + 64), kvn, (k0, k0 + n), n_c=2)
                        mk.op("dve", lambda e: e.tensor_copy(out=KT[0:64, k0:k0 + n], in_=p[0:64, 0:n]), reads=[p], writes=[KT])
                    steps.append(f)
                for g0 in range(0, 66, 8):
                    def f(g0=g0):
                        cnt = min(8, 66 - g0)
                        p = self.bank()
                        for t in range(cnt):
                            kt = g0 + t
                            for j in range(2):
                                mk.op("pe", lambda e: e.matmul(out=p[:, t * 64:(t + 1) * 64], lhsT=kvn[:, j, kt * 128:(kt + 1) * 128],
                                                               rhs=wkv[:, j, h * 128 + 64:h * 128 + 128], start=(j == 0), stop=(j == 1)),
                                      reads=[kvn, wkv], writes=[p], inc=(j == 1 and t == cnt - 1))
                        mk.op("dve", lambda e: e.tensor_copy(out=VA[:, g0:g0 + cnt, o0:o0 + 64], in_=p[:, 0:cnt * 64].rearrange("p (a b) -> p a b", b=64)),
                              reads=[p], writes=[VA])
                    steps.append(f)
                for b0, n in QBLK:
                    def f(b0=b0, n=n):
                        pa = self.bank(); pb = self.bank()
                        self.proj(pa[0:96, 0:n], pa, wq, (h * 96, h * 96 + 96), qan, (b0, b0 + n), n_c=3)
                        self.proj(pb[64:96, 0:n], pb, wq, (768 + h * 32, 800 + h * 32), qan, (b0, b0 + n), n_c=3)
                        mk.op("dve", lambda e: e.tensor_copy(out=QT[0:64, b0:b0 + n], in_=pa[0:64, 0:n]), reads=[pa], writes=[QT])
                        mk.op("dve", lambda e: e.tensor_tensor(out=t1[64:96, 0:n], in0=pa[64:96, 0:n], in1=tabq[64:96, 0, b0:b0 + n], op=ALU.mult),
                              reads=[pa, tabq], writes=[t1])
                        mk.op("dve", lambda e: e.tensor_tensor(out=t2[64:96, 0:n], in0=pb[64:96, 0:n], in1=tabq[64:96, 1, b0:b0 + n], op=ALU.mult),
                              reads=[pb, tabq], writes=[t2])
                        mk.op("pool", lambda e: e.tensor_tensor(out=QT[64:96, b0:b0 + n], in0=t1[64:96, 0:n], in1=t2[64:96, 0:n], op=ALU.add),
                              reads=[t1, t2], writes=[QT])
                    steps.append(f)
                return steps

            for f in gen_steps(0):
                f()
            items = []
            for h in range(8):
                for bi, (b0, n, kts) in enumerate([(0, 512, 66), (512, 512, 66), (1024, 512, 66), (1536, 512, 66), (2048, 256, 66), (2304, 256, 2)]):
                    for kt in range(kts):
                        items.append((h, b0, n, kt, kts))
            pending = []
            deferred = []
            recs = [rec, self.sb(es, [128, 512], F32)]
            nfin = 0
            state = {}
            LAG = 2
            nit = len(items)
            per_head = nit // 8
            for t in range(nit + LAG):
                if t < nit:
                    h, b0, n, kt, kts = items[t]
                    odd = h % 2
                    if kt == 0:
                        state[(h, b0)] = self.bank(hold=True)
                    if t % per_head == 0 and h < 7:
                        pending = gen_steps(h + 1)
                        every = max(1, (per_head - 40) // len(pending))
                    if pending and (t % per_head) % every == 0:
                        pending.pop(0)()
                    ps = self.bank()
                    mk.op("pe", lambda e: e.matmul(out=ps[:, 0:n], lhsT=KTs[odd][0:96, kt * 128:(kt + 1) * 128], rhs=QTs[odd][0:96, b0:b0 + n], start=True, stop=True),
                          reads=[KTs[odd], QTs[odd]], writes=[ps])
                    pt = PTs[t % 4]
                    mk.op("act", lambda e: e.activation(out=pt[:, 0:n], in_=ps[:, 0:n], func=AF.Exp, scale=MLA_SCALE), reads=[ps], writes=[pt])
                while deferred:
                    deferred.pop(0)()
                if t >= LAG:
                    h, b0, n, kt, kts = items[t - LAG]
                    odd = h % 2
                    o0, r0 = (64, 0) if odd else (0, 64)
                    po = state[(h, b0)]
                    pt = PTs[(t - LAG) % 4]
                    mk.op("pe", lambda e: e.matmul(out=po[:, 0:n], lhsT=VAs[odd][:, kt, :], rhs=pt[:, 0:n], start=(kt == 0), stop=(kt == kts - 1)),
                          reads=[VAs[odd], pt], writes=[po], inc=(kt == kts - 1))
                    if kt == kts - 1:
                        mgt = mgs[(h // 2) % 2]; AT = ATs[(h // 2) % 2]
                        fa = dict(po=po, n=n, o0=o0, r0=r0, gate_tile=mgt, gate_ap=mgt[o0:o0 + 64, b0:b0 + n], dst_tile=AT,
                                  dst_ap=AT[o0:o0 + 64, b0:b0 + n], rec=recs[nfin % 2], tmp=tmp)
                        nfin += 1
                        self.att_finish(part="A", **fa)

                        def fb(fa=fa, po=po, odd=odd, b0=b0, h=h, AT=AT):
                            self.att_finish(part="B", **fa)
                            self.release(po)
                            if odd and b0 == 2304:
                                mk.dma("sp", self.cat_d.t.ap()[4 + h // 2], AT[:], reads=[AT], writes=[self.cat_d])
                        deferred.append(fb)
            while deferred:
                deferred.pop(0)()
            while pending:
                pending.pop(0)()
            mk.barrier()

    def att_finish(self, po, n, o0, r0, gate_tile, gate_ap, dst_tile, dst_ap, rec, tmp, extra=None, part=None):
        mk = self.mk
        if part in (None, "A"):
            self._att_finish_a(po, n, r0, rec, extra)
        if part in (None, "B"):
            self._att_finish_b(po, n, o0, r0, gate_tile, gate_ap, dst_tile, dst_ap, rec, tmp)

    def _att_finish_a(self, po, n, r0, rec, extra):
        mk = self.mk
        if extra is not None:
            mk.op("dve", lambda e: e.tensor_tensor(out=rec[r0:r0 + 1, 0:n], in0=po[r0:r0 + 1, 0:n], in1=extra, op=ALU.add), reads=[po], writes=[rec])
            mk.op("act", lambda e: e.activation(out=rec[r0:r0 + 1, 0:n], in_=rec[r0:r0 + 1, 0:n], func=AF.Ln), reads=[rec], writes=[rec])
        else:
            mk.op("act", lambda e: e.activation(out=rec[r0:r0 + 1, 0:n], in_=po[r0:r0 + 1, 0:n], func=AF.Ln), reads=[po], writes=[rec])
        mk.op("act", lambda e: e.activation(out=rec[r0:r0 + 1, 0:n], in_=rec[r0:r0 + 1, 0:n], func=AF.Exp, scale=-1.0), reads=[rec], writes=[rec])

    def _att_finish_b(self, po, n, o0, r0, gate_tile, gate_ap, dst_tile, dst_ap, rec, tmp):
        mk = self.mk
        pb = self.bank()
        mk.op("pe", lambda e: e.matmul(out=pb[o0:o0 + 64, 0:n], lhsT=self.onesf[r0:r0 + 1, 0:64], rhs=rec[r0:r0 + 1, 0:n], start=True, stop=True),
              reads=[self.onesf, rec], writes=[pb])
        mk.op("dve", lambda e: e.tensor_tensor(out=tmp[o0:o0 + 64, 0:n], in0=po[o0:o0 + 64, 0:n], in1=gate_ap, op=ALU.mult),
              reads=[po, gate_tile], writes=[tmp])
        mk.op("dve", lambda e: e.tensor_tensor(out=dst_ap, in0=tmp[o0:o0 + 64, 0:n], in1=pb[o0:o0 + 64, 0:n], op=ALU.mult),
              reads=[tmp, pb], writes=[dst_tile])

    def phase_OUT(self, wname, xsrc, ntiles, tile0, final=False, G=None, side=None):
        mk, din = self.mk, self.din
        with ExitStack() as es:
            wo = self.sb(es, [128, 8, D], BF16)
            self.wload(wo, 0, din[wname], 0, 8, 0, 512)
            self.wload(wo, 512, din[wname], 0, 8, 512, 512)
            catb = [self.sb(es, [128, 8, 512], BF16) for _ in range(3)]
            xts = [self.sb(es, [128, D], F32) for _ in range(4)]
            tmp = [self.sb(es, [128, D], F32) for _ in range(2)]
            xn = [self.sb(es, [128, D], F32) for _ in range(2)]
            junk = self.sb(es, [128, D], BF16); ss = self.sb(es, [128, 1], F32); rstd = self.sb(es, [128, 1], F32)
            fgb = None
            if final:
                fgb = self.sb(es, [128, D], F32)
                fr = self.sb(es, [1, D], F32)
                mk.dma("sp", fr[:], din["finalg"].t.ap(), writes=[fr])
                self.bcast_row(fr, fgb)
            side_steps = side(es) if side is not None else []
            loaded = set()

            def load(ti):
                gi = tile0 + ti
                blk = gi // 4
                if blk not in loaded:
                    loaded.add(blk)
                    cb = catb[blk % 3]
                    mk.dma("sp", cb[:], self.cat_d.t.ap()[:, :, blk * 512:(blk + 1) * 512].rearrange("c p n -> p c n"), reads=[self.cat_d], writes=[cb])
                xt = xts[ti % 4]
                mk.dma("sp", xt[:], xsrc(gi), reads=[self.x1_d], writes=[xt])

            for ti in range(min(2, ntiles)):
                load(ti)
            for ti in range(ntiles):
                if ti + 2 < ntiles:
                    load(ti + 2)
                if side_steps:
                    side_steps.pop(0)()
                gi = tile0 + ti
                blk, t = divmod(gi, 4)
                cb = catb[blk % 3]
                xt = xts[ti % 4]; tm = tmp[ti % 2]; xo = xn[ti % 2]
                Gt = G["Gl" if gi < 18 else "Gc"]
                for half in range(2):
                    p = self.bank()
                    for c in range(8):
                        mk.op("pe", lambda e: e.matmul(out=p[:], lhsT=cb[:, c, t * 128:(t + 1) * 128], rhs=wo[:, c, half * 512:(half + 1) * 512],
                                                       start=(c == 0), stop=(c == 7)), reads=[cb, wo], writes=[p], inc=(c == 7))
                    mk.op("dve", lambda e: e.tensor_tensor(out=tm[:, half * 512:(half + 1) * 512], in0=p[:], in1=Gt[:, half * 512:(half + 1) * 512], op=ALU.mult),
                          reads=[p, Gt], writes=[tm])
                mk.op("pool", lambda e: e.tensor_tensor(out=xo[:], in0=tm[:], in1=xt[:], op=ALU.add), reads=[tm, xt], writes=[xo])
                if not final:
                    mk.dma("sp", self.x1_d.t.ap()[gi], xo[:], reads=[xo], writes=[self.x1_d])
                else:
                    mk.op("act", lambda e: e.activation(out=junk[:], in_=xo[:], func=AF.Square, scale=1.0 / 32.0, accum_out=ss[:]), reads=[xo], writes=[junk, ss])
                    mk.op("act", lambda e: e.activation(out=rstd[:], in_=ss[:], func=AF.Ln, bias=self.epsb[:], scale=1.0), reads=[ss, self.epsb], writes=[rstd])
                    mk.op("act", lambda e: e.activation(out=rstd[:], in_=rstd[:], func=AF.Exp, scale=-0.5), reads=[rstd], writes=[rstd])
                    mk.op("dve", lambda e: e.scalar_tensor_tensor(out=tm[:], in0=xo[:], scalar=rstd[:, 0:1], in1=fgb[:], op0=ALU.mult, op1=ALU.mult),
                          reads=[xo, rstd, fgb], writes=[tm])
                    mk.dma("sp", self.out.t.ap()[ti], tm[:], reads=[tm], writes=[self.out])
            while side_steps:
                side_steps.pop(0)()
            mk.barrier()

    def bcast_row(self, row, dst):
        mk = self.mk
        for half in range(2):
            p = self.bank()
            mk.op("pe", lambda e: e.matmul(out=p[0:64, :], lhsT=self.onesf[0:1, 0:64], rhs=row[0:1, half * 512:(half + 1) * 512], start=True, stop=True),
                  reads=[self.onesf, row], writes=[p])
            mk.op("pe", lambda e: e.matmul(out=p[64:128, :], lhsT=self.onesf[0:1, 0:64], rhs=row[0:1, half * 512:(half + 1) * 512], start=True, stop=True),
                  reads=[self.onesf, row], writes=[p])
            mk.op("dve", lambda e: e.tensor_copy(out=dst[:, half * 512:(half + 1) * 512], in_=p[:]), reads=[p], writes=[dst])

    def layer0(self):
        with ExitStack() as es:
            self.u_all = self.sb(es, [128, 64, 512], BF16)
            self.u_ctx = self.sb(es, [128, 2, 512], BF16)
            with ExitStack() as es_ab:
                with ExitStack() as es_tmp:
                    for f in self.mod_steps(0, es_ab, es_tmp):
                        f()
                    self.mk.barrier()
                self.bc = self.bcs[0]
                self.phase_O()
                self.phase_A(es)
            self.phase_F()
        self.mk.barrier()
        self.phase_ATT()
        din = self.din
        self.es_ab1 = ExitStack()
        self.alloc_bc(1, self.es_ab1)
        self.phase_OUT("wout0", lambda gi: din["xo"].t.ap()[gi] if gi < 18 else din["ctx"].t.ap()[gi - 18], 20, 0,
                       G=self.bcs[0], side=lambda es_tmp: self.mod_steps(1, self.es_ab1, es_tmp))

    def phase_L1(self):
        mk, din = self.mk, self.din
        g1_d = self.g1_d
        with ExitStack() as es:
            QT1 = self.sb(es, [128, 8, NOWN], BF16)
            KT1 = self.sb(es, [128, 4, NQ], BF16)
            VA1 = self.sb(es, [128, 20, 4, 2, 128], BF16)
            mk.op("pool", lambda e: e.memset(VA1[:].rearrange("p a b c d -> p (a b c d)"), 0.0), writes=[VA1])
            mk.op("pool", lambda e: e.memset(VA1[:, :, :, 0, 64:65], 1.0), writes=[VA1])
            mk.op("pool", lambda e: e.memset(VA1[:, :, :, 1, 0:1], 1.0), writes=[VA1])
            self.bc = self.bcs[1]
            with ExitStack() as es2:
                hT = self.sb(es2, [128, 8, NQ], BF16)
                with ExitStack() as es3:
                    self.make_norm_scratch(es3, n=2, nx=3)
                    self.norm_pipe(20, lambda i: self.x1_d.t.ap()[i],
                                   lambda i: (self.bc["Al" if i < 18 else "Ac"], self.bc["Bl" if i < 18 else "Bc"]),
                                   lambda i: (hT, hT[:, :, i * 128:(i + 1) * 128]))
                    mk.barrier()
                tab1 = self.sb(es2, [128, 2, NQ], F32)
                mk.dma("sp", tab1[:], din["tab1"].t.ap().rearrange("t p n -> p t n"), writes=[tab1])
                wps = [self.sb(es2, [128, 8, 256], BF16) for _ in range(2)]
                g0l, g0c = self.G0["Gl"], self.G0["Gc"]
                st = [self.sb(es2, [128, 512], BF16) for _ in range(2)]
                jobs = [("q", hp, [(0, hp * 128, 128), (128, 2560 + hp * 128, 128)]) for hp in range(8)]
                jobs += [("k", kh, [(0, 3584 + kh * 128, 128), (128, 4096 + kh * 128, 128)]) for kh in range(4)]
                jobs += [("g", hp, [(0, 1536 + hp * 128, 128)]) for hp in range(8)]
                jobs += [("v", 0, [(0, 1280, 256)])]

                def jload(ji):
                    for dcol, c0, n in jobs[ji][2]:
                        self.wload(wps[ji % 2], dcol, din["w1in"], 0, 8, c0, n)

                jload(0)
                si = 0
                ri = 0
                for ji, (kind, idx, _) in enumerate(jobs):
                    wp = wps[ji % 2]
                    first = True
                    if kind in ("q", "k"):
                        for b0, n in QBLK:
                            if kind == "q" and b0 >= NOWN:
                                continue
                            n_ = min(n, NOWN - b0) if kind == "q" else n
                            t1 = g0l[:, (ri % 2) * 512:(ri % 2) * 512 + n_]; t2 = g0c[:, (ri % 2) * 512:(ri % 2) * 512 + n_]
                            ri += 1
                            pa = self.bank(); pb = self.bank()
                            self.proj(pa[:, 0:n_], pa, wp, (0, 128), hT, (b0, b0 + n_))
                            self.proj(pb[:, 0:n_], pb, wp, (128, 256), hT, (b0, b0 + n_))
                            if first and ji + 1 < len(jobs):
                                jload(ji + 1); first = False
                            mk.op("dve", lambda e: e.tensor_tensor(out=t1, in0=pa[:, 0:n_], in1=tab1[:, 0, b0:b0 + n_], op=ALU.mult), reads=[pa, tab1], writes=[g0l])
                            mk.op("dve", lambda e: e.tensor_tensor(out=t2, in0=pb[:, 0:n_], in1=tab1[:, 1, b0:b0 + n_], op=ALU.mult), reads=[pb, tab1], writes=[g0c])
                            dst, dap = (QT1, QT1[:, idx, b0:b0 + n_]) if kind == "q" else (KT1, KT1[:, idx, b0:b0 + n_])
                            mk.op("pool", lambda e: e.tensor_tensor(out=dap, in0=t1, in1=t2, op=ALU.add), reads=[g0l, g0c], writes=[dst])
                    elif kind == "g":
                        for b0, n in QBLK:
                            p = self.bank()
                            self.proj(p[:, 0:n], p, wp, (0, 128), hT, (b0, b0 + n))
                            if first and ji + 1 < len(jobs):
                                jload(ji + 1); first = False
                            s_ = st[si % 2]; si += 1
                            mk.op("act", lambda e: e.activation(out=s_[:, 0:n], in_=p[:, 0:n], func=AF.Silu), reads=[p], writes=[s_])
                            mk.dma("sp", g1_d.t.ap()[idx, :, b0:b0 + n], s_[:, 0:n], reads=[s_], writes=[g1_d])
                    else:
                        for i in range(20):
                            p = self.bank()
                            for c in range(8):
                                mk.op("pe", lambda e: e.matmul(out=p[:, 0:256], lhsT=hT[:, c, i * 128:(i + 1) * 128], rhs=wp[:, c, 0:256], start=(c == 0), stop=(c == 7)),
                                      reads=[hT, wp], writes=[p], inc=(c == 7))
                            pv = p[:, 0:256].rearrange("p (a b) -> p a b", b=64)
                            mk.op("dve", lambda e: e.tensor_copy(out=VA1[:, i, :, 0, 0:64], in_=pv), reads=[p], writes=[VA1])
                            mk.op("act", lambda e: e.activation(out=VA1[:, i, :, 1, 64:128], in_=pv, func=AF.Copy), reads=[p], writes=[VA1])
                mk.barrier()
            maskb = self.sb(es, [128, 4, 512], BF16); kval = self.sb(es, [128, 20], F32)
            mk.dma("sp", maskb[:], din["maskb"].t.ap().rearrange("p d s j q -> p d (s j q)"), writes=[maskb])
            mk.dma("sp", kval[:], din["kvalid"].t.ap(), writes=[kval])
            snk = self.sb(es, [128, 16], F32); esk = self.sb(es, [128, 16], F32)
            for r in (0, 64):
                mk.dma("sp", snk[r:r + 1, :], din["sink"].t.ap(), writes=[snk])
            mk.op("act", lambda e: e.activation(out=esk[0:1, :], in_=snk[0:1, :], func=AF.Exp), reads=[snk], writes=[esk])
            mk.op("act", lambda e: e.activation(out=esk[64:65, :], in_=snk[64:65, :], func=AF.Exp), reads=[snk], writes=[esk])
            esr = self.sb(es, [128, 8, 512], F32)
            mk.op("pool", lambda e: e.memset(esr[:].rearrange("p a b -> p (a b)"), 0.0), writes=[esr])
            for kh in range(4):
                for par in range(2):
                    for k_ in range(2):
                        hq = 4 * kh + 2 * k_ + par
                        for r in (0, 64):
                            sl = esr[r:r + 1, kh * 2 + par, k_ * 256:(k_ + 1) * 256]
                            mk.op("dve", lambda e: e.tensor_scalar(out=sl, in0=sl, scalar1=esk[r:r + 1, hq:hq + 1], scalar2=None, op0=ALU.add),
                                  reads=[esk, esr], writes=[esr])
            gt = [self.sb(es, [128, 2, NQ], BF16) for _ in range(2)]
            AT = [self.sb(es, [128, 2, 2048], BF16) for _ in range(2)]
            PTs = [self.sb(es, [128, 512], BF16) for _ in range(18)]
            rec = self.sb(es, [128, 512], F32); tmp = self.sb(es, [128, 512], F32)
            items = [(kh, par, nb) for kh in range(4) for par in range(2) for nb in range(0, 16, 2)]
            st = {}
            ni = len(items)

            def fin_args(u):
                kh, par, nb = items[u]
                g_ = gt[kh % 2]; at = AT[kh % 2]
                o0, r0 = (64, 0) if par else (0, 64)
                q0 = 128 + 128 * nb
                return dict(po=st[u], n=512, o0=o0, r0=r0, gate_tile=g_, gate_ap=g_[o0:o0 + 64, :, q0:q0 + 256], dst_tile=at,
                            dst_ap=at[o0:o0 + 64, :, nb * 128:(nb + 2) * 128], rec=rec, tmp=tmp, extra=esr[r0:r0 + 1, kh * 2 + par, :])

            for t in range(ni + 3):
                if 0 <= t - 3 < ni:
                    self.att_finish(part="A", **fin_args(t - 3))
                if t < ni:
                    kh, par, nb = items[t]
                    if par == 0 and nb == 0:
                        g_ = gt[kh % 2]
                        mk.dma("sp", g_[:], g1_d.t.ap()[2 * kh:2 * kh + 2].rearrange("c p n -> p c n"), reads=[g1_d], writes=[g_])
                    p0 = par * 64
                    q0 = 128 + 128 * nb
                    kts = [(18, None), (19, None), (nb, 0), (nb + 1, 1), (nb + 2, 2), (nb + 3, 3)]
                    for ki, (kt, mi) in enumerate(kts):
                        ps = self.bank()
                        mk.op("pe", lambda e: e.matmul(out=ps[:, :], lhsT=KT1[p0:p0 + 64, kh, kt * 128:(kt + 1) * 128],
                                                       rhs=QT1[p0:p0 + 64, 2 * kh:2 * kh + 2, q0:q0 + 256], start=True, stop=(mi is None)),
                              reads=[KT1, QT1], writes=[ps], inc=(mi is None))
                        if mi is not None:
                            mk.op("pe", lambda e: e.matmul(out=ps[:, :], lhsT=self.ident[:], rhs=maskb[:, mi, :], start=False, stop=True),
                                  reads=[self.ident, maskb], writes=[ps])
                        pt = PTs[(t % 3) * 6 + ki]
                        mk.op("act", lambda e: e.activation(out=pt[:], in_=ps[:, :], func=AF.Exp, scale=GQA_SCALE, bias=kval[:, kt:kt + 1]),
                              reads=[ps, kval], writes=[pt])
                if 0 <= t - 3 < ni:
                    u = t - 3
                    self.att_finish(part="B", **fin_args(u))
                    self.release(st.pop(u))
                    kh, par, nb = items[u]
                    if par == 1 and nb == 14:
                        at = AT[kh % 2]
                        mk.dma("sp", self.cat_d.t.ap()[2 * kh:2 * kh + 2, :, 128:2176].rearrange("c p n -> p c n"), at[:], reads=[at], writes=[self.cat_d])
                if 0 <= t - 1 < ni:
                    u = t - 1
                    kh, par, nb = items[u]
                    po = st[u] = self.bank(hold=True)
                    kts = [18, 19, nb, nb + 1, nb + 2, nb + 3]
                    for ki, kt in enumerate(kts):
                        pt = PTs[(u % 3) * 6 + ki]
                        mk.op("pe", lambda e: e.matmul(out=po[:, :], lhsT=VA1[:, kt, kh, par, :], rhs=pt[:], start=(ki == 0), stop=(ki == 5)),
                              reads=[VA1, pt], writes=[po], inc=(ki == 5))
            mk.barrier()

    def layer1(self):
        self.phase_L1()
        x1 = self.x1_d
        self.phase_OUT("wout1", lambda gi: x1.t.ap()[gi], 16, 1, final=True, G=self.bcs[1])
        self.es_ab1.close()


def build_program(debug=False, upto=99):
    pr = Prog(debug=debug, upto=upto)
    pr.g1_d = pr.mk.dram([8, 128, NQ], BF16, "g1_d", "Internal")
    pr.layer0()
    if upto >= 1:
        pr.layer1()
    pr.mk.finish()
    return pr


def kernel(**inputs):
    W = pack_weights(inputs)
    in_maps = []
    for core in range(8):
        b, q = divmod(core, 4)
        m = core_inputs(inputs, W, b, q)
        in_maps.append({n: m[n] for n, _, _ in IN_SPECS})
    pr = build_program()
    res = run_bass_kernel_spmd(pr.nc, in_maps, core_ids=list(range(8)))
    out = np.zeros((NB, SEQ, D), np.float32)
    for core in range(8):
        b, q = divmod(core, 4)
        out[b, 2048 * q:2048 * (q + 1)] = np.asarray(res.results[core]["out"], np.float32).reshape(2048, D)
    return out
```

```python
import math
import numpy as np
import ml_dtypes
import concourse.bass as bass
import concourse.mybir as mybir
from concourse.bass_utils import run_bass_kernel_spmd

F32 = mybir.dt.float32
BF16 = mybir.dt.bfloat16
ALU = mybir.AluOpType
AF = mybir.ActivationFunctionType
AX = mybir.AxisListType


class Tile:
    __slots__ = ("t", "writers", "readers", "name")

    def __init__(self, t, name=""):
        self.t = t
        self.writers = []
        self.readers = []
        self.name = name

    def __getitem__(self, idx):
        return self.t[idx]


class Eng:
    def __init__(self, mk, name, eng, sem):
        self.mk, self.name, self.eng, self.sem = mk, name, eng, sem
        self.count = 0
        self.seen = {}


class MK:
    N_DMA_SEMS = 24

    def __init__(self, nc):
        self.nc = nc
        self.engs = {}
        for name, eng in (("pe", nc.tensor), ("act", nc.scalar), ("dve", nc.vector),
                          ("pool", nc.gpsimd), ("sp", nc.sync)):
            self.engs[name] = Eng(self, name, eng, nc.alloc_semaphore("s_" + name))
        self.dma_sems = [nc.alloc_semaphore("d%d" % i) for i in range(2 * self.N_DMA_SEMS)]
        self.dma_cnt = [0] * (2 * self.N_DMA_SEMS)
        self.dma_i = {"sw": 0, "hw": 0}
        self.ntile = 0

    def sb(self, shape, dt=BF16, name=None):
        self.ntile += 1
        name = name or "t%d" % self.ntile
        return Tile(self.nc.alloc_sbuf_tensor(name + "_%d" % self.ntile, list(shape), dt), name)

    def ps(self, shape, dt=F32, name=None):
        self.ntile += 1
        name = name or "p%d" % self.ntile
        return Tile(self.nc.alloc_psum_tensor(name + "_%d" % self.ntile, list(shape), dt), name)

    def dram(self, shape, dt, name, kind="Internal"):
        return Tile(self.nc.dram_tensor(name, list(shape), dt, kind=kind), name)

    def _wait(self, E, tok):
        sem, val = tok
        key = sem.num
        if E.seen.get(key, 0) >= val:
            return
        E.seen[key] = val
        E.eng.wait_ge(sem, val)

    def _deps(self, E, reads, writes):
        toks = []
        for t in reads:
            toks += t.writers
        for t in writes:
            toks += t.writers
            toks += t.readers
        best = {}
        for sem, val in toks:
            if best.get(sem.num, (None, 0))[1] < val:
                best[sem.num] = (sem, val)
        for sem, val in best.values():
            if E.name == "pe" and sem.num == E.sem.num:
                continue
            self._wait(E, (sem, val))

    def _mark(self, tok, reads, writes):
        for t in reads:
            t.readers.append(tok)
            if len(t.readers) > 64:
                t.readers = _compress(t.readers)
        for t in writes:
            t.writers = [tok]
            t.readers = []

    def op(self, eng, fn, reads=(), writes=(), inc=True):
        E = self.engs[eng]
        self._deps(E, reads, writes)
        ins = fn(E.eng)
        if inc:
            E.count += 1
            ins.then_inc(E.sem, 1)
            tok = (E.sem, E.count)
        else:
            assert eng == "pe"
            tok = (E.sem, E.count + 1)
        self._mark(tok, reads, writes)
        return tok

    def dma(self, eng, out, in_, reads=(), writes=(), **kw):
        E = self.engs[eng]
        self._deps(E, reads, writes)
        pool = "sw" if eng == "pool" else "hw"
        i = self.dma_i[pool] % self.N_DMA_SEMS + (self.N_DMA_SEMS if pool == "sw" else 0)
        self.dma_i[pool] += 1
        sem = self.dma_sems[i]
        if self.dma_cnt[i] > 0:
            self._wait(E, (sem, 16 * self.dma_cnt[i]))
        self.dma_cnt[i] += 1
        E.eng.dma_start(out=out, in_=in_, **kw).then_inc(sem, 16)
        tok = (sem, 16 * self.dma_cnt[i])
        self._mark(tok, reads, writes)
        return tok

    def barrier(self):
        for E in self.engs.values():
            for F in self.engs.values():
                if F is not E and F.count > 0:
                    self._wait(E, (F.sem, F.count))
            for i, sem in enumerate(self.dma_sems):
                if self.dma_cnt[i] > 0:
                    self._wait(E, (sem, 16 * self.dma_cnt[i]))

    def finish(self):
        E = self.engs["sp"]
        for F in self.engs.values():
            if F is not E and F.count > 0:
                self._wait(E, (F.sem, F.count))
        for i, sem in enumerate(self.dma_sems):
            if self.dma_cnt[i] > 0:
                self._wait(E, (sem, 16 * self.dma_cnt[i]))


def _compress(toks):
    best = {}
    for sem, val in toks:
        if best.get(sem.num, (None, 0))[1] < val:
            best[sem.num] = (sem, val)
    return list(best.values())


D = 1024
SEQ = 8192
NB = 2
CTX = 256
NOWN = 2304
NQ = NOWN + CTX
NKEY = SEQ + CTX
EPS = 1e-6
MLA_SCALE = 1.0 / math.sqrt(96.0)
GQA_SCALE = 1.0 / 8.0
NEG = -30000.0


def _bf(a):
    return np.ascontiguousarray(np.asarray(a, dtype=np.float32)).astype(ml_dtypes.bfloat16)


def _f32(a):
    return np.ascontiguousarray(np.asarray(a, dtype=np.float32))


def _rope_tab(tok, nf):
    tok = np.asarray(tok, dtype=np.float64)
    row = np.floor(tok / 64.0)
    col = tok - 64.0 * row
    inv = 10000.0 ** (-np.arange(nf, dtype=np.float64) / nf)
    ang = np.concatenate([row[:, None] * inv, col[:, None] * inv], axis=-1)
    return np.cos(ang).T, np.sin(ang).T


def host_tables(q):
    T = {}
    T["ident"] = _bf(np.eye(128))
    T["identf"] = _f32(np.eye(128))
    n1 = np.arange(128)[:, None, None]
    n2 = np.arange(64)[None, :, None]
    k1 = np.arange(128)[None, None, :]
    th = 2 * np.pi * ((k1 * (64 * n1 + n2)) % SEQ) / SEQ
    T["T1"] = _bf(np.stack([np.cos(th), -np.sin(th)], axis=2))
    c = np.arange(128)[:, None]
    j = np.arange(128)[None, :]
    ph = 2 * np.pi * ((c * j) % 128) / 128
    Cc, Sc = np.cos(ph), np.sin(ph)
    T["CS"] = _bf(np.stack([np.concatenate([Cc, -Sc], 1), np.concatenate([Sc, Cc], 1)], 1))
    k2 = (16 * q - 1 + np.arange(18)) % 64
    ps = 2 * np.pi * ((np.arange(64)[:, None] * k2[None, :]) % 64) / 64
    f2 = np.stack([np.cos(ps), np.sin(ps)], 1) / 1024.0
    f2b = np.zeros((128, 2, 36))
    f2b[0:64, :, 0:18] = f2
    f2b[64:128, :, 18:36] = f2
    T["F2"] = _bf(f2b)
    n = (np.arange(2)[None, :, None] * 128 + np.arange(128)[:, None, None])
    k = np.arange(256)[None, None, :]
    a = 2 * np.pi * ((n * k) % 256) / 256
    sc = 1.0 / math.sqrt(256.0 * 128.0)
    T["DC"] = _bf(np.stack([np.cos(a) * sc, -np.sin(a) * sc], 2))
    tk = np.zeros((66, 2, 32, 128), np.float32)
    tk[0:2, 0] = 1.0
    for t in range(64):
        cs, sn = _rope_tab(64 * np.arange(128) + t, 8)
        tk[2 + t, 0] = np.concatenate([cs, cs], 0)
        tk[2 + t, 1] = np.concatenate([-sn, sn], 0)
    T["tabK"] = tk
    t0 = 2048 * q - 128
    cs, sn = _rope_tab(t0 + np.arange(NOWN), 8)
    tq = np.zeros((2, 32, NQ), np.float32)
    tq[0, :, NOWN:] = 1.0
    tq[0, :, :NOWN] = np.concatenate([cs, cs], 0)
    tq[1, :, :NOWN] = np.concatenate([-sn, sn], 0)
    T["tabQ"] = tq
    cs, sn = _rope_tab(t0 + np.arange(NOWN), 16)
    t1 = np.zeros((2, 128, NQ), np.float32)
    t1[0, :, NOWN:] = 1.0
    t1[0, :, :NOWN] = np.concatenate([cs, cs, cs, cs], 0)
    t1[1, :, :NOWN] = np.concatenate([-sn, sn, -sn, sn], 0)
    T["tab1"] = t1
    kp = np.arange(128)[:, None]
    qp = np.arange(128)[None, :]
    mprev = np.where(kp >= qp, 0.0, NEG)
    mnext = np.where(kp <= qp, 0.0, NEG)
    full = np.full((128, 128), NEG)
    zero = np.zeros((128, 128))
    rel = {-1: full, 0: mprev, 1: zero, 2: mnext, 3: full}
    T["maskb"] = _bf(np.stack([np.stack([np.stack([rel[d - j] for j in range(2)], 1) for _ in range(2)], 1) for d in range(4)], 1))
    tok = t0 + np.arange(NOWN)
    kv = np.where((tok >= 0) & (tok < SEQ), 0.0, NEG).reshape(18, 128).T
    T["kvalid"] = _f32(np.concatenate([kv, np.zeros((128, 2))], 1))
    sel = np.zeros((2, 2, 128), np.float32)
    sel[0, 0] = 1.0
    sel[1, 1] = 1.0
    T["sel"] = sel
    return T


def pack_weights(inp):
    W = {}
    w = np.asarray(inp["e_w_in"][0], np.float32)
    kpe = w[:, 1664:1696]
    W["w0in"] = _f32(np.concatenate([w, kpe[:, 16:32], kpe[:, 0:16]], 1))
    wq = np.asarray(inp["e_w_qb"][0], np.float32)
    sw = []
    for h in range(8):
        pe = wq[:, h * 96 + 64:h * 96 + 96]
        sw += [pe[:, 16:32], pe[:, 0:16]]
    W["wqb"] = _f32(np.concatenate([wq] + sw, 1))
    W["wkvb"] = _f32(inp["e_w_kvb"][0])
    W["wout0"] = _f32(inp["e_w_out"][0])
    w1 = np.asarray(inp["o_w_in"][0], np.float32)
    qs, k2, k2s = [], [], []
    for h in range(16):
        qh = w1[:, h * 64:(h + 1) * 64]
        qs += [qh[:, 32:64], qh[:, 0:32]]
    for h in range(4):
        kh = w1[:, 1024 + h * 64:1024 + (h + 1) * 64]
        k2 += [kh, kh]
        k2s += [kh[:, 32:64], kh[:, 0:32]] * 2
    W["w1in"] = _f32(np.concatenate([w1] + qs + k2 + k2s, 1))
    W["wout1"] = _f32(inp["o_w_out"][0])
    W["wmod"] = _f32(inp["w_mod"])
    W["bmod"] = _f32(inp["b_mod"])
    W["normg"] = _f32(inp["norm_g"])
    W["finalg"] = _f32(np.asarray(inp["final_g"], np.float32)[None, :])
    W["qnorm"] = _f32(np.asarray(inp["e_q_norm"][0], np.float32).reshape(3, 128).T)
    W["kvnorm"] = _f32(np.asarray(inp["e_kv_norm"][0], np.float32).reshape(2, 128).T)
    W["sink"] = _f32(inp["o_sink"])
    return W


def core_inputs(inp, W, b, q):
    m = dict(W)
    m.update(host_tables(q))
    x = np.asarray(inp["x"], np.float32)[b]
    m["xa"] = _f32(x.reshape(128, 64, D).transpose(1, 0, 2))
    t0 = 2048 * q - 128
    xo = np.zeros((NOWN, D), np.float32)
    lo, hi = max(t0, 0), min(t0 + NOWN, SEQ)
    xo[lo - t0:hi - t0] = x[lo:hi]
    m["xo"] = xo.reshape(18, 128, D)
    m["ctx"] = _f32(np.asarray(inp["ctx"], np.float32)[b].reshape(2, 128, D))
    cc = np.stack([np.asarray(inp["c"], np.float32)[b], np.asarray(inp["c_ctx"], np.float32)], 0)
    m["cT"] = _f32(cc.reshape(2, 8, 128).transpose(2, 1, 0))
    return m


from contextlib import ExitStack

IN_SPECS = [
    ("xa", [64, 128, D], F32), ("xo", [18, 128, D], F32), ("ctx", [2, 128, D], F32), ("cT", [128, 8, 2], F32),
    ("w0in", [D, 2240], F32), ("wqb", [384, 1024], F32), ("wkvb", [256, 1024], F32), ("wout0", [D, D], F32),
    ("w1in", [D, 4608], F32), ("wout1", [D, D], F32), ("wmod", [2, D, 3072], F32), ("bmod", [2, 3072], F32),
    ("normg", [2, D], F32), ("finalg", [1, D], F32), ("qnorm", [128, 3], F32), ("kvnorm", [128, 2], F32),
    ("sink", [1, 16], F32),
    ("ident", [128, 128], BF16), ("identf", [128, 128], F32), ("T1", [128, 64, 2, 128], BF16),
    ("CS", [128, 2, 256], BF16), ("F2", [128, 2, 36], BF16), ("DC", [128, 2, 2, 256], BF16),
    ("tabK", [66, 2, 32, 128], F32), ("tabQ", [2, 32, NQ], F32), ("tab1", [2, 128, NQ], F32),
    ("maskb", [128, 4, 2, 2, 128], BF16), ("kvalid", [128, 20], F32), ("sel", [2, 2, 128], F32),
]

QBLK = [(0, 512), (512, 512), (1024, 512), (1536, 512), (2048, 512)]


class Prog:
    def __init__(self, debug=False, upto=99):
        self.debug = debug
        self.upto = upto
        nc = self.nc = bass.Bass("TRN2", target_bir_lowering=False)
        mk = self.mk = MK(nc)
        self.din = {n: mk.dram(s, dt, n, "ExternalInput") for n, s, dt in IN_SPECS}
        self.out = mk.dram([16, 128, D], F32, "out", "ExternalOutput")
        dk = "ExternalOutput" if debug else "Internal"
        self.qan_d = mk.dram([3, 128, NQ], BF16, "qan_d", dk)
        self.fg_d = mk.dram([4, 128, NQ], BF16, "fg_d", dk)
        self.mg_d = mk.dram([4, 128, NQ], BF16, "mg_d", dk)
        self.kvn_d = mk.dram([2, 128, NKEY], BF16, "kvn_d", dk)
        self.kpe_d = mk.dram([32, NKEY], BF16, "kpe_d", dk)
        self.cat_d = mk.dram([8, 128, NQ], BF16, "cat_d", dk)
        self.x1_d = mk.dram([20, 128, D], F32, "x1_d", dk)
        self.pTs = [mk.ps([128, 1024], BF16, "pT%d" % i) for i in range(2)]
        self.P = [mk.ps([128, 512], F32, "P%d" % i) for i in range(6)]
        self.pi = 0
        self.held = set()
        self.es = ExitStack()
        self.cnt = 0
        P_ = self.sbp
        self.ident = P_([128, 128], BF16); self.onesb = P_([128, 128], BF16); self.onesf = P_([128, 64], F32)
        self.epsb = P_([128, 1], F32)
        self.G0 = {k: P_([128, D], F32) for k in ("Gl", "Gc")}
        self.bcs = {}
        mk.dma("sp", self.ident[:], self.din["ident"][:, :], writes=[self.ident])
        mk.op("pool", lambda e: e.memset(self.onesb[:], 1.0), writes=[self.onesb])
        mk.op("pool", lambda e: e.memset(self.onesf[:], 1.0), writes=[self.onesf])
        mk.op("pool", lambda e: e.memset(self.epsb[:], EPS), writes=[self.epsb])

    def sbp(self, shape, dt=BF16):
        self.cnt += 1
        return Tile(self.nc.alloc_sbuf_tensor("pers%d" % self.cnt, list(shape), dt))

    def sb(self, es, shape, dt=BF16):
        self.cnt += 1
        return Tile(es.enter_context(self.nc.sbuf_tensor("s%d" % self.cnt, list(shape), dt)))

    def bank(self, hold=False):
        while True:
            self.pi = (self.pi + 1) % 6
            if self.pi not in self.held:
                break
        if hold:
            self.held.add(self.pi)
        return self.P[self.pi]

    def release(self, p):
        self.held.discard(self.P.index(p))

    def wload(self, dst, dcol, src, r0, nk, c0, n):
        v = src.t.ap()[r0:r0 + nk * 128, c0:c0 + n].rearrange("(c p) n -> p c n", p=128)
        self.mk.dma("pool", dst[:, 0:nk, dcol:dcol + n], v, writes=[dst], max_dma_last_dim=4096)

    def alloc_bc(self, l, es_ab):
        bc = self.bcs[l] = {}
        for k in ("Al", "Bl", "Ac", "Bc", "Gl", "Gc"):
            if l == 0 and k[0] == "G":
                bc[k] = self.G0[k]
            else:
                bc[k] = self.sb(es_ab, [128, D], F32)

    def mod_steps(self, l, es_ab, es):
        mk, din = self.mk, self.din
        if l not in self.bcs:
            self.alloc_bc(l, es_ab)
        bc = self.bcs[l]
        wms = [self.sb(es, [128, 8, 512], F32) for _ in range(2)]
        cT = self.sb(es, [128, 16], F32); scT = self.sb(es, [128, 16], F32)
        bm = self.sb(es, [2, 3072], F32); g2 = self.sb(es, [2, D], F32); sel = self.sb(es, [2, 2, 128], F32)
        mrow = self.sb(es, [2, 3072], F32); arow = self.sb(es, [2, D], F32)
        steps = []

        def first():
            mk.dma("sp", cT[:], din["cT"].t.ap().rearrange("p c v -> p (c v)"), writes=[cT])
            mk.op("act", lambda e: e.activation(out=scT[:], in_=cT[:], func=AF.Silu), reads=[cT], writes=[scT])
            for r in range(2):
                mk.dma("sp", bm[r:r + 1, :], din["bmod"][l:l + 1, :], writes=[bm])
                mk.dma("sp", g2[r:r + 1, :], din["normg"][l:l + 1, :], writes=[g2])
            mk.dma("sp", sel[:], din["sel"].t.ap(), writes=[sel])
        steps.append(first)
        for nb in range(6):
            def ld(nb=nb):
                wm = wms[nb % 2]
                mk.dma("sp", wm[:], din["wmod"].t.ap()[l, :, nb * 512:(nb + 1) * 512].rearrange("(c p) n -> p c n", p=128), writes=[wm])
            def blk(nb=nb):
                wm = wms[nb % 2]
                p = self.bank()
                for kc in range(8):
                    mk.op("pe", lambda e: e.matmul(out=p[0:2, :], lhsT=scT[:, 2 * kc:2 * kc + 2],
                                                   rhs=wm[:, kc, :], start=(kc == 0), stop=(kc == 7)),
                          reads=[scT, wm], writes=[p], inc=(kc == 7))
                mk.op("dve", lambda e: e.tensor_tensor(out=mrow[:, nb * 512:(nb + 1) * 512], in0=p[0:2, :],
                                                       in1=bm[:, nb * 512:(nb + 1) * 512], op=ALU.add),
                      reads=[p, bm], writes=[mrow])
            steps.append(ld)
            steps.append(blk)
        order = [steps[0], steps[1]]
        for nb in range(6):
            if nb + 1 < 6:
                order.append(steps[1 + 2 * (nb + 1)])
            order.append(steps[2 + 2 * nb])

        def arow_f():
            mk.op("dve", lambda e: e.scalar_tensor_tensor(out=arow[:], in0=mrow[:, D:2 * D], scalar=1.0, in1=g2[:],
                                                          op0=ALU.add, op1=ALU.mult), reads=[mrow, g2], writes=[arow])
        order.append(arow_f)
        for which, sfx in ((0, "l"), (1, "c")):
            for key, src, off in (("A", arow, 0), ("B", mrow, 0), ("G", mrow, 2 * D)):
                def bcf(which=which, sfx=sfx, key=key, src=src, off=off):
                    dst = bc[key + sfx]
                    for half in range(2):
                        p = self.bank()
                        mk.op("pe", lambda e: e.matmul(out=p[:], lhsT=sel[:, which, :],
                                                       rhs=src[:, off + half * 512:off + (half + 1) * 512],
                                                       start=True, stop=True), reads=[sel, src], writes=[p])
                        mk.op("dve", lambda e: e.tensor_copy(out=dst[:, half * 512:(half + 1) * 512], in_=p[:]),
                              reads=[p], writes=[dst])
                order.append(bcf)
        return order

    def make_norm_scratch(self, es, n=3, nx=5):
        self.xts = [self.sb(es, [128, D], F32) for _ in range(nx)]
        self.ns = [dict(junk=self.sb(es, [128, D], BF16), ss=self.sb(es, [128, 1], F32),
                        rstd=self.sb(es, [128, 1], F32), tmp=self.sb(es, [128, D], F32), hb=self.sb(es, [128, D], BF16))
                   for _ in range(n)]

    def norm_s0(self, i, src_ap):
        xt = self.xts[i % len(self.xts)]
        self.mk.dma("sp", xt[:], src_ap, writes=[xt])

    def norm_s1(self, i, src_ap, A, B):
        mk = self.mk
        s = self.ns[i % len(self.ns)]
        xt = self.xts[i % len(self.xts)]
        junk, ss, rstd, tmp, hb = s["junk"], s["ss"], s["rstd"], s["tmp"], s["hb"]
        mk.op("act", lambda e: e.activation(out=junk[:], in_=xt[:], func=AF.Square, scale=1.0 / 32.0, accum_out=ss[:]),
              reads=[xt], writes=[junk, ss])
        mk.op("act", lambda e: e.activation(out=rstd[:], in_=ss[:], func=AF.Ln, bias=self.epsb[:], scale=1.0),
              reads=[ss, self.epsb], writes=[rstd])
        mk.op("act", lambda e: e.activation(out=rstd[:], in_=rstd[:], func=AF.Exp, scale=-0.5), reads=[rstd], writes=[rstd])
        mk.op("dve", lambda e: e.scalar_tensor_tensor(out=tmp[:], in0=xt[:], scalar=rstd[:, 0:1], in1=A[:],
                                                      op0=ALU.mult, op1=ALU.mult), reads=[xt, rstd, A], writes=[tmp])
        mk.op("pool", lambda e: e.tensor_tensor(out=hb[:], in0=tmp[:], in1=B[:], op=ALU.add), reads=[tmp, B], writes=[hb])

    def norm_s2(self, i, dst_tile, dst_ap):
        mk = self.mk
        hb = self.ns[i % len(self.ns)]["hb"]
        pT = self.pTs[i % 2]
        for c in range(8):
            mk.op("pe", lambda e: e.transpose(out=pT[:, c * 128:(c + 1) * 128], in_=hb[:, c * 128:(c + 1) * 128],
                                              identity=self.ident[:]), reads=[hb, self.ident], writes=[pT], inc=(c == 7))
        mk.op("act", lambda e: e.activation(out=dst_ap, in_=pT[:, :].rearrange("p (c n) -> p c n", c=8), func=AF.Copy),
              reads=[pT], writes=[dst_tile])

    def norm_pipe(self, n, src_fn, ab_fn, dst_fn, s3=None):
        for i in range(min(2, n)):
            self.norm_s0(i, src_fn(i))
        for t in range(n + 2):
            if t + 2 < n:
                self.norm_s0(t + 2, src_fn(t + 2))
            if t < n:
                A, B = ab_fn(t)
                self.norm_s1(t, src_fn(t), A, B)
            if 0 <= t - 1 < n:
                dt_, dap = dst_fn(t - 1)
                self.norm_s2(t - 1, dt_, dap)
            if s3 is not None and 0 <= t - 2 < n:
                s3(t - 2)

    def proj(self, p_ap, ptile, w, wcols, h, hcols, n_c=8):
        for c in range(n_c):
            self.mk.op("pe", lambda e: e.matmul(out=p_ap, lhsT=w[:, c, wcols[0]:wcols[1]], rhs=h[:, c, hcols[0]:hcols[1]],
                                                start=(c == 0), stop=(c == n_c - 1)), reads=[w, h], writes=[ptile], inc=(c == n_c - 1))

    def rstd_bcast(self, es_tiles, pss, n, dim, out_tile):
        mk = self.mk
        mk.op("act", lambda e: e.activation(out=out_tile[:, 0:n], in_=pss[:, 0:n], func=AF.Ln, bias=self.epsb[:], scale=1.0 / dim),
              reads=[pss, self.epsb], writes=[out_tile])
        mk.op("act", lambda e: e.activation(out=out_tile[:, 0:n], in_=out_tile[:, 0:n], func=AF.Exp, scale=-0.5),
              reads=[out_tile], writes=[out_tile])

    def phase_O(self):
        mk, din = self.mk, self.din
        with ExitStack() as es:
            hT = self.sb(es, [128, 8, NQ], BF16)
            wps = [self.sb(es, [128, 8, 512], BF16) for _ in range(3)]
            self.wload(wps[0], 0, din["w0in"], 0, 8, 512, 512)
            self.wload(wps[1], 0, din["w0in"], 0, 8, 1696, 512)
            self.wload(wps[2], 0, din["w0in"], 0, 8, 1024, 384)
            with ExitStack() as esn:
                self.make_norm_scratch(esn)
                self.norm_pipe(20, lambda i: din["xo"].t.ap()[i] if i < 18 else din["ctx"].t.ap()[i - 18],
                               lambda i: (self.bc["Al" if i < 18 else "Ac"], self.bc["Bl" if i < 18 else "Bc"]),
                               lambda i: (hT, hT[:, :, i * 128:(i + 1) * 128]))
                mk.barrier()
            stage = [self.sb(es, [128, 512], BF16) for _ in range(3)]
            si = 0
            for wp, dst in ((wps[0], self.fg_d), (wps[1], self.mg_d)):
                for j in range(4):
                    for b0, n in QBLK:
                        p = self.bank()
                        self.proj(p[:, 0:n], p, wp, (j * 128, (j + 1) * 128), hT, (b0, b0 + n))
                        st = stage[si % 3]; si += 1
                        mk.op("act", lambda e: e.activation(out=st[:, 0:n], in_=p[:, 0:n], func=AF.Silu), reads=[p], writes=[st])
                        mk.dma("sp", dst.t.ap()[j, :, b0:b0 + n], st[:, 0:n], reads=[st], writes=[dst])
            wp = wps[2]
            sq = [self.sb(es, [128, 512], BF16) for _ in range(3)]
            rq = self.sb(es, [128, 512], F32)
            for b0, n in QBLK:
                ps = [self.bank() for _ in range(3)]
                pss = self.bank()
                for j in range(3):
                    self.proj(ps[j][:, 0:n], ps[j], wp, (j * 128, (j + 1) * 128), hT, (b0, b0 + n))
                    mk.op("act", lambda e: e.activation(out=sq[j][:, 0:n], in_=ps[j][:, 0:n], func=AF.Square), reads=[ps[j]], writes=[sq[j]])
                for j in range(3):
                    mk.op("pe", lambda e: e.matmul(out=pss[:, 0:n], lhsT=self.onesb[:], rhs=sq[j][:, 0:n], start=(j == 0), stop=(j == 2)),
                          reads=[self.onesb, sq[j]], writes=[pss])
                self.rstd_bcast(None, pss, n, 384.0, rq)
                for j in range(3):
                    st = stage[si % 3]; si += 1
                    mk.op("dve", lambda e: e.tensor_tensor(out=st[:, 0:n], in0=ps[j][:, 0:n], in1=rq[:, 0:n], op=ALU.mult),
                          reads=[ps[j], rq], writes=[st])
                    mk.dma("sp", self.qan_d.t.ap()[j, :, b0:b0 + n], st[:, 0:n], reads=[st], writes=[self.qan_d])
            mk.barrier()

    def phase_A(self, es_outer):
        mk, din = self.mk, self.din
        with ExitStack() as es:
            self.make_norm_scratch(es)
            wA = self.sb(es, [128, 8, 832], BF16)
            self.wload(wA, 0, din["w0in"], 0, 8, 0, 512)
            self.wload(wA, 512, din["w0in"], 0, 8, 1408, 288)
            self.wload(wA, 800, din["w0in"], 0, 8, 2208, 32)
            hTs = [self.sb(es, [128, 8, 128], BF16) for _ in range(3)]
            sqkv = [self.sb(es, [128, 256], BF16) for _ in range(2)]
            rk = [self.sb(es, [128, 128], F32) for _ in range(2)]
            kst = [self.sb(es, [128, 2, 128], BF16) for _ in range(2)]
            tabk = [self.sb(es, [128, 2, 128], F32) for _ in range(3)]
            t1 = [self.sb(es, [128, 128], F32) for _ in range(2)]
            t2 = [self.sb(es, [128, 128], F32) for _ in range(2)]
            pst = [self.sb(es, [128, 128], BF16) for _ in range(2)]

            def s3(i):
                lat = i >= 2
                hT = hTs[i % 3]
                tb = tabk[i % 3]
                mk.dma("sp", tb[64:96, :, :], din["tabK"].t.ap()[i].rearrange("t p n -> p t n"), writes=[tb])
                pkv = self.bank()
                for j in range(2):
                    self.proj(pkv[:, j * 128:(j + 1) * 128], pkv, wA, (512 + j * 128, 640 + j * 128), hT, (0, 128))
                sq = sqkv[i % 2]
                mk.op("act", lambda e: e.activation(out=sq[:], in_=pkv[:, 0:256], func=AF.Square), reads=[pkv], writes=[sq])
                p = self.bank()
                self.mk_tokmajor(p, hT, wA, 0, 512)
                udst, uap = (self.u_all, self.u_all[:, i - 2, :]) if lat else (self.u_ctx, self.u_ctx[:, i, :])
                mk.op("dve", lambda e: e.tensor_copy(out=uap, in_=p[:]), reads=[p], writes=[udst])
                pk = self.bank()
                self.proj(pk[64:96, 0:128], pk, wA, (768, 800), hT, (0, 128))
                self.proj(pk[64:96, 128:256], pk, wA, (800, 832), hT, (0, 128))
                a1, a2 = t1[i % 2], t2[i % 2]
                mk.op("dve", lambda e: e.tensor_tensor(out=a1[64:96, :], in0=pk[64:96, 0:128], in1=tb[64:96, 0, :], op=ALU.mult),
                      reads=[pk, tb], writes=[a1])
                mk.op("dve", lambda e: e.tensor_tensor(out=a2[64:96, :], in0=pk[64:96, 128:256], in1=tb[64:96, 1, :], op=ALU.mult),
                      reads=[pk, tb], writes=[a2])
                ps_ = pst[i % 2]
                mk.op("pool", lambda e: e.tensor_tensor(out=ps_[64:96, :], in0=a1[64:96, :], in1=a2[64:96, :], op=ALU.add),
                      reads=[a1, a2], writes=[ps_])
                mk.dma("sp", self.kpe_d.t.ap()[:, i * 128:(i + 1) * 128], ps_[64:96, :], reads=[ps_], writes=[self.kpe_d])
                pss = self.bank()
                for j in range(2):
                    mk.op("pe", lambda e: e.matmul(out=pss[:, 0:128], lhsT=self.onesb[:], rhs=sq[:, j * 128:(j + 1) * 128],
                                                   start=(j == 0), stop=(j == 1)), reads=[self.onesb, sq], writes=[pss], inc=(j == 1))
                r_ = rk[i % 2]
                self.rstd_bcast(None, pss, 128, 256.0, r_)
                ks = kst[i % 2]
                for j in range(2):
                    mk.op("dve", lambda e: e.tensor_tensor(out=ks[:, j, :], in0=pkv[:, j * 128:(j + 1) * 128], in1=r_[:], op=ALU.mult),
                          reads=[pkv, r_], writes=[ks])
                mk.dma("sp", self.kvn_d.t.ap()[:, :, i * 128:(i + 1) * 128].rearrange("j p n -> p j n"), ks[:], reads=[ks], writes=[self.kvn_d])

            self.norm_pipe(66, lambda i: din["xa"].t.ap()[i - 2] if i >= 2 else din["ctx"].t.ap()[i],
                           lambda i: (self.bc["Al" if i >= 2 else "Ac"], self.bc["Bl" if i >= 2 else "Bc"]),
                           lambda i: (hTs[i % 3], hTs[i % 3][:, :, :]), s3=s3)
            mk.barrier()

    def mk_tokmajor(self, p, hT, w, c0, n):
        for c in range(8):
            self.mk.op("pe", lambda e: e.matmul(out=p[:, 0:n], lhsT=hT[:, c, :], rhs=w[:, c, c0:c0 + n], start=(c == 0), stop=(c == 7)),
                       reads=[hT, w], writes=[p], inc=(c == 7))

    def phase_F(self):
        mk, din = self.mk, self.din
        with ExitStack() as es:
            T1 = self.sb(es, [128, 64, 256], BF16)
            for k in range(4):
                mk.dma("sp", T1[:, k * 16:(k + 1) * 16, :], din["T1"].t.ap()[:, k * 16:(k + 1) * 16].rearrange("p a r k -> p a (r k)"), writes=[T1])
            CS = self.sb(es, [128, 2, 256], BF16); F2 = self.sb(es, [128, 2, 36], BF16); DC = self.sb(es, [128, 2, 512], BF16)
            mk.dma("sp", CS[:], din["CS"].t.ap(), writes=[CS])
            mk.dma("sp", F2[:], din["F2"].t.ap(), writes=[F2])
            mk.dma("sp", DC[:], din["DC"].t.ap().rearrange("p t r k -> p t (r k)"), writes=[DC])
            fg = self.sb(es, [128, 4, NQ], BF16)
            mk.dma("sp", fg[:], self.fg_d.t.ap().rearrange("j p n -> p j n"), reads=[self.fg_d], writes=[fg])
            A1 = self.sb(es, [128, 128, 128], BF16)
            A1w = A1[:, :, :].rearrange("p (a n) (r k) -> p n a r k", a=2, r=2)
            Gb = [self.sb(es, [128, 2, 256], BF16) for _ in range(3)]
            catF = [self.sb(es, [128, NOWN], BF16) for _ in range(2)]
            XT = self.sb(es, [128, 512], BF16); cst = self.sb(es, [128, 256], BF16)
            for g in range(4):
                for pr in range(32):
                    p = self.bank()
                    for s in range(2):
                        n2 = 2 * pr + s
                        mk.op("pe", lambda e: e.matmul(out=p[:, s * 256:(s + 1) * 256], lhsT=self.u_all[:, n2, g * 128:(g + 1) * 128],
                                                       rhs=T1[:, n2, :], start=True, stop=True), reads=[self.u_all, T1], writes=[p], inc=(s == 1))
                    eng = "dve" if pr % 2 == 0 else "act"
                    for s_ in range(2):
                        oap = A1w[:, 2 * pr + s_]
                        iap = p[:, s_ * 256:(s_ + 1) * 256].rearrange("p (r k a) -> p a r k", r=2, a=2)
                        if s_ == 0:
                            mk.op("dve", lambda e: e.tensor_copy(out=oap, in_=iap), reads=[p], writes=[A1])
                        else:
                            mk.op("act", lambda e: e.activation(out=oap, in_=iap, func=AF.Copy), reads=[p], writes=[A1])
                cf = catF[g % 2]
                cfv = cf[:, :].rearrange("p (k2 k1) -> p k1 k2", k1=128)
                fgv = fg[:, g, 0:NOWN].rearrange("p (k2 k1) -> p k1 k2", k1=128)
                fst = {"pacc": None, "a0": 0}

                def s3(pr, gb, fst=fst, cf=cf, cfv=cfv, fgv=fgv):
                    for s_ in range(2):
                        kp = 2 * pr + s_
                        k1 = 2 * kp
                        if k1 % 28 == 0:
                            fst["pacc"] = self.bank(hold=True); fst["a0"] = k1
                        pacc, a0 = fst["pacc"], fst["a0"]
                        sl = (k1 - a0) * 18
                        mk.op("pe", lambda e: e.matmul(out=pacc[:, sl:sl + 36], lhsT=gb[:, s_, 0:128], rhs=F2[:, 0, :], start=True, stop=False),
                              reads=[gb, F2], writes=[pacc], inc=False)
                        mk.op("pe", lambda e: e.matmul(out=pacc[:, sl:sl + 36], lhsT=gb[:, s_, 128:256], rhs=F2[:, 1, :], start=False, stop=True),
                              reads=[gb, F2], writes=[pacc])
                        if (k1 + 1) % 28 == 27 or k1 + 1 == 127:
                            cnt = k1 + 2 - a0
                            mk.op("dve", lambda e: e.tensor_tensor(out=cfv[:, a0:a0 + cnt, :], in0=pacc[:, 0:cnt * 18].rearrange("p (a b) -> p a b", b=18),
                                                                   in1=fgv[:, a0:a0 + cnt, :], op=ALU.mult), reads=[pacc, fg], writes=[cf])
                            self.release(pacc)

                prev = None
                for pr in range(32):
                    p = self.bank()
                    for s_ in range(2):
                        kp = 2 * pr + s_
                        mk.op("pe", lambda e: e.matmul(out=p[:, s_ * 256:(s_ + 1) * 256], lhsT=A1[:, :, kp], rhs=CS[:, 0, :], start=True, stop=False),
                              reads=[A1, CS], writes=[p], inc=False)
                        mk.op("pe", lambda e: e.matmul(out=p[:, s_ * 256:(s_ + 1) * 256], lhsT=A1[:, :, 64 + kp], rhs=CS[:, 1, :], start=False, stop=True),
                              reads=[A1, CS], writes=[p], inc=(s_ == 1))
                    gb = Gb[pr % 3]
                    if pr % 2 == 0:
                        mk.op("dve", lambda e: e.tensor_copy(out=gb[:, :, :], in_=p[:, :].rearrange("p (a b) -> p a b", a=2)), reads=[p], writes=[gb])
                    else:
                        mk.op("act", lambda e: e.activation(out=gb[:, :, :], in_=p[:, :].rearrange("p (a b) -> p a b", a=2), func=AF.Copy), reads=[p], writes=[gb])
                    if prev is not None:
                        s3(*prev)
                    prev = (pr, gb)
                s3(*prev)
                mk.dma("sp", self.cat_d.t.ap()[g, :, 0:NOWN], cf[:, :], reads=[cf], writes=[self.cat_d])
                px = self.bank()
                for nt in range(2):
                    mk.op("pe", lambda e: e.matmul(out=px[:, :], lhsT=self.u_ctx[:, nt, g * 128:(g + 1) * 128], rhs=DC[:, nt, :], start=(nt == 0), stop=(nt == 1)),
                          reads=[self.u_ctx, DC], writes=[px], inc=(nt == 1))
                mk.op("dve", lambda e: e.tensor_copy(out=XT[:], in_=px[:]), reads=[px], writes=[XT])
                py = self.bank()
                mk.op("pe", lambda e: e.matmul(out=py[:, 0:256], lhsT=CS[:, 0, 0:128], rhs=XT[:, 0:256], start=True, stop=False), reads=[CS, XT], writes=[py], inc=False)
                mk.op("pe", lambda e: e.matmul(out=py[:, 0:256], lhsT=CS[:, 1, 0:128], rhs=XT[:, 256:512], start=False, stop=True), reads=[CS, XT], writes=[py])
                mk.op("dve", lambda e: e.tensor_tensor(out=cst[:], in0=py[:, 0:256], in1=fg[:, g, NOWN:NQ], op=ALU.mult), reads=[py, fg], writes=[cst])
                mk.dma("sp", self.cat_d.t.ap()[g, :, NOWN:NQ], cst[:], reads=[cst], writes=[self.cat_d])
            mk.barrier()

    def wscaled(self, es, src, nk, ncols, normname):
        mk = self.mk
        w = self.sb(es, [128, nk, ncols], BF16)
        g = self.sb(es, [128, nk], F32)
        self.wload(w, 0, src, 0, nk, 0, ncols)
        mk.dma("sp", g[:], self.din[normname].t.ap(), writes=[g])
        for c in range(nk):
            mk.op("dve", lambda e: e.tensor_scalar(out=w[:, c, :], in0=w[:, c, :], scalar1=g[:, c:c + 1], scalar2=None, op0=ALU.mult),
                  reads=[w, g], writes=[w])
        return w

    def phase_ATT(self):
        mk, din = self.mk, self.din
        with ExitStack() as es:
            kvn = self.sb(es, [128, 2, NKEY], BF16)
            KTs = [self.sb(es, [128, NKEY], BF16) for _ in range(2)]
            VAs = [self.sb(es, [128, 66, 128], BF16) for _ in range(2)]
            qan = self.sb(es, [128, 3, NQ], BF16)
            QTs = [self.sb(es, [128, NQ], BF16) for _ in range(2)]
            tabq = self.sb(es, [128, 2, NQ], F32)
            mgs = [self.sb(es, [128, NQ], BF16) for _ in range(2)]
            ATs = [self.sb(es, [128, NQ], BF16)] * 2
            for j in range(2):
                mk.dma("sp", kvn[:, j, :], self.kvn_d.t.ap()[j], reads=[self.kvn_d], writes=[kvn])
            for KT in KTs:
                mk.dma("sp", KT[64:96, :], self.kpe_d.t.ap(), reads=[self.kpe_d], writes=[KT])
            mk.dma("sp", qan[:], self.qan_d.t.ap().rearrange("j p n -> p j n"), reads=[self.qan_d], writes=[qan])
            mk.dma("sp", tabq[64:96, :, :], din["tabQ"].t.ap().rearrange("t p n -> p t n"), writes=[tabq])
            wq = self.wscaled(es, din["wqb"], 3, 1024, "qnorm")
            wkv = self.wscaled(es, din["wkvb"], 2, 1024, "kvnorm")
            mk.op("dve", lambda e: e.memset(VAs[0][:, :, 64:128], 0.0), writes=[VAs[0]])
            mk.op("dve", lambda e: e.memset(VAs[0][:, :, 64:65], 1.0), writes=[VAs[0]])
            mk.op("dve", lambda e: e.memset(VAs[1][:, :, 0:64], 0.0), writes=[VAs[1]])
            mk.op("dve", lambda e: e.memset(VAs[1][:, :, 0:1], 1.0), writes=[VAs[1]])
            PTs = [self.sb(es, [128, 512], BF16) for _ in range(4)]
            t1 = self.sb(es, [128, 512], F32); t2 = self.sb(es, [128, 512], F32)
            rec = self.sb(es, [128, 512], F32); tmp = self.sb(es, [128, 512], F32)

            def gen_steps(h):
                odd = h % 2
                o0 = 64 if odd else 0
                KT, VA, QT = KTs[odd], VAs[odd], QTs[odd]
                steps = []
                if not odd:
                    mgt = mgs[(h // 2) % 2]
                    steps.append(lambda: mk.dma("sp", mgt[:], self.mg_d.t.ap()[h // 2], reads=[self.mg_d], writes=[mgt]))
                for kb in range(17):
                    def f(kb=kb):
                        k0 = kb * 512
                        n = min(512, NKEY - k0)
                        p = self.bank()
                        self.proj(p[0:64, 0:n], p, wkv, (h * 128, h * 128 + 64), kvn, (k0, k0 + n), n_c=2)
                        mk.op("dve", lambda e: e.tensor_copy(out=KT[0:64, k0:k0 + n], in_=p[0:64, 0:n]), reads=[p], writes=[KT])
                    steps.append(f)
                for g0 in range(0, 66, 8):
                    def f(g0=g0):
                        cnt = min(8, 66 - g0)
                        p = self.bank()
                        for t in range(cnt):
                            kt = g0 + t
                            for j in range(2):
                                mk.op("pe", lambda e: e.matmul(out=p[:, t * 64:(t + 1) * 64], lhsT=kvn[:, j, kt * 128:(kt + 1) * 128],
                                                               rhs=wkv[:, j, h * 128 + 64:h * 128 + 128], start=(j == 0), stop=(j == 1)),
                                      reads=[kvn, wkv], writes=[p], inc=(j == 1 and t == cnt - 1))
                        mk.op("dve", lambda e: e.tensor_copy(out=VA[:, g0:g0 + cnt, o0:o0 + 64], in_=p[:, 0:cnt * 64].rearrange("p (a b) -> p a b", b=64)),
                              reads=[p], writes=[VA])
                    steps.append(f)
                for b0, n in QBLK:
                    def f(b0=b0, n=n):
                        pa = self.bank(); pb = self.bank()
                        self.proj(pa[0:96, 0:n], pa, wq, (h * 96, h * 96 + 96), qan, (b0, b0 + n), n_c=3)
                        self.proj(pb[64:96, 0:n], pb, wq, (768 + h * 32, 800 + h * 32), qan, (b0, b0 + n), n_c=3)
                        mk.op("dve", lambda e: e.tensor_copy(out=QT[0:64, b0:b0 + n], in_=pa[0:64, 0:n]), reads=[pa], writes=[QT])
                        mk.op("dve", lambda e: e.tensor_tensor(out=t1[64:96, 0:n], in0=pa[64:96, 0:n], in1=tabq[64:96, 0, b0:b0 + n], op=ALU.mult),
                              reads=[pa, tabq], writes=[t1])
                        mk.op("dve", lambda e: e.tensor_tensor(out=t2[64:96, 0:n], in0=pb[64:96, 0:n], in1=tabq[64:96, 1, b0:b0 + n], op=ALU.mult),
                              reads=[pb, tabq], writes=[t2])
                        mk.op("pool", lambda e: e.tensor_tensor(out=QT[64:96, b0:b0 + n], in0=t1[64:96, 0:n], in1=t2[64:96, 0:n], op=ALU.add),
                              reads=[t1, t2], writes=[QT])
                    steps.append(f)
                return steps

            for f in gen_steps(0):
                f()
            items = []
            for h in range(8):
                for bi, (b0, n, kts) in enumerate([(0, 512, 66), (512, 512, 66), (1024, 512, 66), (1536, 512, 66), (2048, 256, 66), (2304, 256, 2)]):
                    for kt in range(kts):
                        items.append((h, b0, n, kt, kts))
            pending = []
            deferred = []
            deferred2 = []
            recs = [rec, self.sb(es, [128, 512], F32)]
            nfin = 0
            state = {}
            LAG = 2
            nit = len(items)
            per_head = nit // 8
            defA = []
            for t in range(nit + LAG + 2):
                for f in defA:
                    f()
                defA = []
                if t < nit:
                    h, b0, n, kt, kts = items[t]
                    odd = h % 2
                    if kt == 0:
                        state[(h, b0)] = self.bank(hold=True)
                    if t % per_head == 0 and h < 7:
                        pending = gen_steps(h + 1)
                        every = max(1, (per_head - 40) // len(pending))
                    if pending and (t % per_head) % every == 0:
                        pending.pop(0)()
                    ps = self.bank()
                    mk.op("pe", lambda e: e.matmul(out=ps[:, 0:n], lhsT=KTs[odd][0:96, kt * 128:(kt + 1) * 128], rhs=QTs[odd][0:96, b0:b0 + n], start=True, stop=True),
                          reads=[KTs[odd], QTs[odd]], writes=[ps])
                    pt = PTs[t % 4]
                    mk.op("act", lambda e: e.activation(out=pt[:, 0:n], in_=ps[:, 0:n], func=AF.Exp, scale=MLA_SCALE), reads=[ps], writes=[pt])
                while deferred:
                    deferred.pop(0)()
                deferred, deferred2 = deferred2, []
                if LAG <= t < nit + LAG:
                    h, b0, n, kt, kts = items[t - LAG]
                    odd = h % 2
                    o0, r0 = (64, 0) if odd else (0, 64)
                    po = state[(h, b0)]
                    pt = PTs[(t - LAG) % 4]
                    mk.op("pe", lambda e: e.matmul(out=po[:, 0:n], lhsT=VAs[odd][:, kt, :], rhs=pt[:, 0:n], start=(kt == 0), stop=(kt == kts - 1)),
                          reads=[VAs[odd], pt], writes=[po], inc=(kt == kts - 1))
                    if kt == kts - 1:
                        mgt = mgs[(h // 2) % 2]; AT = ATs[(h // 2) % 2]
                        fa = dict(po=po, n=n, o0=o0, r0=r0, gate_tile=mgt, gate_ap=mgt[o0:o0 + 64, b0:b0 + n], dst_tile=AT,
                                  dst_ap=AT[o0:o0 + 64, b0:b0 + n], rec=recs[nfin % 2], tmp=tmp)
                        nfin += 1

                        def fb(fa=fa, po=po, odd=odd, b0=b0, h=h, AT=AT):
                            self.att_finish(part="B", **fa)
                            self.release(po)
                            if odd and b0 == 2304:
                                mk.dma("sp", self.cat_d.t.ap()[4 + h // 2], AT[:], reads=[AT], writes=[self.cat_d])

                        def fa_(fa=fa, fb=fb):
                            self.att_finish(part="A", **fa)
                            deferred2.append(fb)
                        defA.append(fa_)
            while deferred:
                deferred.pop(0)()
            while pending:
                pending.pop(0)()
            mk.barrier()

    def att_finish(self, po, n, o0, r0, gate_tile, gate_ap, dst_tile, dst_ap, rec, tmp, extra=None, part=None):
        mk = self.mk
        if part in (None, "A"):
            self._att_finish_a(po, n, r0, rec, extra)
        if part in (None, "B"):
            self._att_finish_b(po, n, o0, r0, gate_tile, gate_ap, dst_tile, dst_ap, rec, tmp)

    def _att_finish_a(self, po, n, r0, rec, extra):
        mk = self.mk
        if extra is not None:
            mk.op("dve", lambda e: e.tensor_tensor(out=rec[r0:r0 + 1, 0:n], in0=po[r0:r0 + 1, 0:n], in1=extra, op=ALU.add), reads=[po], writes=[rec])
            mk.op("act", lambda e: e.activation(out=rec[r0:r0 + 1, 0:n], in_=rec[r0:r0 + 1, 0:n], func=AF.Ln), reads=[rec], writes=[rec])
        else:
            mk.op("act", lambda e: e.activation(out=rec[r0:r0 + 1, 0:n], in_=po[r0:r0 + 1, 0:n], func=AF.Ln), reads=[po], writes=[rec])
        mk.op("act", lambda e: e.activation(out=rec[r0:r0 + 1, 0:n], in_=rec[r0:r0 + 1, 0:n], func=AF.Exp, scale=-1.0), reads=[rec], writes=[rec])

    def _att_finish_b(self, po, n, o0, r0, gate_tile, gate_ap, dst_tile, dst_ap, rec, tmp):
        mk = self.mk
        pb = self.bank()
        mk.op("pe", lambda e: e.matmul(out=pb[o0:o0 + 64, 0:n], lhsT=self.onesf[r0:r0 + 1, 0:64], rhs=rec[r0:r0 + 1, 0:n], start=True, stop=True),
              reads=[self.onesf, rec], writes=[pb])
        mk.op("dve", lambda e: e.tensor_tensor(out=tmp[o0:o0 + 64, 0:n], in0=po[o0:o0 + 64, 0:n], in1=gate_ap, op=ALU.mult),
              reads=[po, gate_tile], writes=[tmp])
        mk.op("dve", lambda e: e.tensor_tensor(out=dst_ap, in0=tmp[o0:o0 + 64, 0:n], in1=pb[o0:o0 + 64, 0:n], op=ALU.mult),
              reads=[tmp, pb], writes=[dst_tile])

    def phase_OUT(self, wname, xsrc, ntiles, tile0, final=False, G=None, side=None):
        mk, din = self.mk, self.din
        with ExitStack() as es:
            wo = self.sb(es, [128, 8, D], BF16)
            self.wload(wo, 0, din[wname], 0, 8, 0, 512)
            self.wload(wo, 512, din[wname], 0, 8, 512, 512)
            catb = [self.sb(es, [128, 8, 512], BF16) for _ in range(3)]
            xts = [self.sb(es, [128, D], F32) for _ in range(4)]
            tmp = [self.sb(es, [128, D], F32) for _ in range(2)]
            xn = [self.sb(es, [128, D], F32) for _ in range(2)]
            junk = self.sb(es, [128, D], BF16); ss = self.sb(es, [128, 1], F32); rstd = self.sb(es, [128, 1], F32)
            fgb = None
            if final:
                fgb = self.sb(es, [128, D], F32)
                fr = self.sb(es, [1, D], F32)
                mk.dma("sp", fr[:], din["finalg"].t.ap(), writes=[fr])
                self.bcast_row(fr, fgb)
            side_steps = side(es) if side is not None else []
            loaded = set()

            def load(ti):
                gi = tile0 + ti
                blk = gi // 4
                if blk not in loaded:
                    loaded.add(blk)
                    cb = catb[blk % 3]
                    mk.dma("sp", cb[:], self.cat_d.t.ap()[:, :, blk * 512:(blk + 1) * 512].rearrange("c p n -> p c n"), reads=[self.cat_d], writes=[cb])
                xt = xts[ti % 4]
                mk.dma("sp", xt[:], xsrc(gi), reads=[self.x1_d], writes=[xt])

            for ti in range(min(2, ntiles)):
                load(ti)
            for ti in range(ntiles):
                if ti + 2 < ntiles:
                    load(ti + 2)
                if side_steps:
                    side_steps.pop(0)()
                gi = tile0 + ti
                blk, t = divmod(gi, 4)
                cb = catb[blk % 3]
                xt = xts[ti % 4]; tm = tmp[ti % 2]; xo = xn[ti % 2]
                Gt = G["Gl" if gi < 18 else "Gc"]
                for half in range(2):
                    p = self.bank()
                    for c in range(8):
                        mk.op("pe", lambda e: e.matmul(out=p[:], lhsT=cb[:, c, t * 128:(t + 1) * 128], rhs=wo[:, c, half * 512:(half + 1) * 512],
                                                       start=(c == 0), stop=(c == 7)), reads=[cb, wo], writes=[p], inc=(c == 7))
                    mk.op("dve", lambda e: e.tensor_tensor(out=tm[:, half * 512:(half + 1) * 512], in0=p[:], in1=Gt[:, half * 512:(half + 1) * 512], op=ALU.mult),
                          reads=[p, Gt], writes=[tm])
                mk.op("pool", lambda e: e.tensor_tensor(out=xo[:], in0=tm[:], in1=xt[:], op=ALU.add), reads=[tm, xt], writes=[xo])
                if not final:
                    mk.dma("sp", self.x1_d.t.ap()[gi], xo[:], reads=[xo], writes=[self.x1_d])
                else:
                    mk.op("act", lambda e: e.activation(out=junk[:], in_=xo[:], func=AF.Square, scale=1.0 / 32.0, accum_out=ss[:]), reads=[xo], writes=[junk, ss])
                    mk.op("act", lambda e: e.activation(out=rstd[:], in_=ss[:], func=AF.Ln, bias=self.epsb[:], scale=1.0), reads=[ss, self.epsb], writes=[rstd])
                    mk.op("act", lambda e: e.activation(out=rstd[:], in_=rstd[:], func=AF.Exp, scale=-0.5), reads=[rstd], writes=[rstd])
                    mk.op("dve", lambda e: e.scalar_tensor_tensor(out=tm[:], in0=xo[:], scalar=rstd[:, 0:1], in1=fgb[:], op0=ALU.mult, op1=ALU.mult),
                          reads=[xo, rstd, fgb], writes=[tm])
                    mk.dma("sp", self.out.t.ap()[ti], tm[:], reads=[tm], writes=[self.out])
            while side_steps:
                side_steps.pop(0)()
            mk.barrier()

    def bcast_row(self, row, dst):
        mk = self.mk
        for half in range(2):
            p = self.bank()
            mk.op("pe", lambda e: e.matmul(out=p[0:64, :], lhsT=self.onesf[0:1, 0:64], rhs=row[0:1, half * 512:(half + 1) * 512], start=True, stop=True),
                  reads=[self.onesf, row], writes=[p])
            mk.op("pe", lambda e: e.matmul(out=p[64:128, :], lhsT=self.onesf[0:1, 0:64], rhs=row[0:1, half * 512:(half + 1) * 512], start=True, stop=True),
                  reads=[self.onesf, row], writes=[p])
            mk.op("dve", lambda e: e.tensor_copy(out=dst[:, half * 512:(half + 1) * 512], in_=p[:]), reads=[p], writes=[dst])

    def layer0(self):
        with ExitStack() as es:
            self.u_all = self.sb(es, [128, 64, 512], BF16)
            self.u_ctx = self.sb(es, [128, 2, 512], BF16)
            with ExitStack() as es_ab:
                with ExitStack() as es_tmp:
                    for f in self.mod_steps(0, es_ab, es_tmp):
                        f()
                    self.mk.barrier()
                self.bc = self.bcs[0]
                self.phase_O()
                self.phase_A(es)
            self.phase_F()
        self.mk.barrier()
        self.phase_ATT()
        din = self.din
        self.es_ab1 = ExitStack()
        self.alloc_bc(1, self.es_ab1)
        self.phase_OUT("wout0", lambda gi: din["xo"].t.ap()[gi] if gi < 18 else din["ctx"].t.ap()[gi - 18], 20, 0,
                       G=self.bcs[0], side=lambda es_tmp: self.mod_steps(1, self.es_ab1, es_tmp))

    def phase_L1(self):
        mk, din = self.mk, self.din
        g1_d = self.g1_d
        with ExitStack() as es:
            QT1 = self.sb(es, [128, 8, NOWN], BF16)
            KT1 = self.sb(es, [128, 4, NQ], BF16)
            VA1 = self.sb(es, [128, 20, 4, 2, 128], BF16)
            mk.op("pool", lambda e: e.memset(VA1[:].rearrange("p a b c d -> p (a b c d)"), 0.0), writes=[VA1])
            mk.op("pool", lambda e: e.memset(VA1[:, :, :, 0, 64:65], 1.0), writes=[VA1])
            mk.op("pool", lambda e: e.memset(VA1[:, :, :, 1, 0:1], 1.0), writes=[VA1])
            self.bc = self.bcs[1]
            with ExitStack() as es2:
                hT = self.sb(es2, [128, 8, NQ], BF16)
                with ExitStack() as es3:
                    self.make_norm_scratch(es3, n=2, nx=3)
                    self.norm_pipe(20, lambda i: self.x1_d.t.ap()[i],
                                   lambda i: (self.bc["Al" if i < 18 else "Ac"], self.bc["Bl" if i < 18 else "Bc"]),
                                   lambda i: (hT, hT[:, :, i * 128:(i + 1) * 128]))
                    mk.barrier()
                tab1 = self.sb(es2, [128, 2, NQ], F32)
                mk.dma("sp", tab1[:], din["tab1"].t.ap().rearrange("t p n -> p t n"), writes=[tab1])
                wps = [self.sb(es2, [128, 8, 256], BF16) for _ in range(2)]
                g0l, g0c = self.G0["Gl"], self.G0["Gc"]
                st = [self.sb(es2, [128, 512], BF16) for _ in range(2)]
                jobs = [("q", hp, [(0, hp * 128, 128), (128, 2560 + hp * 128, 128)]) for hp in range(8)]
                jobs += [("k", kh, [(0, 3584 + kh * 128, 128), (128, 4096 + kh * 128, 128)]) for kh in range(4)]
                jobs += [("g", hp, [(0, 1536 + hp * 128, 128)]) for hp in range(8)]
                jobs += [("v", 0, [(0, 1280, 256)])]

                def jload(ji):
                    for dcol, c0, n in jobs[ji][2]:
                        self.wload(wps[ji % 2], dcol, din["w1in"], 0, 8, c0, n)

                jload(0)
                si = 0
                ri = 0
                for ji, (kind, idx, _) in enumerate(jobs):
                    wp = wps[ji % 2]
                    first = True
                    if kind in ("q", "k"):
                        for b0, n in QBLK:
                            if kind == "q" and b0 >= NOWN:
                                continue
                            n_ = min(n, NOWN - b0) if kind == "q" else n
                            t1 = g0l[:, (ri % 2) * 512:(ri % 2) * 512 + n_]; t2 = g0c[:, (ri % 2) * 512:(ri % 2) * 512 + n_]
                            ri += 1
                            pa = self.bank(); pb = self.bank()
                            self.proj(pa[:, 0:n_], pa, wp, (0, 128), hT, (b0, b0 + n_))
                            self.proj(pb[:, 0:n_], pb, wp, (128, 256), hT, (b0, b0 + n_))
                            if first and ji + 1 < len(jobs):
                                jload(ji + 1); first = False
                            mk.op("dve", lambda e: e.tensor_tensor(out=t1, in0=pa[:, 0:n_], in1=tab1[:, 0, b0:b0 + n_], op=ALU.mult), reads=[pa, tab1], writes=[g0l])
                            mk.op("dve", lambda e: e.tensor_tensor(out=t2, in0=pb[:, 0:n_], in1=tab1[:, 1, b0:b0 + n_], op=ALU.mult), reads=[pb, tab1], writes=[g0c])
                            dst, dap = (QT1, QT1[:, idx, b0:b0 + n_]) if kind == "q" else (KT1, KT1[:, idx, b0:b0 + n_])
                            mk.op("pool", lambda e: e.tensor_tensor(out=dap, in0=t1, in1=t2, op=ALU.add), reads=[g0l, g0c], writes=[dst])
                    elif kind == "g":
                        for b0, n in QBLK:
                            p = self.bank()
                            self.proj(p[:, 0:n], p, wp, (0, 128), hT, (b0, b0 + n))
                            if first and ji + 1 < len(jobs):
                                jload(ji + 1); first = False
                            s_ = st[si % 2]; si += 1
                            mk.op("act", lambda e: e.activation(out=s_[:, 0:n], in_=p[:, 0:n], func=AF.Silu), reads=[p], writes=[s_])
                            mk.dma("sp", g1_d.t.ap()[idx, :, b0:b0 + n], s_[:, 0:n], reads=[s_], writes=[g1_d])
                    else:
                        for i in range(20):
                            p = self.bank()
                            for c in range(8):
                                mk.op("pe", lambda e: e.matmul(out=p[:, 0:256], lhsT=hT[:, c, i * 128:(i + 1) * 128], rhs=wp[:, c, 0:256], start=(c == 0), stop=(c == 7)),
                                      reads=[hT, wp], writes=[p], inc=(c == 7))
                            pv = p[:, 0:256].rearrange("p (a b) -> p a b", b=64)
                            mk.op("dve", lambda e: e.tensor_copy(out=VA1[:, i, :, 0, 0:64], in_=pv), reads=[p], writes=[VA1])
                            mk.op("act", lambda e: e.activation(out=VA1[:, i, :, 1, 64:128], in_=pv, func=AF.Copy), reads=[p], writes=[VA1])
                mk.barrier()
            maskb = self.sb(es, [128, 4, 512], BF16); kval = self.sb(es, [128, 20], F32)
            mk.dma("sp", maskb[:], din["maskb"].t.ap().rearrange("p d s j q -> p d (s j q)"), writes=[maskb])
            mk.dma("sp", kval[:], din["kvalid"].t.ap(), writes=[kval])
            snk = self.sb(es, [128, 16], F32); esk = self.sb(es, [128, 16], F32)
            for r in (0, 64):
                mk.dma("sp", snk[r:r + 1, :], din["sink"].t.ap(), writes=[snk])
            mk.op("act", lambda e: e.activation(out=esk[0:1, :], in_=snk[0:1, :], func=AF.Exp), reads=[snk], writes=[esk])
            mk.op("act", lambda e: e.activation(out=esk[64:65, :], in_=snk[64:65, :], func=AF.Exp), reads=[snk], writes=[esk])
            esr = self.sb(es, [128, 8, 512], F32)
            mk.op("pool", lambda e: e.memset(esr[:].rearrange("p a b -> p (a b)"), 0.0), writes=[esr])
            for kh in range(4):
                for par in range(2):
                    for k_ in range(2):
                        hq = 4 * kh + 2 * k_ + par
                        for r in (0, 64):
                            sl = esr[r:r + 1, kh * 2 + par, k_ * 256:(k_ + 1) * 256]
                            mk.op("dve", lambda e: e.tensor_scalar(out=sl, in0=sl, scalar1=esk[r:r + 1, hq:hq + 1], scalar2=None, op0=ALU.add),
                                  reads=[esk, esr], writes=[esr])
            gt = [self.sb(es, [128, 2, NQ], BF16) for _ in range(2)]
            AT = [self.sb(es, [128, 2, 2048], BF16) for _ in range(2)]
            PTs = [self.sb(es, [128, 512], BF16) for _ in range(18)]
            rec = self.sb(es, [128, 512], F32); tmp = self.sb(es, [128, 512], F32)
            items = [(kh, par, nb) for kh in range(4) for par in range(2) for nb in range(0, 16, 2)]
            st = {}
            ni = len(items)

            def fin_args(u):
                kh, par, nb = items[u]
                g_ = gt[kh % 2]; at = AT[kh % 2]
                o0, r0 = (64, 0) if par else (0, 64)
                q0 = 128 + 128 * nb
                return dict(po=st[u], n=512, o0=o0, r0=r0, gate_tile=g_, gate_ap=g_[o0:o0 + 64, :, q0:q0 + 256], dst_tile=at,
                            dst_ap=at[o0:o0 + 64, :, nb * 128:(nb + 2) * 128], rec=rec, tmp=tmp, extra=esr[r0:r0 + 1, kh * 2 + par, :])

            for t in range(ni + 3):
                if 0 <= t - 3 < ni:
                    self.att_finish(part="A", **fin_args(t - 3))
                if t < ni:
                    kh, par, nb = items[t]
                    if par == 0 and nb == 0:
                        g_ = gt[kh % 2]
                        mk.dma("sp", g_[:], g1_d.t.ap()[2 * kh:2 * kh + 2].rearrange("c p n -> p c n"), reads=[g1_d], writes=[g_])
                    p0 = par * 64
                    q0 = 128 + 128 * nb
                    kts = [(18, None), (19, None), (nb, 0), (nb + 1, 1), (nb + 2, 2), (nb + 3, 3)]
                    for ki, (kt, mi) in enumerate(kts):
                        ps = self.bank()
                        mk.op("pe", lambda e: e.matmul(out=ps[:, :], lhsT=KT1[p0:p0 + 64, kh, kt * 128:(kt + 1) * 128],
                                                       rhs=QT1[p0:p0 + 64, 2 * kh:2 * kh + 2, q0:q0 + 256], start=True, stop=(mi is None)),
                              reads=[KT1, QT1], writes=[ps], inc=(mi is None))
                        if mi is not None:
                            mk.op("pe", lambda e: e.matmul(out=ps[:, :], lhsT=self.ident[:], rhs=maskb[:, mi, :], start=False, stop=True),
                                  reads=[self.ident, maskb], writes=[ps])
                        pt = PTs[(t % 3) * 6 + ki]
                        mk.op("act", lambda e: e.activation(out=pt[:], in_=ps[:, :], func=AF.Exp, scale=GQA_SCALE, bias=kval[:, kt:kt + 1]),
                              reads=[ps, kval], writes=[pt])
                if 0 <= t - 3 < ni:
                    u = t - 3
                    self.att_finish(part="B", **fin_args(u))
                    self.release(st.pop(u))
                    kh, par, nb = items[u]
                    if par == 1 and nb == 14:
                        at = AT[kh % 2]
                        mk.dma("sp", self.cat_d.t.ap()[2 * kh:2 * kh + 2, :, 128:2176].rearrange("c p n -> p c n"), at[:], reads=[at], writes=[self.cat_d])
                if 0 <= t - 1 < ni:
                    u = t - 1
                    kh, par, nb = items[u]
                    po = st[u] = self.bank(hold=True)
                    kts = [18, 19, nb, nb + 1, nb + 2, nb + 3]
                    for ki, kt in enumerate(kts):
                        pt = PTs[(u % 3) * 6 + ki]
                        mk.op("pe", lambda e: e.matmul(out=po[:, :], lhsT=VA1[:, kt, kh, par, :], rhs=pt[:], start=(ki == 0), stop=(ki == 5)),
                              reads=[VA1, pt], writes=[po], inc=(ki == 5))
            mk.barrier()

    def layer1(self):
        self.phase_L1()
        x1 = self.x1_d
        self.phase_OUT("wout1", lambda gi: x1.t.ap()[gi], 16, 1, final=True, G=self.bcs[1])
        self.es_ab1.close()


def build_program(debug=False, upto=99):
    pr = Prog(debug=debug, upto=upto)
    pr.g1_d = pr.mk.dram([8, 128, NQ], BF16, "g1_d", "Internal")
    pr.layer0()
    if upto >= 1:
        pr.layer1()
    pr.mk.finish()
    return pr


def kernel(**inputs):
    W = pack_weights(inputs)
    in_maps = []
    for core in range(8):
        b, q = divmod(core, 4)
        m = core_inputs(inputs, W, b, q)
        in_maps.append({n: m[n] for n, _, _ in IN_SPECS})
    pr = build_program()
    res = run_bass_kernel_spmd(pr.nc, in_maps, core_ids=list(range(8)))
    out = np.zeros((NB, SEQ, D), np.float32)
    for core in range(8):
        b, q = divmod(core, 4)
        out[b, 2048 * q:2048 * (q + 1)] = np.asarray(res.results[core]["out"], np.float32).reshape(2048, D)
    return out
```
